# Optimizing a Trainium2 kernel written in Bass

```python
import jax, jax.numpy as jnp
from jax import lax
import numpy as np

D_MODEL = 1024
BATCH = 8
SEQ = 2048
DEPTH = 4

HEAD_DIM = 64
MIX_WIDTH = D_MODEL
RET_WIDTH = MIX_WIDTH // 4
RET_HEADS = RET_WIDTH // HEAD_DIM
RET_CHUNK = 128
CONV_WIDTH = MIX_WIDTH // 4
CONV_K = 3
NSA_WIDTH = MIX_WIDTH // 2
NSA_HEADS = NSA_WIDTH // HEAD_DIM
NSA_KV_HEADS = 2
NSA_GROUP = NSA_HEADS // NSA_KV_HEADS
NSA_KV_WIDTH = NSA_KV_HEADS * HEAD_DIM
CMP_LEN = 32
CMP_STRIDE = 16
CMP_HIDDEN = 2 * HEAD_DIM
SEL_LEN = 64
SEL_TOP = 8
WINDOW = 512
Q_BLOCK = 128
ROPE_THETA = 10000.0
N_EXPERTS = 32
TOP_K = 4
D_EXPERT = D_MODEL
SWIGLU_LIMIT = 7.0
SWIGLU_ALPHA = 1.702
MOE_BLOCK = 128
DEEPNORM_ALPHA = (2 * DEPTH) ** 0.25
DEEPNORM_BETA = (8 * DEPTH) ** -0.25
LN_EPS = 1e-5
NEG_INF = -1e30
FORCE_SCORE = 1e9
MIX_SPLITS = (RET_WIDTH,) * 4 + (CONV_WIDTH,) * 3 + (NSA_WIDTH,) + (NSA_KV_WIDTH,) * 6 + (NSA_HEADS * 3,)
MIX_VALUE_COLS = (2, 6, 9, 11, 13)

kernel_name = 'hybrid_retention_conv_nsa_moe_trunk'


def layer_norm(x):
    xf = x.astype(jnp.float32)
    mu = jnp.mean(xf, -1, keepdims=True)
    var = jnp.mean(jnp.square(xf - mu), -1, keepdims=True)
    return (xf - mu) * lax.rsqrt(var + LN_EPS)


def rope(t, positions):
    half = t.shape[-1] // 2
    inv = ROPE_THETA ** (-jnp.arange(half, dtype=jnp.float32) / half)
    ang = positions.astype(jnp.float32)[..., None] * inv
    cos = jnp.cos(ang)[:, :, None, :]
    sin = jnp.sin(ang)[:, :, None, :]
    t1 = t[..., :half].astype(jnp.float32)
    t2 = t[..., half:].astype(jnp.float32)
    return jnp.concatenate([t1 * cos - t2 * sin, t2 * cos + t1 * sin], -1).astype(t.dtype)


def retention(q, k, v, positions, gn_w):
    B, S, H, Dh = q.shape
    f32 = jnp.float32
    n_chunks = S // RET_CHUNK
    q = rope(q, positions).astype(f32)
    k = rope(k, positions).astype(f32) * Dh ** -0.5
    v = v.astype(f32)
    log_gamma = jnp.log(1.0 - jnp.power(2.0, -5.0 - jnp.arange(H, dtype=f32)))
    i = jnp.arange(RET_CHUNK, dtype=f32)
    diff = i[:, None] - i[None, :]
    intra = jnp.where(diff >= 0, jnp.exp(diff * log_gamma[:, None, None]), 0.0)
    q_dec = jnp.exp((i + 1.0) * log_gamma[:, None])[..., None]
    k_dec = jnp.exp((RET_CHUNK - 1.0 - i) * log_gamma[:, None])[..., None]
    c_dec = jnp.exp(RET_CHUNK * log_gamma)[:, None, None]

    def chunks(t):
        return t.reshape(B, n_chunks, RET_CHUNK, H, Dh).transpose(1, 0, 3, 2, 4)

    def step(state, qkv):
        qc, kc, vc = qkv
        scores = jnp.einsum('bhid,bhjd->bhij', qc, kc) * intra
        out = jnp.einsum('bhij,bhje->bhie', scores, vc) + jnp.einsum('bhid,bhde->bhie', qc * q_dec, state)
        state = state * c_dec + jnp.einsum('bhjd,bhje->bhde', kc * k_dec, vc)
        return state, out

    state0 = jnp.zeros((B, H, Dh, Dh), f32)
    _, out = lax.scan(step, state0, (chunks(q), chunks(k), chunks(v)))
    out = out.transpose(1, 0, 3, 2, 4)
    return layer_norm(out).reshape(B, S, H * Dh) * gn_w


def short_conv(b_gate, c_gate, h, conv_w):
    u = c_gate * h
    y = lax.conv_general_dilated(u, conv_w[:, None, :].astype(u.dtype), (1,), [(CONV_K - 1, 0)],
                                 dimension_numbers=('NWC', 'WIO', 'NWC'), feature_group_count=CONV_WIDTH)
    return b_gate * y


def nsa_attention(q, k_cmp, v_cmp, k_slc, v_slc, k_win, v_win, gate_logits, positions, cmp_pos, cmp_w1, cmp_w2):
    B, S = q.shape[:2]
    f32 = jnp.float32
    n_cmp = (S - CMP_LEN) // CMP_STRIDE + 1
    n_slc = S // SEL_LEN
    k_sel = min(SEL_TOP, n_slc)
    n_qb = S // Q_BLOCK
    scale = HEAD_DIM ** -0.5

    cidx = np.arange(n_cmp)[:, None] * CMP_STRIDE + np.arange(CMP_LEN)[None, :]

    def compress(t, pos, w1, w2):
        blocks = t[:, cidx] + pos[:, None, :]
        blocks = blocks.transpose(0, 1, 3, 2, 4).reshape(B, n_cmp, NSA_KV_HEADS, CMP_LEN * HEAD_DIM)
        return jax.nn.gelu(blocks @ w1) @ w2

    kc = compress(k_cmp, cmp_pos[0], cmp_w1[0], cmp_w2[0])
    vc = compress(v_cmp, cmp_pos[1], cmp_w1[1], cmp_w2[1])
    cmp_start = jnp.arange(n_cmp) * CMP_STRIDE
    cmp_last = cmp_start + CMP_LEN - 1
    sel_start = jnp.arange(n_slc) * SEL_LEN
    overlap = ((cmp_start[:, None] < sel_start[None, :] + SEL_LEN) &
               (sel_start[None, :] <= cmp_last[:, None])).astype(f32)

    q_grp = q.reshape(B, S, NSA_KV_HEADS, NSA_GROUP, HEAD_DIM)
    q_rot = rope(q, positions).reshape(B, S, NSA_KV_HEADS, NSA_GROUP, HEAD_DIM)
    k_slc = rope(k_slc, positions)
    k_win = rope(k_win, positions)
    ks_b = k_slc.reshape(B, n_slc, SEL_LEN, NSA_KV_HEADS, HEAD_DIM).transpose(0, 3, 1, 2, 4)
    vs_b = v_slc.reshape(B, n_slc, SEL_LEN, NSA_KV_HEADS, HEAD_DIM).transpose(0, 3, 1, 2, 4)
    pad = ((0, 0), (WINDOW, 0), (0, 0), (0, 0))
    kw_pad = jnp.pad(k_win, pad)
    vw_pad = jnp.pad(v_win, pad)
    gates = jax.nn.sigmoid(gate_logits.reshape(B, S, NSA_KV_HEADS, NSA_GROUP, 3))
    b_ix = jnp.arange(B)[:, None, None, None]
    h_ix = jnp.arange(NSA_KV_HEADS)[None, :, None, None]
    blk = jnp.arange(n_slc)

    def block(qb_idx):
        q0 = qb_idx * Q_BLOCK
        t = q0 + jnp.arange(Q_BLOCK)
        qu = lax.dynamic_slice_in_dim(q_grp, q0, Q_BLOCK, axis=1)
        qr = lax.dynamic_slice_in_dim(q_rot, q0, Q_BLOCK, axis=1)
        gb = lax.dynamic_slice_in_dim(gates, q0, Q_BLOCK, axis=1)
        s = jnp.einsum('bqhgd,bchd->bhgqc', qu, kc, preferred_element_type=f32) * scale
        cvalid = cmp_last[None, :] <= t[:, None]
        p_c = jax.nn.softmax(jnp.where(cvalid, s, NEG_INF), axis=-1) * cvalid
        o_c = jnp.einsum('bhgqc,bchd->bqhgd', p_c.astype(vc.dtype), vc)
        imp = jnp.einsum('bhgqc,cj->bhqj', p_c, overlap)
        forced = (blk[None, :] == 0) | (blk[None, :] == (t // SEL_LEN)[:, None])
        svalid = blk[None, :] * SEL_LEN <= t[:, None]
        imp = jnp.where(forced, FORCE_SCORE, jnp.where(svalid, imp, -FORCE_SCORE))
        _, sel = lax.top_k(imp, k_sel)
        kg = ks_b[b_ix, h_ix, sel]
        vg = vs_b[b_ix, h_ix, sel]
        kpos = sel[..., None] * SEL_LEN + jnp.arange(SEL_LEN)
        smask = (kpos <= t[None, None, :, None, None])[:, :, None]
        s = jnp.einsum('bqhgd,bhqkld->bhgqkl', qr, kg, preferred_element_type=f32) * scale
        s = jnp.where(smask, s, NEG_INF).reshape(B, NSA_KV_HEADS, NSA_GROUP, Q_BLOCK, k_sel * SEL_LEN)
        p_s = jax.nn.softmax(s, axis=-1).reshape(B, NSA_KV_HEADS, NSA_GROUP, Q_BLOCK, k_sel, SEL_LEN)
        o_s = jnp.einsum('bhgqkl,bhqkld->bqhgd', p_s.astype(vg.dtype), vg)
        kw = lax.dynamic_slice_in_dim(kw_pad, q0, WINDOW + Q_BLOCK, axis=1)
        vw = lax.dynamic_slice_in_dim(vw_pad, q0, WINDOW + Q_BLOCK, axis=1)
        wpos = q0 - WINDOW + jnp.arange(WINDOW + Q_BLOCK)
        rel = t[:, None] - wpos[None, :]
        wmask = (wpos[None, :] >= 0) & (rel >= 0) & (rel < WINDOW)
        s = jnp.einsum('bqhgd,bkhd->bhgqk', qr, kw, preferred_element_type=f32) * scale
        p_w = jax.nn.softmax(jnp.where(wmask, s, NEG_INF), axis=-1)
        o_w = jnp.einsum('bhgqk,bkhd->bqhgd', p_w.astype(vw.dtype), vw)
        o = gb[..., 0:1] * o_c + gb[..., 1:2] * o_s + gb[..., 2:3] * o_w
        return o.reshape(B, Q_BLOCK, NSA_WIDTH)

    out = lax.map(block, jnp.arange(n_qb))
    return out.transpose(1, 0, 2, 3).reshape(B, S, NSA_WIDTH)


def hybrid_mixer(h, positions, w_in, w_out, ret_gn_w, conv_w, cmp_pos, cmp_w1, cmp_w2):
    B, S, _ = h.shape
    points = [int(p) for p in np.cumsum(MIX_SPLITS)[:-1]]
    (rq, rk, rv, rg, cb, cc, ch, nq, nkc, nvc, nks, nvs, nkw, nvw, ng) = jnp.split(h @ w_in, points, axis=-1)

    def heads(t, n):
        return t.reshape(B, S, n, HEAD_DIM)

    y_ret = jax.nn.silu(rg) * retention(heads(rq, RET_HEADS), heads(rk, RET_HEADS), heads(rv, RET_HEADS),
                                        positions, ret_gn_w).astype(h.dtype)
    y_conv = short_conv(cb, cc, ch, conv_w)
    y_nsa = nsa_attention(heads(nq, NSA_HEADS), heads(nkc, NSA_KV_HEADS), heads(nvc, NSA_KV_HEADS),
                          heads(nks, NSA_KV_HEADS), heads(nvs, NSA_KV_HEADS), heads(nkw, NSA_KV_HEADS),
                          heads(nvw, NSA_KV_HEADS), ng, positions, cmp_pos, cmp_w1, cmp_w2)
    return jnp.concatenate([y_ret, y_conv, y_nsa], axis=-1) @ w_out


def moe_ffn(h, router_w, router_b, w_gu, b_gu, w_down, b_down):
    B, S, D = h.shape
    T = B * S
    xt = h.reshape(T, D)
    logits = (xt @ router_w).astype(jnp.float32) + router_b.astype(jnp.float32)
    top_val, top_idx = lax.top_k(logits, TOP_K)
    gate = jax.nn.softmax(top_val, axis=-1)
    flat_e = top_idx.reshape(-1)
    flat_tok = jnp.repeat(jnp.arange(T, dtype=jnp.int32), TOP_K)
    flat_w = gate.reshape(-1)
    order = jnp.argsort(flat_e)
    se = flat_e[order]
    counts = jnp.bincount(flat_e, length=N_EXPERTS)
    start = jnp.cumsum(counts) - counts
    padded = (counts + MOE_BLOCK - 1) // MOE_BLOCK * MOE_BLOCK
    pend = jnp.cumsum(padded)
    pstart = pend - padded
    dest = pstart[se] + jnp.arange(T * TOP_K) - start[se]
    n_rows = T * TOP_K + N_EXPERTS * MOE_BLOCK
    n_blocks = n_rows // MOE_BLOCK
    row_tok = jnp.full((n_rows,), T, jnp.int32).at[dest].set(flat_tok[order])
    row_w = jnp.zeros((n_rows,), jnp.float32).at[dest].set(flat_w[order])
    block_e = jnp.minimum(jnp.searchsorted(pend, jnp.arange(n_blocks) * MOE_BLOCK, side='right'), N_EXPERTS - 1)
    rows = jnp.concatenate([xt, jnp.zeros((1, D), xt.dtype)], 0)[row_tok].reshape(n_blocks, MOE_BLOCK, D)

    def expert_block(args):
        r, e = args
        gu = r @ w_gu[e] + b_gu[e]
        g = jnp.minimum(gu[:, :D_EXPERT], SWIGLU_LIMIT)
        u = jnp.clip(gu[:, D_EXPERT:], -SWIGLU_LIMIT, SWIGLU_LIMIT)
        act = (u + 1.0) * (g * jax.nn.sigmoid(SWIGLU_ALPHA * g))
        return act @ w_down[e] + b_down[e]

    y = lax.map(expert_block, (rows, block_e)).reshape(n_rows, D)
    y = jax.ops.segment_sum(y.astype(jnp.float32) * row_w[:, None], row_tok, num_segments=T + 1)[:T]
    return y.astype(h.dtype).reshape(B, S, D)


def setup_inputs(seed: int = 0) -> dict:
    key = jax.random.key(seed)
    ks = jax.random.split(key, 20)
    f32 = jnp.float32

    def nrm(k, shape, s):
        return jax.random.normal(k, shape, f32) * s

    n_in = sum(MIX_SPLITS)
    col_scale = np.concatenate([np.full((w,), DEEPNORM_BETA if i in MIX_VALUE_COLS else 1.0, np.float32)
                                for i, w in enumerate(MIX_SPLITS)])
    x = nrm(ks[0], (BATCH, SEQ, D_MODEL), 1.0)
    c = nrm(ks[1], (BATCH, D_MODEL), 1.0)
    positions = (jnp.arange(SEQ, dtype=jnp.int32)[None, :] +
                 jax.random.randint(ks[2], (BATCH, 1), 0, 4096, dtype=jnp.int32))
    w_in = nrm(ks[3], (DEPTH, D_MODEL, n_in), D_MODEL ** -0.5) * jnp.asarray(col_scale)
    w_out = nrm(ks[4], (DEPTH, MIX_WIDTH, D_MODEL), MIX_WIDTH ** -0.5 * DEEPNORM_BETA)
    ret_gn_w = 1.0 + nrm(ks[5], (DEPTH, RET_WIDTH), 0.02)
    conv_w = nrm(ks[6], (DEPTH, CONV_K, CONV_WIDTH), CONV_K ** -0.5)
    cmp_pos = nrm(ks[7], (DEPTH, 2, CMP_LEN, HEAD_DIM), 0.02)
    cmp_w1 = nrm(ks[8], (DEPTH, 2, CMP_LEN * HEAD_DIM, CMP_HIDDEN), (CMP_LEN * HEAD_DIM) ** -0.5)
    cmp_w2 = nrm(ks[9], (DEPTH, 2, CMP_HIDDEN, HEAD_DIM), CMP_HIDDEN ** -0.5)
    ada_w = nrm(ks[10], (DEPTH, D_MODEL, 6 * D_MODEL), 0.01)
    ada_b = nrm(ks[11], (DEPTH, 6 * D_MODEL), 0.01)
    ln_g = 1.0 + nrm(ks[12], (DEPTH, 2, D_MODEL), 0.02)
    ln_b = nrm(ks[13], (DEPTH, 2, D_MODEL), 0.02)
    router_w = nrm(ks[14], (DEPTH, D_MODEL, N_EXPERTS), D_MODEL ** -0.5)
    router_b = nrm(ks[15], (DEPTH, N_EXPERTS), 0.01)
    w_gate_up = nrm(ks[16], (DEPTH, N_EXPERTS, D_MODEL, 2 * D_EXPERT), D_MODEL ** -0.5 * DEEPNORM_BETA)
    b_gate_up = nrm(ks[17], (DEPTH, N_EXPERTS, 2 * D_EXPERT), 0.01)
    w_down = nrm(ks[18], (DEPTH, N_EXPERTS, D_EXPERT, D_MODEL), D_EXPERT ** -0.5 * DEEPNORM_BETA)
    b_down = nrm(ks[19], (DEPTH, N_EXPERTS, D_MODEL), 0.01)
    return {'x': x, 'c': c, 'positions': positions, 'w_in': w_in, 'w_out': w_out, 'ret_gn_w': ret_gn_w,
            'conv_w': conv_w, 'cmp_pos': cmp_pos, 'cmp_w1': cmp_w1, 'cmp_w2': cmp_w2, 'ada_w': ada_w,
            'ada_b': ada_b, 'ln_g': ln_g, 'ln_b': ln_b, 'router_w': router_w, 'router_b': router_b,
            'w_gate_up': w_gate_up, 'b_gate_up': b_gate_up, 'w_down': w_down, 'b_down': b_down}


def reference(x, c, positions, w_in, w_out, ret_gn_w, conv_w, cmp_pos, cmp_w1, cmp_w2, ada_w, ada_b,
              ln_g, ln_b, router_w, router_b, w_gate_up, b_gate_up, w_down, b_down):
    c_act = jax.nn.silu(c)
    for l in range(DEPTH):
        mod = c_act @ ada_w[l] + ada_b[l]
        sh1, sc1, g1, sh2, sc2, g2 = jnp.split(mod, 6, axis=-1)
        h = (layer_norm(x) * (1.0 + sc1[:, None, :]) + sh1[:, None, :]).astype(x.dtype)
        mix = hybrid_mixer(h, positions, w_in[l], w_out[l], ret_gn_w[l], conv_w[l], cmp_pos[l], cmp_w1[l], cmp_w2[l])
        x = (layer_norm(DEEPNORM_ALPHA * x + (1.0 + g1[:, None, :]) * mix) * ln_g[l, 0] + ln_b[l, 0]).astype(x.dtype)
        h = (layer_norm(x) * (1.0 + sc2[:, None, :]) + sh2[:, None, :]).astype(x.dtype)
        ffn = moe_ffn(h, router_w[l], router_b[l], w_gate_up[l], b_gate_up[l], w_down[l], b_down[l])
        x = (layer_norm(DEEPNORM_ALPHA * x + (1.0 + g2[:, None, :]) * ffn) * ln_g[l, 1] + ln_b[l, 1]).astype(x.dtype)
    return x
```

```python
import math
import contextlib
import numpy as np
import ml_dtypes
import concourse.bass as bass
import concourse.mybir as mybir
from concourse.bass_utils import run_bass_kernel_spmd


ENGS = ("pe", "act", "dve", "pool", "sp")
NPOOL = 24


class Rec:
    def __init__(self, nc):
        self.nc = nc
        self.ops = []
        self.lastw = {}
        self.readers = {}
        self.ndma = 0
        self.dma_idx = []

    def eng_obj(self, e):
        nc = self.nc
        return {"pe": nc.tensor, "act": nc.scalar, "dve": nc.vector, "pool": nc.gpsimd, "sp": nc.sync}[e]

    def add(self, eng, fn, reads=(), writes=(), dma=False):
        idx = len(self.ops)
        deps = set()
        reads = list(reads) + ["BARRIER"]
        for r in reads:
            if r in self.lastw:
                deps.add(self.lastw[r])
        for w in writes:
            if w in self.lastw:
                deps.add(self.lastw[w])
            for rd in self.readers.get(w, ()):
                deps.add(rd)
        op = dict(eng=eng, fn=fn, deps=deps, dma=dma, used=False, slot=None)
        if dma:
            op["slot"] = self.ndma % NPOOL
            op["val"] = 16 * (self.ndma // NPOOL + 1)
            prev = self.ndma - NPOOL
            if prev >= 0:
                deps.add(self.dma_idx[prev])
            self.dma_idx.append(idx)
            self.ndma += 1
        self.ops.append(op)
        for r in reads:
            self.readers.setdefault(r, []).append(idx)
        for w in writes:
            self.lastw[w] = idx
            self.readers[w] = []
        return idx

    def barrier(self):
        return self.add("sp", lambda e: e.nop(), reads=(), writes=["BARRIER"])

    def pe(self, fn, reads=(), writes=()):
        return self.add("pe", fn, reads, writes)

    def act(self, fn, reads=(), writes=()):
        return self.add("act", fn, reads, writes)

    def dve(self, fn, reads=(), writes=()):
        return self.add("dve", fn, reads, writes)

    def pool(self, fn, reads=(), writes=()):
        return self.add("pool", fn, reads, writes)

    def dma(self, eng, fn, reads=(), writes=()):
        return self.add(eng, fn, reads, writes, dma=True)

    def emit(self, final_wait_ops=()):
        nc = self.nc
        ops = self.ops
        for i, op in enumerate(ops):
            for d in op["deps"]:
                dop = ops[d]
                if (not dop["dma"]) and dop["eng"] == "pe" and op["eng"] == "pe" and not op["dma"]:
                    continue
                dop["used"] = True
        for i in final_wait_ops:
            ops[i]["used"] = True
        cnt = {e: 0 for e in ENGS}
        for op in ops:
            if not op["dma"] and op["used"]:
                cnt[op["eng"]] += 1
                op["val"] = cnt[op["eng"]]
        import contextlib
        with contextlib.ExitStack() as st:
            esem = {e: st.enter_context(nc.semaphore("es_" + e)) for e in ENGS}
            dsem = [st.enter_context(nc.semaphore("ds_%d" % i)) for i in range(NPOOL)]
            block = st.enter_context(nc.Block())

            def semof(op):
                if op["dma"]:
                    return dsem[op["slot"]], op["val"], ("d", op["slot"])
                return esem[op["eng"]], op["val"], ("e", op["eng"])

            def run_engine(e, engobj):
                seen = {}
                for i, op in enumerate(ops):
                    if op["eng"] != e:
                        continue
                    need = {}
                    for d in op["deps"]:
                        dop = ops[d]
                        if (not dop["dma"]) and dop["eng"] == "pe" and e == "pe" and not op["dma"]:
                            continue
                        s, v, k = semof(dop)
                        if seen.get(k, 0) >= v:
                            continue
                        if k not in need or need[k][1] < v:
                            need[k] = (s, v)
                    for k, (s, v) in need.items():
                        engobj.wait_ge(s, v)
                        seen[k] = v
                    ins = op["fn"](engobj)
                    if op["dma"]:
                        ins.then_inc(dsem[op["slot"]], 16)
                    elif op["used"]:
                        ins.then_inc(esem[e], 1)
                if e == "sp":
                    for i in final_wait_ops:
                        s, v, k = semof(ops[i])
                        engobj.wait_ge(s, v)

            @block.tensor
            def _(eng):
                run_engine("pe", eng)

            @block.scalar
            def _(eng):
                run_engine("act", eng)

            @block.vector
            def _(eng):
                run_engine("dve", eng)

            @block.gpsimd
            def _(eng):
                run_engine("pool", eng)

            @block.sync
            def _(eng):
                run_engine("sp", eng)

F32 = mybir.dt.float32
BF16 = mybir.dt.bfloat16
I32 = mybir.dt.int32
AF = mybir.ActivationFunctionType
ALU = mybir.AluOpType
AX = mybir.AxisListType

D = 1024
S = 2048
NT = 16
LN_EPS = 1e-5
ALPHA = 8.0 ** 0.25
NEGB = -30000.0
SCALE = 0.125
C_RQ, C_RK, C_RV, C_RG = 0, 256, 512, 768
C_CB, C_CC, C_CH = 1024, 1280, 1536
C_NQ = 1792
C_KC, C_VC, C_KS, C_VS, C_KW, C_VW = 2304, 2432, 2560, 2688, 2816, 2944
C_NG = 3072
NIN = 3096
TW = 2432
NE = 32
GELU_A = 1.702


class Rot:
    def __init__(self, items):
        self.items = items
        self.i = 0

    def next(self):
        it = self.items[self.i % len(self.items)]
        self.i += 1
        return it


def host_consts():
    c = {}
    c["ident"] = np.eye(128, dtype=np.float32)
    r = np.arange(128)
    inv = (10000.0 ** (-((r % 64) % 32).astype(np.float64) / 32.0))
    c["ropec"] = np.stack([inv / (2 * np.pi), np.where((r % 64) < 32, -1.0, 1.0)], 1).astype(np.float32)
    gam = 1.0 - 2.0 ** (-5.0 - np.arange(4))
    k = np.arange(128)[:, None]
    m = np.arange(TW)[None, :] - 384
    tabs = []
    for h in range(4):
        e = (m - k).astype(np.float64)
        tabs.append(np.where(e >= 0, np.exp(np.log(gam[h]) * np.maximum(e, 0)) * SCALE, 0.0))
    c["dect"] = np.stack(tabs, 1).astype(ml_dtypes.bfloat16)
    dd = (m - k)
    c["tmask"] = np.stack([(dd >= 0), (dd >= 0) & (dd < 512)], 1).astype(np.float32).astype(ml_dtypes.bfloat16)
    kk = np.arange(128)[:, None]
    qq = np.arange(128)[None, :]
    c["cb_caus"] = np.where(kk > qq, 0.0, 1.0).astype(ml_dtypes.bfloat16)
    c["cb_low"] = np.where(kk <= qq, 0.0, 1.0).astype(ml_dtypes.bfloat16)
    cc = np.arange(128)[:, None]
    tt = np.arange(2048)[None, :]
    c["cb_cmp"] = np.where(16 * cc + 31 > tt, 0.0, 1.0).astype(ml_dtypes.bfloat16)
    cs = np.arange(128) * 16
    cl = cs + 31
    ss = np.arange(32) * 64
    ov = ((cs[:, None] < ss[None, :] + 64) & (ss[None, :] <= cl[:, None])).astype(np.float32)
    ov[127] = 0
    c["overlap"] = ov.astype(ml_dtypes.bfloat16)
    t = (np.arange(16)[None, :, None] * 128 + np.arange(128)[:, None, None])
    j = np.arange(32)[None, None, :]
    c["forced"] = np.where((j == 0) | (j == t // 64), 1e9, 0.0).astype(np.float32)
    b = np.arange(32)[:, None, None]
    kc = np.arange(16)[None, :, None]
    kl = np.arange(128)[None, None, :]
    E = np.zeros((128, 16, 128), np.float32)
    E[0:32] = (b == 2 * kc + (kl >= 64))
    c["Eexp"] = E.astype(ml_dtypes.bfloat16)
    return c


def build(NB=1, NL=1, taps=(), stop=None, nsa_stage=2, nsa_part=9, nsa_hks=(0, 1), nsa_qcs=tuple(range(4)), moe_stop="D", skip_mixer=False, ne_in=NE):
    nc = bass.Bass("TRN2", target_bir_lowering=False)
    R = Rec(nc)
    dram = {}

    def din(name, shape, dt=F32):
        dram[name] = nc.dram_tensor(name, list(shape), dt, kind="ExternalInput").ap()
        return dram[name]

    def dint(name, shape, dt=F32):
        dram[name] = nc.dram_tensor(name, list(shape), dt, kind="Internal").ap()
        return dram[name]

    def dout(name, shape, dt=F32):
        dram[name] = nc.dram_tensor(name, list(shape), dt, kind="ExternalOutput").ap()
        return dram[name]

    x_in = din("x", [NB, S, D])
    c_in = din("c", [NB, D])
    pos_in = din("positions", [NB, S], I32)
    ada_w = din("ada_w", [NL, D, 6 * D])
    ada_b = din("ada_b", [NL, 6 * D])
    w_in = din("w_in", [NL, D, NIN])
    w_out = din("w_out", [NL, D, D])
    gn_w = din("ret_gn_w", [NL, 256])
    conv_w = din("conv_w", [NL, 3, 256])
    cmp_pos = din("cmp_pos", [NL, 2, 32, 64])
    cmp_w1 = din("cmp_w1", [NL, 2, 2048, 128])
    cmp_w2 = din("cmp_w2", [NL, 2, 128, 64])
    ln_g = din("ln_g", [NL, 2, D])
    ln_b = din("ln_b", [NL, 2, D])
    router_w = din("router_w", [NL, D, NE])
    router_b = din("router_b", [NL, NE])
    w_gu = din("w_gate_up", [NL, ne_in, D, 2 * D])
    b_gu = din("b_gate_up", [NL, ne_in, 2 * D])
    w_dn = din("w_down", [NL, ne_in, D, D])
    b_dn = din("b_down", [NL, ne_in, D])
    HC = host_consts()
    HC.update(moe_consts())
    cst = {}
    for k, v in HC.items():
        cst[k] = din("k_" + k, list(v.shape), BF16 if v.dtype == ml_dtypes.bfloat16 else F32)
    out = dout("out", [NB, S, D])
    xbuf = dint("xbuf", [NB, S, D])
    ropebuf = dint("ropebuf", [NB, 2, 128, S])
    NG = NB * 4
    H2d = dint("H2d", [NB, S, D], BF16)
    Gd = dint("Gd", [NB, S, NE])
    CMd = dint("CMd", [NB, S, NE])
    XTd = [dint("XTd%d" % i, [4, 2, 8, 128, 8, 512], BF16) for i in range(NB)]
    PGd = [dint("PGd%d" % i, [4, 2, NE, 128, 512], BF16) for i in range(NB)]
    Yd = [dint("Yd%d" % i, [4, 2, NE, 128, D], BF16) for i in range(NB)]
    tap_t = {}
    for (nm, shape, dt) in taps:
        tap_t[nm] = dout("tap_" + nm, shape, dt)
    fin = []

    top = contextlib.ExitStack()

    uniq = [0]

    def mk(stack):
        def sbuf(name, shape, dt=F32):
            uniq[0] += 1
            return stack.enter_context(nc.sbuf_tensor("%s_%d" % (name, uniq[0]), list(shape), dt))

        def psum(name, shape, dt=F32):
            uniq[0] += 1
            return stack.enter_context(nc.psum_tensor("%s_%d" % (name, uniq[0]), list(shape), dt))
        return sbuf, psum

    with top:
        sbuf, psum = mk(top)
        ident_f = sbuf("ident_f", [128, 128])
        ident_b = sbuf("ident_b", [128, 128], BF16)
        R.dma("sp", lambda e: e.dma_start(out=ident_f[:], in_=cst["ident"][:, :]), writes=["ident_f"])
        R.dve(lambda e: e.tensor_copy(out=ident_b[:], in_=ident_f[:]), reads=["ident_f"], writes=["ident_b"])
        ropec = sbuf("ropec", [128, 2])
        R.dma("sp", lambda e: e.dma_start(out=ropec[:], in_=cst["ropec"][:, :]), writes=["ropec"])
        modbuf = dint("modbuf", [NB, NL, 6 * D])

        with contextlib.ExitStack() as st0:
            sb0, ps0 = mk(st0)
            c_sb = sb0("c_sb", [NB, D])
            cT = sb0("cT", [128, 8, NB], BF16)
            R.dma("sp", lambda e: e.dma_start(out=c_sb[:], in_=c_in[:, :]), writes=["c_sb"])
            R.act(lambda e: e.activation(out=c_sb[:], in_=c_sb[:], func=AF.Silu), reads=["c_sb"], writes=["c_sb"])
            ps_t = ps0("ps_t", [128, 512])
            for kc in range(8):
                R.pe(lambda e, kc=kc: e.transpose(out=ps_t[:, kc * NB:(kc + 1) * NB], in_=c_sb[:, kc * 128:(kc + 1) * 128],
                                                 identity=ident_f[0:NB, 0:NB]),
                     reads=["c_sb", "ident_f"], writes=["ps_t"])
            R.dve(lambda e: e.tensor_copy(out=cT[:].rearrange("p k b -> p (k b)"), in_=ps_t[:, 0:8 * NB]),
                  reads=["ps_t"], writes=["cT"])
            mods = sb0("mods", [NB, 6 * D])
            adab = sb0("adab", [NB, 6 * D])
            adaw = Rot([("adaw%d" % i, sb0("adaw%d" % i, [128, 8, 512], BF16)) for i in range(2)])
            psm = Rot([("ps_m%d" % i, ps0("ps_m%d" % i, [128, 512])) for i in range(2)])
            for l in range(NL):
                for b in range(NB):
                    R.dma("sp", lambda e, b=b, l=l: e.dma_start(out=adab[b:b + 1, :], in_=ada_b[l:l + 1, :]), writes=["adab"])
                for cc in range(12):
                    wn, wb = adaw.next()
                    pn, pm = psm.next()
                    R.dma("pool", lambda e, wb=wb, l=l, cc=cc: e.dma_start(
                        out=wb[:], in_=ada_w[l, :, cc * 512:(cc + 1) * 512].rearrange("(k p) n -> p k n", p=128)),
                        writes=[wn])
                    for kc in range(8):
                        R.pe(lambda e, wb=wb, pm=pm, kc=kc: e.matmul(pm[0:NB, :], lhsT=cT[:, kc, :], rhs=wb[:, kc, :],
                                                                     start=(kc == 0), stop=(kc == 7)),
                             reads=["cT", wn], writes=[pn])
                    R.dve(lambda e, pm=pm, cc=cc: e.tensor_tensor(
                        out=mods[:, cc * 512:(cc + 1) * 512], in0=pm[0:NB, :], in1=adab[:, cc * 512:(cc + 1) * 512], op=ALU.add),
                        reads=[pn, "adab"], writes=["mods"])
                R.dma("sp", lambda e, l=l: e.dma_start(out=modbuf[:, l, :], in_=mods[:]), reads=["mods"], writes=["modbuf"])
            posi = sb0("posi", [128, S], I32)
            y0 = sb0("y0", [128, S])
            r1 = sb0("r1", [128, S])
            ki = sb0("ki", [128, S], I32)
            kf = sb0("kf", [128, S])
            for b in range(NB):
                R.dma("sp", lambda e, b=b: e.dma_start(out=posi[:], in_=pos_in[b:b + 1, :].broadcast_to([128, S])), writes=["posi"])
                R.dve(lambda e: e.tensor_copy(out=y0[:], in_=posi[:]), reads=["posi"], writes=["y0"])
                R.dve(lambda e: e.tensor_scalar(out=y0[:], in0=y0[:], scalar1=ropec[:, 0:1], scalar2=None, op0=ALU.mult),
                      reads=["y0", "ropec"], writes=["y0"])
                for which in range(2):
                    sh = 0.25 if which == 0 else 0.0
                    R.dve(lambda e, sh=sh: e.tensor_scalar(out=r1[:], in0=y0[:], scalar1=sh, scalar2=None, op0=ALU.add),
                          reads=["y0"], writes=["r1"])
                    R.dve(lambda e: e.tensor_copy(out=ki[:], in_=r1[:]), reads=["r1"], writes=["ki"])
                    R.dve(lambda e: e.tensor_copy(out=kf[:], in_=ki[:]), reads=["ki"], writes=["kf"])
                    R.dve(lambda e: e.tensor_tensor(out=r1[:], in0=r1[:], in1=kf[:], op=ALU.subtract), reads=["r1", "kf"], writes=["r1"])
                    R.dve(lambda e: e.tensor_single_scalar(out=kf[:], in_=r1[:], scalar=0.5, op=ALU.is_gt), reads=["r1"], writes=["kf"])
                    R.dve(lambda e: e.tensor_tensor(out=r1[:], in0=r1[:], in1=kf[:], op=ALU.subtract), reads=["r1", "kf"], writes=["r1"])
                    R.dve(lambda e: e.tensor_single_scalar(out=kf[:], in_=r1[:], scalar=-0.5, op=ALU.is_lt), reads=["r1"], writes=["kf"])
                    R.dve(lambda e: e.tensor_tensor(out=r1[:], in0=r1[:], in1=kf[:], op=ALU.add), reads=["r1", "kf"], writes=["r1"])
                    R.act(lambda e: e.activation(out=r1[:], in_=r1[:], func=AF.Sin, scale=2 * math.pi), reads=["r1"], writes=["r1"])
                    if which == 1:
                        R.dve(lambda e: e.tensor_scalar(out=r1[:], in0=r1[:], scalar1=ropec[:, 1:2], scalar2=None, op0=ALU.mult),
                              reads=["r1", "ropec"], writes=["r1"])
                    R.dma("sp", lambda e, b=b, which=which: e.dma_start(out=ropebuf[b, which, :, :], in_=r1[:]),
                          reads=["r1"], writes=["ropebuf%d" % b])
        R.barrier()
        if "mods" in tap_t:
            fin.append(R.dma("sp", lambda e: e.dma_start(out=tap_t["mods"][:, :, :], in_=modbuf[:, :, :]), reads=["modbuf"], writes=["tap_mods"]))
        if "rope" in tap_t:
            fin.append(R.dma("sp", lambda e: e.dma_start(out=tap_t["rope"][:, :, :], in_=ropebuf[0, :, :, :]), reads=["ropebuf0"], writes=["tap_rope"]))

        if stop == "s0":
            R.emit(final_wait_ops=fin)
            return nc

        for l in range(NL):
            x_src = x_in if l == 0 else xbuf
            for b in (range(NB) if not skip_mixer else ()):
                with contextlib.ExitStack() as stm:
                    mixer(nc, R, mk(stm), locals())
                R.barrier()
            if stop == "mix":
                break
            moe(nc, R, mk, locals())
        R.emit(final_wait_ops=fin)
    return nc


def mixer(nc, R, mkp, env):
    sbuf, psum = mkp
    mk = env["mk"]
    l, b = env["l"], env["b"]
    x_src, xbuf, modbuf = env["x_src"], env["xbuf"], env["modbuf"]
    ident_f, ident_b, cst, tap_t, fin = env["ident_f"], env["ident_b"], env["cst"], env["tap_t"], env["fin"]
    w_in, w_out, gn_w, conv_w = env["w_in"], env["w_out"], env["gn_w"], env["conv_w"]
    cmp_pos, cmp_w1, cmp_w2, ln_g, ln_b, ropebuf = env["cmp_pos"], env["cmp_w1"], env["cmp_w2"], env["ln_g"], env["ln_b"], env["ropebuf"]
    NB = env["NB"]

    pA = Rot([("pA%d" % i, psum("pA%d" % i, [128, 512])) for i in range(2)])
    pT = psum("pT", [128, 1024], BF16)
    pO = Rot([("pO%d" % i, psum("pO%d" % i, [128, 512])) for i in range(4)])
    pX = psum("pX", [128, 512])

    def bcast_mod(sbx, dname, which, plus1):
        dst = sbx(dname, [128, D])
        R.dma("sp", lambda e: e.dma_start(out=dst[:], in_=modbuf[b, l:l + 1, which * D:(which + 1) * D].broadcast_to([128, D])),
              reads=["modbuf"], writes=[dname])
        if plus1:
            R.pool(lambda e: e.tensor_scalar(out=dst[:], in0=dst[:], scalar1=1.0, scalar2=None, op0=ALU.add), reads=[dname], writes=[dname])
        return dst

    sh1 = bcast_mod(sbuf, "sh1", 0, False)
    sc1 = bcast_mod(sbuf, "sc1", 1, True)

    hT = sbuf("hT", [128, 8, S], BF16)
    yT = sbuf("yT", [128, 8, S], BF16)
    xc_r = Rot([("xc%d" % i, sbuf("xc%d" % i, [128, D])) for i in range(2)])
    hc_r = Rot([("hc%d" % i, sbuf("hc%d" % i, [128, D], BF16)) for i in range(2)])
    st_r = Rot([("st%d" % i, sbuf("st%d" % i, [128, 16])) for i in range(2)])

    def layer_norm_chunk(xcn, xc, stn, stt, xnn, xn):
        R.dve(lambda e: e.bn_stats(out=stt[:, 0:6], in_=xc[:, 0:512]), reads=[xcn], writes=[stn])
        R.dve(lambda e: e.bn_stats(out=stt[:, 6:12], in_=xc[:, 512:1024]), reads=[xcn], writes=[stn])
        R.dve(lambda e: e.bn_aggr(out=stt[:, 12:14], in_=stt[:, 0:12]), reads=[stn], writes=[stn])
        R.act(lambda e: e.activation(out=stt[:, 14:15], in_=stt[:, 13:14], func=AF.Sqrt, bias=LN_EPS, scale=1.0), reads=[stn], writes=[stn])
        R.dve(lambda e: e.reciprocal(out=stt[:, 15:16], in_=stt[:, 14:15]), reads=[stn], writes=[stn])
        R.dve(lambda e: e.tensor_scalar(out=xn[:], in0=xc[:], scalar1=stt[:, 12:13], scalar2=stt[:, 15:16], op0=ALU.subtract, op1=ALU.mult),
              reads=[xcn, stn], writes=[xnn])

    for tc in range(NT):
        xcn, xc = xc_r.next()
        xnn, xn = xcn, xc
        hcn, hc = hc_r.next()
        stn, stt = st_r.next()
        R.dma("sp", lambda e, xc=xc, tc=tc: e.dma_start(out=xc[:], in_=x_src[b, tc * 128:(tc + 1) * 128, :]), reads=["xb%d_%d" % (b, tc)], writes=[xcn])
        layer_norm_chunk(xcn, xc, stn, stt, xnn, xn)
        R.pool(lambda e, xn=xn: e.tensor_tensor(out=xn[:], in0=xn[:], in1=sc1[:], op=ALU.mult), reads=[xnn, "sc1"], writes=[xnn])
        R.dve(lambda e, xn=xn, hc=hc: e.tensor_tensor(out=hc[:], in0=xn[:], in1=sh1[:], op=ALU.add), reads=[xnn, "sh1"], writes=[hcn])
        for kc in range(8):
            R.pe(lambda e, hc=hc, kc=kc: e.transpose(out=pT[:, kc * 128:(kc + 1) * 128], in_=hc[:, kc * 128:(kc + 1) * 128], identity=ident_b[:]),
                 reads=[hcn, "ident_b"], writes=["pT"])
        R.act(lambda e, tc=tc: e.copy(out=hT[:, :, tc * 128:(tc + 1) * 128], in_=pT[:].rearrange("p (k t) -> p k t", k=8)),
              reads=["pT"], writes=["hT"])
    if "hT" in tap_t and b == 0 and l == 0:
        fin.append(R.dma("sp", lambda e: e.dma_start(out=tap_t["hT"][:, :, :], in_=hT[:]), reads=["hT"], writes=["tap_hT"]))

    wr = Rot([("W%d" % i, sbuf("W%d" % i, [128, 8, 128], BF16)) for i in range(4)])

    def load_w(pieces):
        wn, wb = wr.next()
        for (do, sc, wd) in pieces:
            R.dma("pool", lambda e, wb=wb, do=do, sc=sc, wd=wd: e.dma_start(
                out=wb[:, :, do:do + wd], in_=w_in[l, :, sc:sc + wd].rearrange("(k p) n -> p k n", p=128)), writes=[wn])
        return wn, wb

    def perm_pieces(c0):
        return [(0, c0 + 32, 32), (32, c0, 32), (64, c0 + 96, 32), (96, c0 + 64, 32)]

    def dup_pieces(c0):
        return [(0, c0, 64), (64, c0, 64)]

    def dup_perm_pieces(c0):
        return [(0, c0 + 32, 32), (32, c0, 32), (64, c0 + 32, 32), (96, c0, 32)]

    def proj_fm(wn, wb, tg):
        pn, pt = pA.next()
        for kc in range(8):
            R.pe(lambda e, kc=kc: e.matmul(pt[:, :], lhsT=wb[:, kc, :], rhs=hT[:, kc, tg * 512:(tg + 1) * 512], start=(kc == 0), stop=(kc == 7)),
                 reads=[wn, "hT"], writes=[pn])
        return pn, pt

    def proj_plain(pieces, dst, dname, dsl):
        wn, wb = load_w(pieces)
        for tg in range(4):
            pn, pt = proj_fm(wn, wb, tg)
            R.act(lambda e, pt=pt, tg=tg: e.copy(out=dst[:, dsl, tg * 512:(tg + 1) * 512] if dsl is not None else dst[:, tg * 512:(tg + 1) * 512], in_=pt[:, :]),
                  reads=[pn], writes=[dname])

    with contextlib.ExitStack() as sc_:
        sb1, _ = mk(sc_)
        fm = [sb1("fm%d" % i, [128, 2, S], BF16) for i in range(3)]
        tmpA = sb1("tmpA", [128, S])
        tmpB = sb1("tmpB", [128, S])
        cw = sb1("cw", [128, 2, 3])
        for ch_ in range(2):
            for k_ in range(3):
                R.dma("sp", lambda e, ch_=ch_, k_=k_: e.dma_start(out=cw[:, ch_, k_:k_ + 1], in_=conv_w[l, k_, ch_ * 128:(ch_ + 1) * 128].rearrange("(p o) -> p o", o=1)), writes=["cw"])
        for i, c0 in enumerate((C_CB, C_CC, C_CH)):
            for ch in range(2):
                proj_plain([(0, c0 + ch * 128, 128)], fm[i], "fm%d" % i, ch)
        for ch in range(2):
            R.pool(lambda e, ch=ch: e.tensor_tensor(out=tmpA[:], in0=fm[1][:, ch, :], in1=fm[2][:, ch, :], op=ALU.mult),
                   reads=["fm1", "fm2"], writes=["tmpA"])
            R.dve(lambda e, ch=ch: e.tensor_scalar(out=tmpB[:], in0=tmpA[:], scalar1=cw[:, ch, 2:3], scalar2=None, op0=ALU.mult),
                  reads=["tmpA", "cw"], writes=["tmpB"])
            R.dve(lambda e, ch=ch: e.scalar_tensor_tensor(out=tmpB[:, 1:S], in0=tmpA[:, 0:S - 1], scalar=cw[:, ch, 1:2], in1=tmpB[:, 1:S],
                                                          op0=ALU.mult, op1=ALU.add), reads=["tmpA", "tmpB", "cw"], writes=["tmpB"])
            R.dve(lambda e, ch=ch: e.scalar_tensor_tensor(out=tmpB[:, 2:S], in0=tmpA[:, 0:S - 2], scalar=cw[:, ch, 0:1], in1=tmpB[:, 2:S],
                                                          op0=ALU.mult, op1=ALU.add), reads=["tmpA", "tmpB", "cw"], writes=["tmpB"])
            R.pool(lambda e, ch=ch: e.tensor_tensor(out=yT[:, 2 + ch, :], in0=tmpB[:], in1=fm[0][:, ch, :], op=ALU.mult),
                   reads=["tmpB", "fm0"], writes=["yT"])
    R.barrier()

    def load_rope(sbx):
        cosT = sbx("cosT", [128, S])
        sinT = sbx("sinT", [128, S])
        R.dma("sp", lambda e: e.dma_start(out=cosT[:], in_=ropebuf[b, 0, :, :]), reads=["ropebuf%d" % b], writes=["cosT"])
        R.dma("sp", lambda e: e.dma_start(out=sinT[:], in_=ropebuf[b, 1, :, :]), reads=["ropebuf%d" % b], writes=["sinT"])
        rt = Rot([("rt%d" % i, sbx("rt%d" % i, [128, 512])) for i in range(4)])
        return cosT, sinT, rt

    def proj_rope(p_norm, p_perm, dst, dname, dsl, cosT, sinT, rt):
        wn1, wb1 = load_w(p_norm)
        wn2, wb2 = load_w(p_perm)
        for tg in range(4):
            pn1, pt1 = proj_fm(wn1, wb1, tg)
            pn2, pt2 = proj_fm(wn2, wb2, tg)
            t1n, t1 = rt.next()
            t2n, t2 = rt.next()
            R.dve(lambda e, pt1=pt1, t1=t1, tg=tg: e.tensor_tensor(out=t1[:], in0=pt1[:, :], in1=cosT[:, tg * 512:(tg + 1) * 512], op=ALU.mult),
                  reads=[pn1, "cosT"], writes=[t1n])
            R.dve(lambda e, pt2=pt2, t2=t2, tg=tg: e.tensor_tensor(out=t2[:], in0=pt2[:, :], in1=sinT[:, tg * 512:(tg + 1) * 512], op=ALU.mult),
                  reads=[pn2, "sinT"], writes=[t2n])
            R.pool(lambda e, t1=t1, t2=t2, tg=tg: e.tensor_tensor(
                out=dst[:, dsl, tg * 512:(tg + 1) * 512] if dsl is not None else dst[:, tg * 512:(tg + 1) * 512], in0=t1[:], in1=t2[:], op=ALU.add),
                reads=[t1n, t2n], writes=[dname])

    with contextlib.ExitStack() as sc_:
        sb2, _ = mk(sc_)
        cosT, sinT, rt = load_rope(sb2)
        qTr = sb2("qTr", [128, 2, S], BF16)
        kTr = sb2("kTr", [128, 2, S], BF16)
        vrg = sb2("vrg", [128, NT, 512], BF16)
        dect = sb2("dect", [128, 4, TW], BF16)
        gnw = sb2("gnw", [128, 256])
        R.dma("sp", lambda e: e.dma_start(out=dect[:], in_=cst["dect"][:, :, :]), writes=["dect"])
        R.dma("sp", lambda e: e.dma_start(out=gnw[:], in_=gn_w[l:l + 1, :].broadcast_to([128, 256])), writes=["gnw"])
        for ch in range(2):
            proj_rope([(0, C_RQ + ch * 128, 128)], perm_pieces(C_RQ + ch * 128), qTr, "qTr", ch, cosT, sinT, rt)
            proj_rope([(0, C_RK + ch * 128, 128)], perm_pieces(C_RK + ch * 128), kTr, "kTr", ch, cosT, sinT, rt)
        wv = sb2("wv", [128, 8, 512], BF16)
        R.dma("pool", lambda e: e.dma_start(out=wv[:], in_=w_in[l, :, C_RV:C_RV + 512].rearrange("(k p) n -> p k n", p=128)), writes=["wv"])
        for tc in range(NT):
            pn, pt = pA.next()
            for kc in range(8):
                R.pe(lambda e, pt=pt, kc=kc, tc=tc: e.matmul(pt[:, :], lhsT=hT[:, kc, tc * 128:(tc + 1) * 128], rhs=wv[:, kc, :],
                                                             start=(kc == 0), stop=(kc == 7)), reads=["hT", "wv"], writes=[pn])
            R.act(lambda e, pt=pt, tc=tc: e.copy(out=vrg[:, tc, 0:256], in_=pt[:, 0:256]), reads=[pn], writes=["vrg"])
            R.act(lambda e, pt=pt, tc=tc: e.activation(out=vrg[:, tc, 256:512], in_=pt[:, 256:512], func=AF.Silu), reads=[pn], writes=["vrg"])
        smr = Rot([("sm%d" % i, sb2("sm%d" % i, [128, 512], BF16)) for i in range(3)])
        gA = sb2("gA", [128, 1024])
        gB = sb2("gB", [128, 1024])
        gs = sb2("gs", [128, 64])
        yr = sb2("yr", [128, 4, 256], BF16)
        for qg in range(4):
            raccn = ["pO0", "pO1"]
            racc = [pO.items[0][1], pO.items[1][1]]
            first = [True, True]
            for h in range(4):
                ch, ro = h // 2, (h % 2) * 64
                for kc in range(4 * qg + 4):
                    pn, pt = pA.next()
                    R.pe(lambda e, pt=pt, kc=kc, ch=ch, ro=ro, qg=qg: e.matmul(
                        pt[:, :], lhsT=kTr[ro:ro + 64, ch, kc * 128:(kc + 1) * 128], rhs=qTr[ro:ro + 64, ch, qg * 512:(qg + 1) * 512],
                        start=True, stop=True), reads=["kTr", "qTr"], writes=[pn])
                    smn, sm = smr.next()
                    off = 384 + qg * 512 - kc * 128
                    R.dve(lambda e, pt=pt, sm=sm, h=h, off=off: e.tensor_tensor(out=sm[:], in0=pt[:, :], in1=dect[:, h, off:off + 512], op=ALU.mult),
                          reads=[pn, "dect"], writes=[smn])
                    for j in range(4):
                        qc = 4 * qg + j
                        if kc > qc:
                            continue
                        bk = j // 2
                        st_flag = first[bk]
                        first[bk] = False
                        R.pe(lambda e, sm=sm, j=j, h=h, kc=kc, bk=bk, st_flag=st_flag: e.matmul(
                            racc[bk][:, (j % 2) * 256 + h * 64:(j % 2) * 256 + h * 64 + 64], lhsT=sm[:, j * 128:(j + 1) * 128],
                            rhs=vrg[:, kc, h * 64:h * 64 + 64], start=st_flag, stop=False, skip_group_check=True),
                            reads=[smn, "vrg"], writes=[raccn[bk]])
            for bk in range(2):
                R.act(lambda e, bk=bk: e.copy(out=gA[:, bk * 512:(bk + 1) * 512], in_=racc[bk][:, :]), reads=[raccn[bk]], writes=["gA"])
            g3 = gA[:].rearrange("p (a d) -> p a d", d=64)
            b3 = gB[:].rearrange("p (a d) -> p a d", d=64)
            R.dve(lambda e: e.tensor_reduce(out=gs[:, 0:16], in_=g3, axis=AX.X, op=ALU.add), reads=["gA"], writes=["gs"])
            R.dve(lambda e: e.tensor_scalar(out=gs[:, 0:16], in0=gs[:, 0:16], scalar1=1.0 / 64, scalar2=None, op0=ALU.mult), reads=["gs"], writes=["gs"])
            R.dve(lambda e: e.tensor_tensor(out=g3, in0=g3, in1=gs[:, 0:16].unsqueeze(2).broadcast_to([128, 16, 64]), op=ALU.subtract),
                  reads=["gA", "gs"], writes=["gA"])
            R.pool(lambda e: e.tensor_tensor(out=gB[:], in0=gA[:], in1=gA[:], op=ALU.mult), reads=["gA"], writes=["gB"])
            R.dve(lambda e: e.tensor_reduce(out=gs[:, 16:32], in_=b3, axis=AX.X, op=ALU.add), reads=["gB"], writes=["gs"])
            R.act(lambda e: e.activation(out=gs[:, 32:48], in_=gs[:, 16:32], func=AF.Sqrt, bias=LN_EPS, scale=1.0 / 64), reads=["gs"], writes=["gs"])
            R.dve(lambda e: e.reciprocal(out=gs[:, 48:64], in_=gs[:, 32:48]), reads=["gs"], writes=["gs"])
            R.dve(lambda e: e.tensor_tensor(out=g3, in0=g3, in1=gs[:, 48:64].unsqueeze(2).broadcast_to([128, 16, 64]), op=ALU.mult),
                  reads=["gA", "gs"], writes=["gA"])
            g4 = gA[:].rearrange("p (j c) -> p j c", c=256)
            R.pool(lambda e: e.tensor_tensor(out=g4, in0=g4, in1=gnw[:].unsqueeze(1).broadcast_to([128, 4, 256]), op=ALU.mult),
                   reads=["gA", "gnw"], writes=["gA"])
            R.dve(lambda e, qg=qg: e.tensor_tensor(out=yr[:], in0=g4, in1=vrg[:, 4 * qg:4 * qg + 4, 256:512], op=ALU.mult),
                  reads=["gA", "vrg"], writes=["yr"])
            for j in range(4):
                for ch in range(2):
                    R.pe(lambda e, j=j, ch=ch: e.transpose(out=pT[:, (j * 2 + ch) * 128:(j * 2 + ch + 1) * 128], in_=yr[:, j, ch * 128:(ch + 1) * 128],
                                                         identity=ident_b[:]), reads=["yr", "ident_b"], writes=["pT"])
            R.act(lambda e, qg=qg: e.copy(out=yT[:, 0:2, qg * 512:(qg + 1) * 512].rearrange("p c (j t) -> p j c t", j=4),
                                          in_=pT[:].rearrange("p (j c t) -> p j c t", j=4, c=2)), reads=["pT"], writes=["yT"])
    R.barrier()

    nsa_stage = env.get("nsa_stage", 2)
    nsa_part = env.get("nsa_part", 9)
    nsa_hks = env.get("nsa_hks", (0, 1))
    nsa_qcs = env.get("nsa_qcs", tuple(range(4)))
    with contextlib.ExitStack() as sc_:
        sb3, _ = mk(sc_)
        cosT, sinT, rt = load_rope(sb3)
        kcT = sb3("kcT", [128, S], BF16)
        vcT = sb3("vcT", [128, S], BF16)
        gates = sb3("gates", [128, NT, 24])
        v2 = sb3("v2", [128, NT, 4, 128], BF16)
        qT = sb3("qT", [128, 2, S], BF16)
        qTr2 = sb3("qTr2", [128, 2, S], BF16)
        ksT = sb3("ksT", [128, S], BF16)
        kwT = sb3("kwT", [128, S], BF16)
        w2k = sb3("w2k", [128, 128], BF16)
        w2v = sb3("w2v", [128, 64], BF16)
        pbias = sb3("pbias", [128, 2])
        kcd = [sb3("kcd%d" % i, [128, 128], BF16) for i in range(2)]
        vca = sb3("vca", [128, 2, 128], BF16)
        ovl = sb3("ovl", [128, 32], BF16)
        scC = contextlib.ExitStack()
        sbC, _ = mk(scC)
        w1d = [sbC("w1d%d" % i, [128, 32, 128], BF16) for i in range(2)]
        posr = sbC("posr", [32, 2, 64])
        posT = sbC("posT", [64, 2, 32], BF16)
        R.dma("sp", lambda e: e.dma_start(out=ovl[:], in_=cst["overlap"][:, :]), writes=["ovl"])
        for kind in range(2):
            for cp in range(2):
                R.dma("pool", lambda e, kind=kind, cp=cp: e.dma_start(
                    out=w1d[kind][cp * 64:(cp + 1) * 64, :, :], in_=cmp_w1[l, kind, :, :].rearrange("(l d) n -> d l n", d=64)), writes=["w1d%d" % kind])
            R.dma("sp", lambda e, kind=kind: e.dma_start(out=posr[:, kind, :], in_=cmp_pos[l, kind, :, :]), writes=["posr"])
        R.dma("pool", lambda e: e.dma_start(out=w2k[:, 0:64], in_=cmp_w2[l, 0, :, :]), writes=["w2k"])
        R.dma("pool", lambda e: e.dma_start(out=w2k[:, 64:128], in_=cmp_w2[l, 0, :, :]), writes=["w2k"])
        R.dma("pool", lambda e: e.dma_start(out=w2v[:], in_=cmp_w2[l, 1, :, :]), writes=["w2v"])
        proj_plain([(0, C_KC, 128)], kcT, "kcT", None)
        proj_plain([(0, C_VC, 128)], vcT, "vcT", None)
        wt = sbC("wt", [128, 8, 408], BF16)
        R.dma("pool", lambda e: e.dma_start(out=wt[:], in_=w_in[l, :, C_VS:C_VS + 408].rearrange("(k p) n -> p k n", p=128)), writes=["wt"])
        R.pool(lambda e: e.memset(v2[:].rearrange("p a b c -> p (a b) c")[:, :, 64:128], 1.0), writes=["v2"])
        for tc in range(NT):
            pn, pt = pA.next()
            for kc in range(8):
                R.pe(lambda e, pt=pt, kc=kc, tc=tc: e.matmul(pt[:, 0:408], lhsT=hT[:, kc, tc * 128:(tc + 1) * 128], rhs=wt[:, kc, :],
                                                             start=(kc == 0), stop=(kc == 7)), reads=["hT", "wt"], writes=[pn])
            R.act(lambda e, pt=pt, tc=tc: e.copy(out=v2[:, tc, 0:2, 0:64], in_=pt[:, 0:128].rearrange("p (h d) -> p h d", h=2)), reads=[pn], writes=["v2"])
            R.act(lambda e, pt=pt, tc=tc: e.copy(out=v2[:, tc, 2:4, 0:64], in_=pt[:, 256:384].rearrange("p (h d) -> p h d", h=2)), reads=[pn], writes=["v2"])
            R.act(lambda e, pt=pt, tc=tc: e.activation(out=gates[:, tc, :], in_=pt[:, 384:408], func=AF.Sigmoid), reads=[pn], writes=["gates"])
        for kind in range(2):
            R.pe(lambda e, kind=kind: e.transpose(out=pX[0:64, kind * 32:(kind + 1) * 32], in_=posr[:, kind, :], identity=ident_f[0:32, 0:32]),
                 reads=["posr", "ident_f"], writes=["pX"])
        R.dve(lambda e: e.tensor_copy(out=posT[:].rearrange("p k l -> p (k l)"), in_=pX[0:64, 0:64]), reads=["pX"], writes=["posT"])
        for kind in range(2):
            for li in range(32):
                R.pe(lambda e, kind=kind, li=li: e.matmul(pX[:, 64 + kind:65 + kind], lhsT=w1d[kind][0:64, li, :], rhs=posT[:, kind, li:li + 1],
                                                          start=(li == 0 and kind == 0), stop=(li == 31), skip_group_check=True),
                     reads=["w1d%d" % kind, "posT"], writes=["pX"])
        R.dve(lambda e: e.tensor_copy(out=pbias[:], in_=pX[:, 64:66]), reads=["pX"], writes=["pbias"])
        zt = sbC("zt", [128, 128])
        z2 = sbC("z2", [128, 128])
        blk = sbC("blk", [128, 32, 127], BF16)
        gT = sbC("gT", [128, 128], BF16)
        R.pool(lambda e: e.memset(vca[:], 0.0), writes=["vca"])
        for i_ in range(2):
            R.pool(lambda e, i_=i_: e.memset(kcd[i_][:], 0.0), writes=["kcd"])
        for kind in range(2):
            src = kcT if kind == 0 else vcT
            srcn = "kcT" if kind == 0 else "vcT"
            for li in range(32):
                R.dve(lambda e, li=li, src=src: e.tensor_copy(out=blk[:, li, :], in_=src[:, li:li + 2017:16]), reads=[srcn], writes=["blk"])
            for hk in range(2):
                ro = hk * 64
                pn, pt = pA.next()
                for li in range(32):
                    R.pe(lambda e, pt=pt, kind=kind, li=li, ro=ro: e.matmul(
                        pt[:, 0:127], lhsT=w1d[kind][ro:ro + 64, li, :], rhs=blk[ro:ro + 64, li, :], start=(li == 0), stop=(li == 31)),
                        reads=["w1d%d" % kind, "blk"], writes=[pn])
                R.act(lambda e, pt=pt, kind=kind: e.activation(out=zt[:, 0:127], in_=pt[:, 0:127], func=AF.Identity, bias=pbias[:, kind:kind + 1], scale=1.0),
                      reads=[pn, "pbias"], writes=["zt"])
                R.dve(lambda e: e.tensor_tensor(out=z2[:, 0:127], in0=zt[:, 0:127], in1=zt[:, 0:127], op=ALU.mult), reads=["zt"], writes=["z2"])
                R.dve(lambda e: e.tensor_scalar(out=z2[:, 0:127], in0=z2[:, 0:127], scalar1=0.044715, scalar2=1.0, op0=ALU.mult, op1=ALU.add),
                      reads=["z2"], writes=["z2"])
                R.dve(lambda e: e.tensor_tensor(out=z2[:, 0:127], in0=z2[:, 0:127], in1=zt[:, 0:127], op=ALU.mult), reads=["z2", "zt"], writes=["z2"])
                R.act(lambda e: e.activation(out=z2[:, 0:127], in_=z2[:, 0:127], func=AF.Sigmoid, scale=1.5957691216), reads=["z2"], writes=["z2"])
                R.dve(lambda e: e.tensor_tensor(out=gT[:, 0:127], in0=z2[:, 0:127], in1=zt[:, 0:127], op=ALU.mult), reads=["z2", "zt"], writes=["gT"])
                if kind == 0:
                    R.pe(lambda e: e.matmul(pX[:, 128:255], lhsT=w2k[:, :], rhs=gT[:, 0:127], start=True, stop=True), reads=["w2k", "gT"], writes=["pX"])
                    R.act(lambda e, hk=hk: e.copy(out=kcd[hk][:, 0:127], in_=pX[:, 128:255]), reads=["pX"], writes=["kcd"])
                else:
                    R.pe(lambda e: e.matmul(pX[0:127, 256:320], lhsT=gT[:, 0:127], rhs=w2v[:, :], start=True, stop=True), reads=["w2v", "gT"], writes=["pX"])
                    R.act(lambda e, hk=hk: e.copy(out=vca[0:127, hk, 0:64], in_=pX[0:127, 256:320]), reads=["pX"], writes=["vca"])
        for hk in range(2):
            R.pool(lambda e, hk=hk: e.memset(vca[:, hk, 64:96], 1.0), reads=[], writes=["vca"])
            R.pool(lambda e, hk=hk: e.tensor_copy(out=vca[:, hk, 96:128], in_=ovl[:]), reads=["ovl"], writes=["vca"])

        if "kcd" in tap_t and b == 0 and l == 0:
            for i_ in range(2):
                fin.append(R.dma("sp", lambda e, i_=i_: e.dma_start(out=tap_t["kcd"][:, i_, :], in_=kcd[i_][:]), reads=["kcd"], writes=["tap_kcd%d" % i_]))
            fin.append(R.dma("sp", lambda e: e.dma_start(out=tap_t["vca"][:, :, :], in_=vca[:]), reads=["vca"], writes=["tap_vca"]))
        R.barrier()
        scC.close()
        cbc = sb3("cbc", [128, S], BF16)
        forced = sb3("forced", [128, NT, 32])
        Eexp = sb3("Eexp", [128, NT, 128], BF16)
        tmask = sb3("tmask", [128, 2, TW], BF16)
        R.dma("sp", lambda e: e.dma_start(out=cbc[:], in_=cst["cb_cmp"][:, :]), writes=["cbc"])
        R.dma("sp", lambda e: e.dma_start(out=forced[:], in_=cst["forced"][:, :, :]), writes=["forced"])
        R.dma("sp", lambda e: e.dma_start(out=Eexp[:], in_=cst["Eexp"][:, :, :]), writes=["Eexp"])
        R.dma("sp", lambda e: e.dma_start(out=tmask[:], in_=cst["tmask"][:, :, :]), writes=["tmask"])
        ptr = Rot([("PT%d" % i, sb3("PT%d" % i, [128, 512], BF16)) for i in range(3)])
        mk_r = Rot([("mk%d" % i, sb3("mk%d" % i, [128, 512], BF16)) for i in range(2)])
        selq = sb3("selq", [128, 4, 128], BF16)
        selT = sb3("selT", [128, 512], BF16)
        on = sb3("on", [128, 4, 4, 64])
        tmpo = sb3("tmpo", [128, 4, 64])
        impa = sb3("impa", [128, 4, 32])
        tmpi = sb3("tmpi", [128, 4, 32])
        nst = sb3("nst", [128, 64])
        ynb = sb3("ynb", [128, 4, 256], BF16)
        R.pool(lambda e: e.memset(selq[:], 0.0), writes=["selq"])

        def smm(pn, pt, Ktile, kn, k0, Qsrc, qn, ro, ch, qg):
            R.pe(lambda e: e.matmul(pt[:, :], lhsT=Ktile[ro:ro + 64, k0:k0 + 128], rhs=Qsrc[ro:ro + 64, ch, qg * 512:(qg + 1) * 512],
                                    start=True, stop=True), reads=[kn, qn], writes=[pn])

        def finish(an, acc, g, qg, hk, br, first):
            a3 = acc[:, :].rearrange("p (j c) -> p j c", j=4)
            R.dve(lambda e: e.tensor_scalar(out=nst[:, 0:4], in0=a3[:, :, 64], scalar1=1e-30, scalar2=None, op0=ALU.max), reads=[an], writes=["nst"])
            R.dve(lambda e: e.reciprocal(out=nst[:, 0:4], in_=nst[:, 0:4]), reads=["nst"], writes=["nst"])
            R.dve(lambda e: e.tensor_tensor(out=nst[:, 4:8], in0=nst[:, 0:4], in1=gates[:, 4 * qg:4 * qg + 4, hk * 12 + g * 3 + br], op=ALU.mult),
                  reads=["nst", "gates"], writes=["nst"])
            dst = on[:, :, g, :]
            if first:
                R.dve(lambda e: e.tensor_tensor(out=dst, in0=a3[:, :, 0:64], in1=nst[:, 4:8].unsqueeze(2).broadcast_to([128, 4, 64]), op=ALU.mult),
                      reads=[an, "nst"], writes=["on"])
            else:
                R.dve(lambda e: e.tensor_tensor(out=tmpo[:], in0=a3[:, :, 0:64], in1=nst[:, 4:8].unsqueeze(2).broadcast_to([128, 4, 64]), op=ALU.mult),
                      reads=[an, "nst"], writes=["tmpo"])
                R.pool(lambda e: e.tensor_tensor(out=dst, in0=dst, in1=tmpo[:], op=ALU.add), reads=["on", "tmpo"], writes=["on"])

        for hk in (nsa_hks if nsa_stage >= 2 else ()):
            for ch in range(2):
                c0 = C_NQ + hk * 256 + ch * 128
                proj_plain([(0, c0, 128)], qT, "qT", ch)
                proj_rope([(0, c0, 128)], perm_pieces(c0), qTr2, "qTr2", ch, cosT, sinT, rt)
            proj_rope(dup_pieces(C_KS + hk * 64), dup_perm_pieces(C_KS + hk * 64), ksT, "ksT", None, cosT, sinT, rt)
            proj_rope(dup_pieces(C_KW + hk * 64), dup_perm_pieces(C_KW + hk * 64), kwT, "kwT", None, cosT, sinT, rt)
            for qg in (nsa_qcs if nsa_part >= 1 else ()):
                for g in range(4):
                    ch, ro = g // 2, (g % 2) * 64
                    pn, pt = pA.next()
                    smm(pn, pt, kcd[hk], "kcd", 0, qT, "qT", ro, ch, qg)
                    ptn, PT = ptr.next()
                    R.act(lambda e, pt=pt, PT=PT: e.activation(out=PT[:, :], in_=pt[:, :], func=AF.Exp, scale=SCALE), reads=[pn], writes=[ptn])
                    R.pool(lambda e, PT=PT, qg=qg: e.tensor_tensor(out=PT[:, :], in0=PT[:, :], in1=cbc[:, qg * 512:(qg + 1) * 512], op=ALU.mult),
                           reads=[ptn, "cbc"], writes=[ptn])
                    an, acc = pO.next()
                    for j in range(4):
                        R.pe(lambda e, j=j, acc=acc, PT=PT, hk=hk: e.matmul(acc[:, j * 128:(j + 1) * 128], lhsT=PT[:, j * 128:(j + 1) * 128], rhs=vca[:, hk, :],
                                                                          start=(j == 0), stop=False, skip_group_check=True), reads=[ptn, "vca"], writes=[an])
                    if nsa_part < 2:
                        continue
                    a3 = acc[:, :].rearrange("p (j c) -> p j c", j=4)
                    finish(an, acc, g, qg, hk, 0, True)
                    if g == 0:
                        R.dve(lambda e, a3=a3: e.tensor_tensor(out=impa[:], in0=a3[:, :, 96:128], in1=nst[:, 0:4].unsqueeze(2).broadcast_to([128, 4, 32]), op=ALU.mult),
                              reads=[an, "nst"], writes=["impa"])
                    else:
                        R.dve(lambda e, a3=a3: e.tensor_tensor(out=tmpi[:], in0=a3[:, :, 96:128], in1=nst[:, 0:4].unsqueeze(2).broadcast_to([128, 4, 32]), op=ALU.mult),
                              reads=[an, "nst"], writes=["tmpi"])
                        R.pool(lambda e: e.tensor_tensor(out=impa[:], in0=impa[:], in1=tmpi[:], op=ALU.add), reads=["impa", "tmpi"], writes=["impa"])
                if nsa_part < 3:
                    continue
                R.dve(lambda e, qg=qg: e.tensor_tensor(out=impa[:], in0=impa[:], in1=forced[:, 4 * qg:4 * qg + 4, :], op=ALU.max), reads=["impa", "forced"], writes=["impa"])
                for j in range(4):
                    R.dve(lambda e, j=j: e.max(out=nst[:, 16 + 8 * j:24 + 8 * j], in_=impa[:, j, :]), reads=["impa"], writes=["nst"])
                    R.dve(lambda e, j=j: e.tensor_scalar(out=selq[:, j, 0:32], in0=impa[:, j, :], scalar1=nst[:, 23 + 8 * j:24 + 8 * j], scalar2=None, op0=ALU.is_ge),
                          reads=["impa", "nst"], writes=["selq"])
                for j in range(4):
                    R.pe(lambda e, j=j: e.transpose(out=pT[:, j * 128:(j + 1) * 128], in_=selq[:, j, :], identity=ident_b[:]), reads=["selq", "ident_b"], writes=["pT"])
                R.act(lambda e: e.copy(out=selT[:], in_=pT[:, 0:512]), reads=["pT"], writes=["selT"])
                if nsa_part < 4:
                    continue
                for br in ((1, 2) if nsa_part >= 5 else (1,)):
                    Ksrc, ksn = (ksT, "ksT") if br == 1 else (kwT, "kwT")
                    kcs = list(range(0, 4 * qg + 4)) if br == 1 else list(range(max(0, 4 * qg - 4), 4 * qg + 4))
                    accs = [pO.next() for _ in range(4)]
                    firsts = [True] * 4
                    for kc in kcs:
                        off = 384 + qg * 512 - kc * 128
                        mkn, mkt = mk_r.next()
                        if br == 1:
                            R.pe(lambda e, kc=kc: e.matmul(pX[:, :], lhsT=Eexp[:, kc, :], rhs=selT[:, :], start=True, stop=True), reads=["Eexp", "selT"], writes=["pX"])
                            R.dve(lambda e, mkt=mkt, off=off: e.tensor_tensor(out=mkt[:], in0=pX[:, :], in1=tmask[:, 0, off:off + 512], op=ALU.mult),
                                  reads=["pX", "tmask"], writes=[mkn])
                        for g in range(4):
                            ch, ro = g // 2, (g % 2) * 64
                            pn, pt = pA.next()
                            smm(pn, pt, Ksrc, ksn, kc * 128, qTr2, "qTr2", ro, ch, qg)
                            ptn, PT = ptr.next()
                            R.act(lambda e, pt=pt, PT=PT: e.activation(out=PT[:, :], in_=pt[:, :], func=AF.Exp, scale=SCALE), reads=[pn], writes=[ptn])
                            if br == 1:
                                R.pool(lambda e, PT=PT, mkt=mkt: e.tensor_tensor(out=PT[:, :], in0=PT[:, :], in1=mkt[:], op=ALU.mult), reads=[ptn, mkn], writes=[ptn])
                            else:
                                R.pool(lambda e, PT=PT, off=off: e.tensor_tensor(out=PT[:, :], in0=PT[:, :], in1=tmask[:, 1, off:off + 512], op=ALU.mult),
                                       reads=[ptn, "tmask"], writes=[ptn])
                            an, acc = accs[g]
                            for j in range(4):
                                qc = 4 * qg + j
                                if kc > qc or (br == 2 and kc < qc - 4):
                                    continue
                                stf = firsts[g]
                                firsts[g] = False
                                R.pe(lambda e, j=j, acc=acc, PT=PT, kc=kc, br=br, hk=hk, stf=stf: e.matmul(
                                    acc[:, j * 128:(j + 1) * 128], lhsT=PT[:, j * 128:(j + 1) * 128], rhs=v2[:, kc, (br - 1) * 2 + hk, :],
                                    start=stf, stop=False, skip_group_check=True), reads=[ptn, "v2"], writes=[an])
                    for g in range(4):
                        an, acc = accs[g]
                        finish(an, acc, g, qg, hk, br, False)
                R.act(lambda e: e.copy(out=ynb[:].rearrange("p j c -> p (j c)"), in_=on[:].rearrange("p j g d -> p (j g d)")), reads=["on"], writes=["ynb"])
                for j in range(4):
                    for ch in range(2):
                        R.pe(lambda e, j=j, ch=ch: e.transpose(out=pT[:, (j * 2 + ch) * 128:(j * 2 + ch + 1) * 128], in_=ynb[:, j, ch * 128:(ch + 1) * 128],
                                                             identity=ident_b[:]), reads=["ynb", "ident_b"], writes=["pT"])
                R.act(lambda e, qg=qg, hk=hk: e.copy(out=yT[:, 4 + 2 * hk:6 + 2 * hk, qg * 512:(qg + 1) * 512].rearrange("p c (j t) -> p j c t", j=4),
                                                     in_=pT[:].rearrange("p (j c t) -> p j c t", j=4, c=2)), reads=["pT"], writes=["yT"])
    R.barrier()
    if "yT" in tap_t and b == 0 and l == 0:
        fin.append(R.dma("sp", lambda e: e.dma_start(out=tap_t["yT"][:, :, :], in_=yT[:]), reads=["yT"], writes=["tap_yT2"]))

    with contextlib.ExitStack() as sc_:
        sb4, _ = mk(sc_)
        g1p = bcast_mod(sb4, "g1p", 2, True)
        lng = sb4("lng", [128, D])
        lnb = sb4("lnb", [128, D])
        R.dma("sp", lambda e: e.dma_start(out=lng[:], in_=ln_g[l, 0:1, :].broadcast_to([128, D])), writes=["lng"])
        R.dma("sp", lambda e: e.dma_start(out=lnb[:], in_=ln_b[l, 0:1, :].broadcast_to([128, D])), writes=["lnb"])
        wo = sb4("wo", [128, 8, D], BF16)
        R.dma("pool", lambda e: e.dma_start(out=wo[:], in_=w_out[l, :, :].rearrange("(k p) n -> p k n", p=128)), writes=["wo"])
        mt_r = Rot([("mt%d" % i, sb4("mt%d" % i, [128, D])) for i in range(2)])
        for tc in range(NT):
            xcn, xc = xc_r.next()
            stn, stt = st_r.next()
            mtn, mt = mt_r.next()
            R.dma("sp", lambda e, xc=xc, tc=tc: e.dma_start(out=xc[:], in_=x_src[b, tc * 128:(tc + 1) * 128, :]), reads=["xb%d_%d" % (b, tc)], writes=[xcn])
            for hf in range(2):
                pn, pt = pA.next()
                for kc in range(8):
                    R.pe(lambda e, pt=pt, kc=kc, tc=tc, hf=hf: e.matmul(pt[:, :], lhsT=yT[:, kc, tc * 128:(tc + 1) * 128], rhs=wo[:, kc, hf * 512:(hf + 1) * 512],
                                                                         start=(kc == 0), stop=(kc == 7)), reads=["yT", "wo"], writes=[pn])
                R.dve(lambda e, pt=pt, mt=mt, hf=hf: e.tensor_tensor(out=mt[:, hf * 512:(hf + 1) * 512], in0=pt[:, :], in1=g1p[:, hf * 512:(hf + 1) * 512], op=ALU.mult),
                      reads=[pn, "g1p"], writes=[mtn])
            R.dve(lambda e, xc=xc, mt=mt: e.scalar_tensor_tensor(out=xc[:], in0=xc[:], scalar=ALPHA, in1=mt[:], op0=ALU.mult, op1=ALU.add),
                  reads=[xcn, mtn], writes=[xcn])
            layer_norm_chunk(xcn, xc, stn, stt, xcn, xc)
            R.pool(lambda e, xc=xc: e.tensor_tensor(out=xc[:], in0=xc[:], in1=lng[:], op=ALU.mult), reads=[xcn, "lng"], writes=[xcn])
            R.dve(lambda e, xc=xc: e.tensor_tensor(out=xc[:], in0=xc[:], in1=lnb[:], op=ALU.add), reads=[xcn, "lnb"], writes=[xcn])
            R.dma("sp", lambda e, xc=xc, tc=tc: e.dma_start(out=xbuf[b, tc * 128:(tc + 1) * 128, :], in_=xc[:]), reads=[xcn], writes=["xb%d_%d" % (b, tc)])
    if "x1" in tap_t and b == 0 and l == 0:
        fin.append(R.dma("sp", lambda e: e.dma_start(out=tap_t["x1"][:, :], in_=xbuf[0, :, :]), reads=["xb0_%d" % t_ for t_ in range(NT)], writes=["tap_x1"]))


NE = 32
GELU_A = 1.702


def moe_consts():
    c = {}
    tp = np.arange(128)[:, None]
    t = np.arange(128)[None, :]
    c["ltri"] = (tp < t).astype(np.float32).astype(ml_dtypes.bfloat16)
    c["onesb"] = np.ones((128, 128), np.float32).astype(ml_dtypes.bfloat16)
    c["iota3"] = np.broadcast_to(np.arange(128, dtype=np.float32)[None, None, :], (128, NE, 128)).astype(ml_dtypes.bfloat16).copy()
    return c


def moe(nc, R, mk, env):
    l, NB, NL = env["l"], env["NB"], env["NL"]
    xbuf, modbuf, out, cst = env["xbuf"], env["modbuf"], env["out"], env["cst"]
    ident_f, ident_b, tap_t, fin = env["ident_f"], env["ident_b"], env["tap_t"], env["fin"]
    ln_g, ln_b = env["ln_g"], env["ln_b"]
    router_w, router_b, w_gu, b_gu, w_dn, b_dn = env["router_w"], env["router_b"], env["w_gu"], env["b_gu"], env["w_dn"], env["b_dn"]
    H2d, Gd, CMd, XTd, PGd, Yd = env["H2d"], env["Gd"], env["CMd"], env["XTd"], env["PGd"], env["Yd"]
    last = (l == NL - 1)
    moe_stop = env.get("moe_stop", "D")

    def ln_chunk(xc, xcn, stt, stn):
        R.dve(lambda e: e.bn_stats(out=stt[:, 0:6], in_=xc[:, 0:512]), reads=[xcn], writes=[stn])
        R.dve(lambda e: e.bn_stats(out=stt[:, 6:12], in_=xc[:, 512:1024]), reads=[xcn], writes=[stn])
        R.dve(lambda e: e.bn_aggr(out=stt[:, 12:14], in_=stt[:, 0:12]), reads=[stn], writes=[stn])
        R.act(lambda e: e.activation(out=stt[:, 14:15], in_=stt[:, 13:14], func=AF.Sqrt, bias=LN_EPS, scale=1.0), reads=[stn], writes=[stn])
        R.dve(lambda e: e.reciprocal(out=stt[:, 15:16], in_=stt[:, 14:15]), reads=[stn], writes=[stn])
        R.dve(lambda e: e.tensor_scalar(out=xc[:], in0=xc[:], scalar1=stt[:, 12:13], scalar2=stt[:, 15:16], op0=ALU.subtract, op1=ALU.mult),
              reads=[xcn, stn], writes=[xcn])

    def bcast_row(sbx, name, src_ap, plus1=False, width=D):
        dst = sbx(name, [128, width])
        R.dma("sp", lambda e: e.dma_start(out=dst[:], in_=src_ap.broadcast_to([128, width])), reads=["modbuf"], writes=[name])
        if plus1:
            R.pool(lambda e: e.tensor_scalar(out=dst[:], in0=dst[:], scalar1=1.0, scalar2=None, op0=ALU.add), reads=[name], writes=[name])
        return dst

    with contextlib.ExitStack() as sA:
        sb, ps = mk(sA)
        wr = sb("wr", [128, 8, NE])
        wrh = sb("wrh", [128, 8, NE], BF16)
        wrl = sb("wrl", [128, 8, NE], BF16)
        R.dma("sp", lambda e: e.dma_start(out=wr[:], in_=router_w[l, :, :].rearrange("(k p) n -> p k n", p=128)), writes=["wr"])
        R.dve(lambda e: e.tensor_copy(out=wrh[:], in_=wr[:]), reads=["wr"], writes=["wrh"])
        R.dve(lambda e: e.tensor_tensor(out=wrl[:], in0=wr[:], in1=wrh[:], op=ALU.subtract), reads=["wr", "wrh"], writes=["wrl"])
        rb = bcast_row(sb, "rb", router_b[l:l + 1, :], width=NE)
        ltri = sb("ltri", [128, 128], BF16)
        onesb = sb("onesb", [128, 128], BF16)
        R.dma("sp", lambda e: e.dma_start(out=ltri[:], in_=cst["ltri"][:, :]), writes=["ltri"])
        R.dma("sp", lambda e: e.dma_start(out=onesb[:], in_=cst["onesb"][:, :]), writes=["onesb"])
        xc_r = Rot([("xa%d" % i, sb("xa%d" % i, [128, D])) for i in range(2)])
        hb_r = Rot([("hb%d" % i, sb("hb%d" % i, [128, D], BF16)) for i in range(2)])
        st_r = Rot([("sta%d" % i, sb("sta%d" % i, [128, 16])) for i in range(2)])
        hT_r = Rot([("hTa%d" % i, sb("hTa%d" % i, [128, 2, D], BF16)) for i in range(2)])
        hl_r = Rot([("hl%d" % i, sb("hl%d" % i, [128, D], BF16)) for i in range(2)])
        lg_r = Rot([("lg%d" % i, sb("lg%d" % i, [128, 96])) for i in range(2)])
        maskb = sb("maskb", [128, NT, NE], BF16)
        Gs = sb("Gs", [128, NT, NE])
        CMs = sb("CMs", [128, NT, NE])
        pTh = ps("pTh", [128, D], BF16)
        pTl = ps("pTl", [128, D], BF16)
        pl_r = Rot([("pl%d" % i, ps("pl%d" % i, [128, 512])) for i in range(2)])
        sh2 = sb("sh2", [128, D])
        sc2 = sb("sc2", [128, D])
        for b in range(NB):
            R.dma("sp", lambda e, b=b: e.dma_start(out=sh2[:], in_=modbuf[b, l:l + 1, 3 * D:4 * D].broadcast_to([128, D])), reads=["modbuf"], writes=["sh2_0"])
            R.dma("sp", lambda e, b=b: e.dma_start(out=sc2[:], in_=modbuf[b, l:l + 1, 4 * D:5 * D].broadcast_to([128, D])), reads=["modbuf"], writes=["sc2_0"])
            R.pool(lambda e: e.tensor_scalar(out=sc2[:], in0=sc2[:], scalar1=1.0, scalar2=None, op0=ALU.add), reads=["sc2_0"], writes=["sc2_0"])
            for tc in range(NT):
                xcn, xc = xc_r.next()
                hbn, hb = hb_r.next()
                stn, stt = st_r.next()
                htn, hTc = hT_r.next()
                lgn, lg = lg_r.next()
                pln, pl = pl_r.next()
                R.dma("sp", lambda e, xc=xc, tc=tc, b=b: e.dma_start(out=xc[:], in_=xbuf[b, tc * 128:(tc + 1) * 128, :]), reads=["xb%d_%d" % (b, tc)], writes=[xcn])
                ln_chunk(xc, xcn, stt, stn)
                R.pool(lambda e, xc=xc: e.tensor_tensor(out=xc[:], in0=xc[:], in1=sc2[:], op=ALU.mult), reads=[xcn, "sc2_0"], writes=[xcn])
                R.dve(lambda e, xc=xc: e.tensor_tensor(out=xc[:], in0=xc[:], in1=sh2[:], op=ALU.add), reads=[xcn, "sh2_0"], writes=[xcn])
                R.act(lambda e, xc=xc, hb=hb: e.copy(out=hb[:], in_=xc[:]), reads=[xcn], writes=[hbn])
                R.dma("sp", lambda e, hb=hb, tc=tc, b=b: e.dma_start(out=H2d[b, tc * 128:(tc + 1) * 128, :], in_=hb[:]), reads=[hbn], writes=["H2d%d" % b])
                hln, hl = hl_r.next()
                R.dve(lambda e, xc=xc, hb=hb, hl=hl: e.tensor_tensor(out=hl[:], in0=xc[:], in1=hb[:], op=ALU.subtract), reads=[xcn, hbn], writes=[hln])
                for kc in range(8):
                    R.pe(lambda e, hb=hb, kc=kc: e.transpose(out=pTh[:, kc * 128:(kc + 1) * 128], in_=hb[:, kc * 128:(kc + 1) * 128], identity=ident_b[:]),
                         reads=[hbn, "ident_b"], writes=["pTh"])
                for kc in range(8):
                    R.pe(lambda e, hl=hl, kc=kc: e.transpose(out=pTl[:, kc * 128:(kc + 1) * 128], in_=hl[:, kc * 128:(kc + 1) * 128], identity=ident_b[:]),
                         reads=[hln, "ident_b"], writes=["pTl"])
                R.act(lambda e, hTc=hTc: e.copy(out=hTc[:, 0, :], in_=pTh[:]), reads=["pTh"], writes=[htn])
                R.act(lambda e, hTc=hTc: e.copy(out=hTc[:, 1, :], in_=pTl[:]), reads=["pTl"], writes=[htn])
                terms = [(0, wrh, "wrh"), (1, wrh, "wrh"), (0, wrl, "wrl")]
                for ti, (hs, wt_, wn_) in enumerate(terms):
                    for kc in range(8):
                        R.pe(lambda e, hTc=hTc, kc=kc, pl=pl, hs=hs, wt_=wt_, ti=ti: e.matmul(pl[:, 0:NE], lhsT=hTc[:, hs, kc * 128:(kc + 1) * 128], rhs=wt_[:, kc, :],
                                                                                     start=(ti == 0 and kc == 0), stop=(ti == 2 and kc == 7)),
                             reads=[htn, wn_], writes=[pln])
                R.dve(lambda e, lg=lg, pl=pl: e.tensor_tensor(out=lg[:, 0:32], in0=pl[:, 0:NE], in1=rb[:], op=ALU.add), reads=[pln, "rb"], writes=[lgn])
                R.dve(lambda e, lg=lg: e.max(out=lg[:, 32:40], in_=lg[:, 0:32]), reads=[lgn], writes=[lgn])
                R.dve(lambda e, lg=lg: e.tensor_scalar(out=lg[:, 64:96], in0=lg[:, 0:32], scalar1=lg[:, 35:36], scalar2=None, op0=ALU.is_ge), reads=[lgn], writes=[lgn])
                R.dve(lambda e, lg=lg: e.tensor_scalar(out=lg[:, 40:41], in0=lg[:, 32:33], scalar1=-1.0, scalar2=None, op0=ALU.mult), reads=[lgn], writes=[lgn])
                R.act(lambda e, lg=lg: e.activation(out=lg[:, 0:32], in_=lg[:, 0:32], func=AF.Exp, bias=lg[:, 40:41], scale=1.0), reads=[lgn], writes=[lgn])
                R.dve(lambda e, lg=lg: e.tensor_tensor(out=lg[:, 0:32], in0=lg[:, 0:32], in1=lg[:, 64:96], op=ALU.mult), reads=[lgn], writes=[lgn])
                R.dve(lambda e, lg=lg: e.tensor_reduce(out=lg[:, 41:42], in_=lg[:, 0:32], axis=AX.X, op=ALU.add), reads=[lgn], writes=[lgn])
                R.dve(lambda e, lg=lg: e.reciprocal(out=lg[:, 42:43], in_=lg[:, 41:42]), reads=[lgn], writes=[lgn])
                R.dve(lambda e, lg=lg, tc=tc: e.tensor_scalar(out=Gs[:, tc, :], in0=lg[:, 0:32], scalar1=lg[:, 42:43], scalar2=None, op0=ALU.mult), reads=[lgn], writes=["Gs"])
                R.pool(lambda e, lg=lg, tc=tc: e.tensor_copy(out=maskb[:, tc, :], in_=lg[:, 64:96]), reads=[lgn], writes=["maskb"])
            for tc in range(NT):
                c = tc % 4
                g0 = tc - c
                pln, pl = pl_r.next()
                for cp in range(c):
                    R.pe(lambda e, pl=pl, cp=cp, g0=g0: e.matmul(pl[:, 0:NE], lhsT=onesb[:], rhs=maskb[:, g0 + cp, :], start=(cp == 0), stop=False),
                         reads=["onesb", "maskb"], writes=[pln])
                R.pe(lambda e, pl=pl, tc=tc, c=c: e.matmul(pl[:, 0:NE], lhsT=ltri[:], rhs=maskb[:, tc, :], start=(c == 0), stop=True), reads=["ltri", "maskb"], writes=[pln])
                R.dve(lambda e, pl=pl, tc=tc: e.scalar_tensor_tensor(out=CMs[:, tc, :], in0=pl[:, 0:NE], scalar=1.0, in1=maskb[:, tc, :], op0=ALU.add, op1=ALU.mult),
                      reads=[pln, "maskb"], writes=["CMs"])
            R.dve(lambda e: e.tensor_scalar(out=CMs[:], in0=CMs[:], scalar1=-1.0, scalar2=None, op0=ALU.add), reads=["CMs"], writes=["CMs"])
            R.dma("sp", lambda e, b=b: e.dma_start(out=Gd[b, :, :].rearrange("(c p) e -> p c e", p=128), in_=Gs[:]), reads=["Gs"], writes=["Gd%d" % b])
            R.dma("sp", lambda e, b=b: e.dma_start(out=CMd[b, :, :].rearrange("(c p) e -> p c e", p=128), in_=CMs[:]), reads=["CMs"], writes=["CMd%d" % b])
    R.barrier()
    if "G" in tap_t and l == 0:
        fin.append(R.dma("sp", lambda e: e.dma_start(out=tap_t["G"][:, :], in_=Gd[0, :, :]), reads=["Gd0"], writes=["tap_G"]))
        fin.append(R.dma("sp", lambda e: e.dma_start(out=tap_t["CM"][:, :], in_=CMd[0, :, :]), reads=["CMd0"], writes=["tap_CM"]))
    if moe_stop == "A":
        return

    with contextlib.ExitStack() as sB:
        sb, ps = mk(sB)
        iota3 = sb("iota3", [128, NE, 128], BF16)
        R.dma("sp", lambda e: e.dma_start(out=iota3[:], in_=cst["iota3"][:, :, :]), writes=["iota3"])
        h2g = sb("h2g", [128, 4, D], BF16)
        Gg = sb("Gg", [128, 4, NE])
        CMg = sb("CMg", [128, 4, NE])
        CMj = sb("CMj", [128, 4, NE])
        P = [sb("P%d" % i, [128, NE, 128], BF16) for i in range(4)]
        Pg = [sb("Pg%d" % i, [128, NE, 128], BF16) for i in range(4)]
        xe_r = Rot([("xe%d" % i, sb("xe%d" % i, [128, 8, 512], BF16)) for i in range(2)])
        pgt_r = Rot([("pgt%d" % i, sb("pgt%d" % i, [128, 1024], BF16)) for i in range(2)])
        pg_r = Rot([("pB%d" % i, ps("pB%d" % i, [128, 512])) for i in range(4)])
        pTb_r = Rot([("pTb%d" % i, ps("pTb%d" % i, [128, 1024], BF16)) for i in range(2)])
        for g in range(NB * 4):
            b, gq = g // 4, g % 4
            R.dma("sp", lambda e, b=b, gq=gq: e.dma_start(out=h2g[:], in_=H2d[b, gq * 512:(gq + 1) * 512, :].rearrange("(c p) d -> p c d", p=128)),
                  reads=["H2d%d" % b], writes=["h2g"])
            R.dma("sp", lambda e, b=b, gq=gq: e.dma_start(out=Gg[:], in_=Gd[b, gq * 512:(gq + 1) * 512, :].rearrange("(c p) e -> p c e", p=128)),
                  reads=["Gd%d" % b], writes=["Gg"])
            R.dma("sp", lambda e, b=b, gq=gq: e.dma_start(out=CMg[:], in_=CMd[b, gq * 512:(gq + 1) * 512, :].rearrange("(c p) e -> p c e", p=128)),
                  reads=["CMd%d" % b], writes=["CMg"])
            for jh in range(2):
                R.dve(lambda e, jh=jh: e.tensor_scalar(out=CMj[:], in0=CMg[:], scalar1=-128.0 * jh, scalar2=None, op0=ALU.add), reads=["CMg"], writes=["CMj"])
                for c in range(4):
                    R.dve(lambda e, c=c: e.tensor_tensor(out=P[c][:], in0=iota3[:], in1=CMj[:, c, :].unsqueeze(2).broadcast_to([128, NE, 128]), op=ALU.is_equal),
                          reads=["iota3", "CMj"], writes=["P%d" % c])
                    R.pool(lambda e, c=c: e.tensor_tensor(out=Pg[c][:], in0=P[c][:], in1=Gg[:, c, :].unsqueeze(2).broadcast_to([128, NE, 128]), op=ALU.mult),
                           reads=["P%d" % c, "Gg"], writes=["Pg%d" % c])
                for eq in range(8):
                    xen, xe = xe_r.next()
                    for dk in range(8):
                        pn, pt = pg_r.next()
                        for c in range(4):
                            R.pe(lambda e, pt=pt, c=c, dk=dk, eq=eq: e.matmul(pt[:, :], lhsT=h2g[:, c, dk * 128:(dk + 1) * 128],
                                                                             rhs=P[c][:, eq * 4:(eq + 1) * 4, :].rearrange("p e j -> p (e j)"), start=(c == 0), stop=(c == 3)),
                                 reads=["h2g", "P%d" % c], writes=[pn])
                        if dk % 2 == 0:
                            R.act(lambda e, pt=pt, xe=xe, dk=dk: e.copy(out=xe[:, dk, :], in_=pt[:, :]), reads=[pn], writes=[xen])
                        else:
                            R.dve(lambda e, pt=pt, xe=xe, dk=dk: e.tensor_copy(out=xe[:, dk, :], in_=pt[:, :]), reads=[pn], writes=[xen])
                    R.dma("sp", lambda e, xe=xe, g=g, eq=eq, jh=jh: e.dma_start(out=XTd[g // 4][g % 4, jh, eq, :, :, :], in_=xe[:]), reads=[xen], writes=["XTd%d" % g])
                for e2 in range(NE // 2):
                    ptn, ptb = pTb_r.next()
                    pgn, pgt = pgt_r.next()
                    for ee in range(2):
                        for c in range(4):
                            R.pe(lambda e, ptb=ptb, ee=ee, c=c, e2=e2: e.transpose(out=ptb[:, ee * 512 + c * 128:ee * 512 + (c + 1) * 128], in_=Pg[c][:, e2 * 2 + ee, :], identity=ident_b[:]),
                                 reads=["Pg%d" % c, "ident_b"], writes=[ptn])
                    R.act(lambda e, ptb=ptb, pgt=pgt: e.copy(out=pgt[:], in_=ptb[:]), reads=[ptn], writes=[pgn])
                    R.dma("sp", lambda e, pgt=pgt, g=g, e2=e2, jh=jh: e.dma_start(out=PGd[g // 4][g % 4, jh, e2 * 2:e2 * 2 + 2, :, :].rearrange("e j t -> j e t"),
                                                                              in_=pgt[:].rearrange("p (e t) -> p e t", e=2)), reads=[pgn], writes=["PGd%d" % g])
    R.barrier()
    if moe_stop == "B":
        return

    with contextlib.ExitStack() as sC:
        sb, ps = mk(sC)
        wgu_r = Rot([("wgu%d" % i, sb("wgu%d" % i, [128, 8, 2 * D], BF16)) for i in range(2)])
        wd_r = Rot([("wd%d" % i, sb("wd%d" % i, [128, 8, D], BF16)) for i in range(2)])
        brow_r = Rot([("brow%d" % i, sb("brow%d" % i, [16, 128])) for i in range(2)])
        bgu_r = Rot([("bgu%d" % i, sb("bgu%d" % i, [128, 16])) for i in range(2)])
        xt_r = Rot([("xt%d" % i, sb("xt%d" % i, [128, 8, 512], BF16)) for i in range(2)])
        at_r = Rot([("at%d" % i, sb("at%d" % i, [128, 8, 512], BF16)) for i in range(2)])
        gc_r = Rot([("gc%d" % i, sb("gc%d" % i, [128, 512])) for i in range(2)])
        sl_r = Rot([("sl%d" % i, sb("sl%d" % i, [128, 512])) for i in range(2)])
        u0_r = Rot([("u0%d" % i, sb("u0%d" % i, [128, 512])) for i in range(2)])
        ys_r = Rot([("ys%d" % i, sb("ys%d" % i, [128, D], BF16)) for i in range(2)])
        pgu_r = Rot([("pC%d" % i, ps("pC%d" % i, [128, 512])) for i in range(4)])
        pdn_r = Rot([("pD%d" % i, ps("pD%d" % i, [128, 512])) for i in range(3)])
        pXc = ps("pXc", [128, 512])
        for ex in range(NE):
            wgn, wgu = wgu_r.next()
            wdn, wd = wd_r.next()
            brn, brow = brow_r.next()
            bgn, bgu = bgu_r.next()
            for hf in range(2):
                R.dma("pool", lambda e, wgu=wgu, ex=ex, hf=hf: e.dma_start(out=wgu[:, :, hf * D:(hf + 1) * D],
                                                                          in_=w_gu[l, ex, :, hf * D:(hf + 1) * D].rearrange("(k p) n -> p k n", p=128)), writes=[wgn])
            R.dma("pool", lambda e, wd=wd, ex=ex: e.dma_start(out=wd[:], in_=w_dn[l, ex, :, :].rearrange("(k p) n -> p k n", p=128)), writes=[wdn])
            R.dma("sp", lambda e, brow=brow, ex=ex: e.dma_start(out=brow[:], in_=b_gu[l, ex, :].rearrange("(m p) -> m p", p=128)), writes=[brn])
            R.pe(lambda e, brow=brow: e.transpose(out=pXc[:, 0:16], in_=brow[:], identity=ident_f[0:16, 0:16]), reads=[brn, "ident_f"], writes=["pXc"])
            R.dve(lambda e, bgu=bgu: e.tensor_copy(out=bgu[:], in_=pXc[:, 0:16]), reads=["pXc"], writes=[bgn])
            for b, jh in [(b_, j_) for b_ in range(NB) for j_ in range(2)]:
                xtn, xt = xt_r.next()
                atn, at = at_r.next()
                for gq in range(4):
                    R.dma("sp", lambda e, xt=xt, b=b, gq=gq, ex=ex, jh=jh: e.dma_start(
                        out=xt[:, :, gq * 128:(gq + 1) * 128], in_=XTd[b][gq, jh, ex // 4, :, :, (ex % 4) * 128:(ex % 4 + 1) * 128]),
                        reads=["XTd%d" % (b * 4 + gq)], writes=[xtn])
                for m in range(8):
                    pgn_, pgp = pgu_r.next()
                    pun_, pup = pgu_r.next()
                    for (pp, pnm, mm) in ((pgp, pgn_, m), (pup, pun_, m + 8)):
                        for dk in range(8):
                            R.pe(lambda e, pp=pp, mm=mm, dk=dk, wgu=wgu, xt=xt: e.matmul(pp[:, :], lhsT=wgu[:, dk, mm * 128:(mm + 1) * 128], rhs=xt[:, dk, :],
                                                                                     start=(dk == 0), stop=(dk == 7)), reads=[wgn, xtn], writes=[pnm])
                    gcn, gc = gc_r.next()
                    sln, sl = sl_r.next()
                    u0n, u0 = u0_r.next()
                    R.dve(lambda e, gc=gc, pgp=pgp, bgu=bgu, m=m: e.tensor_scalar(out=gc[:], in0=pgp[:, :], scalar1=bgu[:, m:m + 1], scalar2=7.0, op0=ALU.add, op1=ALU.min),
                          reads=[pgn_, bgn], writes=[gcn])
                    R.act(lambda e, gc=gc, sl=sl: e.activation(out=sl[:], in_=gc[:], func=AF.Silu, scale=GELU_A), reads=[gcn], writes=[sln])
                    R.act(lambda e, u0=u0, pup=pup, bgu=bgu, m=m: e.activation(out=u0[:], in_=pup[:, :], func=AF.Identity, bias=bgu[:, m + 8:m + 9], scale=1.0),
                          reads=[pun_, bgn], writes=[u0n])
                    R.pool(lambda e, u0=u0: e.tensor_scalar(out=u0[:], in0=u0[:], scalar1=7.0, scalar2=-7.0, op0=ALU.min, op1=ALU.max), reads=[u0n], writes=[u0n])
                    R.dve(lambda e, u0=u0, sl=sl, at=at, m=m: e.scalar_tensor_tensor(out=at[:, m, :], in0=u0[:], scalar=1.0, in1=sl[:], op0=ALU.add, op1=ALU.mult),
                          reads=[u0n, sln], writes=[atn])
                for gq in range(4):
                    ysn, ys = ys_r.next()
                    for hf in range(2):
                        pdn_, pdp = pdn_r.next()
                        for m in range(8):
                            R.pe(lambda e, pdp=pdp, m=m, gq=gq, hf=hf, at=at, wd=wd: e.matmul(pdp[:, :], lhsT=at[:, m, gq * 128:(gq + 1) * 128], rhs=wd[:, m, hf * 512:(hf + 1) * 512],
                                                                                          start=(m == 0), stop=(m == 7)), reads=[atn, wdn], writes=[pdn_])
                        R.act(lambda e, pdp=pdp, ys=ys, hf=hf: e.activation(out=ys[:, hf * 512:(hf + 1) * 512], in_=pdp[:, :], func=AF.Copy, scale=1.0 / GELU_A),
                              reads=[pdn_], writes=[ysn])
                    R.dma("sp", lambda e, ys=ys, b=b, gq=gq, ex=ex, jh=jh: e.dma_start(out=Yd[b][gq, jh, ex, :, :], in_=ys[:]), reads=[ysn], writes=["Yd%d" % (b * 4 + gq)])
    R.barrier()
    if moe_stop == "C":
        return

    with contextlib.ExitStack() as sD:
        sb, ps = mk(sD)
        Ysb = sb("Ysb", [128, NE, D], BF16)
        PGs = sb("PGs", [128, NE, 512], BF16)
        Bdp = sb("Bdp", [128, D], BF16)
        Gpad = sb("Gpad", [128, 128], BF16)
        GTp_r = Rot([("GTp%d" % i, sb("GTp%d" % i, [128, 128], BF16)) for i in range(2)])
        Gg2 = sb("Gg2", [128, 4, NE])
        facc = sb("facc", [128, 4, D])
        lng = sb("lng2", [128, D])
        lnb = sb("lnb2", [128, D])
        g2p = sb("g2p", [128, D])
        R.dma("sp", lambda e: e.dma_start(out=lng[:], in_=ln_g[l, 1:2, :].broadcast_to([128, D])), writes=["lng2"])
        R.dma("sp", lambda e: e.dma_start(out=lnb[:], in_=ln_b[l, 1:2, :].broadcast_to([128, D])), writes=["lnb2"])
        R.pool(lambda e: e.memset(Bdp[:], 0.0), writes=["Bdp"])
        R.pool(lambda e: e.memset(Gpad[:], 0.0), writes=["Gpad"])
        R.dma("pool", lambda e: e.dma_start(out=Bdp[0:NE, :], in_=b_dn[l, :, :]), reads=["Bdp"], writes=["Bdp"])
        xc_r = Rot([("xd%d" % i, sb("xd%d" % i, [128, D])) for i in range(2)])
        mt_r = Rot([("md%d" % i, sb("md%d" % i, [128, D])) for i in range(2)])
        st_r = Rot([("std%d" % i, sb("std%d" % i, [128, 16])) for i in range(2)])
        pc_r = Rot([("pE%d" % i, ps("pE%d" % i, [128, 512])) for i in range(4)])
        pTd = ps("pTd", [128, 1024], BF16)
        for g in range(NB * 4):
            b, gq = g // 4, g % 4
            if gq == 0:
                R.dma("sp", lambda e, b=b: e.dma_start(out=g2p[:], in_=modbuf[b, l:l + 1, 5 * D:6 * D].broadcast_to([128, D])), reads=["modbuf"], writes=["g2p"])
                R.pool(lambda e: e.tensor_scalar(out=g2p[:], in0=g2p[:], scalar1=1.0, scalar2=None, op0=ALU.add), reads=["g2p"], writes=["g2p"])
            R.dma("sp", lambda e, b=b, gq=gq: e.dma_start(out=Gg2[:], in_=Gd[b, gq * 512:(gq + 1) * 512, :].rearrange("(c p) e -> p c e", p=128)),
                  reads=["Gd%d" % b], writes=["Gg2"])
            for jh in range(2):
                for q4 in range(4):
                    R.dma("sp", lambda e, g=g, q4=q4, jh=jh: e.dma_start(out=Ysb[:, q4 * 8:(q4 + 1) * 8, :], in_=Yd[g // 4][g % 4, jh, q4 * 8:(q4 + 1) * 8, :, :].rearrange("e j d -> j e d")),
                          reads=["Yd%d" % g], writes=["Ysb"])
                    R.dma("sp", lambda e, g=g, q4=q4, jh=jh: e.dma_start(out=PGs[:, q4 * 8:(q4 + 1) * 8, :], in_=PGd[g // 4][g % 4, jh, q4 * 8:(q4 + 1) * 8, :, :].rearrange("e j t -> j e t")),
                          reads=["PGd%d" % g], writes=["PGs"])
                for c in range(4):
                    tc = gq * 4 + c
                    if jh == 0:
                        gtn, GTp = GTp_r.next()
                        R.dve(lambda e, c=c: e.tensor_copy(out=Gpad[:, 0:NE], in_=Gg2[:, c, :]), reads=["Gg2", "Gpad"], writes=["Gpad"])
                        R.pe(lambda e: e.transpose(out=pTd[:, 0:128], in_=Gpad[:], identity=ident_b[:]), reads=["Gpad", "ident_b"], writes=["pTd"])
                        R.act(lambda e, GTp=GTp: e.copy(out=GTp[:], in_=pTd[:, 0:128]), reads=["pTd"], writes=[gtn])
                    else:
                        xcn, xc = xc_r.next()
                        mtn, mt = mt_r.next()
                        stn, stt = st_r.next()
                        R.dma("sp", lambda e, xc=xc, tc=tc, b=b: e.dma_start(out=xc[:], in_=xbuf[b, tc * 128:(tc + 1) * 128, :]), reads=["xb%d_%d" % (b, tc)], writes=[xcn])
                    for hf in range(2):
                        pn, pt = pc_r.next()
                        for ex in range(NE):
                            R.pe(lambda e, pt=pt, ex=ex, c=c, hf=hf: e.matmul(pt[:, :], lhsT=PGs[:, ex, c * 128:(c + 1) * 128], rhs=Ysb[:, ex, hf * 512:(hf + 1) * 512],
                                                                             start=(ex == 0), stop=(jh == 1 and ex == NE - 1)), reads=["PGs", "Ysb"], writes=[pn])
                        if jh == 0:
                            R.pe(lambda e, pt=pt, GTp=GTp, hf=hf: e.matmul(pt[:, :], lhsT=GTp[:], rhs=Bdp[:, hf * 512:(hf + 1) * 512], start=False, stop=True),
                                 reads=[gtn, "Bdp"], writes=[pn])
                            R.act(lambda e, pt=pt, c=c, hf=hf: e.copy(out=facc[:, c, hf * 512:(hf + 1) * 512], in_=pt[:, :]), reads=[pn], writes=["facc"])
                        else:
                            R.dve(lambda e, pt=pt, mt=mt, hf=hf, c=c: e.tensor_tensor(out=mt[:, hf * 512:(hf + 1) * 512], in0=pt[:, :], in1=facc[:, c, hf * 512:(hf + 1) * 512], op=ALU.add),
                                  reads=[pn, "facc"], writes=[mtn])
                    if jh == 1:
                        R.pool(lambda e, mt=mt: e.tensor_tensor(out=mt[:], in0=mt[:], in1=g2p[:], op=ALU.mult), reads=[mtn, "g2p"], writes=[mtn])
                        R.dve(lambda e, xc=xc, mt=mt: e.scalar_tensor_tensor(out=xc[:], in0=xc[:], scalar=ALPHA, in1=mt[:], op0=ALU.mult, op1=ALU.add),
                              reads=[xcn, mtn], writes=[xcn])
                        ln_chunk(xc, xcn, stt, stn)
                        R.pool(lambda e, xc=xc: e.tensor_tensor(out=xc[:], in0=xc[:], in1=lng[:], op=ALU.mult), reads=[xcn, "lng2"], writes=[xcn])
                        R.dve(lambda e, xc=xc: e.tensor_tensor(out=xc[:], in0=xc[:], in1=lnb[:], op=ALU.add), reads=[xcn, "lnb2"], writes=[xcn])
                        if last:
                            fin.append(R.dma("sp", lambda e, xc=xc, tc=tc, b=b: e.dma_start(out=out[b, tc * 128:(tc + 1) * 128, :], in_=xc[:]), reads=[xcn], writes=["out%d_%d" % (b, tc)]))
                        else:
                            R.dma("sp", lambda e, xc=xc, tc=tc, b=b: e.dma_start(out=xbuf[b, tc * 128:(tc + 1) * 128, :], in_=xc[:]), reads=[xcn], writes=["xb%d_%d" % (b, tc)])
    R.barrier()


N_CORES = 1
FUSED = False
DEPTH = 4
_PROG = {}


def _get_prog(NB, NL):
    key = (NB, NL)
    if key not in _PROG:
        _PROG[key] = build(NB, NL)
    return _PROG[key]


def _consts():
    hc = host_consts()
    hc.update(moe_consts())
    return {"k_" + k: v for k, v in hc.items()}


_PER_LAYER = ("ada_w", "ada_b", "w_in", "w_out", "ret_gn_w", "conv_w", "cmp_pos", "cmp_w1", "cmp_w2", "ln_g", "ln_b",
              "router_w", "router_b", "w_gate_up", "b_gate_up", "w_down", "b_down")


def kernel(**inputs):
    B = inputs["x"].shape[0]
    NB = B // N_CORES
    f32 = np.float32
    x = np.ascontiguousarray(inputs["x"], dtype=f32)
    c = np.ascontiguousarray(inputs["c"], dtype=f32)
    pos = np.ascontiguousarray(inputs["positions"], dtype=np.int32)
    consts = _consts()
    layer_sets = [list(range(DEPTH))] if FUSED else [[l] for l in range(DEPTH)]
    for ls in layer_sets:
        nc = _get_prog(NB, len(ls))
        w = {k: np.ascontiguousarray(np.asarray(inputs[k])[ls[0]:ls[-1] + 1], dtype=f32) for k in _PER_LAYER}
        in_maps = []
        for ci in range(N_CORES):
            m = dict(w)
            m.update(consts)
            m["x"] = np.ascontiguousarray(x[ci * NB:(ci + 1) * NB])
            m["c"] = np.ascontiguousarray(c[ci * NB:(ci + 1) * NB])
            m["positions"] = np.ascontiguousarray(pos[ci * NB:(ci + 1) * NB])
            in_maps.append(m)
        res = run_bass_kernel_spmd(nc, in_maps, core_ids=list(range(N_CORES)))
        x = np.concatenate([np.asarray(r["out"], dtype=f32) for r in res.results], axis=0)
    return x
```

```python
import math
import contextlib
import numpy as np
import ml_dtypes
import concourse.bass as bass
import concourse.mybir as mybir
from concourse.bass_utils import run_bass_kernel_spmd


ENGS = ("pe", "act", "dve", "pool", "sp")
NPOOL = 24


class Rec:
    def __init__(self, nc):
        self.nc = nc
        self.ops = []
        self.lastw = {}
        self.readers = {}
        self.ndma = 0
        self.dma_idx = []

    def eng_obj(self, e):
        nc = self.nc
        return {"pe": nc.tensor, "act": nc.scalar, "dve": nc.vector, "pool": nc.gpsimd, "sp": nc.sync}[e]

    def add(self, eng, fn, reads=(), writes=(), dma=False):
        idx = len(self.ops)
        deps = set()
        reads = list(reads) + ["BARRIER"]
        for r in reads:
            if r in self.lastw:
                deps.add(self.lastw[r])
        for w in writes:
            if w in self.lastw:
                deps.add(self.lastw[w])
            for rd in self.readers.get(w, ()):
                deps.add(rd)
        op = dict(eng=eng, fn=fn, deps=deps, dma=dma, used=False, slot=None)
        if dma:
            op["slot"] = self.ndma % NPOOL
            op["val"] = 16 * (self.ndma // NPOOL + 1)
            prev = self.ndma - NPOOL
            if prev >= 0:
                deps.add(self.dma_idx[prev])
            self.dma_idx.append(idx)
            self.ndma += 1
        self.ops.append(op)
        for r in reads:
            self.readers.setdefault(r, []).append(idx)
        for w in writes:
            self.lastw[w] = idx
            self.readers[w] = []
        return idx

    def barrier(self):
        return self.add("sp", lambda e: e.nop(), reads=(), writes=["BARRIER"])

    def pe(self, fn, reads=(), writes=()):
        return self.add("pe", fn, reads, writes)

    def act(self, fn, reads=(), writes=()):
        return self.add("act", fn, reads, writes)

    def dve(self, fn, reads=(), writes=()):
        return self.add("dve", fn, reads, writes)

    def pool(self, fn, reads=(), writes=()):
        return self.add("pool", fn, reads, writes)

    def dma(self, eng, fn, reads=(), writes=()):
        return self.add(eng, fn, reads, writes, dma=True)

    def emit(self, final_wait_ops=()):
        nc = self.nc
        ops = self.ops
        for i, op in enumerate(ops):
            for d in op["deps"]:
                dop = ops[d]
                if (not dop["dma"]) and dop["eng"] == "pe" and op["eng"] == "pe" and not op["dma"]:
                    continue
                dop["used"] = True
        for i in final_wait_ops:
            ops[i]["used"] = True
        cnt = {e: 0 for e in ENGS}
        for op in ops:
            if not op["dma"] and op["used"]:
                cnt[op["eng"]] += 1
                op["val"] = cnt[op["eng"]]
        import contextlib
        with contextlib.ExitStack() as st:
            esem = {e: st.enter_context(nc.semaphore("es_" + e)) for e in ENGS}
            dsem = [st.enter_context(nc.semaphore("ds_%d" % i)) for i in range(NPOOL)]
            block = st.enter_context(nc.Block())

            def semof(op):
                if op["dma"]:
                    return dsem[op["slot"]], op["val"], ("d", op["slot"])
                return esem[op["eng"]], op["val"], ("e", op["eng"])

            def run_engine(e, engobj):
                seen = {}
                for i, op in enumerate(ops):
                    if op["eng"] != e:
                        continue
                    need = {}
                    for d in op["deps"]:
                        dop = ops[d]
                        if (not dop["dma"]) and dop["eng"] == "pe" and e == "pe" and not op["dma"]:
                            continue
                        s, v, k = semof(dop)
                        if seen.get(k, 0) >= v:
                            continue
                        if k not in need or need[k][1] < v:
                            need[k] = (s, v)
                    for k, (s, v) in need.items():
                        engobj.wait_ge(s, v)
                        seen[k] = v
                    ins = op["fn"](engobj)
                    if op["dma"]:
                        ins.then_inc(dsem[op["slot"]], 16)
                    elif op["used"]:
                        ins.then_inc(esem[e], 1)
                if e == "sp":
                    for i in final_wait_ops:
                        s, v, k = semof(ops[i])
                        engobj.wait_ge(s, v)

            @block.tensor
            def _(eng):
                run_engine("pe", eng)

            @block.scalar
            def _(eng):
                run_engine("act", eng)

            @block.vector
            def _(eng):
                run_engine("dve", eng)

            @block.gpsimd
            def _(eng):
                run_engine("pool", eng)

            @block.sync
            def _(eng):
                run_engine("sp", eng)

F32 = mybir.dt.float32
BF16 = mybir.dt.bfloat16
I32 = mybir.dt.int32
AF = mybir.ActivationFunctionType
ALU = mybir.AluOpType
AX = mybir.AxisListType

D = 1024
S = 2048
NT = 16
LN_EPS = 1e-5
ALPHA = 8.0 ** 0.25
NEGB = -30000.0
SCALE = 0.125
C_RQ, C_RK, C_RV, C_RG = 0, 256, 512, 768
C_CB, C_CC, C_CH = 1024, 1280, 1536
C_NQ = 1792
C_KC, C_VC, C_KS, C_VS, C_KW, C_VW = 2304, 2432, 2560, 2688, 2816, 2944
C_NG = 3072
NIN = 3096
TW = 2432
NE = 32
GELU_A = 1.702


class Rot:
    def __init__(self, items):
        self.items = items
        self.i = 0

    def next(self):
        it = self.items[self.i % len(self.items)]
        self.i += 1
        return it


def host_consts():
    c = {}
    c["ident"] = np.eye(128, dtype=np.float32)
    r = np.arange(128)
    inv = (10000.0 ** (-((r % 64) % 32).astype(np.float64) / 32.0))
    c["ropec"] = np.stack([inv / (2 * np.pi), np.where((r % 64) < 32, -1.0, 1.0)], 1).astype(np.float32)
    gam = 1.0 - 2.0 ** (-5.0 - np.arange(4))
    k = np.arange(128)[:, None]
    m = np.arange(TW)[None, :] - 384
    tabs = []
    for h in range(4):
        e = (m - k).astype(np.float64)
        tabs.append(np.where(e >= 0, np.exp(np.log(gam[h]) * np.maximum(e, 0)) * SCALE, 0.0))
    c["dect"] = np.stack(tabs, 1).astype(ml_dtypes.bfloat16)
    dd = (m - k)
    c["tmask"] = np.stack([(dd >= 0), (dd >= 0) & (dd < 512)], 1).astype(np.float32).astype(ml_dtypes.bfloat16)
    kk = np.arange(128)[:, None]
    qq = np.arange(128)[None, :]
    c["cb_caus"] = np.where(kk > qq, 0.0, 1.0).astype(ml_dtypes.bfloat16)
    c["cb_low"] = np.where(kk <= qq, 0.0, 1.0).astype(ml_dtypes.bfloat16)
    cc = np.arange(128)[:, None]
    tt = np.arange(2048)[None, :]
    c["cb_cmp"] = np.where(16 * cc + 31 > tt, 0.0, 1.0).astype(ml_dtypes.bfloat16)
    cs = np.arange(128) * 16
    cl = cs + 31
    ss = np.arange(32) * 64
    ov = ((cs[:, None] < ss[None, :] + 64) & (ss[None, :] <= cl[:, None])).astype(np.float32)
    ov[127] = 0
    c["overlap"] = ov.astype(ml_dtypes.bfloat16)
    t = (np.arange(16)[None, :, None] * 128 + np.arange(128)[:, None, None])
    j = np.arange(32)[None, None, :]
    c["forced"] = np.where((j == 0) | (j == t // 64), 1e9, 0.0).astype(np.float32)
    b = np.arange(32)[:, None, None]
    kc = np.arange(16)[None, :, None]
    kl = np.arange(128)[None, None, :]
    E = np.zeros((128, 16, 128), np.float32)
    E[0:32] = (b == 2 * kc + (kl >= 64))
    c["Eexp"] = E.astype(ml_dtypes.bfloat16)
    return c


def build(NB=1, NL=1, taps=(), stop=None, nsa_stage=2, nsa_part=9, nsa_hks=(0, 1), nsa_qcs=tuple(range(4)), moe_stop="D", skip_mixer=False, ne_in=NE):
    nc = bass.Bass("TRN2", target_bir_lowering=False)
    R = Rec(nc)
    dram = {}

    def din(name, shape, dt=F32):
        dram[name] = nc.dram_tensor(name, list(shape), dt, kind="ExternalInput").ap()
        return dram[name]

    def dint(name, shape, dt=F32):
        dram[name] = nc.dram_tensor(name, list(shape), dt, kind="Internal").ap()
        return dram[name]

    def dout(name, shape, dt=F32):
        dram[name] = nc.dram_tensor(name, list(shape), dt, kind="ExternalOutput").ap()
        return dram[name]

    x_in = din("x", [NB, S, D])
    c_in = din("c", [NB, D])
    pos_in = din("positions", [NB, S], I32)
    ada_w = din("ada_w", [NL, D, 6 * D])
    ada_b = din("ada_b", [NL, 6 * D])
    w_in = din("w_in", [NL, D, NIN])
    w_out = din("w_out", [NL, D, D])
    gn_w = din("ret_gn_w", [NL, 256])
    conv_w = din("conv_w", [NL, 3, 256])
    cmp_pos = din("cmp_pos", [NL, 2, 32, 64])
    cmp_w1 = din("cmp_w1", [NL, 2, 2048, 128])
    cmp_w2 = din("cmp_w2", [NL, 2, 128, 64])
    ln_g = din("ln_g", [NL, 2, D])
    ln_b = din("ln_b", [NL, 2, D])
    router_w = din("router_w", [NL, D, NE])
    router_b = din("router_b", [NL, NE])
    w_gu = din("w_gate_up", [NL, ne_in, D, 2 * D])
    b_gu = din("b_gate_up", [NL, ne_in, 2 * D])
    w_dn = din("w_down", [NL, ne_in, D, D])
    b_dn = din("b_down", [NL, ne_in, D])
    HC = host_consts()
    HC.update(moe_consts())
    cst = {}
    for k, v in HC.items():
        cst[k] = din("k_" + k, list(v.shape), BF16 if v.dtype == ml_dtypes.bfloat16 else F32)
    out = dout("out", [NB, S, D])
    xbuf = dint("xbuf", [NB, S, D])
    ropebuf = dint("ropebuf", [NB, 2, 128, S])
    NG = NB * 4
    H2d = dint("H2d", [NB, S, D], BF16)
    Gd = dint("Gd", [NB, S, NE])
    CMd = dint("CMd", [NB, S, NE])
    XTd = [dint("XTd%d" % i, [4, 2, 8, 128, 8, 512], BF16) for i in range(NB)]
    PGd = [dint("PGd%d" % i, [4, 2, NE, 128, 512], BF16) for i in range(NB)]
    Yd = [dint("Yd%d" % i, [4, 2, NE, 128, D], BF16) for i in range(NB)]
    tap_t = {}
    for (nm, shape, dt) in taps:
        tap_t[nm] = dout("tap_" + nm, shape, dt)
    fin = []

    top = contextlib.ExitStack()

    uniq = [0]

    def mk(stack):
        def sbuf(name, shape, dt=F32):
            uniq[0] += 1
            return stack.enter_context(nc.sbuf_tensor("%s_%d" % (name, uniq[0]), list(shape), dt))

        def psum(name, shape, dt=F32):
            uniq[0] += 1
            return stack.enter_context(nc.psum_tensor("%s_%d" % (name, uniq[0]), list(shape), dt))
        return sbuf, psum

    with top:
        sbuf, psum = mk(top)
        ident_f = sbuf("ident_f", [128, 128])
        ident_b = sbuf("ident_b", [128, 128], BF16)
        R.dma("sp", lambda e: e.dma_start(out=ident_f[:], in_=cst["ident"][:, :]), writes=["ident_f"])
        R.dve(lambda e: e.tensor_copy(out=ident_b[:], in_=ident_f[:]), reads=["ident_f"], writes=["ident_b"])
        ropec = sbuf("ropec", [128, 2])
        R.dma("sp", lambda e: e.dma_start(out=ropec[:], in_=cst["ropec"][:, :]), writes=["ropec"])
        modbuf = dint("modbuf", [NB, NL, 6 * D])

        with contextlib.ExitStack() as st0:
            sb0, ps0 = mk(st0)
            c_sb = sb0("c_sb", [NB, D])
            cT = sb0("cT", [128, 8, NB], BF16)
            R.dma("sp", lambda e: e.dma_start(out=c_sb[:], in_=c_in[:, :]), writes=["c_sb"])
            R.act(lambda e: e.activation(out=c_sb[:], in_=c_sb[:], func=AF.Silu), reads=["c_sb"], writes=["c_sb"])
            ps_t = ps0("ps_t", [128, 512])
            for kc in range(8):
                R.pe(lambda e, kc=kc: e.transpose(out=ps_t[:, kc * NB:(kc + 1) * NB], in_=c_sb[:, kc * 128:(kc + 1) * 128],
                                                 identity=ident_f[0:NB, 0:NB]),
                     reads=["c_sb", "ident_f"], writes=["ps_t"])
            R.dve(lambda e: e.tensor_copy(out=cT[:].rearrange("p k b -> p (k b)"), in_=ps_t[:, 0:8 * NB]),
                  reads=["ps_t"], writes=["cT"])
            mods = sb0("mods", [NB, 6 * D])
            adab = sb0("adab", [NB, 6 * D])
            adaw = Rot([("adaw%d" % i, sb0("adaw%d" % i, [128, 8, 512], BF16)) for i in range(2)])
            psm = Rot([("ps_m%d" % i, ps0("ps_m%d" % i, [128, 512])) for i in range(2)])
            for l in range(NL):
                for b in range(NB):
                    R.dma("sp", lambda e, b=b, l=l: e.dma_start(out=adab[b:b + 1, :], in_=ada_b[l:l + 1, :]), writes=["adab"])
                for cc in range(12):
                    wn, wb = adaw.next()
                    pn, pm = psm.next()
                    R.dma("pool", lambda e, wb=wb, l=l, cc=cc: e.dma_start(
                        out=wb[:], in_=ada_w[l, :, cc * 512:(cc + 1) * 512].rearrange("(k p) n -> p k n", p=128)),
                        writes=[wn])
                    for kc in range(8):
                        R.pe(lambda e, wb=wb, pm=pm, kc=kc: e.matmul(pm[0:NB, :], lhsT=cT[:, kc, :], rhs=wb[:, kc, :],
                                                                     start=(kc == 0), stop=(kc == 7)),
                             reads=["cT", wn], writes=[pn])
                    R.dve(lambda e, pm=pm, cc=cc: e.tensor_tensor(
                        out=mods[:, cc * 512:(cc + 1) * 512], in0=pm[0:NB, :], in1=adab[:, cc * 512:(cc + 1) * 512], op=ALU.add),
                        reads=[pn, "adab"], writes=["mods"])
                R.dma("sp", lambda e, l=l: e.dma_start(out=modbuf[:, l, :], in_=mods[:]), reads=["mods"], writes=["modbuf"])
            posi = sb0("posi", [128, S], I32)
            y0 = sb0("y0", [128, S])
            r1 = sb0("r1", [128, S])
            ki = sb0("ki", [128, S], I32)
            kf = sb0("kf", [128, S])
            for b in range(NB):
                R.dma("sp", lambda e, b=b: e.dma_start(out=posi[:], in_=pos_in[b:b + 1, :].broadcast_to([128, S])), writes=["posi"])
                R.dve(lambda e: e.tensor_copy(out=y0[:], in_=posi[:]), reads=["posi"], writes=["y0"])
                R.dve(lambda e: e.tensor_scalar(out=y0[:], in0=y0[:], scalar1=ropec[:, 0:1], scalar2=None, op0=ALU.mult),
                      reads=["y0", "ropec"], writes=["y0"])
                for which in range(2):
                    sh = 0.25 if which == 0 else 0.0
                    R.dve(lambda e, sh=sh: e.tensor_scalar(out=r1[:], in0=y0[:], scalar1=sh, scalar2=None, op0=ALU.add),
                          reads=["y0"], writes=["r1"])
                    R.dve(lambda e: e.tensor_copy(out=ki[:], in_=r1[:]), reads=["r1"], writes=["ki"])
                    R.dve(lambda e: e.tensor_copy(out=kf[:], in_=ki[:]), reads=["ki"], writes=["kf"])
                    R.dve(lambda e: e.tensor_tensor(out=r1[:], in0=r1[:], in1=kf[:], op=ALU.subtract), reads=["r1", "kf"], writes=["r1"])
                    R.dve(lambda e: e.tensor_single_scalar(out=kf[:], in_=r1[:], scalar=0.5, op=ALU.is_gt), reads=["r1"], writes=["kf"])
                    R.dve(lambda e: e.tensor_tensor(out=r1[:], in0=r1[:], in1=kf[:], op=ALU.subtract), reads=["r1", "kf"], writes=["r1"])
                    R.dve(lambda e: e.tensor_single_scalar(out=kf[:], in_=r1[:], scalar=-0.5, op=ALU.is_lt), reads=["r1"], writes=["kf"])
                    R.dve(lambda e: e.tensor_tensor(out=r1[:], in0=r1[:], in1=kf[:], op=ALU.add), reads=["r1", "kf"], writes=["r1"])
                    R.act(lambda e: e.activation(out=r1[:], in_=r1[:], func=AF.Sin, scale=2 * math.pi), reads=["r1"], writes=["r1"])
                    if which == 1:
                        R.dve(lambda e: e.tensor_scalar(out=r1[:], in0=r1[:], scalar1=ropec[:, 1:2], scalar2=None, op0=ALU.mult),
                              reads=["r1", "ropec"], writes=["r1"])
                    R.dma("sp", lambda e, b=b, which=which: e.dma_start(out=ropebuf[b, which, :, :], in_=r1[:]),
                          reads=["r1"], writes=["ropebuf%d" % b])
        R.barrier()
        if "mods" in tap_t:
            fin.append(R.dma("sp", lambda e: e.dma_start(out=tap_t["mods"][:, :, :], in_=modbuf[:, :, :]), reads=["modbuf"], writes=["tap_mods"]))
        if "rope" in tap_t:
            fin.append(R.dma("sp", lambda e: e.dma_start(out=tap_t["rope"][:, :, :], in_=ropebuf[0, :, :, :]), reads=["ropebuf0"], writes=["tap_rope"]))

        if stop == "s0":
            R.emit(final_wait_ops=fin)
            return nc

        for l in range(NL):
            x_src = x_in if l == 0 else xbuf
            for b in (range(NB) if not skip_mixer else ()):
                with contextlib.ExitStack() as stm:
                    mixer(nc, R, mk(stm), locals())
                R.barrier()
            if stop == "mix":
                break
            moe(nc, R, mk, locals())
        R.emit(final_wait_ops=fin)
    return nc


def mixer(nc, R, mkp, env):
    sbuf, psum = mkp
    mk = env["mk"]
    l, b = env["l"], env["b"]
    x_src, xbuf, modbuf = env["x_src"], env["xbuf"], env["modbuf"]
    ident_f, ident_b, cst, tap_t, fin = env["ident_f"], env["ident_b"], env["cst"], env["tap_t"], env["fin"]
    w_in, w_out, gn_w, conv_w = env["w_in"], env["w_out"], env["gn_w"], env["conv_w"]
    cmp_pos, cmp_w1, cmp_w2, ln_g, ln_b, ropebuf = env["cmp_pos"], env["cmp_w1"], env["cmp_w2"], env["ln_g"], env["ln_b"], env["ropebuf"]
    NB = env["NB"]

    pA = Rot([("pA%d" % i, psum("pA%d" % i, [128, 512])) for i in range(2)])
    pT = psum("pT", [128, 1024], BF16)
    pO = Rot([("pO%d" % i, psum("pO%d" % i, [128, 512])) for i in range(4)])
    pX = psum("pX", [128, 512])

    def bcast_mod(sbx, dname, which, plus1):
        dst = sbx(dname, [128, D])
        R.dma("sp", lambda e: e.dma_start(out=dst[:], in_=modbuf[b, l:l + 1, which * D:(which + 1) * D].broadcast_to([128, D])),
              reads=["modbuf"], writes=[dname])
        if plus1:
            R.pool(lambda e: e.tensor_scalar(out=dst[:], in0=dst[:], scalar1=1.0, scalar2=None, op0=ALU.add), reads=[dname], writes=[dname])
        return dst

    sh1 = bcast_mod(sbuf, "sh1", 0, False)
    sc1 = bcast_mod(sbuf, "sc1", 1, True)

    hT = sbuf("hT", [128, 8, S], BF16)
    yT = sbuf("yT", [128, 8, S], BF16)
    xc_r = Rot([("xc%d" % i, sbuf("xc%d" % i, [128, D])) for i in range(2)])
    hc_r = Rot([("hc%d" % i, sbuf("hc%d" % i, [128, D], BF16)) for i in range(2)])
    st_r = Rot([("st%d" % i, sbuf("st%d" % i, [128, 16])) for i in range(2)])

    def layer_norm_chunk(xcn, xc, stn, stt, xnn, xn):
        R.dve(lambda e: e.bn_stats(out=stt[:, 0:6], in_=xc[:, 0:512]), reads=[xcn], writes=[stn])
        R.dve(lambda e: e.bn_stats(out=stt[:, 6:12], in_=xc[:, 512:1024]), reads=[xcn], writes=[stn])
        R.dve(lambda e: e.bn_aggr(out=stt[:, 12:14], in_=stt[:, 0:12]), reads=[stn], writes=[stn])
        R.act(lambda e: e.activation(out=stt[:, 14:15], in_=stt[:, 13:14], func=AF.Sqrt, bias=LN_EPS, scale=1.0), reads=[stn], writes=[stn])
        R.dve(lambda e: e.reciprocal(out=stt[:, 15:16], in_=stt[:, 14:15]), reads=[stn], writes=[stn])
        R.dve(lambda e: e.tensor_scalar(out=xn[:], in0=xc[:], scalar1=stt[:, 12:13], scalar2=stt[:, 15:16], op0=ALU.subtract, op1=ALU.mult),
              reads=[xcn, stn], writes=[xnn])

    for tc in range(NT):
        xcn, xc = xc_r.next()
        xnn, xn = xcn, xc
        hcn, hc = hc_r.next()
        stn, stt = st_r.next()
        R.dma("sp", lambda e, xc=xc, tc=tc: e.dma_start(out=xc[:], in_=x_src[b, tc * 128:(tc + 1) * 128, :]), reads=["xb%d_%d" % (b, tc)], writes=[xcn])
        layer_norm_chunk(xcn, xc, stn, stt, xnn, xn)
        R.pool(lambda e, xn=xn: e.tensor_tensor(out=xn[:], in0=xn[:], in1=sc1[:], op=ALU.mult), reads=[xnn, "sc1"], writes=[xnn])
        R.dve(lambda e, xn=xn, hc=hc: e.tensor_tensor(out=hc[:], in0=xn[:], in1=sh1[:], op=ALU.add), reads=[xnn, "sh1"], writes=[hcn])
        for kc in range(8):
            R.pe(lambda e, hc=hc, kc=kc: e.transpose(out=pT[:, kc * 128:(kc + 1) * 128], in_=hc[:, kc * 128:(kc + 1) * 128], identity=ident_b[:]),
                 reads=[hcn, "ident_b"], writes=["pT"])
        R.act(lambda e, tc=tc: e.copy(out=hT[:, :, tc * 128:(tc + 1) * 128], in_=pT[:].rearrange("p (k t) -> p k t", k=8)),
              reads=["pT"], writes=["hT"])
    if "hT" in tap_t and b == 0 and l == 0:
        fin.append(R.dma("sp", lambda e: e.dma_start(out=tap_t["hT"][:, :, :], in_=hT[:]), reads=["hT"], writes=["tap_hT"]))

    wr = Rot([("W%d" % i, sbuf("W%d" % i, [128, 8, 128], BF16)) for i in range(4)])

    def load_w(pieces):
        wn, wb = wr.next()
        for (do, sc, wd) in pieces:
            R.dma("pool", lambda e, wb=wb, do=do, sc=sc, wd=wd: e.dma_start(
                out=wb[:, :, do:do + wd], in_=w_in[l, :, sc:sc + wd].rearrange("(k p) n -> p k n", p=128)), writes=[wn])
        return wn, wb

    def perm_pieces(c0):
        return [(0, c0 + 32, 32), (32, c0, 32), (64, c0 + 96, 32), (96, c0 + 64, 32)]

    def dup_pieces(c0):
        return [(0, c0, 64), (64, c0, 64)]

    def dup_perm_pieces(c0):
        return [(0, c0 + 32, 32), (32, c0, 32), (64, c0 + 32, 32), (96, c0, 32)]

    def proj_fm(wn, wb, tg):
        pn, pt = pA.next()
        for kc in range(8):
            R.pe(lambda e, kc=kc: e.matmul(pt[:, :], lhsT=wb[:, kc, :], rhs=hT[:, kc, tg * 512:(tg + 1) * 512], start=(kc == 0), stop=(kc == 7)),
                 reads=[wn, "hT"], writes=[pn])
        return pn, pt

    def proj_plain(pieces, dst, dname, dsl):
        wn, wb = load_w(pieces)
        for tg in range(4):
            pn, pt = proj_fm(wn, wb, tg)
            R.act(lambda e, pt=pt, tg=tg: e.copy(out=dst[:, dsl, tg * 512:(tg + 1) * 512] if dsl is not None else dst[:, tg * 512:(tg + 1) * 512], in_=pt[:, :]),
                  reads=[pn], writes=[dname])

    with contextlib.ExitStack() as sc_:
        sb1, _ = mk(sc_)
        fm = [sb1("fm%d" % i, [128, 2, S], BF16) for i in range(3)]
        tmpA = sb1("tmpA", [128, S])
        tmpB = sb1("tmpB", [128, S])
        cw = sb1("cw", [128, 2, 3])
        for ch_ in range(2):
            for k_ in range(3):
                R.dma("sp", lambda e, ch_=ch_, k_=k_: e.dma_start(out=cw[:, ch_, k_:k_ + 1], in_=conv_w[l, k_, ch_ * 128:(ch_ + 1) * 128].rearrange("(p o) -> p o", o=1)), writes=["cw"])
        for i, c0 in enumerate((C_CB, C_CC, C_CH)):
            for ch in range(2):
                proj_plain([(0, c0 + ch * 128, 128)], fm[i], "fm%d" % i, ch)
        for ch in range(2):
            R.pool(lambda e, ch=ch: e.tensor_tensor(out=tmpA[:], in0=fm[1][:, ch, :], in1=fm[2][:, ch, :], op=ALU.mult),
                   reads=["fm1", "fm2"], writes=["tmpA"])
            R.dve(lambda e, ch=ch: e.tensor_scalar(out=tmpB[:], in0=tmpA[:], scalar1=cw[:, ch, 2:3], scalar2=None, op0=ALU.mult),
                  reads=["tmpA", "cw"], writes=["tmpB"])
            R.dve(lambda e, ch=ch: e.scalar_tensor_tensor(out=tmpB[:, 1:S], in0=tmpA[:, 0:S - 1], scalar=cw[:, ch, 1:2], in1=tmpB[:, 1:S],
                                                          op0=ALU.mult, op1=ALU.add), reads=["tmpA", "tmpB", "cw"], writes=["tmpB"])
            R.dve(lambda e, ch=ch: e.scalar_tensor_tensor(out=tmpB[:, 2:S], in0=tmpA[:, 0:S - 2], scalar=cw[:, ch, 0:1], in1=tmpB[:, 2:S],
                                                          op0=ALU.mult, op1=ALU.add), reads=["tmpA", "tmpB", "cw"], writes=["tmpB"])
            R.pool(lambda e, ch=ch: e.tensor_tensor(out=yT[:, 2 + ch, :], in0=tmpB[:], in1=fm[0][:, ch, :], op=ALU.mult),
                   reads=["tmpB", "fm0"], writes=["yT"])
    R.barrier()

    def load_rope(sbx):
        cosT = sbx("cosT", [128, S])
        sinT = sbx("sinT", [128, S])
        R.dma("sp", lambda e: e.dma_start(out=cosT[:], in_=ropebuf[b, 0, :, :]), reads=["ropebuf%d" % b], writes=["cosT"])
        R.dma("sp", lambda e: e.dma_start(out=sinT[:], in_=ropebuf[b, 1, :, :]), reads=["ropebuf%d" % b], writes=["sinT"])
        rt = Rot([("rt%d" % i, sbx("rt%d" % i, [128, 512])) for i in range(4)])
        return cosT, sinT, rt

    def proj_rope(p_norm, p_perm, dst, dname, dsl, cosT, sinT, rt):
        wn1, wb1 = load_w(p_norm)
        wn2, wb2 = load_w(p_perm)
        for tg in range(4):
            pn1, pt1 = proj_fm(wn1, wb1, tg)
            pn2, pt2 = proj_fm(wn2, wb2, tg)
            t1n, t1 = rt.next()
            t2n, t2 = rt.next()
            R.dve(lambda e, pt1=pt1, t1=t1, tg=tg: e.tensor_tensor(out=t1[:], in0=pt1[:, :], in1=cosT[:, tg * 512:(tg + 1) * 512], op=ALU.mult),
                  reads=[pn1, "cosT"], writes=[t1n])
            R.dve(lambda e, pt2=pt2, t2=t2, tg=tg: e.tensor_tensor(out=t2[:], in0=pt2[:, :], in1=sinT[:, tg * 512:(tg + 1) * 512], op=ALU.mult),
                  reads=[pn2, "sinT"], writes=[t2n])
            R.pool(lambda e, t1=t1, t2=t2, tg=tg: e.tensor_tensor(
                out=dst[:, dsl, tg * 512:(tg + 1) * 512] if dsl is not None else dst[:, tg * 512:(tg + 1) * 512], in0=t1[:], in1=t2[:], op=ALU.add),
                reads=[t1n, t2n], writes=[dname])

    with contextlib.ExitStack() as sc_:
        sb2, _ = mk(sc_)
        cosT, sinT, rt = load_rope(sb2)
        qTr = sb2("qTr", [128, 2, S], BF16)
        kTr = sb2("kTr", [128, 2, S], BF16)
        vrg = sb2("vrg", [128, NT, 512], BF16)
        dect = sb2("dect", [128, 4, TW], BF16)
        gnw = sb2("gnw", [128, 256])
        R.dma("sp", lambda e: e.dma_start(out=dect[:], in_=cst["dect"][:, :, :]), writes=["dect"])
        R.dma("sp", lambda e: e.dma_start(out=gnw[:], in_=gn_w[l:l + 1, :].broadcast_to([128, 256])), writes=["gnw"])
        for ch in range(2):
            proj_rope([(0, C_RQ + ch * 128, 128)], perm_pieces(C_RQ + ch * 128), qTr, "qTr", ch, cosT, sinT, rt)
            proj_rope([(0, C_RK + ch * 128, 128)], perm_pieces(C_RK + ch * 128), kTr, "kTr", ch, cosT, sinT, rt)
        wv = sb2("wv", [128, 8, 512], BF16)
        R.dma("pool", lambda e: e.dma_start(out=wv[:], in_=w_in[l, :, C_RV:C_RV + 512].rearrange("(k p) n -> p k n", p=128)), writes=["wv"])
        for tc in range(NT):
            pn, pt = pA.next()
            for kc in range(8):
                R.pe(lambda e, pt=pt, kc=kc, tc=tc: e.matmul(pt[:, :], lhsT=hT[:, kc, tc * 128:(tc + 1) * 128], rhs=wv[:, kc, :],
                                                             start=(kc == 0), stop=(kc == 7)), reads=["hT", "wv"], writes=[pn])
            R.act(lambda e, pt=pt, tc=tc: e.copy(out=vrg[:, tc, 0:256], in_=pt[:, 0:256]), reads=[pn], writes=["vrg"])
            R.act(lambda e, pt=pt, tc=tc: e.activation(out=vrg[:, tc, 256:512], in_=pt[:, 256:512], func=AF.Silu), reads=[pn], writes=["vrg"])
        smr = Rot([("sm%d" % i, sb2("sm%d" % i, [128, 512], BF16)) for i in range(3)])
        gA = sb2("gA", [128, 1024])
        gB = sb2("gB", [128, 1024])
        gs = sb2("gs", [128, 64])
        yr = sb2("yr", [128, 4, 256], BF16)
        for qg in range(4):
            raccn = ["pO0", "pO1"]
            racc = [pO.items[0][1], pO.items[1][1]]
            first = [True, True]
            for h in range(4):
                ch, ro = h // 2, (h % 2) * 64
                for kc in range(4 * qg + 4):
                    pn, pt = pA.next()
                    R.pe(lambda e, pt=pt, kc=kc, ch=ch, ro=ro, qg=qg: e.matmul(
                        pt[:, :], lhsT=kTr[ro:ro + 64, ch, kc * 128:(kc + 1) * 128], rhs=qTr[ro:ro + 64, ch, qg * 512:(qg + 1) * 512],
                        start=True, stop=True), reads=["kTr", "qTr"], writes=[pn])
                    smn, sm = smr.next()
                    off = 384 + qg * 512 - kc * 128
                    R.dve(lambda e, pt=pt, sm=sm, h=h, off=off: e.tensor_tensor(out=sm[:], in0=pt[:, :], in1=dect[:, h, off:off + 512], op=ALU.mult),
                          reads=[pn, "dect"], writes=[smn])
                    for j in range(4):
                        qc = 4 * qg + j
                        if kc > qc:
                            continue
                        bk = j // 2
                        st_flag = first[bk]
                        first[bk] = False
                        R.pe(lambda e, sm=sm, j=j, h=h, kc=kc, bk=bk, st_flag=st_flag: e.matmul(
                            racc[bk][:, (j % 2) * 256 + h * 64:(j % 2) * 256 + h * 64 + 64], lhsT=sm[:, j * 128:(j + 1) * 128],
                            rhs=vrg[:, kc, h * 64:h * 64 + 64], start=st_flag, stop=False, skip_group_check=True),
                            reads=[smn, "vrg"], writes=[raccn[bk]])
            for bk in range(2):
                R.act(lambda e, bk=bk: e.copy(out=gA[:, bk * 512:(bk + 1) * 512], in_=racc[bk][:, :]), reads=[raccn[bk]], writes=["gA"])
            g3 = gA[:].rearrange("p (a d) -> p a d", d=64)
            b3 = gB[:].rearrange("p (a d) -> p a d", d=64)
            R.dve(lambda e: e.tensor_reduce(out=gs[:, 0:16], in_=g3, axis=AX.X, op=ALU.add), reads=["gA"], writes=["gs"])
            R.dve(lambda e: e.tensor_scalar(out=gs[:, 0:16], in0=gs[:, 0:16], scalar1=1.0 / 64, scalar2=None, op0=ALU.mult), reads=["gs"], writes=["gs"])
            R.dve(lambda e: e.tensor_tensor(out=g3, in0=g3, in1=gs[:, 0:16].unsqueeze(2).broadcast_to([128, 16, 64]), op=ALU.subtract),
                  reads=["gA", "gs"], writes=["gA"])
            R.pool(lambda e: e.tensor_tensor(out=gB[:], in0=gA[:], in1=gA[:], op=ALU.mult), reads=["gA"], writes=["gB"])
            R.dve(lambda e: e.tensor_reduce(out=gs[:, 16:32], in_=b3, axis=AX.X, op=ALU.add), reads=["gB"], writes=["gs"])
            R.act(lambda e: e.activation(out=gs[:, 32:48], in_=gs[:, 16:32], func=AF.Sqrt, bias=LN_EPS, scale=1.0 / 64), reads=["gs"], writes=["gs"])
            R.dve(lambda e: e.reciprocal(out=gs[:, 48:64], in_=gs[:, 32:48]), reads=["gs"], writes=["gs"])
            R.dve(lambda e: e.tensor_tensor(out=g3, in0=g3, in1=gs[:, 48:64].unsqueeze(2).broadcast_to([128, 16, 64]), op=ALU.mult),
                  reads=["gA", "gs"], writes=["gA"])
            g4 = gA[:].rearrange("p (j c) -> p j c", c=256)
            R.pool(lambda e: e.tensor_tensor(out=g4, in0=g4, in1=gnw[:].unsqueeze(1).broadcast_to([128, 4, 256]), op=ALU.mult),
                   reads=["gA", "gnw"], writes=["gA"])
            R.dve(lambda e, qg=qg: e.tensor_tensor(out=yr[:], in0=g4, in1=vrg[:, 4 * qg:4 * qg + 4, 256:512], op=ALU.mult),
                  reads=["gA", "vrg"], writes=["yr"])
            for j in range(4):
                for ch in range(2):
                    R.pe(lambda e, j=j, ch=ch: e.transpose(out=pT[:, (j * 2 + ch) * 128:(j * 2 + ch + 1) * 128], in_=yr[:, j, ch * 128:(ch + 1) * 128],
                                                         identity=ident_b[:]), reads=["yr", "ident_b"], writes=["pT"])
            R.act(lambda e, qg=qg: e.copy(out=yT[:, 0:2, qg * 512:(qg + 1) * 512].rearrange("p c (j t) -> p j c t", j=4),
                                          in_=pT[:].rearrange("p (j c t) -> p j c t", j=4, c=2)), reads=["pT"], writes=["yT"])
    R.barrier()

    nsa_stage = env.get("nsa_stage", 2)
    nsa_part = env.get("nsa_part", 9)
    nsa_hks = env.get("nsa_hks", (0, 1))
    nsa_qcs = env.get("nsa_qcs", tuple(range(4)))
    with contextlib.ExitStack() as sc_:
        sb3, _ = mk(sc_)
        cosT, sinT, rt = load_rope(sb3)
        kcT = sb3("kcT", [128, S], BF16)
        vcT = sb3("vcT", [128, S], BF16)
        gates = sb3("gates", [128, NT, 24])
        v2 = sb3("v2", [128, NT, 4, 128], BF16)
        qT = sb3("qT", [128, 2, S], BF16)
        qTr2 = sb3("qTr2", [128, 2, S], BF16)
        ksT = sb3("ksT", [128, S], BF16)
        kwT = sb3("kwT", [128, S], BF16)
        w2k = sb3("w2k", [128, 128], BF16)
        w2v = sb3("w2v", [128, 64], BF16)
        pbias = sb3("pbias", [128, 2])
        kcd = [sb3("kcd%d" % i, [128, 128], BF16) for i in range(2)]
        vca = sb3("vca", [128, 2, 128], BF16)
        ovl = sb3("ovl", [128, 32], BF16)
        scC = contextlib.ExitStack()
        sbC, _ = mk(scC)
        w1d = [sbC("w1d%d" % i, [128, 32, 128], BF16) for i in range(2)]
        posr = sbC("posr", [32, 2, 64])
        posT = sbC("posT", [64, 2, 32], BF16)
        R.dma("sp", lambda e: e.dma_start(out=ovl[:], in_=cst["overlap"][:, :]), writes=["ovl"])
        for kind in range(2):
            for cp in range(2):
                R.dma("pool", lambda e, kind=kind, cp=cp: e.dma_start(
                    out=w1d[kind][cp * 64:(cp + 1) * 64, :, :], in_=cmp_w1[l, kind, :, :].rearrange("(l d) n -> d l n", d=64)), writes=["w1d%d" % kind])
            R.dma("sp", lambda e, kind=kind: e.dma_start(out=posr[:, kind, :], in_=cmp_pos[l, kind, :, :]), writes=["posr"])
        R.dma("pool", lambda e: e.dma_start(out=w2k[:, 0:64], in_=cmp_w2[l, 0, :, :]), writes=["w2k"])
        R.dma("pool", lambda e: e.dma_start(out=w2k[:, 64:128], in_=cmp_w2[l, 0, :, :]), writes=["w2k"])
        R.dma("pool", lambda e: e.dma_start(out=w2v[:], in_=cmp_w2[l, 1, :, :]), writes=["w2v"])
        proj_plain([(0, C_KC, 128)], kcT, "kcT", None)
        proj_plain([(0, C_VC, 128)], vcT, "vcT", None)
        wt = sbC("wt", [128, 8, 408], BF16)
        R.dma("pool", lambda e: e.dma_start(out=wt[:], in_=w_in[l, :, C_VS:C_VS + 408].rearrange("(k p) n -> p k n", p=128)), writes=["wt"])
        R.pool(lambda e: e.memset(v2[:].rearrange("p a b c -> p (a b) c")[:, :, 64:128], 1.0), writes=["v2"])
        for tc in range(NT):
            pn, pt = pA.next()
            for kc in range(8):
                R.pe(lambda e, pt=pt, kc=kc, tc=tc: e.matmul(pt[:, 0:408], lhsT=hT[:, kc, tc * 128:(tc + 1) * 128], rhs=wt[:, kc, :],
                                                             start=(kc == 0), stop=(kc == 7)), reads=["hT", "wt"], writes=[pn])
            R.act(lambda e, pt=pt, tc=tc: e.copy(out=v2[:, tc, 0:2, 0:64], in_=pt[:, 0:128].rearrange("p (h d) -> p h d", h=2)), reads=[pn], writes=["v2"])
            R.act(lambda e, pt=pt, tc=tc: e.copy(out=v2[:, tc, 2:4, 0:64], in_=pt[:, 256:384].rearrange("p (h d) -> p h d", h=2)), reads=[pn], writes=["v2"])
            R.act(lambda e, pt=pt, tc=tc: e.activation(out=gates[:, tc, :], in_=pt[:, 384:408], func=AF.Sigmoid), reads=[pn], writes=["gates"])
        for kind in range(2):
            R.pe(lambda e, kind=kind: e.transpose(out=pX[0:64, kind * 32:(kind + 1) * 32], in_=posr[:, kind, :], identity=ident_f[0:32, 0:32]),
                 reads=["posr", "ident_f"], writes=["pX"])
        R.dve(lambda e: e.tensor_copy(out=posT[:].rearrange("p k l -> p (k l)"), in_=pX[0:64, 0:64]), reads=["pX"], writes=["posT"])
        for kind in range(2):
            for li in range(32):
                R.pe(lambda e, kind=kind, li=li: e.matmul(pX[:, 64 + kind:65 + kind], lhsT=w1d[kind][0:64, li, :], rhs=posT[:, kind, li:li + 1],
                                                          start=(li == 0 and kind == 0), stop=(li == 31), skip_group_check=True),
                     reads=["w1d%d" % kind, "posT"], writes=["pX"])
        R.dve(lambda e: e.tensor_copy(out=pbias[:], in_=pX[:, 64:66]), reads=["pX"], writes=["pbias"])
        zt = sbC("zt", [128, 128])
        z2 = sbC("z2", [128, 128])
        blk = sbC("blk", [128, 32, 127], BF16)
        gT = sbC("gT", [128, 128], BF16)
        R.pool(lambda e: e.memset(vca[:], 0.0), writes=["vca"])
        for i_ in range(2):
            R.pool(lambda e, i_=i_: e.memset(kcd[i_][:], 0.0), writes=["kcd"])
        for kind in range(2):
            src = kcT if kind == 0 else vcT
            srcn = "kcT" if kind == 0 else "vcT"
            for li in range(32):
                R.dve(lambda e, li=li, src=src: e.tensor_copy(out=blk[:, li, :], in_=src[:, li:li + 2017:16]), reads=[srcn], writes=["blk"])
            for hk in range(2):
                ro = hk * 64
                pn, pt = pA.next()
                for li in range(32):
                    R.pe(lambda e, pt=pt, kind=kind, li=li, ro=ro: e.matmul(
                        pt[:, 0:127], lhsT=w1d[kind][ro:ro + 64, li, :], rhs=blk[ro:ro + 64, li, :], start=(li == 0), stop=(li == 31)),
                        reads=["w1d%d" % kind, "blk"], writes=[pn])
                R.act(lambda e, pt=pt, kind=kind: e.activation(out=zt[:, 0:127], in_=pt[:, 0:127], func=AF.Identity, bias=pbias[:, kind:kind + 1], scale=1.0),
                      reads=[pn, "pbias"], writes=["zt"])
                R.dve(lambda e: e.tensor_tensor(out=z2[:, 0:127], in0=zt[:, 0:127], in1=zt[:, 0:127], op=ALU.mult), reads=["zt"], writes=["z2"])
                R.dve(lambda e: e.tensor_scalar(out=z2[:, 0:127], in0=z2[:, 0:127], scalar1=0.044715, scalar2=1.0, op0=ALU.mult, op1=ALU.add),
                      reads=["z2"], writes=["z2"])
                R.dve(lambda e: e.tensor_tensor(out=z2[:, 0:127], in0=z2[:, 0:127], in1=zt[:, 0:127], op=ALU.mult), reads=["z2", "zt"], writes=["z2"])
                R.act(lambda e: e.activation(out=z2[:, 0:127], in_=z2[:, 0:127], func=AF.Sigmoid, scale=1.5957691216), reads=["z2"], writes=["z2"])
                R.dve(lambda e: e.tensor_tensor(out=gT[:, 0:127], in0=z2[:, 0:127], in1=zt[:, 0:127], op=ALU.mult), reads=["z2", "zt"], writes=["gT"])
                if kind == 0:
                    R.pe(lambda e: e.matmul(pX[:, 128:255], lhsT=w2k[:, :], rhs=gT[:, 0:127], start=True, stop=True), reads=["w2k", "gT"], writes=["pX"])
                    R.act(lambda e, hk=hk: e.copy(out=kcd[hk][:, 0:127], in_=pX[:, 128:255]), reads=["pX"], writes=["kcd"])
                else:
                    R.pe(lambda e: e.matmul(pX[0:127, 256:320], lhsT=gT[:, 0:127], rhs=w2v[:, :], start=True, stop=True), reads=["w2v", "gT"], writes=["pX"])
                    R.act(lambda e, hk=hk: e.copy(out=vca[0:127, hk, 0:64], in_=pX[0:127, 256:320]), reads=["pX"], writes=["vca"])
        for hk in range(2):
            R.pool(lambda e, hk=hk: e.memset(vca[:, hk, 64:96], 1.0), reads=[], writes=["vca"])
            R.pool(lambda e, hk=hk: e.tensor_copy(out=vca[:, hk, 96:128], in_=ovl[:]), reads=["ovl"], writes=["vca"])

        if "kcd" in tap_t and b == 0 and l == 0:
            for i_ in range(2):
                fin.append(R.dma("sp", lambda e, i_=i_: e.dma_start(out=tap_t["kcd"][:, i_, :], in_=kcd[i_][:]), reads=["kcd"], writes=["tap_kcd%d" % i_]))
            fin.append(R.dma("sp", lambda e: e.dma_start(out=tap_t["vca"][:, :, :], in_=vca[:]), reads=["vca"], writes=["tap_vca"]))
        R.barrier()
        scC.close()
        cbc = sb3("cbc", [128, S], BF16)
        forced = sb3("forced", [128, NT, 32])
        Eexp = sb3("Eexp", [128, NT, 128], BF16)
        tmask = sb3("tmask", [128, 2, TW], BF16)
        R.dma("sp", lambda e: e.dma_start(out=cbc[:], in_=cst["cb_cmp"][:, :]), writes=["cbc"])
        R.dma("sp", lambda e: e.dma_start(out=forced[:], in_=cst["forced"][:, :, :]), writes=["forced"])
        R.dma("sp", lambda e: e.dma_start(out=Eexp[:], in_=cst["Eexp"][:, :, :]), writes=["Eexp"])
        R.dma("sp", lambda e: e.dma_start(out=tmask[:], in_=cst["tmask"][:, :, :]), writes=["tmask"])
        ptr = Rot([("PT%d" % i, sb3("PT%d" % i, [128, 512], BF16)) for i in range(3)])
        mk_r = Rot([("mk%d" % i, sb3("mk%d" % i, [128, 512], BF16)) for i in range(2)])
        selq = sb3("selq", [128, 4, 128], BF16)
        selT = sb3("selT", [128, 512], BF16)
        on = sb3("on", [128, 4, 4, 64])
        tmpo = sb3("tmpo", [128, 4, 64])
        impa = sb3("impa", [128, 4, 32])
        tmpi = sb3("tmpi", [128, 4, 32])
        nst = sb3("nst", [128, 64])
        ynb = sb3("ynb", [128, 4, 256], BF16)
        R.pool(lambda e: e.memset(selq[:], 0.0), writes=["selq"])

        def smm(pn, pt, Ktile, kn, k0, Qsrc, qn, ro, ch, qg):
            R.pe(lambda e: e.matmul(pt[:, :], lhsT=Ktile[ro:ro + 64, k0:k0 + 128], rhs=Qsrc[ro:ro + 64, ch, qg * 512:(qg + 1) * 512],
                                    start=True, stop=True), reads=[kn, qn], writes=[pn])

        def finish(an, acc, g, qg, hk, br, first):
            a3 = acc[:, :].rearrange("p (j c) -> p j c", j=4)
            R.dve(lambda e: e.tensor_scalar(out=nst[:, 0:4], in0=a3[:, :, 64], scalar1=1e-30, scalar2=None, op0=ALU.max), reads=[an], writes=["nst"])
            R.dve(lambda e: e.reciprocal(out=nst[:, 0:4], in_=nst[:, 0:4]), reads=["nst"], writes=["nst"])
            R.dve(lambda e: e.tensor_tensor(out=nst[:, 4:8], in0=nst[:, 0:4], in1=gates[:, 4 * qg:4 * qg + 4, hk * 12 + g * 3 + br], op=ALU.mult),
                  reads=["nst", "gates"], writes=["nst"])
            dst = on[:, :, g, :]
            if first:
                R.dve(lambda e: e.tensor_tensor(out=dst, in0=a3[:, :, 0:64], in1=nst[:, 4:8].unsqueeze(2).broadcast_to([128, 4, 64]), op=ALU.mult),
                      reads=[an, "nst"], writes=["on"])
            else:
                R.dve(lambda e: e.tensor_tensor(out=tmpo[:], in0=a3[:, :, 0:64], in1=nst[:, 4:8].unsqueeze(2).broadcast_to([128, 4, 64]), op=ALU.mult),
                      reads=[an, "nst"], writes=["tmpo"])
                R.pool(lambda e: e.tensor_tensor(out=dst, in0=dst, in1=tmpo[:], op=ALU.add), reads=["on", "tmpo"], writes=["on"])

        for hk in (nsa_hks if nsa_stage >= 2 else ()):
            for ch in range(2):
                c0 = C_NQ + hk * 256 + ch * 128
                proj_plain([(0, c0, 128)], qT, "qT", ch)
                proj_rope([(0, c0, 128)], perm_pieces(c0), qTr2, "qTr2", ch, cosT, sinT, rt)
            proj_rope(dup_pieces(C_KS + hk * 64), dup_perm_pieces(C_KS + hk * 64), ksT, "ksT", None, cosT, sinT, rt)
            proj_rope(dup_pieces(C_KW + hk * 64), dup_perm_pieces(C_KW + hk * 64), kwT, "kwT", None, cosT, sinT, rt)
            for qg in (nsa_qcs if nsa_part >= 1 else ()):
                for g in range(4):
                    ch, ro = g // 2, (g % 2) * 64
                    pn, pt = pA.next()
                    smm(pn, pt, kcd[hk], "kcd", 0, qT, "qT", ro, ch, qg)
                    ptn, PT = ptr.next()
                    R.act(lambda e, pt=pt, PT=PT: e.activation(out=PT[:, :], in_=pt[:, :], func=AF.Exp, scale=SCALE), reads=[pn], writes=[ptn])
                    R.pool(lambda e, PT=PT, qg=qg: e.tensor_tensor(out=PT[:, :], in0=PT[:, :], in1=cbc[:, qg * 512:(qg + 1) * 512], op=ALU.mult),
                           reads=[ptn, "cbc"], writes=[ptn])
                    an, acc = pO.next()
                    for j in range(4):
                        R.pe(lambda e, j=j, acc=acc, PT=PT, hk=hk: e.matmul(acc[:, j * 128:(j + 1) * 128], lhsT=PT[:, j * 128:(j + 1) * 128], rhs=vca[:, hk, :],
                                                                          start=(j == 0), stop=False, skip_group_check=True), reads=[ptn, "vca"], writes=[an])
                    if nsa_part < 2:
                        continue
                    a3 = acc[:, :].rearrange("p (j c) -> p j c", j=4)
                    finish(an, acc, g, qg, hk, 0, True)
                    if g == 0:
                        R.dve(lambda e, a3=a3: e.tensor_tensor(out=impa[:], in0=a3[:, :, 96:128], in1=nst[:, 0:4].unsqueeze(2).broadcast_to([128, 4, 32]), op=ALU.mult),
                              reads=[an, "nst"], writes=["impa"])
                    else:
                        R.dve(lambda e, a3=a3: e.tensor_tensor(out=tmpi[:], in0=a3[:, :, 96:128], in1=nst[:, 0:4].unsqueeze(2).broadcast_to([128, 4, 32]), op=ALU.mult),
                              reads=[an, "nst"], writes=["tmpi"])
                        R.pool(lambda e: e.tensor_tensor(out=impa[:], in0=impa[:], in1=tmpi[:], op=ALU.add), reads=["impa", "tmpi"], writes=["impa"])
                if nsa_part < 3:
                    continue
                R.dve(lambda e, qg=qg: e.tensor_tensor(out=impa[:], in0=impa[:], in1=forced[:, 4 * qg:4 * qg + 4, :], op=ALU.max), reads=["impa", "forced"], writes=["impa"])
                for j in range(4):
                    R.dve(lambda e, j=j: e.max(out=nst[:, 16 + 8 * j:24 + 8 * j], in_=impa[:, j, :]), reads=["impa"], writes=["nst"])
                    R.dve(lambda e, j=j: e.tensor_scalar(out=selq[:, j, 0:32], in0=impa[:, j, :], scalar1=nst[:, 23 + 8 * j:24 + 8 * j], scalar2=None, op0=ALU.is_ge),
                          reads=["impa", "nst"], writes=["selq"])
                for j in range(4):
                    R.pe(lambda e, j=j: e.transpose(out=pT[:, j * 128:(j + 1) * 128], in_=selq[:, j, :], identity=ident_b[:]), reads=["selq", "ident_b"], writes=["pT"])
                R.act(lambda e: e.copy(out=selT[:], in_=pT[:, 0:512]), reads=["pT"], writes=["selT"])
                if nsa_part < 4:
                    continue
                for br in ((1, 2) if nsa_part >= 5 else (1,)):
                    Ksrc, ksn = (ksT, "ksT") if br == 1 else (kwT, "kwT")
                    kcs = list(range(0, 4 * qg + 4)) if br == 1 else list(range(max(0, 4 * qg - 4), 4 * qg + 4))
                    accs = [pO.next() for _ in range(4)]
                    firsts = [True] * 4
                    for kc in kcs:
                        off = 384 + qg * 512 - kc * 128
                        mkn, mkt = mk_r.next()
                        if br == 1:
                            R.pe(lambda e, kc=kc: e.matmul(pX[:, :], lhsT=Eexp[:, kc, :], rhs=selT[:, :], start=True, stop=True), reads=["Eexp", "selT"], writes=["pX"])
                            R.dve(lambda e, mkt=mkt, off=off: e.tensor_tensor(out=mkt[:], in0=pX[:, :], in1=tmask[:, 0, off:off + 512], op=ALU.mult),
                                  reads=["pX", "tmask"], writes=[mkn])
                        for g in range(4):
                            ch, ro = g // 2, (g % 2) * 64
                            pn, pt = pA.next()
                            smm(pn, pt, Ksrc, ksn, kc * 128, qTr2, "qTr2", ro, ch, qg)
                            ptn, PT = ptr.next()
                            R.act(lambda e, pt=pt, PT=PT: e.activation(out=PT[:, :], in_=pt[:, :], func=AF.Exp, scale=SCALE), reads=[pn], writes=[ptn])
                            if br == 1:
                                R.pool(lambda e, PT=PT, mkt=mkt: e.tensor_tensor(out=PT[:, :], in0=PT[:, :], in1=mkt[:], op=ALU.mult), reads=[ptn, mkn], writes=[ptn])
                            else:
                                R.pool(lambda e, PT=PT, off=off: e.tensor_tensor(out=PT[:, :], in0=PT[:, :], in1=tmask[:, 1, off:off + 512], op=ALU.mult),
                                       reads=[ptn, "tmask"], writes=[ptn])
                            an, acc = accs[g]
                            for j in range(4):
                                qc = 4 * qg + j
                                if kc > qc or (br == 2 and kc < qc - 4):
                                    continue
                                stf = firsts[g]
                                firsts[g] = False
                                R.pe(lambda e, j=j, acc=acc, PT=PT, kc=kc, br=br, hk=hk, stf=stf: e.matmul(
                                    acc[:, j * 128:(j + 1) * 128], lhsT=PT[:, j * 128:(j + 1) * 128], rhs=v2[:, kc, (br - 1) * 2 + hk, :],
                                    start=stf, stop=False, skip_group_check=True), reads=[ptn, "v2"], writes=[an])
                    for g in range(4):
                        an, acc = accs[g]
                        finish(an, acc, g, qg, hk, br, False)
                R.act(lambda e: e.copy(out=ynb[:].rearrange("p j c -> p (j c)"), in_=on[:].rearrange("p j g d -> p (j g d)")), reads=["on"], writes=["ynb"])
                for j in range(4):
                    for ch in range(2):
                        R.pe(lambda e, j=j, ch=ch: e.transpose(out=pT[:, (j * 2 + ch) * 128:(j * 2 + ch + 1) * 128], in_=ynb[:, j, ch * 128:(ch + 1) * 128],
                                                             identity=ident_b[:]), reads=["ynb", "ident_b"], writes=["pT"])
                R.act(lambda e, qg=qg, hk=hk: e.copy(out=yT[:, 4 + 2 * hk:6 + 2 * hk, qg * 512:(qg + 1) * 512].rearrange("p c (j t) -> p j c t", j=4),
                                                     in_=pT[:].rearrange("p (j c t) -> p j c t", j=4, c=2)), reads=["pT"], writes=["yT"])
    R.barrier()
    if "yT" in tap_t and b == 0 and l == 0:
        fin.append(R.dma("sp", lambda e: e.dma_start(out=tap_t["yT"][:, :, :], in_=yT[:]), reads=["yT"], writes=["tap_yT2"]))

    with contextlib.ExitStack() as sc_:
        sb4, _ = mk(sc_)
        g1p = bcast_mod(sb4, "g1p", 2, True)
        lng = sb4("lng", [128, D])
        lnb = sb4("lnb", [128, D])
        R.dma("sp", lambda e: e.dma_start(out=lng[:], in_=ln_g[l, 0:1, :].broadcast_to([128, D])), writes=["lng"])
        R.dma("sp", lambda e: e.dma_start(out=lnb[:], in_=ln_b[l, 0:1, :].broadcast_to([128, D])), writes=["lnb"])
        wo = sb4("wo", [128, 8, D], BF16)
        R.dma("pool", lambda e: e.dma_start(out=wo[:], in_=w_out[l, :, :].rearrange("(k p) n -> p k n", p=128)), writes=["wo"])
        mt_r = Rot([("mt%d" % i, sb4("mt%d" % i, [128, D])) for i in range(2)])
        for tc in range(NT):
            xcn, xc = xc_r.next()
            stn, stt = st_r.next()
            mtn, mt = mt_r.next()
            R.dma("sp", lambda e, xc=xc, tc=tc: e.dma_start(out=xc[:], in_=x_src[b, tc * 128:(tc + 1) * 128, :]), reads=["xb%d_%d" % (b, tc)], writes=[xcn])
            for hf in range(2):
                pn, pt = pA.next()
                for kc in range(8):
                    R.pe(lambda e, pt=pt, kc=kc, tc=tc, hf=hf: e.matmul(pt[:, :], lhsT=yT[:, kc, tc * 128:(tc + 1) * 128], rhs=wo[:, kc, hf * 512:(hf + 1) * 512],
                                                                         start=(kc == 0), stop=(kc == 7)), reads=["yT", "wo"], writes=[pn])
                R.dve(lambda e, pt=pt, mt=mt, hf=hf: e.tensor_tensor(out=mt[:, hf * 512:(hf + 1) * 512], in0=pt[:, :], in1=g1p[:, hf * 512:(hf + 1) * 512], op=ALU.mult),
                      reads=[pn, "g1p"], writes=[mtn])
            R.dve(lambda e, xc=xc, mt=mt: e.scalar_tensor_tensor(out=xc[:], in0=xc[:], scalar=ALPHA, in1=mt[:], op0=ALU.mult, op1=ALU.add),
                  reads=[xcn, mtn], writes=[xcn])
            layer_norm_chunk(xcn, xc, stn, stt, xcn, xc)
            R.pool(lambda e, xc=xc: e.tensor_tensor(out=xc[:], in0=xc[:], in1=lng[:], op=ALU.mult), reads=[xcn, "lng"], writes=[xcn])
            R.dve(lambda e, xc=xc: e.tensor_tensor(out=xc[:], in0=xc[:], in1=lnb[:], op=ALU.add), reads=[xcn, "lnb"], writes=[xcn])
            R.dma("sp", lambda e, xc=xc, tc=tc: e.dma_start(out=xbuf[b, tc * 128:(tc + 1) * 128, :], in_=xc[:]), reads=[xcn], writes=["xb%d_%d" % (b, tc)])
    if "x1" in tap_t and b == 0 and l == 0:
        fin.append(R.dma("sp", lambda e: e.dma_start(out=tap_t["x1"][:, :], in_=xbuf[0, :, :]), reads=["xb0_%d" % t_ for t_ in range(NT)], writes=["tap_x1"]))


NE = 32
GELU_A = 1.702


def moe_consts():
    c = {}
    tp = np.arange(128)[:, None]
    t = np.arange(128)[None, :]
    c["ltri"] = (tp < t).astype(np.float32).astype(ml_dtypes.bfloat16)
    c["onesb"] = np.ones((128, 128), np.float32).astype(ml_dtypes.bfloat16)
    c["iota3"] = np.broadcast_to(np.arange(128, dtype=np.float32)[None, None, :], (128, NE, 128)).astype(ml_dtypes.bfloat16).copy()
    return c


def moe(nc, R, mk, env):
    l, NB, NL = env["l"], env["NB"], env["NL"]
    xbuf, modbuf, out, cst = env["xbuf"], env["modbuf"], env["out"], env["cst"]
    ident_f, ident_b, tap_t, fin = env["ident_f"], env["ident_b"], env["tap_t"], env["fin"]
    ln_g, ln_b = env["ln_g"], env["ln_b"]
    router_w, router_b, w_gu, b_gu, w_dn, b_dn = env["router_w"], env["router_b"], env["w_gu"], env["b_gu"], env["w_dn"], env["b_dn"]
    H2d, Gd, CMd, XTd, PGd, Yd = env["H2d"], env["Gd"], env["CMd"], env["XTd"], env["PGd"], env["Yd"]
    last = (l == NL - 1)
    moe_stop = env.get("moe_stop", "D")

    def ln_chunk(xc, xcn, stt, stn):
        R.dve(lambda e: e.bn_stats(out=stt[:, 0:6], in_=xc[:, 0:512]), reads=[xcn], writes=[stn])
        R.dve(lambda e: e.bn_stats(out=stt[:, 6:12], in_=xc[:, 512:1024]), reads=[xcn], writes=[stn])
        R.dve(lambda e: e.bn_aggr(out=stt[:, 12:14], in_=stt[:, 0:12]), reads=[stn], writes=[stn])
        R.act(lambda e: e.activation(out=stt[:, 14:15], in_=stt[:, 13:14], func=AF.Sqrt, bias=LN_EPS, scale=1.0), reads=[stn], writes=[stn])
        R.dve(lambda e: e.reciprocal(out=stt[:, 15:16], in_=stt[:, 14:15]), reads=[stn], writes=[stn])
        R.dve(lambda e: e.tensor_scalar(out=xc[:], in0=xc[:], scalar1=stt[:, 12:13], scalar2=stt[:, 15:16], op0=ALU.subtract, op1=ALU.mult),
              reads=[xcn, stn], writes=[xcn])

    def bcast_row(sbx, name, src_ap, plus1=False, width=D):
        dst = sbx(name, [128, width])
        R.dma("sp", lambda e: e.dma_start(out=dst[:], in_=src_ap.broadcast_to([128, width])), reads=["modbuf"], writes=[name])
        if plus1:
            R.pool(lambda e: e.tensor_scalar(out=dst[:], in0=dst[:], scalar1=1.0, scalar2=None, op0=ALU.add), reads=[name], writes=[name])
        return dst

    with contextlib.ExitStack() as sA:
        sb, ps = mk(sA)
        wr = sb("wr", [128, 8, NE])
        wrh = sb("wrh", [128, 8, NE], BF16)
        wrl = sb("wrl", [128, 8, NE], BF16)
        R.dma("sp", lambda e: e.dma_start(out=wr[:], in_=router_w[l, :, :].rearrange("(k p) n -> p k n", p=128)), writes=["wr"])
        R.dve(lambda e: e.tensor_copy(out=wrh[:], in_=wr[:]), reads=["wr"], writes=["wrh"])
        R.dve(lambda e: e.tensor_tensor(out=wrl[:], in0=wr[:], in1=wrh[:], op=ALU.subtract), reads=["wr", "wrh"], writes=["wrl"])
        rb = bcast_row(sb, "rb", router_b[l:l + 1, :], width=NE)
        ltri = sb("ltri", [128, 128], BF16)
        onesb = sb("onesb", [128, 128], BF16)
        R.dma("sp", lambda e: e.dma_start(out=ltri[:], in_=cst["ltri"][:, :]), writes=["ltri"])
        R.dma("sp", lambda e: e.dma_start(out=onesb[:], in_=cst["onesb"][:, :]), writes=["onesb"])
        xc_r = Rot([("xa%d" % i, sb("xa%d" % i, [128, D])) for i in range(2)])
        hb_r = Rot([("hb%d" % i, sb("hb%d" % i, [128, D], BF16)) for i in range(2)])
        st_r = Rot([("sta%d" % i, sb("sta%d" % i, [128, 16])) for i in range(2)])
        hT_r = Rot([("hTa%d" % i, sb("hTa%d" % i, [128, 2, D], BF16)) for i in range(2)])
        hl_r = Rot([("hl%d" % i, sb("hl%d" % i, [128, D], BF16)) for i in range(2)])
        lg_r = Rot([("lg%d" % i, sb("lg%d" % i, [128, 96])) for i in range(2)])
        maskb = sb("maskb", [128, NT, NE], BF16)
        Gs = sb("Gs", [128, NT, NE])
        CMs = sb("CMs", [128, NT, NE])
        pTh = ps("pTh", [128, D], BF16)
        pTl = ps("pTl", [128, D], BF16)
        pl_r = Rot([("pl%d" % i, ps("pl%d" % i, [128, 512])) for i in range(2)])
        sh2 = sb("sh2", [128, D])
        sc2 = sb("sc2", [128, D])
        for b in range(NB):
            R.dma("sp", lambda e, b=b: e.dma_start(out=sh2[:], in_=modbuf[b, l:l + 1, 3 * D:4 * D].broadcast_to([128, D])), reads=["modbuf"], writes=["sh2_0"])
            R.dma("sp", lambda e, b=b: e.dma_start(out=sc2[:], in_=modbuf[b, l:l + 1, 4 * D:5 * D].broadcast_to([128, D])), reads=["modbuf"], writes=["sc2_0"])
            R.pool(lambda e: e.tensor_scalar(out=sc2[:], in0=sc2[:], scalar1=1.0, scalar2=None, op0=ALU.add), reads=["sc2_0"], writes=["sc2_0"])
            for tc in range(NT):
                xcn, xc = xc_r.next()
                hbn, hb = hb_r.next()
                stn, stt = st_r.next()
                htn, hTc = hT_r.next()
                lgn, lg = lg_r.next()
                pln, pl = pl_r.next()
                R.dma("sp", lambda e, xc=xc, tc=tc, b=b: e.dma_start(out=xc[:], in_=xbuf[b, tc * 128:(tc + 1) * 128, :]), reads=["xb%d_%d" % (b, tc)], writes=[xcn])
                ln_chunk(xc, xcn, stt, stn)
                R.pool(lambda e, xc=xc: e.tensor_tensor(out=xc[:], in0=xc[:], in1=sc2[:], op=ALU.mult), reads=[xcn, "sc2_0"], writes=[xcn])
                R.dve(lambda e, xc=xc: e.tensor_tensor(out=xc[:], in0=xc[:], in1=sh2[:], op=ALU.add), reads=[xcn, "sh2_0"], writes=[xcn])
                R.act(lambda e, xc=xc, hb=hb: e.copy(out=hb[:], in_=xc[:]), reads=[xcn], writes=[hbn])
                R.dma("sp", lambda e, hb=hb, tc=tc, b=b: e.dma_start(out=H2d[b, tc * 128:(tc + 1) * 128, :], in_=hb[:]), reads=[hbn], writes=["H2d%d" % b])
                hln, hl = hl_r.next()
                R.dve(lambda e, xc=xc, hb=hb, hl=hl: e.tensor_tensor(out=hl[:], in0=xc[:], in1=hb[:], op=ALU.subtract), reads=[xcn, hbn], writes=[hln])
                for kc in range(8):
                    R.pe(lambda e, hb=hb, kc=kc: e.transpose(out=pTh[:, kc * 128:(kc + 1) * 128], in_=hb[:, kc * 128:(kc + 1) * 128], identity=ident_b[:]),
                         reads=[hbn, "ident_b"], writes=["pTh"])
                for kc in range(8):
                    R.pe(lambda e, hl=hl, kc=kc: e.transpose(out=pTl[:, kc * 128:(kc + 1) * 128], in_=hl[:, kc * 128:(kc + 1) * 128], identity=ident_b[:]),
                         reads=[hln, "ident_b"], writes=["pTl"])
                R.act(lambda e, hTc=hTc: e.copy(out=hTc[:, 0, :], in_=pTh[:]), reads=["pTh"], writes=[htn])
                R.act(lambda e, hTc=hTc: e.copy(out=hTc[:, 1, :], in_=pTl[:]), reads=["pTl"], writes=[htn])
                terms = [(0, wrh, "wrh"), (1, wrh, "wrh"), (0, wrl, "wrl")]
                for ti, (hs, wt_, wn_) in enumerate(terms):
                    for kc in range(8):
                        R.pe(lambda e, hTc=hTc, kc=kc, pl=pl, hs=hs, wt_=wt_, ti=ti: e.matmul(pl[:, 0:NE], lhsT=hTc[:, hs, kc * 128:(kc + 1) * 128], rhs=wt_[:, kc, :],
                                                                                     start=(ti == 0 and kc == 0), stop=(ti == 2 and kc == 7)),
                             reads=[htn, wn_], writes=[pln])
                R.dve(lambda e, lg=lg, pl=pl: e.tensor_tensor(out=lg[:, 0:32], in0=pl[:, 0:NE], in1=rb[:], op=ALU.add), reads=[pln, "rb"], writes=[lgn])
                R.dve(lambda e, lg=lg: e.max(out=lg[:, 32:40], in_=lg[:, 0:32]), reads=[lgn], writes=[lgn])
                R.dve(lambda e, lg=lg: e.tensor_scalar(out=lg[:, 64:96], in0=lg[:, 0:32], scalar1=lg[:, 35:36], scalar2=None, op0=ALU.is_ge), reads=[lgn], writes=[lgn])
                R.dve(lambda e, lg=lg: e.tensor_scalar(out=lg[:, 40:41], in0=lg[:, 32:33], scalar1=-1.0, scalar2=None, op0=ALU.mult), reads=[lgn], writes=[lgn])
                R.act(lambda e, lg=lg: e.activation(out=lg[:, 0:32], in_=lg[:, 0:32], func=AF.Exp, bias=lg[:, 40:41], scale=1.0), reads=[lgn], writes=[lgn])
                R.dve(lambda e, lg=lg: e.tensor_tensor(out=lg[:, 0:32], in0=lg[:, 0:32], in1=lg[:, 64:96], op=ALU.mult), reads=[lgn], writes=[lgn])
                R.dve(lambda e, lg=lg: e.tensor_reduce(out=lg[:, 41:42], in_=lg[:, 0:32], axis=AX.X, op=ALU.add), reads=[lgn], writes=[lgn])
                R.dve(lambda e, lg=lg: e.reciprocal(out=lg[:, 42:43], in_=lg[:, 41:42]), reads=[lgn], writes=[lgn])
                R.dve(lambda e, lg=lg, tc=tc: e.tensor_scalar(out=Gs[:, tc, :], in0=lg[:, 0:32], scalar1=lg[:, 42:43], scalar2=None, op0=ALU.mult), reads=[lgn], writes=["Gs"])
                R.pool(lambda e, lg=lg, tc=tc: e.tensor_copy(out=maskb[:, tc, :], in_=lg[:, 64:96]), reads=[lgn], writes=["maskb"])
            for tc in range(NT):
                c = tc % 4
                g0 = tc - c
                pln, pl = pl_r.next()
                for cp in range(c):
                    R.pe(lambda e, pl=pl, cp=cp, g0=g0: e.matmul(pl[:, 0:NE], lhsT=onesb[:], rhs=maskb[:, g0 + cp, :], start=(cp == 0), stop=False),
                         reads=["onesb", "maskb"], writes=[pln])
                R.pe(lambda e, pl=pl, tc=tc, c=c: e.matmul(pl[:, 0:NE], lhsT=ltri[:], rhs=maskb[:, tc, :], start=(c == 0), stop=True), reads=["ltri", "maskb"], writes=[pln])
                R.dve(lambda e, pl=pl, tc=tc: e.scalar_tensor_tensor(out=CMs[:, tc, :], in0=pl[:, 0:NE], scalar=1.0, in1=maskb[:, tc, :], op0=ALU.add, op1=ALU.mult),
                      reads=[pln, "maskb"], writes=["CMs"])
            R.dve(lambda e: e.tensor_scalar(out=CMs[:], in0=CMs[:], scalar1=-1.0, scalar2=None, op0=ALU.add), reads=["CMs"], writes=["CMs"])
            R.dma("sp", lambda e, b=b: e.dma_start(out=Gd[b, :, :].rearrange("(c p) e -> p c e", p=128), in_=Gs[:]), reads=["Gs"], writes=["Gd%d" % b])
            R.dma("sp", lambda e, b=b: e.dma_start(out=CMd[b, :, :].rearrange("(c p) e -> p c e", p=128), in_=CMs[:]), reads=["CMs"], writes=["CMd%d" % b])
    R.barrier()
    if "G" in tap_t and l == 0:
        fin.append(R.dma("sp", lambda e: e.dma_start(out=tap_t["G"][:, :], in_=Gd[0, :, :]), reads=["Gd0"], writes=["tap_G"]))
        fin.append(R.dma("sp", lambda e: e.dma_start(out=tap_t["CM"][:, :], in_=CMd[0, :, :]), reads=["CMd0"], writes=["tap_CM"]))
    if moe_stop == "A":
        return

    with contextlib.ExitStack() as sB:
        sb, ps = mk(sB)
        iota3 = sb("iota3", [128, NE, 128], BF16)
        R.dma("sp", lambda e: e.dma_start(out=iota3[:], in_=cst["iota3"][:, :, :]), writes=["iota3"])
        h2g = sb("h2g", [128, 4, D], BF16)
        Gg = sb("Gg", [128, 4, NE])
        CMg = sb("CMg", [128, 4, NE])
        CMj = sb("CMj", [128, 4, NE])
        P = [sb("P%d" % i, [128, NE, 128], BF16) for i in range(4)]
        Pg = [sb("Pg%d" % i, [128, NE, 128], BF16) for i in range(4)]
        xe_r = Rot([("xe%d" % i, sb("xe%d" % i, [128, 8, 512], BF16)) for i in range(2)])
        pgt_r = Rot([("pgt%d" % i, sb("pgt%d" % i, [128, 1024], BF16)) for i in range(2)])
        pg_r = Rot([("pB%d" % i, ps("pB%d" % i, [128, 512])) for i in range(4)])
        pTb_r = Rot([("pTb%d" % i, ps("pTb%d" % i, [128, 1024], BF16)) for i in range(2)])
        for g in range(NB * 4):
            b, gq = g // 4, g % 4
            R.dma("sp", lambda e, b=b, gq=gq: e.dma_start(out=h2g[:], in_=H2d[b, gq * 512:(gq + 1) * 512, :].rearrange("(c p) d -> p c d", p=128)),
                  reads=["H2d%d" % b], writes=["h2g"])
            R.dma("sp", lambda e, b=b, gq=gq: e.dma_start(out=Gg[:], in_=Gd[b, gq * 512:(gq + 1) * 512, :].rearrange("(c p) e -> p c e", p=128)),
                  reads=["Gd%d" % b], writes=["Gg"])
            R.dma("sp", lambda e, b=b, gq=gq: e.dma_start(out=CMg[:], in_=CMd[b, gq * 512:(gq + 1) * 512, :].rearrange("(c p) e -> p c e", p=128)),
                  reads=["CMd%d" % b], writes=["CMg"])
            for jh in range(2):
                R.dve(lambda e, jh=jh: e.tensor_scalar(out=CMj[:], in0=CMg[:], scalar1=-128.0 * jh, scalar2=None, op0=ALU.add), reads=["CMg"], writes=["CMj"])
                for c in range(4):
                    R.dve(lambda e, c=c: e.tensor_tensor(out=P[c][:], in0=iota3[:], in1=CMj[:, c, :].unsqueeze(2).broadcast_to([128, NE, 128]), op=ALU.is_equal),
                          reads=["iota3", "CMj"], writes=["P%d" % c])
                    R.pool(lambda e, c=c: e.tensor_tensor(out=Pg[c][:], in0=P[c][:], in1=Gg[:, c, :].unsqueeze(2).broadcast_to([128, NE, 128]), op=ALU.mult),
                           reads=["P%d" % c, "Gg"], writes=["Pg%d" % c])
                for eq in range(8):
                    xen, xe = xe_r.next()
                    for dk in range(8):
                        pn, pt = pg_r.next()
                        for c in range(4):
                            R.pe(lambda e, pt=pt, c=c, dk=dk, eq=eq: e.matmul(pt[:, :], lhsT=h2g[:, c, dk * 128:(dk + 1) * 128],
                                                                             rhs=P[c][:, eq * 4:(eq + 1) * 4, :].rearrange("p e j -> p (e j)"), start=(c == 0), stop=(c == 3)),
                                 reads=["h2g", "P%d" % c], writes=[pn])
                        if dk % 2 == 0:
                            R.act(lambda e, pt=pt, xe=xe, dk=dk: e.copy(out=xe[:, dk, :], in_=pt[:, :]), reads=[pn], writes=[xen])
                        else:
                            R.dve(lambda e, pt=pt, xe=xe, dk=dk: e.tensor_copy(out=xe[:, dk, :], in_=pt[:, :]), reads=[pn], writes=[xen])
                    R.dma("sp", lambda e, xe=xe, g=g, eq=eq, jh=jh: e.dma_start(out=XTd[g // 4][g % 4, jh, eq, :, :, :], in_=xe[:]), reads=[xen], writes=["XTd%d" % g])
                for e2 in range(NE // 2):
                    ptn, ptb = pTb_r.next()
                    pgn, pgt = pgt_r.next()
                    for ee in range(2):
                        for c in range(4):
                            R.pe(lambda e, ptb=ptb, ee=ee, c=c, e2=e2: e.transpose(out=ptb[:, ee * 512 + c * 128:ee * 512 + (c + 1) * 128], in_=Pg[c][:, e2 * 2 + ee, :], identity=ident_b[:]),
                                 reads=["Pg%d" % c, "ident_b"], writes=[ptn])
                    R.act(lambda e, ptb=ptb, pgt=pgt: e.copy(out=pgt[:], in_=ptb[:]), reads=[ptn], writes=[pgn])
                    R.dma("sp", lambda e, pgt=pgt, g=g, e2=e2, jh=jh: e.dma_start(out=PGd[g // 4][g % 4, jh, e2 * 2:e2 * 2 + 2, :, :].rearrange("e j t -> j e t"),
                                                                              in_=pgt[:].rearrange("p (e t) -> p e t", e=2)), reads=[pgn], writes=["PGd%d" % g])
    R.barrier()
    if moe_stop == "B":
        return

    with contextlib.ExitStack() as sC:
        sb, ps = mk(sC)
        wgu_r = Rot([("wgu%d" % i, sb("wgu%d" % i, [128, 8, 2 * D], BF16)) for i in range(2)])
        wd_r = Rot([("wd%d" % i, sb("wd%d" % i, [128, 8, D], BF16)) for i in range(2)])
        brow_r = Rot([("brow%d" % i, sb("brow%d" % i, [16, 128])) for i in range(2)])
        bgu_r = Rot([("bgu%d" % i, sb("bgu%d" % i, [128, 16])) for i in range(2)])
        xt_r = Rot([("xt%d" % i, sb("xt%d" % i, [128, 8, 512], BF16)) for i in range(2)])
        at_r = Rot([("at%d" % i, sb("at%d" % i, [128, 8, 512], BF16)) for i in range(2)])
        gc_r = Rot([("gc%d" % i, sb("gc%d" % i, [128, 512])) for i in range(2)])
        sl_r = Rot([("sl%d" % i, sb("sl%d" % i, [128, 512])) for i in range(2)])
        u0_r = Rot([("u0%d" % i, sb("u0%d" % i, [128, 512])) for i in range(2)])
        ys_r = Rot([("ys%d" % i, sb("ys%d" % i, [128, D], BF16)) for i in range(2)])
        pgu_r = Rot([("pC%d" % i, ps("pC%d" % i, [128, 512])) for i in range(4)])
        pdn_r = Rot([("pD%d" % i, ps("pD%d" % i, [128, 512])) for i in range(3)])
        pXc = ps("pXc", [128, 512])
        for ex in range(NE):
            wgn, wgu = wgu_r.next()
            wdn, wd = wd_r.next()
            brn, brow = brow_r.next()
            bgn, bgu = bgu_r.next()
            for hf in range(2):
                R.dma("pool", lambda e, wgu=wgu, ex=ex, hf=hf: e.dma_start(out=wgu[:, :, hf * D:(hf + 1) * D],
                                                                          in_=w_gu[l, ex, :, hf * D:(hf + 1) * D].rearrange("(k p) n -> p k n", p=128)), writes=[wgn])
            R.dma("pool", lambda e, wd=wd, ex=ex: e.dma_start(out=wd[:], in_=w_dn[l, ex, :, :].rearrange("(k p) n -> p k n", p=128)), writes=[wdn])
            R.dma("sp", lambda e, brow=brow, ex=ex: e.dma_start(out=brow[:], in_=b_gu[l, ex, :].rearrange("(m p) -> m p", p=128)), writes=[brn])
            R.pe(lambda e, brow=brow: e.transpose(out=pXc[:, 0:16], in_=brow[:], identity=ident_f[0:16, 0:16]), reads=[brn, "ident_f"], writes=["pXc"])
            R.dve(lambda e, bgu=bgu: e.tensor_copy(out=bgu[:], in_=pXc[:, 0:16]), reads=["pXc"], writes=[bgn])
            for b, jh in [(b_, j_) for b_ in range(NB) for j_ in range(2)]:
                xtn, xt = xt_r.next()
                atn, at = at_r.next()
                for gq in range(4):
                    R.dma("sp", lambda e, xt=xt, b=b, gq=gq, ex=ex, jh=jh: e.dma_start(
                        out=xt[:, :, gq * 128:(gq + 1) * 128], in_=XTd[b][gq, jh, ex // 4, :, :, (ex % 4) * 128:(ex % 4 + 1) * 128]),
                        reads=["XTd%d" % (b * 4 + gq)], writes=[xtn])
                for m in range(8):
                    pgn_, pgp = pgu_r.next()
                    pun_, pup = pgu_r.next()
                    for (pp, pnm, mm) in ((pgp, pgn_, m), (pup, pun_, m + 8)):
                        for dk in range(8):
                            R.pe(lambda e, pp=pp, mm=mm, dk=dk, wgu=wgu, xt=xt: e.matmul(pp[:, :], lhsT=wgu[:, dk, mm * 128:(mm + 1) * 128], rhs=xt[:, dk, :],
                                                                                     start=(dk == 0), stop=(dk == 7)), reads=[wgn, xtn], writes=[pnm])
                    gcn, gc = gc_r.next()
                    sln, sl = sl_r.next()
                    u0n, u0 = u0_r.next()
                    R.dve(lambda e, gc=gc, pgp=pgp, bgu=bgu, m=m: e.tensor_scalar(out=gc[:], in0=pgp[:, :], scalar1=bgu[:, m:m + 1], scalar2=7.0, op0=ALU.add, op1=ALU.min),
                          reads=[pgn_, bgn], writes=[gcn])
                    R.act(lambda e, gc=gc, sl=sl: e.activation(out=sl[:], in_=gc[:], func=AF.Silu, scale=GELU_A), reads=[gcn], writes=[sln])
                    R.act(lambda e, u0=u0, pup=pup, bgu=bgu, m=m: e.activation(out=u0[:], in_=pup[:, :], func=AF.Identity, bias=bgu[:, m + 8:m + 9], scale=1.0),
                          reads=[pun_, bgn], writes=[u0n])
                    R.pool(lambda e, u0=u0: e.tensor_scalar(out=u0[:], in0=u0[:], scalar1=7.0, scalar2=-7.0, op0=ALU.min, op1=ALU.max), reads=[u0n], writes=[u0n])
                    R.dve(lambda e, u0=u0, sl=sl, at=at, m=m: e.scalar_tensor_tensor(out=at[:, m, :], in0=u0[:], scalar=1.0, in1=sl[:], op0=ALU.add, op1=ALU.mult),
                          reads=[u0n, sln], writes=[atn])
                for gq in range(4):
                    ysn, ys = ys_r.next()
                    for hf in range(2):
                        pdn_, pdp = pdn_r.next()
                        for m in range(8):
                            R.pe(lambda e, pdp=pdp, m=m, gq=gq, hf=hf, at=at, wd=wd: e.matmul(pdp[:, :], lhsT=at[:, m, gq * 128:(gq + 1) * 128], rhs=wd[:, m, hf * 512:(hf + 1) * 512],
                                                                                          start=(m == 0), stop=(m == 7)), reads=[atn, wdn], writes=[pdn_])
                        R.act(lambda e, pdp=pdp, ys=ys, hf=hf: e.activation(out=ys[:, hf * 512:(hf + 1) * 512], in_=pdp[:, :], func=AF.Copy, scale=1.0 / GELU_A),
                              reads=[pdn_], writes=[ysn])
                    R.dma("sp", lambda e, ys=ys, b=b, gq=gq, ex=ex, jh=jh: e.dma_start(out=Yd[b][gq, jh, ex, :, :], in_=ys[:]), reads=[ysn], writes=["Yd%d" % (b * 4 + gq)])
    R.barrier()
    if moe_stop == "C":
        return

    with contextlib.ExitStack() as sD:
        sb, ps = mk(sD)
        Ysb = sb("Ysb", [128, NE, D], BF16)
        PGs = sb("PGs", [128, NE, 512], BF16)
        Bdp = sb("Bdp", [128, D], BF16)
        Gpad = sb("Gpad", [128, 128], BF16)
        GTp_r = Rot([("GTp%d" % i, sb("GTp%d" % i, [128, 128], BF16)) for i in range(2)])
        Gg2 = sb("Gg2", [128, 4, NE])
        facc = sb("facc", [128, 4, D])
        lng = sb("lng2", [128, D])
        lnb = sb("lnb2", [128, D])
        g2p = sb("g2p", [128, D])
        R.dma("sp", lambda e: e.dma_start(out=lng[:], in_=ln_g[l, 1:2, :].broadcast_to([128, D])), writes=["lng2"])
        R.dma("sp", lambda e: e.dma_start(out=lnb[:], in_=ln_b[l, 1:2, :].broadcast_to([128, D])), writes=["lnb2"])
        R.pool(lambda e: e.memset(Bdp[:], 0.0), writes=["Bdp"])
        R.pool(lambda e: e.memset(Gpad[:], 0.0), writes=["Gpad"])
        R.dma("pool", lambda e: e.dma_start(out=Bdp[0:NE, :], in_=b_dn[l, :, :]), reads=["Bdp"], writes=["Bdp"])
        xc_r = Rot([("xd%d" % i, sb("xd%d" % i, [128, D])) for i in range(2)])
        mt_r = Rot([("md%d" % i, sb("md%d" % i, [128, D])) for i in range(2)])
        st_r = Rot([("std%d" % i, sb("std%d" % i, [128, 16])) for i in range(2)])
        pc_r = Rot([("pE%d" % i, ps("pE%d" % i, [128, 512])) for i in range(4)])
        pTd = ps("pTd", [128, 1024], BF16)
        for g in range(NB * 4):
            b, gq = g // 4, g % 4
            if gq == 0:
                R.dma("sp", lambda e, b=b: e.dma_start(out=g2p[:], in_=modbuf[b, l:l + 1, 5 * D:6 * D].broadcast_to([128, D])), reads=["modbuf"], writes=["g2p"])
                R.pool(lambda e: e.tensor_scalar(out=g2p[:], in0=g2p[:], scalar1=1.0, scalar2=None, op0=ALU.add), reads=["g2p"], writes=["g2p"])
            R.dma("sp", lambda e, b=b, gq=gq: e.dma_start(out=Gg2[:], in_=Gd[b, gq * 512:(gq + 1) * 512, :].rearrange("(c p) e -> p c e", p=128)),
                  reads=["Gd%d" % b], writes=["Gg2"])
            for jh in range(2):
                for q4 in range(4):
                    R.dma("sp", lambda e, g=g, q4=q4, jh=jh: e.dma_start(out=Ysb[:, q4 * 8:(q4 + 1) * 8, :], in_=Yd[g // 4][g % 4, jh, q4 * 8:(q4 + 1) * 8, :, :].rearrange("e j d -> j e d")),
                          reads=["Yd%d" % g], writes=["Ysb"])
                    R.dma("sp", lambda e, g=g, q4=q4, jh=jh: e.dma_start(out=PGs[:, q4 * 8:(q4 + 1) * 8, :], in_=PGd[g // 4][g % 4, jh, q4 * 8:(q4 + 1) * 8, :, :].rearrange("e j t -> j e t")),
                          reads=["PGd%d" % g], writes=["PGs"])
                for c in range(4):
                    tc = gq * 4 + c
                    if jh == 0:
                        gtn, GTp = GTp_r.next()
                        R.dve(lambda e, c=c: e.tensor_copy(out=Gpad[:, 0:NE], in_=Gg2[:, c, :]), reads=["Gg2", "Gpad"], writes=["Gpad"])
                        R.pe(lambda e: e.transpose(out=pTd[:, 0:128], in_=Gpad[:], identity=ident_b[:]), reads=["Gpad", "ident_b"], writes=["pTd"])
                        R.act(lambda e, GTp=GTp: e.copy(out=GTp[:], in_=pTd[:, 0:128]), reads=["pTd"], writes=[gtn])
                    else:
                        xcn, xc = xc_r.next()
                        mtn, mt = mt_r.next()
                        stn, stt = st_r.next()
                        R.dma("sp", lambda e, xc=xc, tc=tc, b=b: e.dma_start(out=xc[:], in_=xbuf[b, tc * 128:(tc + 1) * 128, :]), reads=["xb%d_%d" % (b, tc)], writes=[xcn])
                    for hf in range(2):
                        pn, pt = pc_r.next()
                        for ex in range(NE):
                            R.pe(lambda e, pt=pt, ex=ex, c=c, hf=hf: e.matmul(pt[:, :], lhsT=PGs[:, ex, c * 128:(c + 1) * 128], rhs=Ysb[:, ex, hf * 512:(hf + 1) * 512],
                                                                             start=(ex == 0), stop=(jh == 1 and ex == NE - 1)), reads=["PGs", "Ysb"], writes=[pn])
                        if jh == 0:
                            R.pe(lambda e, pt=pt, GTp=GTp, hf=hf: e.matmul(pt[:, :], lhsT=GTp[:], rhs=Bdp[:, hf * 512:(hf + 1) * 512], start=False, stop=True),
                                 reads=[gtn, "Bdp"], writes=[pn])
                            R.act(lambda e, pt=pt, c=c, hf=hf: e.copy(out=facc[:, c, hf * 512:(hf + 1) * 512], in_=pt[:, :]), reads=[pn], writes=["facc"])
                        else:
                            R.dve(lambda e, pt=pt, mt=mt, hf=hf, c=c: e.tensor_tensor(out=mt[:, hf * 512:(hf + 1) * 512], in0=pt[:, :], in1=facc[:, c, hf * 512:(hf + 1) * 512], op=ALU.add),
                                  reads=[pn, "facc"], writes=[mtn])
                    if jh == 1:
                        R.pool(lambda e, mt=mt: e.tensor_tensor(out=mt[:], in0=mt[:], in1=g2p[:], op=ALU.mult), reads=[mtn, "g2p"], writes=[mtn])
                        R.dve(lambda e, xc=xc, mt=mt: e.scalar_tensor_tensor(out=xc[:], in0=xc[:], scalar=ALPHA, in1=mt[:], op0=ALU.mult, op1=ALU.add),
                              reads=[xcn, mtn], writes=[xcn])
                        ln_chunk(xc, xcn, stt, stn)
                        R.pool(lambda e, xc=xc: e.tensor_tensor(out=xc[:], in0=xc[:], in1=lng[:], op=ALU.mult), reads=[xcn, "lng2"], writes=[xcn])
                        R.dve(lambda e, xc=xc: e.tensor_tensor(out=xc[:], in0=xc[:], in1=lnb[:], op=ALU.add), reads=[xcn, "lnb2"], writes=[xcn])
                        if last:
                            fin.append(R.dma("sp", lambda e, xc=xc, tc=tc, b=b: e.dma_start(out=out[b, tc * 128:(tc + 1) * 128, :], in_=xc[:]), reads=[xcn], writes=["out%d_%d" % (b, tc)]))
                        else:
                            R.dma("sp", lambda e, xc=xc, tc=tc, b=b: e.dma_start(out=xbuf[b, tc * 128:(tc + 1) * 128, :], in_=xc[:]), reads=[xcn], writes=["xb%d_%d" % (b, tc)])
    R.barrier()


N_CORES = 4
FUSED = True
DEPTH = 4
_PROG = {}


def _get_prog(NB, NL):
    key = (NB, NL)
    if key not in _PROG:
        _PROG[key] = build(NB, NL)
    return _PROG[key]


def _consts():
    hc = host_consts()
    hc.update(moe_consts())
    return {"k_" + k: v for k, v in hc.items()}


_PER_LAYER = ("ada_w", "ada_b", "w_in", "w_out", "ret_gn_w", "conv_w", "cmp_pos", "cmp_w1", "cmp_w2", "ln_g", "ln_b",
              "router_w", "router_b", "w_gate_up", "b_gate_up", "w_down", "b_down")


def kernel(**inputs):
    B = inputs["x"].shape[0]
    NB = B // N_CORES
    f32 = np.float32
    x = np.ascontiguousarray(inputs["x"], dtype=f32)
    c = np.ascontiguousarray(inputs["c"], dtype=f32)
    pos = np.ascontiguousarray(inputs["positions"], dtype=np.int32)
    consts = _consts()
    layer_sets = [list(range(DEPTH))] if FUSED else [[l] for l in range(DEPTH)]
    for ls in layer_sets:
        nc = _get_prog(NB, len(ls))
        w = {k: np.ascontiguousarray(np.asarray(inputs[k])[ls[0]:ls[-1] + 1], dtype=f32) for k in _PER_LAYER}
        in_maps = []
        for ci in range(N_CORES):
            m = dict(w)
            m.update(consts)
            m["x"] = np.ascontiguousarray(x[ci * NB:(ci + 1) * NB])
            m["c"] = np.ascontiguousarray(c[ci * NB:(ci + 1) * NB])
            m["positions"] = np.ascontiguousarray(pos[ci * NB:(ci + 1) * NB])
            in_maps.append(m)
        res = run_bass_kernel_spmd(nc, in_maps, core_ids=list(range(N_CORES)))
        x = np.concatenate([np.asarray(r["out"], dtype=f32) for r in res.results], axis=0)
    return x
```

```python
import math
import contextlib
import numpy as np
import ml_dtypes
import concourse.bass as bass
import concourse.mybir as mybir
from concourse.bass_utils import run_bass_kernel_spmd


ENGS = ("pe", "act", "dve", "pool", "sp")
NPOOL = 24


class Rec:
    def __init__(self, nc):
        self.nc = nc
        self.ops = []
        self.lastw = {}
        self.readers = {}
        self.ndma = 0
        self.dma_idx = []

    def eng_obj(self, e):
        nc = self.nc
        return {"pe": nc.tensor, "act": nc.scalar, "dve": nc.vector, "pool": nc.gpsimd, "sp": nc.sync}[e]

    def add(self, eng, fn, reads=(), writes=(), dma=False):
        idx = len(self.ops)
        deps = set()
        reads = list(reads) + ["BARRIER"]
        for r in reads:
            if r in self.lastw:
                deps.add(self.lastw[r])
        for w in writes:
            if w in self.lastw:
                deps.add(self.lastw[w])
            for rd in self.readers.get(w, ()):
                deps.add(rd)
        op = dict(eng=eng, fn=fn, deps=deps, dma=dma, used=False, slot=None)
        if dma:
            op["slot"] = self.ndma % NPOOL
            op["val"] = 16 * (self.ndma // NPOOL + 1)
            prev = self.ndma - NPOOL
            if prev >= 0:
                deps.add(self.dma_idx[prev])
            self.dma_idx.append(idx)
            self.ndma += 1
        self.ops.append(op)
        for r in reads:
            self.readers.setdefault(r, []).append(idx)
        for w in writes:
            self.lastw[w] = idx
            self.readers[w] = []
        return idx

    def barrier(self):
        return self.add("sp", lambda e: e.nop(), reads=(), writes=["BARRIER"])

    def pe(self, fn, reads=(), writes=()):
        return self.add("pe", fn, reads, writes)

    def act(self, fn, reads=(), writes=()):
        return self.add("act", fn, reads, writes)

    def dve(self, fn, reads=(), writes=()):
        return self.add("dve", fn, reads, writes)

    def pool(self, fn, reads=(), writes=()):
        return self.add("pool", fn, reads, writes)

    def dma(self, eng, fn, reads=(), writes=()):
        return self.add(eng, fn, reads, writes, dma=True)

    def emit(self, final_wait_ops=()):
        nc = self.nc
        ops = self.ops
        for i, op in enumerate(ops):
            for d in op["deps"]:
                dop = ops[d]
                if (not dop["dma"]) and dop["eng"] == "pe" and op["eng"] == "pe" and not op["dma"]:
                    continue
                dop["used"] = True
        for i in final_wait_ops:
            ops[i]["used"] = True
        cnt = {e: 0 for e in ENGS}
        for op in ops:
            if not op["dma"] and op["used"]:
                cnt[op["eng"]] += 1
                op["val"] = cnt[op["eng"]]
        import contextlib
        with contextlib.ExitStack() as st:
            esem = {e: st.enter_context(nc.semaphore("es_" + e)) for e in ENGS}
            dsem = [st.enter_context(nc.semaphore("ds_%d" % i)) for i in range(NPOOL)]
            block = st.enter_context(nc.Block())

            def semof(op):
                if op["dma"]:
                    return dsem[op["slot"]], op["val"], ("d", op["slot"])
                return esem[op["eng"]], op["val"], ("e", op["eng"])

            def run_engine(e, engobj):
                seen = {}
                for i, op in enumerate(ops):
                    if op["eng"] != e:
                        continue
                    need = {}
                    for d in op["deps"]:
                        dop = ops[d]
                        if (not dop["dma"]) and dop["eng"] == "pe" and e == "pe" and not op["dma"]:
                            continue
                        s, v, k = semof(dop)
                        if seen.get(k, 0) >= v:
                            continue
                        if k not in need or need[k][1] < v:
                            need[k] = (s, v)
                    for k, (s, v) in need.items():
                        engobj.wait_ge(s, v)
                        seen[k] = v
                    ins = op["fn"](engobj)
                    if op["dma"]:
                        ins.then_inc(dsem[op["slot"]], 16)
                    elif op["used"]:
                        ins.then_inc(esem[e], 1)
                if e == "sp":
                    for i in final_wait_ops:
                        s, v, k = semof(ops[i])
                        engobj.wait_ge(s, v)

            @block.tensor
            def _(eng):
                run_engine("pe", eng)

            @block.scalar
            def _(eng):
                run_engine("act", eng)

            @block.vector
            def _(eng):
                run_engine("dve", eng)

            @block.gpsimd
            def _(eng):
                run_engine("pool", eng)

            @block.sync
            def _(eng):
                run_engine("sp", eng)

F32 = mybir.dt.float32
BF16 = mybir.dt.bfloat16
I32 = mybir.dt.int32
AF = mybir.ActivationFunctionType
ALU = mybir.AluOpType
AX = mybir.AxisListType

D = 1024
S = 2048
NT = 16
LN_EPS = 1e-5
ALPHA = 8.0 ** 0.25
NEGB = -30000.0
SCALE = 0.125
C_RQ, C_RK, C_RV, C_RG = 0, 256, 512, 768
C_CB, C_CC, C_CH = 1024, 1280, 1536
C_NQ = 1792
C_KC, C_VC, C_KS, C_VS, C_KW, C_VW = 2304, 2432, 2560, 2688, 2816, 2944
C_NG = 3072
NIN = 3096
TW = 2432
NE = 32
GELU_A = 1.702


class Rot:
    def __init__(self, items):
        self.items = items
        self.i = 0

    def next(self):
        it = self.items[self.i % len(self.items)]
        self.i += 1
        return it


def host_consts():
    c = {}
    c["ident"] = np.eye(128, dtype=np.float32)
    r = np.arange(128)
    inv = (10000.0 ** (-((r % 64) % 32).astype(np.float64) / 32.0))
    c["ropec"] = np.stack([inv / (2 * np.pi), np.where((r % 64) < 32, -1.0, 1.0)], 1).astype(np.float32)
    gam = 1.0 - 2.0 ** (-5.0 - np.arange(4))
    k = np.arange(128)[:, None]
    m = np.arange(TW)[None, :] - 384
    tabs = []
    for h in range(4):
        e = (m - k).astype(np.float64)
        tabs.append(np.where(e >= 0, np.exp(np.log(gam[h]) * np.maximum(e, 0)) * SCALE, 0.0))
    c["dect"] = np.stack(tabs, 1).astype(ml_dtypes.bfloat16)
    dd = (m - k)
    c["tmask"] = np.stack([(dd >= 0), (dd >= 0) & (dd < 512)], 1).astype(np.float32).astype(ml_dtypes.bfloat16)
    kk = np.arange(128)[:, None]
    qq = np.arange(128)[None, :]
    c["cb_caus"] = np.where(kk > qq, 0.0, 1.0).astype(ml_dtypes.bfloat16)
    c["cb_low"] = np.where(kk <= qq, 0.0, 1.0).astype(ml_dtypes.bfloat16)
    cc = np.arange(128)[:, None]
    tt = np.arange(2048)[None, :]
    c["cb_cmp"] = np.where(16 * cc + 31 > tt, 0.0, 1.0).astype(ml_dtypes.bfloat16)
    cs = np.arange(128) * 16
    cl = cs + 31
    ss = np.arange(32) * 64
    ov = ((cs[:, None] < ss[None, :] + 64) & (ss[None, :] <= cl[:, None])).astype(np.float32)
    ov[127] = 0
    c["overlap"] = ov.astype(ml_dtypes.bfloat16)
    t = (np.arange(16)[None, :, None] * 128 + np.arange(128)[:, None, None])
    j = np.arange(32)[None, None, :]
    c["forced"] = np.where((j == 0) | (j == t // 64), 1e9, 0.0).astype(np.float32)
    b = np.arange(32)[:, None, None]
    kc = np.arange(16)[None, :, None]
    kl = np.arange(128)[None, None, :]
    E = np.zeros((128, 16, 128), np.float32)
    E[0:32] = (b == 2 * kc + (kl >= 64))
    c["Eexp"] = E.astype(ml_dtypes.bfloat16)
    return c


def build(NB=1, NL=1, taps=(), stop=None, nsa_stage=2, nsa_part=9, nsa_hks=(0, 1), nsa_qcs=tuple(range(4)), moe_stop="D", skip_mixer=False, ne_in=NE):
    nc = bass.Bass("TRN2", target_bir_lowering=False)
    R = Rec(nc)
    dram = {}

    def din(name, shape, dt=F32):
        dram[name] = nc.dram_tensor(name, list(shape), dt, kind="ExternalInput").ap()
        return dram[name]

    def dint(name, shape, dt=F32):
        dram[name] = nc.dram_tensor(name, list(shape), dt, kind="Internal").ap()
        return dram[name]

    def dout(name, shape, dt=F32):
        dram[name] = nc.dram_tensor(name, list(shape), dt, kind="ExternalOutput").ap()
        return dram[name]

    x_in = din("x", [NB, S, D])
    c_in = din("c", [NB, D])
    pos_in = din("positions", [NB, S], I32)
    ada_w = din("ada_w", [NL, D, 6 * D])
    ada_b = din("ada_b", [NL, 6 * D])
    w_in = din("w_in", [NL, D, NIN])
    w_out = din("w_out", [NL, D, D])
    gn_w = din("ret_gn_w", [NL, 256])
    conv_w = din("conv_w", [NL, 3, 256])
    cmp_pos = din("cmp_pos", [NL, 2, 32, 64])
    cmp_w1 = din("cmp_w1", [NL, 2, 2048, 128])
    cmp_w2 = din("cmp_w2", [NL, 2, 128, 64])
    ln_g = din("ln_g", [NL, 2, D])
    ln_b = din("ln_b", [NL, 2, D])
    router_w = din("router_w", [NL, D, NE])
    router_b = din("router_b", [NL, NE])
    w_gu = din("w_gate_up", [NL, ne_in, D, 2 * D])
    b_gu = din("b_gate_up", [NL, ne_in, 2 * D])
    w_dn = din("w_down", [NL, ne_in, D, D])
    b_dn = din("b_down", [NL, ne_in, D])
    HC = host_consts()
    HC.update(moe_consts())
    cst = {}
    for k, v in HC.items():
        cst[k] = din("k_" + k, list(v.shape), BF16 if v.dtype == ml_dtypes.bfloat16 else F32)
    out = dout("out", [NB, S, D])
    xbuf = dint("xbuf", [NB, S, D])
    ropebuf = dint("ropebuf", [NB, 2, 128, S])
    NG = NB * 4
    H2d = dint("H2d", [NB, S, D], BF16)
    Gd = dint("Gd", [NB, S, NE])
    CMd = dint("CMd", [NB, S, NE])
    XTd = [dint("XTd%d" % i, [4, 2, 8, 128, 8, 512], BF16) for i in range(NB)]
    PGd = [dint("PGd%d" % i, [4, 2, NE, 128, 512], BF16) for i in range(NB)]
    Yd = [dint("Yd%d" % i, [4, 2, NE, 128, D], BF16) for i in range(NB)]
    tap_t = {}
    for (nm, shape, dt) in taps:
        tap_t[nm] = dout("tap_" + nm, shape, dt)
    fin = []

    top = contextlib.ExitStack()

    uniq = [0]

    def mk(stack):
        def sbuf(name, shape, dt=F32):
            uniq[0] += 1
            return stack.enter_context(nc.sbuf_tensor("%s_%d" % (name, uniq[0]), list(shape), dt))

        def psum(name, shape, dt=F32):
            uniq[0] += 1
            return stack.enter_context(nc.psum_tensor("%s_%d" % (name, uniq[0]), list(shape), dt))
        return sbuf, psum

    with top:
        sbuf, psum = mk(top)
        ident_f = sbuf("ident_f", [128, 128])
        ident_b = sbuf("ident_b", [128, 128], BF16)
        R.dma("sp", lambda e: e.dma_start(out=ident_f[:], in_=cst["ident"][:, :]), writes=["ident_f"])
        R.dve(lambda e: e.tensor_copy(out=ident_b[:], in_=ident_f[:]), reads=["ident_f"], writes=["ident_b"])
        ropec = sbuf("ropec", [128, 2])
        R.dma("sp", lambda e: e.dma_start(out=ropec[:], in_=cst["ropec"][:, :]), writes=["ropec"])
        modbuf = dint("modbuf", [NB, NL, 6 * D])

        with contextlib.ExitStack() as st0:
            sb0, ps0 = mk(st0)
            c_sb = sb0("c_sb", [NB, D])
            cT = sb0("cT", [128, 8, NB], BF16)
            R.dma("sp", lambda e: e.dma_start(out=c_sb[:], in_=c_in[:, :]), writes=["c_sb"])
            R.act(lambda e: e.activation(out=c_sb[:], in_=c_sb[:], func=AF.Silu), reads=["c_sb"], writes=["c_sb"])
            ps_t = ps0("ps_t", [128, 512])
            for kc in range(8):
                R.pe(lambda e, kc=kc: e.transpose(out=ps_t[:, kc * NB:(kc + 1) * NB], in_=c_sb[:, kc * 128:(kc + 1) * 128],
                                                 identity=ident_f[0:NB, 0:NB]),
                     reads=["c_sb", "ident_f"], writes=["ps_t"])
            R.dve(lambda e: e.tensor_copy(out=cT[:].rearrange("p k b -> p (k b)"), in_=ps_t[:, 0:8 * NB]),
                  reads=["ps_t"], writes=["cT"])
            mods = sb0("mods", [NB, 6 * D])
            adab = sb0("adab", [NB, 6 * D])
            adaw = Rot([("adaw%d" % i, sb0("adaw%d" % i, [128, 8, 512], BF16)) for i in range(2)])
            psm = Rot([("ps_m%d" % i, ps0("ps_m%d" % i, [128, 512])) for i in range(2)])
            for l in range(NL):
                for b in range(NB):
                    R.dma("sp", lambda e, b=b, l=l: e.dma_start(out=adab[b:b + 1, :], in_=ada_b[l:l + 1, :]), writes=["adab"])
                for cc in range(12):
                    wn, wb = adaw.next()
                    pn, pm = psm.next()
                    R.dma("pool", lambda e, wb=wb, l=l, cc=cc: e.dma_start(
                        out=wb[:], in_=ada_w[l, :, cc * 512:(cc + 1) * 512].rearrange("(k p) n -> p k n", p=128)),
                        writes=[wn])
                    for kc in range(8):
                        R.pe(lambda e, wb=wb, pm=pm, kc=kc: e.matmul(pm[0:NB, :], lhsT=cT[:, kc, :], rhs=wb[:, kc, :],
                                                                     start=(kc == 0), stop=(kc == 7)),
                             reads=["cT", wn], writes=[pn])
                    R.dve(lambda e, pm=pm, cc=cc: e.tensor_tensor(
                        out=mods[:, cc * 512:(cc + 1) * 512], in0=pm[0:NB, :], in1=adab[:, cc * 512:(cc + 1) * 512], op=ALU.add),
                        reads=[pn, "adab"], writes=["mods"])
                R.dma("sp", lambda e, l=l: e.dma_start(out=modbuf[:, l, :], in_=mods[:]), reads=["mods"], writes=["modbuf"])
            posi = sb0("posi", [128, S], I32)
            y0 = sb0("y0", [128, S])
            r1 = sb0("r1", [128, S])
            ki = sb0("ki", [128, S], I32)
            kf = sb0("kf", [128, S])
            for b in range(NB):
                R.dma("sp", lambda e, b=b: e.dma_start(out=posi[:], in_=pos_in[b:b + 1, :].broadcast_to([128, S])), writes=["posi"])
                R.dve(lambda e: e.tensor_copy(out=y0[:], in_=posi[:]), reads=["posi"], writes=["y0"])
                R.dve(lambda e: e.tensor_scalar(out=y0[:], in0=y0[:], scalar1=ropec[:, 0:1], scalar2=None, op0=ALU.mult),
                      reads=["y0", "ropec"], writes=["y0"])
                for which in range(2):
                    sh = 0.25 if which == 0 else 0.0
                    R.dve(lambda e, sh=sh: e.tensor_scalar(out=r1[:], in0=y0[:], scalar1=sh, scalar2=None, op0=ALU.add),
                          reads=["y0"], writes=["r1"])
                    R.dve(lambda e: e.tensor_copy(out=ki[:], in_=r1[:]), reads=["r1"], writes=["ki"])
                    R.dve(lambda e: e.tensor_copy(out=kf[:], in_=ki[:]), reads=["ki"], writes=["kf"])
                    R.dve(lambda e: e.tensor_tensor(out=r1[:], in0=r1[:], in1=kf[:], op=ALU.subtract), reads=["r1", "kf"], writes=["r1"])
                    R.dve(lambda e: e.tensor_single_scalar(out=kf[:], in_=r1[:], scalar=0.5, op=ALU.is_gt), reads=["r1"], writes=["kf"])
                    R.dve(lambda e: e.tensor_tensor(out=r1[:], in0=r1[:], in1=kf[:], op=ALU.subtract), reads=["r1", "kf"], writes=["r1"])
                    R.dve(lambda e: e.tensor_single_scalar(out=kf[:], in_=r1[:], scalar=-0.5, op=ALU.is_lt), reads=["r1"], writes=["kf"])
                    R.dve(lambda e: e.tensor_tensor(out=r1[:], in0=r1[:], in1=kf[:], op=ALU.add), reads=["r1", "kf"], writes=["r1"])
                    R.act(lambda e: e.activation(out=r1[:], in_=r1[:], func=AF.Sin, scale=2 * math.pi), reads=["r1"], writes=["r1"])
                    if which == 1:
                        R.dve(lambda e: e.tensor_scalar(out=r1[:], in0=r1[:], scalar1=ropec[:, 1:2], scalar2=None, op0=ALU.mult),
                              reads=["r1", "ropec"], writes=["r1"])
                    R.dma("sp", lambda e, b=b, which=which: e.dma_start(out=ropebuf[b, which, :, :], in_=r1[:]),
                          reads=["r1"], writes=["ropebuf%d" % b])
        R.barrier()
        if "mods" in tap_t:
            fin.append(R.dma("sp", lambda e: e.dma_start(out=tap_t["mods"][:, :, :], in_=modbuf[:, :, :]), reads=["modbuf"], writes=["tap_mods"]))
        if "rope" in tap_t:
            fin.append(R.dma("sp", lambda e: e.dma_start(out=tap_t["rope"][:, :, :], in_=ropebuf[0, :, :, :]), reads=["ropebuf0"], writes=["tap_rope"]))

        if stop == "s0":
            R.emit(final_wait_ops=fin)
            return nc

        for l in range(NL):
            x_src = x_in if l == 0 else xbuf
            for b in (range(NB) if not skip_mixer else ()):
                with contextlib.ExitStack() as stm:
                    mixer(nc, R, mk(stm), locals())
                R.barrier()
            if stop == "mix":
                break
            moe(nc, R, mk, locals())
        R.emit(final_wait_ops=fin)
    return nc


def mixer(nc, R, mkp, env):
    sbuf, psum = mkp
    mk = env["mk"]
    l, b = env["l"], env["b"]
    x_src, xbuf, modbuf = env["x_src"], env["xbuf"], env["modbuf"]
    ident_f, ident_b, cst, tap_t, fin = env["ident_f"], env["ident_b"], env["cst"], env["tap_t"], env["fin"]
    w_in, w_out, gn_w, conv_w = env["w_in"], env["w_out"], env["gn_w"], env["conv_w"]
    cmp_pos, cmp_w1, cmp_w2, ln_g, ln_b, ropebuf = env["cmp_pos"], env["cmp_w1"], env["cmp_w2"], env["ln_g"], env["ln_b"], env["ropebuf"]
    NB = env["NB"]

    pA = Rot([("pA%d" % i, psum("pA%d" % i, [128, 512])) for i in range(2)])
    pT = psum("pT", [128, 1024], BF16)
    pO = Rot([("pO%d" % i, psum("pO%d" % i, [128, 512])) for i in range(4)])
    pX = psum("pX", [128, 512])

    def bcast_mod(sbx, dname, which, plus1):
        dst = sbx(dname, [128, D])
        R.dma("sp", lambda e: e.dma_start(out=dst[:], in_=modbuf[b, l:l + 1, which * D:(which + 1) * D].broadcast_to([128, D])),
              reads=["modbuf"], writes=[dname])
        if plus1:
            R.pool(lambda e: e.tensor_scalar(out=dst[:], in0=dst[:], scalar1=1.0, scalar2=None, op0=ALU.add), reads=[dname], writes=[dname])
        return dst

    sh1 = bcast_mod(sbuf, "sh1", 0, False)
    sc1 = bcast_mod(sbuf, "sc1", 1, True)

    hT = sbuf("hT", [128, 8, S], BF16)
    yT = sbuf("yT", [128, 8, S], BF16)
    xc_r = Rot([("xc%d" % i, sbuf("xc%d" % i, [128, D])) for i in range(2)])
    hc_r = Rot([("hc%d" % i, sbuf("hc%d" % i, [128, D], BF16)) for i in range(2)])
    st_r = Rot([("st%d" % i, sbuf("st%d" % i, [128, 16])) for i in range(2)])

    def layer_norm_chunk(xcn, xc, stn, stt, xnn, xn):
        R.dve(lambda e: e.bn_stats(out=stt[:, 0:6], in_=xc[:, 0:512]), reads=[xcn], writes=[stn])
        R.dve(lambda e: e.bn_stats(out=stt[:, 6:12], in_=xc[:, 512:1024]), reads=[xcn], writes=[stn])
        R.dve(lambda e: e.bn_aggr(out=stt[:, 12:14], in_=stt[:, 0:12]), reads=[stn], writes=[stn])
        R.act(lambda e: e.activation(out=stt[:, 14:15], in_=stt[:, 13:14], func=AF.Sqrt, bias=LN_EPS, scale=1.0), reads=[stn], writes=[stn])
        R.dve(lambda e: e.reciprocal(out=stt[:, 15:16], in_=stt[:, 14:15]), reads=[stn], writes=[stn])
        R.dve(lambda e: e.tensor_scalar(out=xn[:], in0=xc[:], scalar1=stt[:, 12:13], scalar2=stt[:, 15:16], op0=ALU.subtract, op1=ALU.mult),
              reads=[xcn, stn], writes=[xnn])

    for tc in range(NT):
        xcn, xc = xc_r.next()
        xnn, xn = xcn, xc
        hcn, hc = hc_r.next()
        stn, stt = st_r.next()
        R.dma("sp", lambda e, xc=xc, tc=tc: e.dma_start(out=xc[:], in_=x_src[b, tc * 128:(tc + 1) * 128, :]), reads=["xb%d_%d" % (b, tc)], writes=[xcn])
        layer_norm_chunk(xcn, xc, stn, stt, xnn, xn)
        R.pool(lambda e, xn=xn: e.tensor_tensor(out=xn[:], in0=xn[:], in1=sc1[:], op=ALU.mult), reads=[xnn, "sc1"], writes=[xnn])
        R.dve(lambda e, xn=xn, hc=hc: e.tensor_tensor(out=hc[:], in0=xn[:], in1=sh1[:], op=ALU.add), reads=[xnn, "sh1"], writes=[hcn])
        for kc in range(8):
            R.pe(lambda e, hc=hc, kc=kc: e.transpose(out=pT[:, kc * 128:(kc + 1) * 128], in_=hc[:, kc * 128:(kc + 1) * 128], identity=ident_b[:]),
                 reads=[hcn, "ident_b"], writes=["pT"])
        R.act(lambda e, tc=tc: e.copy(out=hT[:, :, tc * 128:(tc + 1) * 128], in_=pT[:].rearrange("p (k t) -> p k t", k=8)),
              reads=["pT"], writes=["hT"])
    if "hT" in tap_t and b == 0 and l == 0:
        fin.append(R.dma("sp", lambda e: e.dma_start(out=tap_t["hT"][:, :, :], in_=hT[:]), reads=["hT"], writes=["tap_hT"]))

    wr = Rot([("W%d" % i, sbuf("W%d" % i, [128, 8, 128], BF16)) for i in range(4)])

    def load_w(pieces):
        wn, wb = wr.next()
        for (do, sc, wd) in pieces:
            R.dma("pool", lambda e, wb=wb, do=do, sc=sc, wd=wd: e.dma_start(
                out=wb[:, :, do:do + wd], in_=w_in[l, :, sc:sc + wd].rearrange("(k p) n -> p k n", p=128)), writes=[wn])
        return wn, wb

    def perm_pieces(c0):
        return [(0, c0 + 32, 32), (32, c0, 32), (64, c0 + 96, 32), (96, c0 + 64, 32)]

    def dup_pieces(c0):
        return [(0, c0, 64), (64, c0, 64)]

    def dup_perm_pieces(c0):
        return [(0, c0 + 32, 32), (32, c0, 32), (64, c0 + 32, 32), (96, c0, 32)]

    def proj_fm(wn, wb, tg):
        pn, pt = pA.next()
        for kc in range(8):
            R.pe(lambda e, kc=kc: e.matmul(pt[:, :], lhsT=wb[:, kc, :], rhs=hT[:, kc, tg * 512:(tg + 1) * 512], start=(kc == 0), stop=(kc == 7)),
                 reads=[wn, "hT"], writes=[pn])
        return pn, pt

    def proj_plain(pieces, dst, dname, dsl):
        wn, wb = load_w(pieces)
        for tg in range(4):
            pn, pt = proj_fm(wn, wb, tg)
            R.act(lambda e, pt=pt, tg=tg: e.copy(out=dst[:, dsl, tg * 512:(tg + 1) * 512] if dsl is not None else dst[:, tg * 512:(tg + 1) * 512], in_=pt[:, :]),
                  reads=[pn], writes=[dname])

    with contextlib.ExitStack() as sc_:
        sb1, _ = mk(sc_)
        fm = [sb1("fm%d" % i, [128, 2, S], BF16) for i in range(3)]
        tmpA = sb1("tmpA", [128, S])
        tmpB = sb1("tmpB", [128, S])
        cw = sb1("cw", [128, 2, 3])
        for ch_ in range(2):
            for k_ in range(3):
                R.dma("sp", lambda e, ch_=ch_, k_=k_: e.dma_start(out=cw[:, ch_, k_:k_ + 1], in_=conv_w[l, k_, ch_ * 128:(ch_ + 1) * 128].rearrange("(p o) -> p o", o=1)), writes=["cw"])
        for i, c0 in enumerate((C_CB, C_CC, C_CH)):
            for ch in range(2):
                proj_plain([(0, c0 + ch * 128, 128)], fm[i], "fm%d" % i, ch)
        for ch in range(2):
            R.pool(lambda e, ch=ch: e.tensor_tensor(out=tmpA[:], in0=fm[1][:, ch, :], in1=fm[2][:, ch, :], op=ALU.mult),
                   reads=["fm1", "fm2"], writes=["tmpA"])
            R.dve(lambda e, ch=ch: e.tensor_scalar(out=tmpB[:], in0=tmpA[:], scalar1=cw[:, ch, 2:3], scalar2=None, op0=ALU.mult),
                  reads=["tmpA", "cw"], writes=["tmpB"])
            R.dve(lambda e, ch=ch: e.scalar_tensor_tensor(out=tmpB[:, 1:S], in0=tmpA[:, 0:S - 1], scalar=cw[:, ch, 1:2], in1=tmpB[:, 1:S],
                                                          op0=ALU.mult, op1=ALU.add), reads=["tmpA", "tmpB", "cw"], writes=["tmpB"])
            R.dve(lambda e, ch=ch: e.scalar_tensor_tensor(out=tmpB[:, 2:S], in0=tmpA[:, 0:S - 2], scalar=cw[:, ch, 0:1], in1=tmpB[:, 2:S],
                                                          op0=ALU.mult, op1=ALU.add), reads=["tmpA", "tmpB", "cw"], writes=["tmpB"])
            R.pool(lambda e, ch=ch: e.tensor_tensor(out=yT[:, 2 + ch, :], in0=tmpB[:], in1=fm[0][:, ch, :], op=ALU.mult),
                   reads=["tmpB", "fm0"], writes=["yT"])
    R.barrier()

    def load_rope(sbx):
        cosT = sbx("cosT", [128, S])
        sinT = sbx("sinT", [128, S])
        R.dma("sp", lambda e: e.dma_start(out=cosT[:], in_=ropebuf[b, 0, :, :]), reads=["ropebuf%d" % b], writes=["cosT"])
        R.dma("sp", lambda e: e.dma_start(out=sinT[:], in_=ropebuf[b, 1, :, :]), reads=["ropebuf%d" % b], writes=["sinT"])
        rt = Rot([("rt%d" % i, sbx("rt%d" % i, [128, 512])) for i in range(4)])
        return cosT, sinT, rt

    def proj_rope(p_norm, p_perm, dst, dname, dsl, cosT, sinT, rt):
        wn1, wb1 = load_w(p_norm)
        wn2, wb2 = load_w(p_perm)
        for tg in range(4):
            pn1, pt1 = proj_fm(wn1, wb1, tg)
            pn2, pt2 = proj_fm(wn2, wb2, tg)
            t1n, t1 = rt.next()
            t2n, t2 = rt.next()
            R.dve(lambda e, pt1=pt1, t1=t1, tg=tg: e.tensor_tensor(out=t1[:], in0=pt1[:, :], in1=cosT[:, tg * 512:(tg + 1) * 512], op=ALU.mult),
                  reads=[pn1, "cosT"], writes=[t1n])
            R.dve(lambda e, pt2=pt2, t2=t2, tg=tg: e.tensor_tensor(out=t2[:], in0=pt2[:, :], in1=sinT[:, tg * 512:(tg + 1) * 512], op=ALU.mult),
                  reads=[pn2, "sinT"], writes=[t2n])
            R.pool(lambda e, t1=t1, t2=t2, tg=tg: e.tensor_tensor(
                out=dst[:, dsl, tg * 512:(tg + 1) * 512] if dsl is not None else dst[:, tg * 512:(tg + 1) * 512], in0=t1[:], in1=t2[:], op=ALU.add),
                reads=[t1n, t2n], writes=[dname])

    with contextlib.ExitStack() as sc_:
        sb2, _ = mk(sc_)
        cosT, sinT, rt = load_rope(sb2)
        qTr = sb2("qTr", [128, 2, S], BF16)
        kTr = sb2("kTr", [128, 2, S], BF16)
        vrg = sb2("vrg", [128, NT, 512], BF16)
        dect = sb2("dect", [128, 4, TW], BF16)
        gnw = sb2("gnw", [128, 256])
        R.dma("sp", lambda e: e.dma_start(out=dect[:], in_=cst["dect"][:, :, :]), writes=["dect"])
        R.dma("sp", lambda e: e.dma_start(out=gnw[:], in_=gn_w[l:l + 1, :].broadcast_to([128, 256])), writes=["gnw"])
        for ch in range(2):
            proj_rope([(0, C_RQ + ch * 128, 128)], perm_pieces(C_RQ + ch * 128), qTr, "qTr", ch, cosT, sinT, rt)
            proj_rope([(0, C_RK + ch * 128, 128)], perm_pieces(C_RK + ch * 128), kTr, "kTr", ch, cosT, sinT, rt)
        wv = sb2("wv", [128, 8, 512], BF16)
        R.dma("pool", lambda e: e.dma_start(out=wv[:], in_=w_in[l, :, C_RV:C_RV + 512].rearrange("(k p) n -> p k n", p=128)), writes=["wv"])
        for tc in range(NT):
            pn, pt = pA.next()
            for kc in range(8):
                R.pe(lambda e, pt=pt, kc=kc, tc=tc: e.matmul(pt[:, :], lhsT=hT[:, kc, tc * 128:(tc + 1) * 128], rhs=wv[:, kc, :],
                                                             start=(kc == 0), stop=(kc == 7)), reads=["hT", "wv"], writes=[pn])
            R.act(lambda e, pt=pt, tc=tc: e.copy(out=vrg[:, tc, 0:256], in_=pt[:, 0:256]), reads=[pn], writes=["vrg"])
            R.act(lambda e, pt=pt, tc=tc: e.activation(out=vrg[:, tc, 256:512], in_=pt[:, 256:512], func=AF.Silu), reads=[pn], writes=["vrg"])
        smr = Rot([("sm%d" % i, sb2("sm%d" % i, [128, 512], BF16)) for i in range(3)])
        gA = sb2("gA", [128, 1024])
        gB = sb2("gB", [128, 1024])
        gs = sb2("gs", [128, 64])
        yr = sb2("yr", [128, 4, 256], BF16)
        for qg in range(4):
            raccn = ["pO0", "pO1"]
            racc = [pO.items[0][1], pO.items[1][1]]
            first = [True, True]
            for h in range(4):
                ch, ro = h // 2, (h % 2) * 64
                for kc in range(4 * qg + 4):
                    pn, pt = pA.next()
                    R.pe(lambda e, pt=pt, kc=kc, ch=ch, ro=ro, qg=qg: e.matmul(
                        pt[:, :], lhsT=kTr[ro:ro + 64, ch, kc * 128:(kc + 1) * 128], rhs=qTr[ro:ro + 64, ch, qg * 512:(qg + 1) * 512],
                        start=True, stop=True), reads=["kTr", "qTr"], writes=[pn])
                    smn, sm = smr.next()
                    off = 384 + qg * 512 - kc * 128
                    R.dve(lambda e, pt=pt, sm=sm, h=h, off=off: e.tensor_tensor(out=sm[:], in0=pt[:, :], in1=dect[:, h, off:off + 512], op=ALU.mult),
                          reads=[pn, "dect"], writes=[smn])
                    for j in range(4):
                        qc = 4 * qg + j
                        if kc > qc:
                            continue
                        bk = j // 2
                        st_flag = first[bk]
                        first[bk] = False
                        R.pe(lambda e, sm=sm, j=j, h=h, kc=kc, bk=bk, st_flag=st_flag: e.matmul(
                            racc[bk][:, (j % 2) * 256 + h * 64:(j % 2) * 256 + h * 64 + 64], lhsT=sm[:, j * 128:(j + 1) * 128],
                            rhs=vrg[:, kc, h * 64:h * 64 + 64], start=st_flag, stop=False, skip_group_check=True),
                            reads=[smn, "vrg"], writes=[raccn[bk]])
            for bk in range(2):
                R.act(lambda e, bk=bk: e.copy(out=gA[:, bk * 512:(bk + 1) * 512], in_=racc[bk][:, :]), reads=[raccn[bk]], writes=["gA"])
            g3 = gA[:].rearrange("p (a d) -> p a d", d=64)
            b3 = gB[:].rearrange("p (a d) -> p a d", d=64)
            R.dve(lambda e: e.tensor_reduce(out=gs[:, 0:16], in_=g3, axis=AX.X, op=ALU.add), reads=["gA"], writes=["gs"])
            R.dve(lambda e: e.tensor_scalar(out=gs[:, 0:16], in0=gs[:, 0:16], scalar1=1.0 / 64, scalar2=None, op0=ALU.mult), reads=["gs"], writes=["gs"])
            R.dve(lambda e: e.tensor_tensor(out=g3, in0=g3, in1=gs[:, 0:16].unsqueeze(2).broadcast_to([128, 16, 64]), op=ALU.subtract),
                  reads=["gA", "gs"], writes=["gA"])
            R.pool(lambda e: e.tensor_tensor(out=gB[:], in0=gA[:], in1=gA[:], op=ALU.mult), reads=["gA"], writes=["gB"])
            R.dve(lambda e: e.tensor_reduce(out=gs[:, 16:32], in_=b3, axis=AX.X, op=ALU.add), reads=["gB"], writes=["gs"])
            R.act(lambda e: e.activation(out=gs[:, 32:48], in_=gs[:, 16:32], func=AF.Sqrt, bias=LN_EPS, scale=1.0 / 64), reads=["gs"], writes=["gs"])
            R.dve(lambda e: e.reciprocal(out=gs[:, 48:64], in_=gs[:, 32:48]), reads=["gs"], writes=["gs"])
            R.dve(lambda e: e.tensor_tensor(out=g3, in0=g3, in1=gs[:, 48:64].unsqueeze(2).broadcast_to([128, 16, 64]), op=ALU.mult),
                  reads=["gA", "gs"], writes=["gA"])
            g4 = gA[:].rearrange("p (j c) -> p j c", c=256)
            R.pool(lambda e: e.tensor_tensor(out=g4, in0=g4, in1=gnw[:].unsqueeze(1).broadcast_to([128, 4, 256]), op=ALU.mult),
                   reads=["gA", "gnw"], writes=["gA"])
            R.dve(lambda e, qg=qg: e.tensor_tensor(out=yr[:], in0=g4, in1=vrg[:, 4 * qg:4 * qg + 4, 256:512], op=ALU.mult),
                  reads=["gA", "vrg"], writes=["yr"])
            for j in range(4):
                for ch in range(2):
                    R.pe(lambda e, j=j, ch=ch: e.transpose(out=pT[:, (j * 2 + ch) * 128:(j * 2 + ch + 1) * 128], in_=yr[:, j, ch * 128:(ch + 1) * 128],
                                                         identity=ident_b[:]), reads=["yr", "ident_b"], writes=["pT"])
            R.act(lambda e, qg=qg: e.copy(out=yT[:, 0:2, qg * 512:(qg + 1) * 512].rearrange("p c (j t) -> p j c t", j=4),
                                          in_=pT[:].rearrange("p (j c t) -> p j c t", j=4, c=2)), reads=["pT"], writes=["yT"])
    R.barrier()

    nsa_stage = env.get("nsa_stage", 2)
    nsa_part = env.get("nsa_part", 9)
    nsa_hks = env.get("nsa_hks", (0, 1))
    nsa_qcs = env.get("nsa_qcs", tuple(range(4)))
    with contextlib.ExitStack() as sc_:
        sb3, _ = mk(sc_)
        cosT, sinT, rt = load_rope(sb3)
        kcT = sb3("kcT", [128, S], BF16)
        vcT = sb3("vcT", [128, S], BF16)
        gates = sb3("gates", [128, NT, 24])
        v2 = sb3("v2", [128, NT, 4, 128], BF16)
        qT = sb3("qT", [128, 2, S], BF16)
        qTr2 = sb3("qTr2", [128, 2, S], BF16)
        ksT = sb3("ksT", [128, S], BF16)
        kwT = sb3("kwT", [128, S], BF16)
        w2k = sb3("w2k", [128, 128], BF16)
        w2v = sb3("w2v", [128, 64], BF16)
        pbias = sb3("pbias", [128, 2])
        kcd = [sb3("kcd%d" % i, [128, 128], BF16) for i in range(2)]
        vca = sb3("vca", [128, 2, 128], BF16)
        ovl = sb3("ovl", [128, 32], BF16)
        scC = contextlib.ExitStack()
        sbC, _ = mk(scC)
        w1d = [sbC("w1d%d" % i, [128, 32, 128], BF16) for i in range(2)]
        posr = sbC("posr", [32, 2, 64])
        posT = sbC("posT", [64, 2, 32], BF16)
        R.dma("sp", lambda e: e.dma_start(out=ovl[:], in_=cst["overlap"][:, :]), writes=["ovl"])
        for kind in range(2):
            for cp in range(2):
                R.dma("pool", lambda e, kind=kind, cp=cp: e.dma_start(
                    out=w1d[kind][cp * 64:(cp + 1) * 64, :, :], in_=cmp_w1[l, kind, :, :].rearrange("(l d) n -> d l n", d=64)), writes=["w1d%d" % kind])
            R.dma("sp", lambda e, kind=kind: e.dma_start(out=posr[:, kind, :], in_=cmp_pos[l, kind, :, :]), writes=["posr"])
        R.dma("pool", lambda e: e.dma_start(out=w2k[:, 0:64], in_=cmp_w2[l, 0, :, :]), writes=["w2k"])
        R.dma("pool", lambda e: e.dma_start(out=w2k[:, 64:128], in_=cmp_w2[l, 0, :, :]), writes=["w2k"])
        R.dma("pool", lambda e: e.dma_start(out=w2v[:], in_=cmp_w2[l, 1, :, :]), writes=["w2v"])
        proj_plain([(0, C_KC, 128)], kcT, "kcT", None)
        proj_plain([(0, C_VC, 128)], vcT, "vcT", None)
        wt = sbC("wt", [128, 8, 408], BF16)
        R.dma("pool", lambda e: e.dma_start(out=wt[:], in_=w_in[l, :, C_VS:C_VS + 408].rearrange("(k p) n -> p k n", p=128)), writes=["wt"])
        R.pool(lambda e: e.memset(v2[:].rearrange("p a b c -> p (a b) c")[:, :, 64:128], 1.0), writes=["v2"])
        for tc in range(NT):
            pn, pt = pA.next()
            for kc in range(8):
                R.pe(lambda e, pt=pt, kc=kc, tc=tc: e.matmul(pt[:, 0:408], lhsT=hT[:, kc, tc * 128:(tc + 1) * 128], rhs=wt[:, kc, :],
                                                             start=(kc == 0), stop=(kc == 7)), reads=["hT", "wt"], writes=[pn])
            R.act(lambda e, pt=pt, tc=tc: e.copy(out=v2[:, tc, 0:2, 0:64], in_=pt[:, 0:128].rearrange("p (h d) -> p h d", h=2)), reads=[pn], writes=["v2"])
            R.act(lambda e, pt=pt, tc=tc: e.copy(out=v2[:, tc, 2:4, 0:64], in_=pt[:, 256:384].rearrange("p (h d) -> p h d", h=2)), reads=[pn], writes=["v2"])
            R.act(lambda e, pt=pt, tc=tc: e.activation(out=gates[:, tc, :], in_=pt[:, 384:408], func=AF.Sigmoid), reads=[pn], writes=["gates"])
        for kind in range(2):
            R.pe(lambda e, kind=kind: e.transpose(out=pX[0:64, kind * 32:(kind + 1) * 32], in_=posr[:, kind, :], identity=ident_f[0:32, 0:32]),
                 reads=["posr", "ident_f"], writes=["pX"])
        R.dve(lambda e: e.tensor_copy(out=posT[:].rearrange("p k l -> p (k l)"), in_=pX[0:64, 0:64]), reads=["pX"], writes=["posT"])
        for kind in range(2):
            for li in range(32):
                R.pe(lambda e, kind=kind, li=li: e.matmul(pX[:, 64 + kind:65 + kind], lhsT=w1d[kind][0:64, li, :], rhs=posT[:, kind, li:li + 1],
                                                          start=(li == 0 and kind == 0), stop=(li == 31), skip_group_check=True),
                     reads=["w1d%d" % kind, "posT"], writes=["pX"])
        R.dve(lambda e: e.tensor_copy(out=pbias[:], in_=pX[:, 64:66]), reads=["pX"], writes=["pbias"])
        zt = sbC("zt", [128, 128])
        z2 = sbC("z2", [128, 128])
        blk = sbC("blk", [128, 32, 127], BF16)
        gT = sbC("gT", [128, 128], BF16)
        R.pool(lambda e: e.memset(vca[:], 0.0), writes=["vca"])
        for i_ in range(2):
            R.pool(lambda e, i_=i_: e.memset(kcd[i_][:], 0.0), writes=["kcd"])
        for kind in range(2):
            src = kcT if kind == 0 else vcT
            srcn = "kcT" if kind == 0 else "vcT"
            for li in range(32):
                R.dve(lambda e, li=li, src=src: e.tensor_copy(out=blk[:, li, :], in_=src[:, li:li + 2017:16]), reads=[srcn], writes=["blk"])
            for hk in range(2):
                ro = hk * 64
                pn, pt = pA.next()
                for li in range(32):
                    R.pe(lambda e, pt=pt, kind=kind, li=li, ro=ro: e.matmul(
                        pt[:, 0:127], lhsT=w1d[kind][ro:ro + 64, li, :], rhs=blk[ro:ro + 64, li, :], start=(li == 0), stop=(li == 31)),
                        reads=["w1d%d" % kind, "blk"], writes=[pn])
                R.act(lambda e, pt=pt, kind=kind: e.activation(out=zt[:, 0:127], in_=pt[:, 0:127], func=AF.Identity, bias=pbias[:, kind:kind + 1], scale=1.0),
                      reads=[pn, "pbias"], writes=["zt"])
                R.dve(lambda e: e.tensor_tensor(out=z2[:, 0:127], in0=zt[:, 0:127], in1=zt[:, 0:127], op=ALU.mult), reads=["zt"], writes=["z2"])
                R.dve(lambda e: e.tensor_scalar(out=z2[:, 0:127], in0=z2[:, 0:127], scalar1=0.044715, scalar2=1.0, op0=ALU.mult, op1=ALU.add),
                      reads=["z2"], writes=["z2"])
                R.dve(lambda e: e.tensor_tensor(out=z2[:, 0:127], in0=z2[:, 0:127], in1=zt[:, 0:127], op=ALU.mult), reads=["z2", "zt"], writes=["z2"])
                R.act(lambda e: e.activation(out=z2[:, 0:127], in_=z2[:, 0:127], func=AF.Sigmoid, scale=1.5957691216), reads=["z2"], writes=["z2"])
                R.dve(lambda e: e.tensor_tensor(out=gT[:, 0:127], in0=z2[:, 0:127], in1=zt[:, 0:127], op=ALU.mult), reads=["z2", "zt"], writes=["gT"])
                if kind == 0:
                    R.pe(lambda e: e.matmul(pX[:, 128:255], lhsT=w2k[:, :], rhs=gT[:, 0:127], start=True, stop=True), reads=["w2k", "gT"], writes=["pX"])
                    R.act(lambda e, hk=hk: e.copy(out=kcd[hk][:, 0:127], in_=pX[:, 128:255]), reads=["pX"], writes=["kcd"])
                else:
                    R.pe(lambda e: e.matmul(pX[0:127, 256:320], lhsT=gT[:, 0:127], rhs=w2v[:, :], start=True, stop=True), reads=["w2v", "gT"], writes=["pX"])
                    R.act(lambda e, hk=hk: e.copy(out=vca[0:127, hk, 0:64], in_=pX[0:127, 256:320]), reads=["pX"], writes=["vca"])
        for hk in range(2):
            R.pool(lambda e, hk=hk: e.memset(vca[:, hk, 64:96], 1.0), reads=[], writes=["vca"])
            R.pool(lambda e, hk=hk: e.tensor_copy(out=vca[:, hk, 96:128], in_=ovl[:]), reads=["ovl"], writes=["vca"])

        if "kcd" in tap_t and b == 0 and l == 0:
            for i_ in range(2):
                fin.append(R.dma("sp", lambda e, i_=i_: e.dma_start(out=tap_t["kcd"][:, i_, :], in_=kcd[i_][:]), reads=["kcd"], writes=["tap_kcd%d" % i_]))
            fin.append(R.dma("sp", lambda e: e.dma_start(out=tap_t["vca"][:, :, :], in_=vca[:]), reads=["vca"], writes=["tap_vca"]))
        R.barrier()
        scC.close()
        cbc = sb3("cbc", [128, S], BF16)
        forced = sb3("forced", [128, NT, 32])
        Eexp = sb3("Eexp", [128, NT, 128], BF16)
        tmask = sb3("tmask", [128, 2, TW], BF16)
        R.dma("sp", lambda e: e.dma_start(out=cbc[:], in_=cst["cb_cmp"][:, :]), writes=["cbc"])
        R.dma("sp", lambda e: e.dma_start(out=forced[:], in_=cst["forced"][:, :, :]), writes=["forced"])
        R.dma("sp", lambda e: e.dma_start(out=Eexp[:], in_=cst["Eexp"][:, :, :]), writes=["Eexp"])
        R.dma("sp", lambda e: e.dma_start(out=tmask[:], in_=cst["tmask"][:, :, :]), writes=["tmask"])
        ptr = Rot([("PT%d" % i, sb3("PT%d" % i, [128, 512], BF16)) for i in range(3)])
        mk_r = Rot([("mk%d" % i, sb3("mk%d" % i, [128, 512], BF16)) for i in range(2)])
        selq = sb3("selq", [128, 4, 128], BF16)
        selT = sb3("selT", [128, 512], BF16)
        on = sb3("on", [128, 4, 4, 64])
        tmpo = sb3("tmpo", [128, 4, 64])
        impa = sb3("impa", [128, 4, 32])
        tmpi = sb3("tmpi", [128, 4, 32])
        nst = sb3("nst", [128, 64])
        ynb = sb3("ynb", [128, 4, 256], BF16)
        R.pool(lambda e: e.memset(selq[:], 0.0), writes=["selq"])

        def smm(pn, pt, Ktile, kn, k0, Qsrc, qn, ro, ch, qg):
            R.pe(lambda e: e.matmul(pt[:, :], lhsT=Ktile[ro:ro + 64, k0:k0 + 128], rhs=Qsrc[ro:ro + 64, ch, qg * 512:(qg + 1) * 512],
                                    start=True, stop=True), reads=[kn, qn], writes=[pn])

        def finish(an, acc, g, qg, hk, br, first):
            a3 = acc[:, :].rearrange("p (j c) -> p j c", j=4)
            R.dve(lambda e: e.tensor_scalar(out=nst[:, 0:4], in0=a3[:, :, 64], scalar1=1e-30, scalar2=None, op0=ALU.max), reads=[an], writes=["nst"])
            R.dve(lambda e: e.reciprocal(out=nst[:, 0:4], in_=nst[:, 0:4]), reads=["nst"], writes=["nst"])
            R.dve(lambda e: e.tensor_tensor(out=nst[:, 4:8], in0=nst[:, 0:4], in1=gates[:, 4 * qg:4 * qg + 4, hk * 12 + g * 3 + br], op=ALU.mult),
                  reads=["nst", "gates"], writes=["nst"])
            dst = on[:, :, g, :]
            if first:
                R.dve(lambda e: e.tensor_tensor(out=dst, in0=a3[:, :, 0:64], in1=nst[:, 4:8].unsqueeze(2).broadcast_to([128, 4, 64]), op=ALU.mult),
                      reads=[an, "nst"], writes=["on"])
            else:
                R.dve(lambda e: e.tensor_tensor(out=tmpo[:], in0=a3[:, :, 0:64], in1=nst[:, 4:8].unsqueeze(2).broadcast_to([128, 4, 64]), op=ALU.mult),
                      reads=[an, "nst"], writes=["tmpo"])
                R.pool(lambda e: e.tensor_tensor(out=dst, in0=dst, in1=tmpo[:], op=ALU.add), reads=["on", "tmpo"], writes=["on"])

        for hk in (nsa_hks if nsa_stage >= 2 else ()):
            for ch in range(2):
                c0 = C_NQ + hk * 256 + ch * 128
                proj_plain([(0, c0, 128)], qT, "qT", ch)
                proj_rope([(0, c0, 128)], perm_pieces(c0), qTr2, "qTr2", ch, cosT, sinT, rt)
            proj_rope(dup_pieces(C_KS + hk * 64), dup_perm_pieces(C_KS + hk * 64), ksT, "ksT", None, cosT, sinT, rt)
            proj_rope(dup_pieces(C_KW + hk * 64), dup_perm_pieces(C_KW + hk * 64), kwT, "kwT", None, cosT, sinT, rt)
            for qg in (nsa_qcs if nsa_part >= 1 else ()):
                for g in range(4):
                    ch, ro = g // 2, (g % 2) * 64
                    pn, pt = pA.next()
                    smm(pn, pt, kcd[hk], "kcd", 0, qT, "qT", ro, ch, qg)
                    ptn, PT = ptr.next()
                    R.act(lambda e, pt=pt, PT=PT: e.activation(out=PT[:, :], in_=pt[:, :], func=AF.Exp, scale=SCALE), reads=[pn], writes=[ptn])
                    R.pool(lambda e, PT=PT, qg=qg: e.tensor_tensor(out=PT[:, :], in0=PT[:, :], in1=cbc[:, qg * 512:(qg + 1) * 512], op=ALU.mult),
                           reads=[ptn, "cbc"], writes=[ptn])
                    an, acc = pO.next()
                    for j in range(4):
                        R.pe(lambda e, j=j, acc=acc, PT=PT, hk=hk: e.matmul(acc[:, j * 128:(j + 1) * 128], lhsT=PT[:, j * 128:(j + 1) * 128], rhs=vca[:, hk, :],
                                                                          start=(j == 0), stop=False, skip_group_check=True), reads=[ptn, "vca"], writes=[an])
                    if nsa_part < 2:
                        continue
                    a3 = acc[:, :].rearrange("p (j c) -> p j c", j=4)
                    finish(an, acc, g, qg, hk, 0, True)
                    if g == 0:
                        R.dve(lambda e, a3=a3: e.tensor_tensor(out=impa[:], in0=a3[:, :, 96:128], in1=nst[:, 0:4].unsqueeze(2).broadcast_to([128, 4, 32]), op=ALU.mult),
                              reads=[an, "nst"], writes=["impa"])
                    else:
                        R.dve(lambda e, a3=a3: e.tensor_tensor(out=tmpi[:], in0=a3[:, :, 96:128], in1=nst[:, 0:4].unsqueeze(2).broadcast_to([128, 4, 32]), op=ALU.mult),
                              reads=[an, "nst"], writes=["tmpi"])
                        R.pool(lambda e: e.tensor_tensor(out=impa[:], in0=impa[:], in1=tmpi[:], op=ALU.add), reads=["impa", "tmpi"], writes=["impa"])
                if nsa_part < 3:
                    continue
                R.dve(lambda e, qg=qg: e.tensor_tensor(out=impa[:], in0=impa[:], in1=forced[:, 4 * qg:4 * qg + 4, :], op=ALU.max), reads=["impa", "forced"], writes=["impa"])
                for j in range(4):
                    R.dve(lambda e, j=j: e.max(out=nst[:, 16 + 8 * j:24 + 8 * j], in_=impa[:, j, :]), reads=["impa"], writes=["nst"])
                    R.dve(lambda e, j=j: e.tensor_scalar(out=selq[:, j, 0:32], in0=impa[:, j, :], scalar1=nst[:, 23 + 8 * j:24 + 8 * j], scalar2=None, op0=ALU.is_ge),
                          reads=["impa", "nst"], writes=["selq"])
                for j in range(4):
                    R.pe(lambda e, j=j: e.transpose(out=pT[:, j * 128:(j + 1) * 128], in_=selq[:, j, :], identity=ident_b[:]), reads=["selq", "ident_b"], writes=["pT"])
                R.act(lambda e: e.copy(out=selT[:], in_=pT[:, 0:512]), reads=["pT"], writes=["selT"])
                if nsa_part < 4:
                    continue
                for br in ((1, 2) if nsa_part >= 5 else (1,)):
                    Ksrc, ksn = (ksT, "ksT") if br == 1 else (kwT, "kwT")
                    kcs = list(range(0, 4 * qg + 4)) if br == 1 else list(range(max(0, 4 * qg - 4), 4 * qg + 4))
                    accs = [pO.next() for _ in range(4)]
                    firsts = [True] * 4
                    for kc in kcs:
                        off = 384 + qg * 512 - kc * 128
                        mkn, mkt = mk_r.next()
                        if br == 1:
                            R.pe(lambda e, kc=kc: e.matmul(pX[:, :], lhsT=Eexp[:, kc, :], rhs=selT[:, :], start=True, stop=True), reads=["Eexp", "selT"], writes=["pX"])
                            R.dve(lambda e, mkt=mkt, off=off: e.tensor_tensor(out=mkt[:], in0=pX[:, :], in1=tmask[:, 0, off:off + 512], op=ALU.mult),
                                  reads=["pX", "tmask"], writes=[mkn])
                        for g in range(4):
                            ch, ro = g // 2, (g % 2) * 64
                            pn, pt = pA.next()
                            smm(pn, pt, Ksrc, ksn, kc * 128, qTr2, "qTr2", ro, ch, qg)
                            ptn, PT = ptr.next()
                            R.act(lambda e, pt=pt, PT=PT: e.activation(out=PT[:, :], in_=pt[:, :], func=AF.Exp, scale=SCALE), reads=[pn], writes=[ptn])
                            if br == 1:
                                R.pool(lambda e, PT=PT, mkt=mkt: e.tensor_tensor(out=PT[:, :], in0=PT[:, :], in1=mkt[:], op=ALU.mult), reads=[ptn, mkn], writes=[ptn])
                            else:
                                R.pool(lambda e, PT=PT, off=off: e.tensor_tensor(out=PT[:, :], in0=PT[:, :], in1=tmask[:, 1, off:off + 512], op=ALU.mult),
                                       reads=[ptn, "tmask"], writes=[ptn])
                            an, acc = accs[g]
                            for j in range(4):
                                qc = 4 * qg + j
                                if kc > qc or (br == 2 and kc < qc - 4):
                                    continue
                                stf = firsts[g]
                                firsts[g] = False
                                R.pe(lambda e, j=j, acc=acc, PT=PT, kc=kc, br=br, hk=hk, stf=stf: e.matmul(
                                    acc[:, j * 128:(j + 1) * 128], lhsT=PT[:, j * 128:(j + 1) * 128], rhs=v2[:, kc, (br - 1) * 2 + hk, :],
                                    start=stf, stop=False, skip_group_check=True), reads=[ptn, "v2"], writes=[an])
                    for g in range(4):
                        an, acc = accs[g]
                        finish(an, acc, g, qg, hk, br, False)
                R.act(lambda e: e.copy(out=ynb[:].rearrange("p j c -> p (j c)"), in_=on[:].rearrange("p j g d -> p (j g d)")), reads=["on"], writes=["ynb"])
                for j in range(4):
                    for ch in range(2):
                        R.pe(lambda e, j=j, ch=ch: e.transpose(out=pT[:, (j * 2 + ch) * 128:(j * 2 + ch + 1) * 128], in_=ynb[:, j, ch * 128:(ch + 1) * 128],
                                                             identity=ident_b[:]), reads=["ynb", "ident_b"], writes=["pT"])
                R.act(lambda e, qg=qg, hk=hk: e.copy(out=yT[:, 4 + 2 * hk:6 + 2 * hk, qg * 512:(qg + 1) * 512].rearrange("p c (j t) -> p j c t", j=4),
                                                     in_=pT[:].rearrange("p (j c t) -> p j c t", j=4, c=2)), reads=["pT"], writes=["yT"])
    R.barrier()
    if "yT" in tap_t and b == 0 and l == 0:
        fin.append(R.dma("sp", lambda e: e.dma_start(out=tap_t["yT"][:, :, :], in_=yT[:]), reads=["yT"], writes=["tap_yT2"]))

    with contextlib.ExitStack() as sc_:
        sb4, _ = mk(sc_)
        g1p = bcast_mod(sb4, "g1p", 2, True)
        lng = sb4("lng", [128, D])
        lnb = sb4("lnb", [128, D])
        R.dma("sp", lambda e: e.dma_start(out=lng[:], in_=ln_g[l, 0:1, :].broadcast_to([128, D])), writes=["lng"])
        R.dma("sp", lambda e: e.dma_start(out=lnb[:], in_=ln_b[l, 0:1, :].broadcast_to([128, D])), writes=["lnb"])
        wo = sb4("wo", [128, 8, D], BF16)
        R.dma("pool", lambda e: e.dma_start(out=wo[:], in_=w_out[l, :, :].rearrange("(k p) n -> p k n", p=128)), writes=["wo"])
        mt_r = Rot([("mt%d" % i, sb4("mt%d" % i, [128, D])) for i in range(2)])
        for tc in range(NT):
            xcn, xc = xc_r.next()
            stn, stt = st_r.next()
            mtn, mt = mt_r.next()
            R.dma("sp", lambda e, xc=xc, tc=tc: e.dma_start(out=xc[:], in_=x_src[b, tc * 128:(tc + 1) * 128, :]), reads=["xb%d_%d" % (b, tc)], writes=[xcn])
            for hf in range(2):
                pn, pt = pA.next()
                for kc in range(8):
                    R.pe(lambda e, pt=pt, kc=kc, tc=tc, hf=hf: e.matmul(pt[:, :], lhsT=yT[:, kc, tc * 128:(tc + 1) * 128], rhs=wo[:, kc, hf * 512:(hf + 1) * 512],
                                                                         start=(kc == 0), stop=(kc == 7)), reads=["yT", "wo"], writes=[pn])
                R.dve(lambda e, pt=pt, mt=mt, hf=hf: e.tensor_tensor(out=mt[:, hf * 512:(hf + 1) * 512], in0=pt[:, :], in1=g1p[:, hf * 512:(hf + 1) * 512], op=ALU.mult),
                      reads=[pn, "g1p"], writes=[mtn])
            R.dve(lambda e, xc=xc, mt=mt: e.scalar_tensor_tensor(out=xc[:], in0=xc[:], scalar=ALPHA, in1=mt[:], op0=ALU.mult, op1=ALU.add),
                  reads=[xcn, mtn], writes=[xcn])
            layer_norm_chunk(xcn, xc, stn, stt, xcn, xc)
            R.pool(lambda e, xc=xc: e.tensor_tensor(out=xc[:], in0=xc[:], in1=lng[:], op=ALU.mult), reads=[xcn, "lng"], writes=[xcn])
            R.dve(lambda e, xc=xc: e.tensor_tensor(out=xc[:], in0=xc[:], in1=lnb[:], op=ALU.add), reads=[xcn, "lnb"], writes=[xcn])
            R.dma("sp", lambda e, xc=xc, tc=tc: e.dma_start(out=xbuf[b, tc * 128:(tc + 1) * 128, :], in_=xc[:]), reads=[xcn], writes=["xb%d_%d" % (b, tc)])
    if "x1" in tap_t and b == 0 and l == 0:
        fin.append(R.dma("sp", lambda e: e.dma_start(out=tap_t["x1"][:, :], in_=xbuf[0, :, :]), reads=["xb0_%d" % t_ for t_ in range(NT)], writes=["tap_x1"]))


NE = 32
GELU_A = 1.702


def moe_consts():
    c = {}
    tp = np.arange(128)[:, None]
    t = np.arange(128)[None, :]
    c["ltri"] = (tp < t).astype(np.float32).astype(ml_dtypes.bfloat16)
    c["onesb"] = np.ones((128, 128), np.float32).astype(ml_dtypes.bfloat16)
    c["iota3"] = np.broadcast_to(np.arange(128, dtype=np.float32)[None, None, :], (128, NE, 128)).astype(ml_dtypes.bfloat16).copy()
    return c


def moe(nc, R, mk, env):
    l, NB, NL = env["l"], env["NB"], env["NL"]
    xbuf, modbuf, out, cst = env["xbuf"], env["modbuf"], env["out"], env["cst"]
    ident_f, ident_b, tap_t, fin = env["ident_f"], env["ident_b"], env["tap_t"], env["fin"]
    ln_g, ln_b = env["ln_g"], env["ln_b"]
    router_w, router_b, w_gu, b_gu, w_dn, b_dn = env["router_w"], env["router_b"], env["w_gu"], env["b_gu"], env["w_dn"], env["b_dn"]
    H2d, Gd, CMd, XTd, PGd, Yd = env["H2d"], env["Gd"], env["CMd"], env["XTd"], env["PGd"], env["Yd"]
    last = (l == NL - 1)
    moe_stop = env.get("moe_stop", "D")

    def ln_chunk(xc, xcn, stt, stn):
        R.dve(lambda e: e.bn_stats(out=stt[:, 0:6], in_=xc[:, 0:512]), reads=[xcn], writes=[stn])
        R.dve(lambda e: e.bn_stats(out=stt[:, 6:12], in_=xc[:, 512:1024]), reads=[xcn], writes=[stn])
        R.dve(lambda e: e.bn_aggr(out=stt[:, 12:14], in_=stt[:, 0:12]), reads=[stn], writes=[stn])
        R.act(lambda e: e.activation(out=stt[:, 14:15], in_=stt[:, 13:14], func=AF.Sqrt, bias=LN_EPS, scale=1.0), reads=[stn], writes=[stn])
        R.dve(lambda e: e.reciprocal(out=stt[:, 15:16], in_=stt[:, 14:15]), reads=[stn], writes=[stn])
        R.dve(lambda e: e.tensor_scalar(out=xc[:], in0=xc[:], scalar1=stt[:, 12:13], scalar2=stt[:, 15:16], op0=ALU.subtract, op1=ALU.mult),
              reads=[xcn, stn], writes=[xcn])

    def bcast_row(sbx, name, src_ap, plus1=False, width=D):
        dst = sbx(name, [128, width])
        R.dma("sp", lambda e: e.dma_start(out=dst[:], in_=src_ap.broadcast_to([128, width])), reads=["modbuf"], writes=[name])
        if plus1:
            R.pool(lambda e: e.tensor_scalar(out=dst[:], in0=dst[:], scalar1=1.0, scalar2=None, op0=ALU.add), reads=[name], writes=[name])
        return dst

    with contextlib.ExitStack() as sA:
        sb, ps = mk(sA)
        wr = sb("wr", [128, 8, NE])
        wrh = sb("wrh", [128, 8, NE], BF16)
        wrl = sb("wrl", [128, 8, NE], BF16)
        R.dma("sp", lambda e: e.dma_start(out=wr[:], in_=router_w[l, :, :].rearrange("(k p) n -> p k n", p=128)), writes=["wr"])
        R.dve(lambda e: e.tensor_copy(out=wrh[:], in_=wr[:]), reads=["wr"], writes=["wrh"])
        R.dve(lambda e: e.tensor_tensor(out=wrl[:], in0=wr[:], in1=wrh[:], op=ALU.subtract), reads=["wr", "wrh"], writes=["wrl"])
        rb = bcast_row(sb, "rb", router_b[l:l + 1, :], width=NE)
        ltri = sb("ltri", [128, 128], BF16)
        onesb = sb("onesb", [128, 128], BF16)
        R.dma("sp", lambda e: e.dma_start(out=ltri[:], in_=cst["ltri"][:, :]), writes=["ltri"])
        R.dma("sp", lambda e: e.dma_start(out=onesb[:], in_=cst["onesb"][:, :]), writes=["onesb"])
        xc_r = Rot([("xa%d" % i, sb("xa%d" % i, [128, D])) for i in range(2)])
        hb_r = Rot([("hb%d" % i, sb("hb%d" % i, [128, D], BF16)) for i in range(2)])
        st_r = Rot([("sta%d" % i, sb("sta%d" % i, [128, 16])) for i in range(2)])
        hT_r = Rot([("hTa%d" % i, sb("hTa%d" % i, [128, 2, D], BF16)) for i in range(2)])
        hl_r = Rot([("hl%d" % i, sb("hl%d" % i, [128, D], BF16)) for i in range(2)])
        lg_r = Rot([("lg%d" % i, sb("lg%d" % i, [128, 96])) for i in range(2)])
        maskb = sb("maskb", [128, NT, NE], BF16)
        Gs = sb("Gs", [128, NT, NE])
        CMs = sb("CMs", [128, NT, NE])
        pTh = ps("pTh", [128, D], BF16)
        pTl = ps("pTl", [128, D], BF16)
        pl_r = Rot([("pl%d" % i, ps("pl%d" % i, [128, 512])) for i in range(2)])
        sh2 = sb("sh2", [128, D])
        sc2 = sb("sc2", [128, D])
        for b in range(NB):
            R.dma("sp", lambda e, b=b: e.dma_start(out=sh2[:], in_=modbuf[b, l:l + 1, 3 * D:4 * D].broadcast_to([128, D])), reads=["modbuf"], writes=["sh2_0"])
            R.dma("sp", lambda e, b=b: e.dma_start(out=sc2[:], in_=modbuf[b, l:l + 1, 4 * D:5 * D].broadcast_to([128, D])), reads=["modbuf"], writes=["sc2_0"])
            R.pool(lambda e: e.tensor_scalar(out=sc2[:], in0=sc2[:], scalar1=1.0, scalar2=None, op0=ALU.add), reads=["sc2_0"], writes=["sc2_0"])
            for tc in range(NT):
                xcn, xc = xc_r.next()
                hbn, hb = hb_r.next()
                stn, stt = st_r.next()
                htn, hTc = hT_r.next()
                lgn, lg = lg_r.next()
                pln, pl = pl_r.next()
                R.dma("sp", lambda e, xc=xc, tc=tc, b=b: e.dma_start(out=xc[:], in_=xbuf[b, tc * 128:(tc + 1) * 128, :]), reads=["xb%d_%d" % (b, tc)], writes=[xcn])
                ln_chunk(xc, xcn, stt, stn)
                R.pool(lambda e, xc=xc: e.tensor_tensor(out=xc[:], in0=xc[:], in1=sc2[:], op=ALU.mult), reads=[xcn, "sc2_0"], writes=[xcn])
                R.dve(lambda e, xc=xc: e.tensor_tensor(out=xc[:], in0=xc[:], in1=sh2[:], op=ALU.add), reads=[xcn, "sh2_0"], writes=[xcn])
                R.act(lambda e, xc=xc, hb=hb: e.copy(out=hb[:], in_=xc[:]), reads=[xcn], writes=[hbn])
                R.dma("sp", lambda e, hb=hb, tc=tc, b=b: e.dma_start(out=H2d[b, tc * 128:(tc + 1) * 128, :], in_=hb[:]), reads=[hbn], writes=["H2d%d" % b])
                hln, hl = hl_r.next()
                R.dve(lambda e, xc=xc, hb=hb, hl=hl: e.tensor_tensor(out=hl[:], in0=xc[:], in1=hb[:], op=ALU.subtract), reads=[xcn, hbn], writes=[hln])
                for kc in range(8):
                    R.pe(lambda e, hb=hb, kc=kc: e.transpose(out=pTh[:, kc * 128:(kc + 1) * 128], in_=hb[:, kc * 128:(kc + 1) * 128], identity=ident_b[:]),
                         reads=[hbn, "ident_b"], writes=["pTh"])
                for kc in range(8):
                    R.pe(lambda e, hl=hl, kc=kc: e.transpose(out=pTl[:, kc * 128:(kc + 1) * 128], in_=hl[:, kc * 128:(kc + 1) * 128], identity=ident_b[:]),
                         reads=[hln, "ident_b"], writes=["pTl"])
                R.act(lambda e, hTc=hTc: e.copy(out=hTc[:, 0, :], in_=pTh[:]), reads=["pTh"], writes=[htn])
                R.act(lambda e, hTc=hTc: e.copy(out=hTc[:, 1, :], in_=pTl[:]), reads=["pTl"], writes=[htn])
                terms = [(0, wrh, "wrh"), (1, wrh, "wrh"), (0, wrl, "wrl")]
                for ti, (hs, wt_, wn_) in enumerate(terms):
                    for kc in range(8):
                        R.pe(lambda e, hTc=hTc, kc=kc, pl=pl, hs=hs, wt_=wt_, ti=ti: e.matmul(pl[:, 0:NE], lhsT=hTc[:, hs, kc * 128:(kc + 1) * 128], rhs=wt_[:, kc, :],
                                                                                     start=(ti == 0 and kc == 0), stop=(ti == 2 and kc == 7)),
                             reads=[htn, wn_], writes=[pln])
                R.dve(lambda e, lg=lg, pl=pl: e.tensor_tensor(out=lg[:, 0:32], in0=pl[:, 0:NE], in1=rb[:], op=ALU.add), reads=[pln, "rb"], writes=[lgn])
                R.dve(lambda e, lg=lg: e.max(out=lg[:, 32:40], in_=lg[:, 0:32]), reads=[lgn], writes=[lgn])
                R.dve(lambda e, lg=lg: e.tensor_scalar(out=lg[:, 64:96], in0=lg[:, 0:32], scalar1=lg[:, 35:36], scalar2=None, op0=ALU.is_ge), reads=[lgn], writes=[lgn])
                R.dve(lambda e, lg=lg: e.tensor_scalar(out=lg[:, 40:41], in0=lg[:, 32:33], scalar1=-1.0, scalar2=None, op0=ALU.mult), reads=[lgn], writes=[lgn])
                R.act(lambda e, lg=lg: e.activation(out=lg[:, 0:32], in_=lg[:, 0:32], func=AF.Exp, bias=lg[:, 40:41], scale=1.0), reads=[lgn], writes=[lgn])
                R.dve(lambda e, lg=lg: e.tensor_tensor(out=lg[:, 0:32], in0=lg[:, 0:32], in1=lg[:, 64:96], op=ALU.mult), reads=[lgn], writes=[lgn])
                R.dve(lambda e, lg=lg: e.tensor_reduce(out=lg[:, 41:42], in_=lg[:, 0:32], axis=AX.X, op=ALU.add), reads=[lgn], writes=[lgn])
                R.dve(lambda e, lg=lg: e.reciprocal(out=lg[:, 42:43], in_=lg[:, 41:42]), reads=[lgn], writes=[lgn])
                R.dve(lambda e, lg=lg, tc=tc: e.tensor_scalar(out=Gs[:, tc, :], in0=lg[:, 0:32], scalar1=lg[:, 42:43], scalar2=None, op0=ALU.mult), reads=[lgn], writes=["Gs"])
                R.pool(lambda e, lg=lg, tc=tc: e.tensor_copy(out=maskb[:, tc, :], in_=lg[:, 64:96]), reads=[lgn], writes=["maskb"])
            for tc in range(NT):
                c = tc % 4
                g0 = tc - c
                pln, pl = pl_r.next()
                for cp in range(c):
                    R.pe(lambda e, pl=pl, cp=cp, g0=g0: e.matmul(pl[:, 0:NE], lhsT=onesb[:], rhs=maskb[:, g0 + cp, :], start=(cp == 0), stop=False),
                         reads=["onesb", "maskb"], writes=[pln])
                R.pe(lambda e, pl=pl, tc=tc, c=c: e.matmul(pl[:, 0:NE], lhsT=ltri[:], rhs=maskb[:, tc, :], start=(c == 0), stop=True), reads=["ltri", "maskb"], writes=[pln])
                R.dve(lambda e, pl=pl, tc=tc: e.scalar_tensor_tensor(out=CMs[:, tc, :], in0=pl[:, 0:NE], scalar=1.0, in1=maskb[:, tc, :], op0=ALU.add, op1=ALU.mult),
                      reads=[pln, "maskb"], writes=["CMs"])
            R.dve(lambda e: e.tensor_scalar(out=CMs[:], in0=CMs[:], scalar1=-1.0, scalar2=None, op0=ALU.add), reads=["CMs"], writes=["CMs"])
            R.dma("sp", lambda e, b=b: e.dma_start(out=Gd[b, :, :].rearrange("(c p) e -> p c e", p=128), in_=Gs[:]), reads=["Gs"], writes=["Gd%d" % b])
            R.dma("sp", lambda e, b=b: e.dma_start(out=CMd[b, :, :].rearrange("(c p) e -> p c e", p=128), in_=CMs[:]), reads=["CMs"], writes=["CMd%d" % b])
    R.barrier()
    if "G" in tap_t and l == 0:
        fin.append(R.dma("sp", lambda e: e.dma_start(out=tap_t["G"][:, :], in_=Gd[0, :, :]), reads=["Gd0"], writes=["tap_G"]))
        fin.append(R.dma("sp", lambda e: e.dma_start(out=tap_t["CM"][:, :], in_=CMd[0, :, :]), reads=["CMd0"], writes=["tap_CM"]))
    if moe_stop == "A":
        return

    with contextlib.ExitStack() as sB:
        sb, ps = mk(sB)
        iota3 = sb("iota3", [128, NE, 128], BF16)
        R.dma("sp", lambda e: e.dma_start(out=iota3[:], in_=cst["iota3"][:, :, :]), writes=["iota3"])
        h2g = sb("h2g", [128, 4, D], BF16)
        Gg = sb("Gg", [128, 4, NE])
        CMg = sb("CMg", [128, 4, NE])
        CMj = sb("CMj", [128, 4, NE])
        P = [sb("P%d" % i, [128, NE, 128], BF16) for i in range(4)]
        Pg = [sb("Pg%d" % i, [128, NE, 128], BF16) for i in range(4)]
        xe_r = Rot([("xe%d" % i, sb("xe%d" % i, [128, 8, 512], BF16)) for i in range(2)])
        pgt_r = Rot([("pgt%d" % i, sb("pgt%d" % i, [128, 1024], BF16)) for i in range(2)])
        pg_r = Rot([("pB%d" % i, ps("pB%d" % i, [128, 512])) for i in range(4)])
        pTb_r = Rot([("pTb%d" % i, ps("pTb%d" % i, [128, 1024], BF16)) for i in range(2)])
        for g in range(NB * 4):
            b, gq = g // 4, g % 4
            R.dma("sp", lambda e, b=b, gq=gq: e.dma_start(out=h2g[:], in_=H2d[b, gq * 512:(gq + 1) * 512, :].rearrange("(c p) d -> p c d", p=128)),
                  reads=["H2d%d" % b], writes=["h2g"])
            R.dma("sp", lambda e, b=b, gq=gq: e.dma_start(out=Gg[:], in_=Gd[b, gq * 512:(gq + 1) * 512, :].rearrange("(c p) e -> p c e", p=128)),
                  reads=["Gd%d" % b], writes=["Gg"])
            R.dma("sp", lambda e, b=b, gq=gq: e.dma_start(out=CMg[:], in_=CMd[b, gq * 512:(gq + 1) * 512, :].rearrange("(c p) e -> p c e", p=128)),
                  reads=["CMd%d" % b], writes=["CMg"])
            for jh in range(2):
                R.dve(lambda e, jh=jh: e.tensor_scalar(out=CMj[:], in0=CMg[:], scalar1=-128.0 * jh, scalar2=None, op0=ALU.add), reads=["CMg"], writes=["CMj"])
                for c in range(4):
                    R.dve(lambda e, c=c: e.tensor_tensor(out=P[c][:], in0=iota3[:], in1=CMj[:, c, :].unsqueeze(2).broadcast_to([128, NE, 128]), op=ALU.is_equal),
                          reads=["iota3", "CMj"], writes=["P%d" % c])
                    R.pool(lambda e, c=c: e.tensor_tensor(out=Pg[c][:], in0=P[c][:], in1=Gg[:, c, :].unsqueeze(2).broadcast_to([128, NE, 128]), op=ALU.mult),
                           reads=["P%d" % c, "Gg"], writes=["Pg%d" % c])
                for eq in range(8):
                    xen, xe = xe_r.next()
                    for dk in range(8):
                        pn, pt = pg_r.next()
                        for c in range(4):
                            R.pe(lambda e, pt=pt, c=c, dk=dk, eq=eq: e.matmul(pt[:, :], lhsT=h2g[:, c, dk * 128:(dk + 1) * 128],
                                                                             rhs=P[c][:, eq * 4:(eq + 1) * 4, :].rearrange("p e j -> p (e j)"), start=(c == 0), stop=(c == 3)),
                                 reads=["h2g", "P%d" % c], writes=[pn])
                        if dk % 2 == 0:
                            R.act(lambda e, pt=pt, xe=xe, dk=dk: e.copy(out=xe[:, dk, :], in_=pt[:, :]), reads=[pn], writes=[xen])
                        else:
                            R.dve(lambda e, pt=pt, xe=xe, dk=dk: e.tensor_copy(out=xe[:, dk, :], in_=pt[:, :]), reads=[pn], writes=[xen])
                    R.dma("sp", lambda e, xe=xe, g=g, eq=eq, jh=jh: e.dma_start(out=XTd[g // 4][g % 4, jh, eq, :, :, :], in_=xe[:]), reads=[xen], writes=["XTd%d" % g])
                for e2 in range(NE // 2):
                    ptn, ptb = pTb_r.next()
                    pgn, pgt = pgt_r.next()
                    for ee in range(2):
                        for c in range(4):
                            R.pe(lambda e, ptb=ptb, ee=ee, c=c, e2=e2: e.transpose(out=ptb[:, ee * 512 + c * 128:ee * 512 + (c + 1) * 128], in_=Pg[c][:, e2 * 2 + ee, :], identity=ident_b[:]),
                                 reads=["Pg%d" % c, "ident_b"], writes=[ptn])
                    R.act(lambda e, ptb=ptb, pgt=pgt: e.copy(out=pgt[:], in_=ptb[:]), reads=[ptn], writes=[pgn])
                    R.dma("sp", lambda e, pgt=pgt, g=g, e2=e2, jh=jh: e.dma_start(out=PGd[g // 4][g % 4, jh, e2 * 2:e2 * 2 + 2, :, :].rearrange("e j t -> j e t"),
                                                                              in_=pgt[:].rearrange("p (e t) -> p e t", e=2)), reads=[pgn], writes=["PGd%d" % g])
    R.barrier()
    if moe_stop == "B":
        return

    with contextlib.ExitStack() as sC:
        sb, ps = mk(sC)
        wgu_r = Rot([("wgu%d" % i, sb("wgu%d" % i, [128, 8, 2 * D], BF16)) for i in range(2)])
        wd_r = Rot([("wd%d" % i, sb("wd%d" % i, [128, 8, D], BF16)) for i in range(2)])
        brow_r = Rot([("brow%d" % i, sb("brow%d" % i, [16, 128])) for i in range(2)])
        bgu_r = Rot([("bgu%d" % i, sb("bgu%d" % i, [128, 16])) for i in range(2)])
        xt_r = Rot([("xt%d" % i, sb("xt%d" % i, [128, 8, 512], BF16)) for i in range(2)])
        at_r = Rot([("at%d" % i, sb("at%d" % i, [128, 8, 512], BF16)) for i in range(2)])
        gc_r = Rot([("gc%d" % i, sb("gc%d" % i, [128, 512])) for i in range(2)])
        sl_r = Rot([("sl%d" % i, sb("sl%d" % i, [128, 512])) for i in range(2)])
        u0_r = Rot([("u0%d" % i, sb("u0%d" % i, [128, 512])) for i in range(2)])
        ys_r = Rot([("ys%d" % i, sb("ys%d" % i, [128, D], BF16)) for i in range(2)])
        pgu_r = Rot([("pC%d" % i, ps("pC%d" % i, [128, 512])) for i in range(4)])
        pdn_r = Rot([("pD%d" % i, ps("pD%d" % i, [128, 512])) for i in range(3)])
        pXc = ps("pXc", [128, 512])
        for ex in range(NE):
            wgn, wgu = wgu_r.next()
            wdn, wd = wd_r.next()
            brn, brow = brow_r.next()
            bgn, bgu = bgu_r.next()
            for hf in range(2):
                R.dma("pool", lambda e, wgu=wgu, ex=ex, hf=hf: e.dma_start(out=wgu[:, :, hf * D:(hf + 1) * D],
                                                                          in_=w_gu[l, ex, :, hf * D:(hf + 1) * D].rearrange("(k p) n -> p k n", p=128)), writes=[wgn])
            R.dma("pool", lambda e, wd=wd, ex=ex: e.dma_start(out=wd[:], in_=w_dn[l, ex, :, :].rearrange("(k p) n -> p k n", p=128)), writes=[wdn])
            R.dma("sp", lambda e, brow=brow, ex=ex: e.dma_start(out=brow[:], in_=b_gu[l, ex, :].rearrange("(m p) -> m p", p=128)), writes=[brn])
            R.pe(lambda e, brow=brow: e.transpose(out=pXc[:, 0:16], in_=brow[:], identity=ident_f[0:16, 0:16]), reads=[brn, "ident_f"], writes=["pXc"])
            R.dve(lambda e, bgu=bgu: e.tensor_copy(out=bgu[:], in_=pXc[:, 0:16]), reads=["pXc"], writes=[bgn])
            for b, jh in [(b_, j_) for b_ in range(NB) for j_ in range(2)]:
                xtn, xt = xt_r.next()
                atn, at = at_r.next()
                for gq in range(4):
                    R.dma("sp", lambda e, xt=xt, b=b, gq=gq, ex=ex, jh=jh: e.dma_start(
                        out=xt[:, :, gq * 128:(gq + 1) * 128], in_=XTd[b][gq, jh, ex // 4, :, :, (ex % 4) * 128:(ex % 4 + 1) * 128]),
                        reads=["XTd%d" % (b * 4 + gq)], writes=[xtn])
                for m in range(8):
                    pgn_, pgp = pgu_r.next()
                    pun_, pup = pgu_r.next()
                    for (pp, pnm, mm) in ((pgp, pgn_, m), (pup, pun_, m + 8)):
                        for dk in range(8):
                            R.pe(lambda e, pp=pp, mm=mm, dk=dk, wgu=wgu, xt=xt: e.matmul(pp[:, :], lhsT=wgu[:, dk, mm * 128:(mm + 1) * 128], rhs=xt[:, dk, :],
                                                                                     start=(dk == 0), stop=(dk == 7)), reads=[wgn, xtn], writes=[pnm])
                    gcn, gc = gc_r.next()
                    sln, sl = sl_r.next()
                    u0n, u0 = u0_r.next()
                    R.dve(lambda e, gc=gc, pgp=pgp, bgu=bgu, m=m: e.tensor_scalar(out=gc[:], in0=pgp[:, :], scalar1=bgu[:, m:m + 1], scalar2=7.0, op0=ALU.add, op1=ALU.min),
                          reads=[pgn_, bgn], writes=[gcn])
                    R.act(lambda e, gc=gc, sl=sl: e.activation(out=sl[:], in_=gc[:], func=AF.Silu, scale=GELU_A), reads=[gcn], writes=[sln])
                    R.act(lambda e, u0=u0, pup=pup, bgu=bgu, m=m: e.activation(out=u0[:], in_=pup[:, :], func=AF.Identity, bias=bgu[:, m + 8:m + 9], scale=1.0),
                          reads=[pun_, bgn], writes=[u0n])
                    R.pool(lambda e, u0=u0: e.tensor_scalar(out=u0[:], in0=u0[:], scalar1=7.0, scalar2=-7.0, op0=ALU.min, op1=ALU.max), reads=[u0n], writes=[u0n])
                    R.dve(lambda e, u0=u0, sl=sl, at=at, m=m: e.scalar_tensor_tensor(out=at[:, m, :], in0=u0[:], scalar=1.0, in1=sl[:], op0=ALU.add, op1=ALU.mult),
                          reads=[u0n, sln], writes=[atn])
                for gq in range(4):
                    ysn, ys = ys_r.next()
                    for hf in range(2):
                        pdn_, pdp = pdn_r.next()
                        for m in range(8):
                            R.pe(lambda e, pdp=pdp, m=m, gq=gq, hf=hf, at=at, wd=wd: e.matmul(pdp[:, :], lhsT=at[:, m, gq * 128:(gq + 1) * 128], rhs=wd[:, m, hf * 512:(hf + 1) * 512],
                                                                                          start=(m == 0), stop=(m == 7)), reads=[atn, wdn], writes=[pdn_])
                        R.act(lambda e, pdp=pdp, ys=ys, hf=hf: e.activation(out=ys[:, hf * 512:(hf + 1) * 512], in_=pdp[:, :], func=AF.Copy, scale=1.0 / GELU_A),
                              reads=[pdn_], writes=[ysn])
                    R.dma("sp", lambda e, ys=ys, b=b, gq=gq, ex=ex, jh=jh: e.dma_start(out=Yd[b][gq, jh, ex, :, :], in_=ys[:]), reads=[ysn], writes=["Yd%d" % (b * 4 + gq)])
    R.barrier()
    if moe_stop == "C":
        return

    with contextlib.ExitStack() as sD:
        sb, ps = mk(sD)
        Ysb = sb("Ysb", [128, NE, D], BF16)
        PGs = sb("PGs", [128, NE, 512], BF16)
        Bdp = sb("Bdp", [128, D], BF16)
        Gpad = sb("Gpad", [128, 128], BF16)
        GTp_r = Rot([("GTp%d" % i, sb("GTp%d" % i, [128, 128], BF16)) for i in range(2)])
        Gg2 = sb("Gg2", [128, 4, NE])
        facc = sb("facc", [128, 4, D])
        lng = sb("lng2", [128, D])
        lnb = sb("lnb2", [128, D])
        g2p = sb("g2p", [128, D])
        R.dma("sp", lambda e: e.dma_start(out=lng[:], in_=ln_g[l, 1:2, :].broadcast_to([128, D])), writes=["lng2"])
        R.dma("sp", lambda e: e.dma_start(out=lnb[:], in_=ln_b[l, 1:2, :].broadcast_to([128, D])), writes=["lnb2"])
        R.pool(lambda e: e.memset(Bdp[:], 0.0), writes=["Bdp"])
        R.pool(lambda e: e.memset(Gpad[:], 0.0), writes=["Gpad"])
        R.dma("pool", lambda e: e.dma_start(out=Bdp[0:NE, :], in_=b_dn[l, :, :]), reads=["Bdp"], writes=["Bdp"])
        xc_r = Rot([("xd%d" % i, sb("xd%d" % i, [128, D])) for i in range(2)])
        mt_r = Rot([("md%d" % i, sb("md%d" % i, [128, D])) for i in range(2)])
        st_r = Rot([("std%d" % i, sb("std%d" % i, [128, 16])) for i in range(2)])
        pc_r = Rot([("pE%d" % i, ps("pE%d" % i, [128, 512])) for i in range(4)])
        pTd = ps("pTd", [128, 1024], BF16)
        for g in range(NB * 4):
            b, gq = g // 4, g % 4
            if gq == 0:
                R.dma("sp", lambda e, b=b: e.dma_start(out=g2p[:], in_=modbuf[b, l:l + 1, 5 * D:6 * D].broadcast_to([128, D])), reads=["modbuf"], writes=["g2p"])
                R.pool(lambda e: e.tensor_scalar(out=g2p[:], in0=g2p[:], scalar1=1.0, scalar2=None, op0=ALU.add), reads=["g2p"], writes=["g2p"])
            R.dma("sp", lambda e, b=b, gq=gq: e.dma_start(out=Gg2[:], in_=Gd[b, gq * 512:(gq + 1) * 512, :].rearrange("(c p) e -> p c e", p=128)),
                  reads=["Gd%d" % b], writes=["Gg2"])
            for jh in range(2):
                for q4 in range(4):
                    R.dma("sp", lambda e, g=g, q4=q4, jh=jh: e.dma_start(out=Ysb[:, q4 * 8:(q4 + 1) * 8, :], in_=Yd[g // 4][g % 4, jh, q4 * 8:(q4 + 1) * 8, :, :].rearrange("e j d -> j e d")),
                          reads=["Yd%d" % g], writes=["Ysb"])
                    R.dma("sp", lambda e, g=g, q4=q4, jh=jh: e.dma_start(out=PGs[:, q4 * 8:(q4 + 1) * 8, :], in_=PGd[g // 4][g % 4, jh, q4 * 8:(q4 + 1) * 8, :, :].rearrange("e j t -> j e t")),
                          reads=["PGd%d" % g], writes=["PGs"])
                for c in range(4):
                    tc = gq * 4 + c
                    if jh == 0:
                        gtn, GTp = GTp_r.next()
                        R.dve(lambda e, c=c: e.tensor_copy(out=Gpad[:, 0:NE], in_=Gg2[:, c, :]), reads=["Gg2", "Gpad"], writes=["Gpad"])
                        R.pe(lambda e: e.transpose(out=pTd[:, 0:128], in_=Gpad[:], identity=ident_b[:]), reads=["Gpad", "ident_b"], writes=["pTd"])
                        R.act(lambda e, GTp=GTp: e.copy(out=GTp[:], in_=pTd[:, 0:128]), reads=["pTd"], writes=[gtn])
                    else:
                        xcn, xc = xc_r.next()
                        mtn, mt = mt_r.next()
                        stn, stt = st_r.next()
                        R.dma("sp", lambda e, xc=xc, tc=tc, b=b: e.dma_start(out=xc[:], in_=xbuf[b, tc * 128:(tc + 1) * 128, :]), reads=["xb%d_%d" % (b, tc)], writes=[xcn])
                    for hf in range(2):
                        pn, pt = pc_r.next()
                        for ex in range(NE):
                            R.pe(lambda e, pt=pt, ex=ex, c=c, hf=hf: e.matmul(pt[:, :], lhsT=PGs[:, ex, c * 128:(c + 1) * 128], rhs=Ysb[:, ex, hf * 512:(hf + 1) * 512],
                                                                             start=(ex == 0), stop=(jh == 1 and ex == NE - 1)), reads=["PGs", "Ysb"], writes=[pn])
                        if jh == 0:
                            R.pe(lambda e, pt=pt, GTp=GTp, hf=hf: e.matmul(pt[:, :], lhsT=GTp[:], rhs=Bdp[:, hf * 512:(hf + 1) * 512], start=False, stop=True),
                                 reads=[gtn, "Bdp"], writes=[pn])
                            R.act(lambda e, pt=pt, c=c, hf=hf: e.copy(out=facc[:, c, hf * 512:(hf + 1) * 512], in_=pt[:, :]), reads=[pn], writes=["facc"])
                        else:
                            R.dve(lambda e, pt=pt, mt=mt, hf=hf, c=c: e.tensor_tensor(out=mt[:, hf * 512:(hf + 1) * 512], in0=pt[:, :], in1=facc[:, c, hf * 512:(hf + 1) * 512], op=ALU.add),
                                  reads=[pn, "facc"], writes=[mtn])
                    if jh == 1:
                        R.pool(lambda e, mt=mt: e.tensor_tensor(out=mt[:], in0=mt[:], in1=g2p[:], op=ALU.mult), reads=[mtn, "g2p"], writes=[mtn])
                        R.dve(lambda e, xc=xc, mt=mt: e.scalar_tensor_tensor(out=xc[:], in0=xc[:], scalar=ALPHA, in1=mt[:], op0=ALU.mult, op1=ALU.add),
                              reads=[xcn, mtn], writes=[xcn])
                        ln_chunk(xc, xcn, stt, stn)
                        R.pool(lambda e, xc=xc: e.tensor_tensor(out=xc[:], in0=xc[:], in1=lng[:], op=ALU.mult), reads=[xcn, "lng2"], writes=[xcn])
                        R.dve(lambda e, xc=xc: e.tensor_tensor(out=xc[:], in0=xc[:], in1=lnb[:], op=ALU.add), reads=[xcn, "lnb2"], writes=[xcn])
                        if last:
                            fin.append(R.dma("sp", lambda e, xc=xc, tc=tc, b=b: e.dma_start(out=out[b, tc * 128:(tc + 1) * 128, :], in_=xc[:]), reads=[xcn], writes=["out%d_%d" % (b, tc)]))
                        else:
                            R.dma("sp", lambda e, xc=xc, tc=tc, b=b: e.dma_start(out=xbuf[b, tc * 128:(tc + 1) * 128, :], in_=xc[:]), reads=[xcn], writes=["xb%d_%d" % (b, tc)])
    R.barrier()


N_CORES = 8
FUSED = True
DEPTH = 4
_PROG = {}


def _get_prog(NB, NL):
    key = (NB, NL)
    if key not in _PROG:
        _PROG[key] = build(NB, NL)
    return _PROG[key]


def _consts():
    hc = host_consts()
    hc.update(moe_consts())
    return {"k_" + k: v for k, v in hc.items()}


_PER_LAYER = ("ada_w", "ada_b", "w_in", "w_out", "ret_gn_w", "conv_w", "cmp_pos", "cmp_w1", "cmp_w2", "ln_g", "ln_b",
              "router_w", "router_b", "w_gate_up", "b_gate_up", "w_down", "b_down")


def kernel(**inputs):
    B = inputs["x"].shape[0]
    NB = B // N_CORES
    f32 = np.float32
    x = np.ascontiguousarray(inputs["x"], dtype=f32)
    c = np.ascontiguousarray(inputs["c"], dtype=f32)
    pos = np.ascontiguousarray(inputs["positions"], dtype=np.int32)
    consts = _consts()
    layer_sets = [list(range(DEPTH))] if FUSED else [[l] for l in range(DEPTH)]
    for ls in layer_sets:
        nc = _get_prog(NB, len(ls))
        w = {k: np.ascontiguousarray(np.asarray(inputs[k])[ls[0]:ls[-1] + 1], dtype=f32) for k in _PER_LAYER}
        in_maps = []
        for ci in range(N_CORES):
            m = dict(w)
            m.update(consts)
            m["x"] = np.ascontiguousarray(x[ci * NB:(ci + 1) * NB])
            m["c"] = np.ascontiguousarray(c[ci * NB:(ci + 1) * NB])
            m["positions"] = np.ascontiguousarray(pos[ci * NB:(ci + 1) * NB])
            in_maps.append(m)
        res = run_bass_kernel_spmd(nc, in_maps, core_ids=list(range(N_CORES)))
        x = np.concatenate([np.asarray(r["out"], dtype=f32) for r in res.results], axis=0)
    return x
```

```python
import math
import contextlib
import numpy as np
import ml_dtypes
import concourse.bass as bass
import concourse.mybir as mybir
from concourse.bass_utils import run_bass_kernel_spmd


ENGS = ("pe", "act", "dve", "pool", "sp")
NPOOL = 24


class Rec:
    def __init__(self, nc):
        self.nc = nc
        self.ops = []
        self.lastw = {}
        self.readers = {}
        self.ndma = 0
        self.dma_idx = []

    def eng_obj(self, e):
        nc = self.nc
        return {"pe": nc.tensor, "act": nc.scalar, "dve": nc.vector, "pool": nc.gpsimd, "sp": nc.sync}[e]

    def add(self, eng, fn, reads=(), writes=(), dma=False):
        idx = len(self.ops)
        deps = set()
        reads = list(reads) + ["BARRIER"]
        for r in reads:
            if r in self.lastw:
                deps.add(self.lastw[r])
        for w in writes:
            if w in self.lastw:
                deps.add(self.lastw[w])
            for rd in self.readers.get(w, ()):
                deps.add(rd)
        op = dict(eng=eng, fn=fn, deps=deps, dma=dma, used=False, slot=None)
        if dma:
            op["slot"] = self.ndma % NPOOL
            op["val"] = 16 * (self.ndma // NPOOL + 1)
            prev = self.ndma - NPOOL
            if prev >= 0:
                deps.add(self.dma_idx[prev])
            self.dma_idx.append(idx)
            self.ndma += 1
        self.ops.append(op)
        for r in reads:
            self.readers.setdefault(r, []).append(idx)
        for w in writes:
            self.lastw[w] = idx
            self.readers[w] = []
        return idx

    def barrier(self):
        return self.add("sp", lambda e: e.nop(), reads=(), writes=["BARRIER"])

    def pe(self, fn, reads=(), writes=()):
        return self.add("pe", fn, reads, writes)

    def act(self, fn, reads=(), writes=()):
        return self.add("act", fn, reads, writes)

    def dve(self, fn, reads=(), writes=()):
        return self.add("dve", fn, reads, writes)

    def pool(self, fn, reads=(), writes=()):
        return self.add("pool", fn, reads, writes)

    def dma(self, eng, fn, reads=(), writes=()):
        return self.add(eng, fn, reads, writes, dma=True)

    def emit(self, final_wait_ops=()):
        nc = self.nc
        ops = self.ops
        for i, op in enumerate(ops):
            for d in op["deps"]:
                dop = ops[d]
                if (not dop["dma"]) and dop["eng"] == "pe" and op["eng"] == "pe" and not op["dma"]:
                    continue
                dop["used"] = True
        for i in final_wait_ops:
            ops[i]["used"] = True
        cnt = {e: 0 for e in ENGS}
        for op in ops:
            if not op["dma"] and op["used"]:
                cnt[op["eng"]] += 1
                op["val"] = cnt[op["eng"]]
        import contextlib
        with contextlib.ExitStack() as st:
            esem = {e: st.enter_context(nc.semaphore("es_" + e)) for e in ENGS}
            dsem = [st.enter_context(nc.semaphore("ds_%d" % i)) for i in range(NPOOL)]
            block = st.enter_context(nc.Block())

            def semof(op):
                if op["dma"]:
                    return dsem[op["slot"]], op["val"], ("d", op["slot"])
                return esem[op["eng"]], op["val"], ("e", op["eng"])

            def run_engine(e, engobj):
                seen = {}
                for i, op in enumerate(ops):
                    if op["eng"] != e:
                        continue
                    need = {}
                    for d in op["deps"]:
                        dop = ops[d]
                        if (not dop["dma"]) and dop["eng"] == "pe" and e == "pe" and not op["dma"]:
                            continue
                        s, v, k = semof(dop)
                        if seen.get(k, 0) >= v:
                            continue
                        if k not in need or need[k][1] < v:
                            need[k] = (s, v)
                    for k, (s, v) in need.items():
                        engobj.wait_ge(s, v)
                        seen[k] = v
                    ins = op["fn"](engobj)
                    if op["dma"]:
                        ins.then_inc(dsem[op["slot"]], 16)
                    elif op["used"]:
                        ins.then_inc(esem[e], 1)
                if e == "sp":
                    for i in final_wait_ops:
                        s, v, k = semof(ops[i])
                        engobj.wait_ge(s, v)

            @block.tensor
            def _(eng):
                run_engine("pe", eng)

            @block.scalar
            def _(eng):
                run_engine("act", eng)

            @block.vector
            def _(eng):
                run_engine("dve", eng)

            @block.gpsimd
            def _(eng):
                run_engine("pool", eng)

            @block.sync
            def _(eng):
                run_engine("sp", eng)

F32 = mybir.dt.float32
BF16 = mybir.dt.bfloat16
I32 = mybir.dt.int32
AF = mybir.ActivationFunctionType
ALU = mybir.AluOpType
AX = mybir.AxisListType

D = 1024
S = 2048
NT = 16
LN_EPS = 1e-5
ALPHA = 8.0 ** 0.25
NEGB = -30000.0
SCALE = 0.125
C_RQ, C_RK, C_RV, C_RG = 0, 256, 512, 768
C_CB, C_CC, C_CH = 1024, 1280, 1536
C_NQ = 1792
C_KC, C_VC, C_KS, C_VS, C_KW, C_VW = 2304, 2432, 2560, 2688, 2816, 2944
C_NG = 3072
NIN = 3096
TW = 2432
NE = 32
GELU_A = 1.702


class Rot:
    def __init__(self, items):
        self.items = items
        self.i = 0

    def next(self):
        it = self.items[self.i % len(self.items)]
        self.i += 1
        return it


def host_consts():
    c = {}
    c["ident"] = np.eye(128, dtype=np.float32)
    r = np.arange(128)
    inv = (10000.0 ** (-((r % 64) % 32).astype(np.float64) / 32.0))
    c["ropec"] = np.stack([inv / (2 * np.pi), np.where((r % 64) < 32, -1.0, 1.0)], 1).astype(np.float32)
    gam = 1.0 - 2.0 ** (-5.0 - np.arange(4))
    k = np.arange(128)[:, None]
    m = np.arange(TW)[None, :] - 384
    tabs = []
    for h in range(4):
        e = (m - k).astype(np.float64)
        tabs.append(np.where(e >= 0, np.exp(np.log(gam[h]) * np.maximum(e, 0)) * SCALE, 0.0))
    c["dect"] = np.stack(tabs, 1).astype(ml_dtypes.bfloat16)
    dd = (m - k)
    c["tmask"] = np.stack([(dd >= 0), (dd >= 0) & (dd < 512)], 1).astype(np.float32).astype(ml_dtypes.bfloat16)
    kk = np.arange(128)[:, None]
    qq = np.arange(128)[None, :]
    c["cb_caus"] = np.where(kk > qq, 0.0, 1.0).astype(ml_dtypes.bfloat16)
    c["cb_low"] = np.where(kk <= qq, 0.0, 1.0).astype(ml_dtypes.bfloat16)
    cc = np.arange(128)[:, None]
    tt = np.arange(2048)[None, :]
    c["cb_cmp"] = np.where(16 * cc + 31 > tt, 0.0, 1.0).astype(ml_dtypes.bfloat16)
    cs = np.arange(128) * 16
    cl = cs + 31
    ss = np.arange(32) * 64
    ov = ((cs[:, None] < ss[None, :] + 64) & (ss[None, :] <= cl[:, None])).astype(np.float32)
    ov[127] = 0
    c["overlap"] = ov.astype(ml_dtypes.bfloat16)
    t = (np.arange(16)[None, :, None] * 128 + np.arange(128)[:, None, None])
    j = np.arange(32)[None, None, :]
    c["forced"] = np.where((j == 0) | (j == t // 64), 1e9, 0.0).astype(np.float32)
    b = np.arange(32)[:, None, None]
    kc = np.arange(16)[None, :, None]
    kl = np.arange(128)[None, None, :]
    E = np.zeros((128, 16, 128), np.float32)
    E[0:32] = (b == 2 * kc + (kl >= 64))
    c["Eexp"] = E.astype(ml_dtypes.bfloat16)
    return c


def build(NB=1, NL=1, taps=(), stop=None, nsa_stage=2, nsa_part=9, nsa_hks=(0, 1), nsa_qcs=tuple(range(4)), moe_stop="D", skip_mixer=False, ne_in=NE):
    nc = bass.Bass("TRN2", target_bir_lowering=False)
    R = Rec(nc)
    dram = {}

    def din(name, shape, dt=F32):
        dram[name] = nc.dram_tensor(name, list(shape), dt, kind="ExternalInput").ap()
        return dram[name]

    def dint(name, shape, dt=F32):
        dram[name] = nc.dram_tensor(name, list(shape), dt, kind="Internal").ap()
        return dram[name]

    def dout(name, shape, dt=F32):
        dram[name] = nc.dram_tensor(name, list(shape), dt, kind="ExternalOutput").ap()
        return dram[name]

    x_in = din("x", [NB, S, D])
    c_in = din("c", [NB, D])
    pos_in = din("positions", [NB, S], I32)
    ada_w = din("ada_w", [NL, D, 6 * D])
    ada_b = din("ada_b", [NL, 6 * D])
    w_in = din("w_in", [NL, D, NIN])
    w_out = din("w_out", [NL, D, D])
    gn_w = din("ret_gn_w", [NL, 256])
    conv_w = din("conv_w", [NL, 3, 256])
    cmp_pos = din("cmp_pos", [NL, 2, 32, 64])
    cmp_w1 = din("cmp_w1", [NL, 2, 2048, 128])
    cmp_w2 = din("cmp_w2", [NL, 2, 128, 64])
    ln_g = din("ln_g", [NL, 2, D])
    ln_b = din("ln_b", [NL, 2, D])
    router_w = din("router_w", [NL, D, NE])
    router_b = din("router_b", [NL, NE])
    w_gu = din("w_gate_up", [NL, ne_in, D, 2 * D])
    b_gu = din("b_gate_up", [NL, ne_in, 2 * D])
    w_dn = din("w_down", [NL, ne_in, D, D])
    b_dn = din("b_down", [NL, ne_in, D])
    HC = host_consts()
    HC.update(moe_consts())
    cst = {}
    for k, v in HC.items():
        cst[k] = din("k_" + k, list(v.shape), BF16 if v.dtype == ml_dtypes.bfloat16 else F32)
    out = dout("out", [NB, S, D])
    xbuf = dint("xbuf", [NB, S, D])
    ropebuf = dint("ropebuf", [NB, 2, 128, S])
    NG = NB * 4
    H2d = dint("H2d", [NB, S, D], BF16)
    Gd = dint("Gd", [NB, S, NE])
    CMd = dint("CMd", [NB, S, NE])
    XTd = [dint("XTd%d" % i, [4, 2, 8, 128, 8, 512], BF16) for i in range(NB)]
    PGd = [dint("PGd%d" % i, [4, 2, NE, 128, 512], BF16) for i in range(NB)]
    Yd = [dint("Yd%d" % i, [4, 2, NE, 128, D], BF16) for i in range(NB)]
    tap_t = {}
    for (nm, shape, dt) in taps:
        tap_t[nm] = dout("tap_" + nm, shape, dt)
    fin = []

    top = contextlib.ExitStack()

    uniq = [0]

    def mk(stack):
        def sbuf(name, shape, dt=F32):
            uniq[0] += 1
            return stack.enter_context(nc.sbuf_tensor("%s_%d" % (name, uniq[0]), list(shape), dt))

        def psum(name, shape, dt=F32):
            uniq[0] += 1
            return stack.enter_context(nc.psum_tensor("%s_%d" % (name, uniq[0]), list(shape), dt))
        return sbuf, psum

    with top:
        sbuf, psum = mk(top)
        ident_f = sbuf("ident_f", [128, 128])
        ident_b = sbuf("ident_b", [128, 128], BF16)
        R.dma("sp", lambda e: e.dma_start(out=ident_f[:], in_=cst["ident"][:, :]), writes=["ident_f"])
        R.dve(lambda e: e.tensor_copy(out=ident_b[:], in_=ident_f[:]), reads=["ident_f"], writes=["ident_b"])
        ropec = sbuf("ropec", [128, 2])
        R.dma("sp", lambda e: e.dma_start(out=ropec[:], in_=cst["ropec"][:, :]), writes=["ropec"])
        modbuf = dint("modbuf", [NB, NL, 6 * D])

        with contextlib.ExitStack() as st0:
            sb0, ps0 = mk(st0)
            c_sb = sb0("c_sb", [NB, D])
            cT = sb0("cT", [128, 8, NB], BF16)
            R.dma("sp", lambda e: e.dma_start(out=c_sb[:], in_=c_in[:, :]), writes=["c_sb"])
            R.act(lambda e: e.activation(out=c_sb[:], in_=c_sb[:], func=AF.Silu), reads=["c_sb"], writes=["c_sb"])
            ps_t = ps0("ps_t", [128, 512])
            for kc in range(8):
                R.pe(lambda e, kc=kc: e.transpose(out=ps_t[:, kc * NB:(kc + 1) * NB], in_=c_sb[:, kc * 128:(kc + 1) * 128],
                                                 identity=ident_f[0:NB, 0:NB]),
                     reads=["c_sb", "ident_f"], writes=["ps_t"])
            R.dve(lambda e: e.tensor_copy(out=cT[:].rearrange("p k b -> p (k b)"), in_=ps_t[:, 0:8 * NB]),
                  reads=["ps_t"], writes=["cT"])
            mods = sb0("mods", [NB, 6 * D])
            adab = sb0("adab", [NB, 6 * D])
            adaw = Rot([("adaw%d" % i, sb0("adaw%d" % i, [128, 8, 512], BF16)) for i in range(2)])
            psm = Rot([("ps_m%d" % i, ps0("ps_m%d" % i, [128, 512])) for i in range(2)])
            for l in range(NL):
                for b in range(NB):
                    R.dma("sp", lambda e, b=b, l=l: e.dma_start(out=adab[b:b + 1, :], in_=ada_b[l:l + 1, :]), writes=["adab"])
                for cc in range(12):
                    wn, wb = adaw.next()
                    pn, pm = psm.next()
                    R.dma("pool", lambda e, wb=wb, l=l, cc=cc: e.dma_start(
                        out=wb[:], in_=ada_w[l, :, cc * 512:(cc + 1) * 512].rearrange("(k p) n -> p k n", p=128)),
                        writes=[wn])
                    for kc in range(8):
                        R.pe(lambda e, wb=wb, pm=pm, kc=kc: e.matmul(pm[0:NB, :], lhsT=cT[:, kc, :], rhs=wb[:, kc, :],
                                                                     start=(kc == 0), stop=(kc == 7)),
                             reads=["cT", wn], writes=[pn])
                    R.dve(lambda e, pm=pm, cc=cc: e.tensor_tensor(
                        out=mods[:, cc * 512:(cc + 1) * 512], in0=pm[0:NB, :], in1=adab[:, cc * 512:(cc + 1) * 512], op=ALU.add),
                        reads=[pn, "adab"], writes=["mods"])
                R.dma("sp", lambda e, l=l: e.dma_start(out=modbuf[:, l, :], in_=mods[:]), reads=["mods"], writes=["modbuf"])
            posi = sb0("posi", [128, S], I32)
            y0 = sb0("y0", [128, S])
            r1 = sb0("r1", [128, S])
            ki = sb0("ki", [128, S], I32)
            kf = sb0("kf", [128, S])
            for b in range(NB):
                R.dma("sp", lambda e, b=b: e.dma_start(out=posi[:], in_=pos_in[b:b + 1, :].broadcast_to([128, S])), writes=["posi"])
                R.dve(lambda e: e.tensor_copy(out=y0[:], in_=posi[:]), reads=["posi"], writes=["y0"])
                R.dve(lambda e: e.tensor_scalar(out=y0[:], in0=y0[:], scalar1=ropec[:, 0:1], scalar2=None, op0=ALU.mult),
                      reads=["y0", "ropec"], writes=["y0"])
                for which in range(2):
                    sh = 0.25 if which == 0 else 0.0
                    R.dve(lambda e, sh=sh: e.tensor_scalar(out=r1[:], in0=y0[:], scalar1=sh, scalar2=None, op0=ALU.add),
                          reads=["y0"], writes=["r1"])
                    R.dve(lambda e: e.tensor_copy(out=ki[:], in_=r1[:]), reads=["r1"], writes=["ki"])
                    R.dve(lambda e: e.tensor_copy(out=kf[:], in_=ki[:]), reads=["ki"], writes=["kf"])
                    R.dve(lambda e: e.tensor_tensor(out=r1[:], in0=r1[:], in1=kf[:], op=ALU.subtract), reads=["r1", "kf"], writes=["r1"])
                    R.dve(lambda e: e.tensor_single_scalar(out=kf[:], in_=r1[:], scalar=0.5, op=ALU.is_gt), reads=["r1"], writes=["kf"])
                    R.dve(lambda e: e.tensor_tensor(out=r1[:], in0=r1[:], in1=kf[:], op=ALU.subtract), reads=["r1", "kf"], writes=["r1"])
                    R.dve(lambda e: e.tensor_single_scalar(out=kf[:], in_=r1[:], scalar=-0.5, op=ALU.is_lt), reads=["r1"], writes=["kf"])
                    R.dve(lambda e: e.tensor_tensor(out=r1[:], in0=r1[:], in1=kf[:], op=ALU.add), reads=["r1", "kf"], writes=["r1"])
                    R.act(lambda e: e.activation(out=r1[:], in_=r1[:], func=AF.Sin, scale=2 * math.pi), reads=["r1"], writes=["r1"])
                    if which == 1:
                        R.dve(lambda e: e.tensor_scalar(out=r1[:], in0=r1[:], scalar1=ropec[:, 1:2], scalar2=None, op0=ALU.mult),
                              reads=["r1", "ropec"], writes=["r1"])
                    R.dma("sp", lambda e, b=b, which=which: e.dma_start(out=ropebuf[b, which, :, :], in_=r1[:]),
                          reads=["r1"], writes=["ropebuf%d" % b])
        R.barrier()
        if "mods" in tap_t:
            fin.append(R.dma("sp", lambda e: e.dma_start(out=tap_t["mods"][:, :, :], in_=modbuf[:, :, :]), reads=["modbuf"], writes=["tap_mods"]))
        if "rope" in tap_t:
            fin.append(R.dma("sp", lambda e: e.dma_start(out=tap_t["rope"][:, :, :], in_=ropebuf[0, :, :, :]), reads=["ropebuf0"], writes=["tap_rope"]))

        if stop == "s0":
            R.emit(final_wait_ops=fin)
            return nc

        for l in range(NL):
            x_src = x_in if l == 0 else xbuf
            for b in (range(NB) if not skip_mixer else ()):
                with contextlib.ExitStack() as stm:
                    mixer(nc, R, mk(stm), locals())
                R.barrier()
            if stop == "mix":
                break
            moe(nc, R, mk, locals())
        R.emit(final_wait_ops=fin)
    return nc


def mixer(nc, R, mkp, env):
    sbuf, psum = mkp
    mk = env["mk"]
    l, b = env["l"], env["b"]
    x_src, xbuf, modbuf = env["x_src"], env["xbuf"], env["modbuf"]
    ident_f, ident_b, cst, tap_t, fin = env["ident_f"], env["ident_b"], env["cst"], env["tap_t"], env["fin"]
    w_in, w_out, gn_w, conv_w = env["w_in"], env["w_out"], env["gn_w"], env["conv_w"]
    cmp_pos, cmp_w1, cmp_w2, ln_g, ln_b, ropebuf = env["cmp_pos"], env["cmp_w1"], env["cmp_w2"], env["ln_g"], env["ln_b"], env["ropebuf"]
    NB = env["NB"]

    pA = Rot([("pA%d" % i, psum("pA%d" % i, [128, 512])) for i in range(2)])
    pT = psum("pT", [128, 1024], BF16)
    pO = Rot([("pO%d" % i, psum("pO%d" % i, [128, 512])) for i in range(4)])
    pX = psum("pX", [128, 512])

    def bcast_mod(sbx, dname, which, plus1):
        dst = sbx(dname, [128, D])
        R.dma("sp", lambda e: e.dma_start(out=dst[:], in_=modbuf[b, l:l + 1, which * D:(which + 1) * D].broadcast_to([128, D])),
              reads=["modbuf"], writes=[dname])
        if plus1:
            R.pool(lambda e: e.tensor_scalar(out=dst[:], in0=dst[:], scalar1=1.0, scalar2=None, op0=ALU.add), reads=[dname], writes=[dname])
        return dst

    sh1 = bcast_mod(sbuf, "sh1", 0, False)
    sc1 = bcast_mod(sbuf, "sc1", 1, True)

    hT = sbuf("hT", [128, 8, S], BF16)
    yT = sbuf("yT", [128, 8, S], BF16)
    xc_r = Rot([("xc%d" % i, sbuf("xc%d" % i, [128, D])) for i in range(2)])
    hc_r = Rot([("hc%d" % i, sbuf("hc%d" % i, [128, D], BF16)) for i in range(2)])
    st_r = Rot([("st%d" % i, sbuf("st%d" % i, [128, 16])) for i in range(2)])

    def layer_norm_chunk(xcn, xc, stn, stt, xnn, xn):
        R.dve(lambda e: e.bn_stats(out=stt[:, 0:6], in_=xc[:, 0:512]), reads=[xcn], writes=[stn])
        R.dve(lambda e: e.bn_stats(out=stt[:, 6:12], in_=xc[:, 512:1024]), reads=[xcn], writes=[stn])
        R.dve(lambda e: e.bn_aggr(out=stt[:, 12:14], in_=stt[:, 0:12]), reads=[stn], writes=[stn])
        R.act(lambda e: e.activation(out=stt[:, 14:15], in_=stt[:, 13:14], func=AF.Sqrt, bias=LN_EPS, scale=1.0), reads=[stn], writes=[stn])
        R.dve(lambda e: e.reciprocal(out=stt[:, 15:16], in_=stt[:, 14:15]), reads=[stn], writes=[stn])
        R.dve(lambda e: e.tensor_scalar(out=xn[:], in0=xc[:], scalar1=stt[:, 12:13], scalar2=stt[:, 15:16], op0=ALU.subtract, op1=ALU.mult),
              reads=[xcn, stn], writes=[xnn])

    for tc in range(NT):
        xcn, xc = xc_r.next()
        xnn, xn = xcn, xc
        hcn, hc = hc_r.next()
        stn, stt = st_r.next()
        R.dma("sp", lambda e, xc=xc, tc=tc: e.dma_start(out=xc[:], in_=x_src[b, tc * 128:(tc + 1) * 128, :]), reads=["xb%d_%d" % (b, tc)], writes=[xcn])
        layer_norm_chunk(xcn, xc, stn, stt, xnn, xn)
        R.pool(lambda e, xn=xn: e.tensor_tensor(out=xn[:], in0=xn[:], in1=sc1[:], op=ALU.mult), reads=[xnn, "sc1"], writes=[xnn])
        R.dve(lambda e, xn=xn, hc=hc: e.tensor_tensor(out=hc[:], in0=xn[:], in1=sh1[:], op=ALU.add), reads=[xnn, "sh1"], writes=[hcn])
        for kc in range(8):
            R.pe(lambda e, hc=hc, kc=kc: e.transpose(out=pT[:, kc * 128:(kc + 1) * 128], in_=hc[:, kc * 128:(kc + 1) * 128], identity=ident_b[:]),
                 reads=[hcn, "ident_b"], writes=["pT"])
        R.act(lambda e, tc=tc: e.copy(out=hT[:, :, tc * 128:(tc + 1) * 128], in_=pT[:].rearrange("p (k t) -> p k t", k=8)),
              reads=["pT"], writes=["hT"])
    if "hT" in tap_t and b == 0 and l == 0:
        fin.append(R.dma("sp", lambda e: e.dma_start(out=tap_t["hT"][:, :, :], in_=hT[:]), reads=["hT"], writes=["tap_hT"]))

    wr = Rot([("W%d" % i, sbuf("W%d" % i, [128, 8, 128], BF16)) for i in range(4)])

    def load_w(pieces):
        wn, wb = wr.next()
        for (do, sc, wd) in pieces:
            R.dma("pool", lambda e, wb=wb, do=do, sc=sc, wd=wd: e.dma_start(
                out=wb[:, :, do:do + wd], in_=w_in[l, :, sc:sc + wd].rearrange("(k p) n -> p k n", p=128)), writes=[wn])
        return wn, wb

    def perm_pieces(c0):
        return [(0, c0 + 32, 32), (32, c0, 32), (64, c0 + 96, 32), (96, c0 + 64, 32)]

    def dup_pieces(c0):
        return [(0, c0, 64), (64, c0, 64)]

    def dup_perm_pieces(c0):
        return [(0, c0 + 32, 32), (32, c0, 32), (64, c0 + 32, 32), (96, c0, 32)]

    def proj_fm(wn, wb, tg):
        pn, pt = pA.next()
        for kc in range(8):
            R.pe(lambda e, kc=kc: e.matmul(pt[:, :], lhsT=wb[:, kc, :], rhs=hT[:, kc, tg * 512:(tg + 1) * 512], start=(kc == 0), stop=(kc == 7)),
                 reads=[wn, "hT"], writes=[pn])
        return pn, pt

    def proj_plain(pieces, dst, dname, dsl):
        wn, wb = load_w(pieces)
        for tg in range(4):
            pn, pt = proj_fm(wn, wb, tg)
            R.act(lambda e, pt=pt, tg=tg: e.copy(out=dst[:, dsl, tg * 512:(tg + 1) * 512] if dsl is not None else dst[:, tg * 512:(tg + 1) * 512], in_=pt[:, :]),
                  reads=[pn], writes=[dname])

    with contextlib.ExitStack() as sc_:
        sb1, _ = mk(sc_)
        fm = [sb1("fm%d" % i, [128, 2, S], BF16) for i in range(3)]
        tmpA = sb1("tmpA", [128, S])
        tmpB = sb1("tmpB", [128, S])
        cw = sb1("cw", [128, 2, 3])
        for ch_ in range(2):
            for k_ in range(3):
                R.dma("sp", lambda e, ch_=ch_, k_=k_: e.dma_start(out=cw[:, ch_, k_:k_ + 1], in_=conv_w[l, k_, ch_ * 128:(ch_ + 1) * 128].rearrange("(p o) -> p o", o=1)), writes=["cw"])
        for i, c0 in enumerate((C_CB, C_CC, C_CH)):
            for ch in range(2):
                proj_plain([(0, c0 + ch * 128, 128)], fm[i], "fm%d" % i, ch)
        for ch in range(2):
            R.pool(lambda e, ch=ch: e.tensor_tensor(out=tmpA[:], in0=fm[1][:, ch, :], in1=fm[2][:, ch, :], op=ALU.mult),
                   reads=["fm1", "fm2"], writes=["tmpA"])
            R.dve(lambda e, ch=ch: e.tensor_scalar(out=tmpB[:], in0=tmpA[:], scalar1=cw[:, ch, 2:3], scalar2=None, op0=ALU.mult),
                  reads=["tmpA", "cw"], writes=["tmpB"])
            R.dve(lambda e, ch=ch: e.scalar_tensor_tensor(out=tmpB[:, 1:S], in0=tmpA[:, 0:S - 1], scalar=cw[:, ch, 1:2], in1=tmpB[:, 1:S],
                                                          op0=ALU.mult, op1=ALU.add), reads=["tmpA", "tmpB", "cw"], writes=["tmpB"])
            R.dve(lambda e, ch=ch: e.scalar_tensor_tensor(out=tmpB[:, 2:S], in0=tmpA[:, 0:S - 2], scalar=cw[:, ch, 0:1], in1=tmpB[:, 2:S],
                                                          op0=ALU.mult, op1=ALU.add), reads=["tmpA", "tmpB", "cw"], writes=["tmpB"])
            R.pool(lambda e, ch=ch: e.tensor_tensor(out=yT[:, 2 + ch, :], in0=tmpB[:], in1=fm[0][:, ch, :], op=ALU.mult),
                   reads=["tmpB", "fm0"], writes=["yT"])
    R.barrier()

    def load_rope(sbx):
        cosT = sbx("cosT", [128, S])
        sinT = sbx("sinT", [128, S])
        R.dma("sp", lambda e: e.dma_start(out=cosT[:], in_=ropebuf[b, 0, :, :]), reads=["ropebuf%d" % b], writes=["cosT"])
        R.dma("sp", lambda e: e.dma_start(out=sinT[:], in_=ropebuf[b, 1, :, :]), reads=["ropebuf%d" % b], writes=["sinT"])
        rt = Rot([("rt%d" % i, sbx("rt%d" % i, [128, 512])) for i in range(4)])
        return cosT, sinT, rt

    def proj_rope(p_norm, p_perm, dst, dname, dsl, cosT, sinT, rt):
        wn1, wb1 = load_w(p_norm)
        wn2, wb2 = load_w(p_perm)
        for tg in range(4):
            pn1, pt1 = proj_fm(wn1, wb1, tg)
            pn2, pt2 = proj_fm(wn2, wb2, tg)
            t1n, t1 = rt.next()
            t2n, t2 = rt.next()
            R.dve(lambda e, pt1=pt1, t1=t1, tg=tg: e.tensor_tensor(out=t1[:], in0=pt1[:, :], in1=cosT[:, tg * 512:(tg + 1) * 512], op=ALU.mult),
                  reads=[pn1, "cosT"], writes=[t1n])
            R.dve(lambda e, pt2=pt2, t2=t2, tg=tg: e.tensor_tensor(out=t2[:], in0=pt2[:, :], in1=sinT[:, tg * 512:(tg + 1) * 512], op=ALU.mult),
                  reads=[pn2, "sinT"], writes=[t2n])
            R.pool(lambda e, t1=t1, t2=t2, tg=tg: e.tensor_tensor(
                out=dst[:, dsl, tg * 512:(tg + 1) * 512] if dsl is not None else dst[:, tg * 512:(tg + 1) * 512], in0=t1[:], in1=t2[:], op=ALU.add),
                reads=[t1n, t2n], writes=[dname])

    with contextlib.ExitStack() as sc_:
        sb2, _ = mk(sc_)
        cosT, sinT, rt = load_rope(sb2)
        qTr = sb2("qTr", [128, 2, S], BF16)
        kTr = sb2("kTr", [128, 2, S], BF16)
        vrg = sb2("vrg", [128, NT, 512], BF16)
        dect = sb2("dect", [128, 4, TW], BF16)
        gnw = sb2("gnw", [128, 256])
        R.dma("sp", lambda e: e.dma_start(out=dect[:], in_=cst["dect"][:, :, :]), writes=["dect"])
        R.dma("sp", lambda e: e.dma_start(out=gnw[:], in_=gn_w[l:l + 1, :].broadcast_to([128, 256])), writes=["gnw"])
        for ch in range(2):
            proj_rope([(0, C_RQ + ch * 128, 128)], perm_pieces(C_RQ + ch * 128), qTr, "qTr", ch, cosT, sinT, rt)
            proj_rope([(0, C_RK + ch * 128, 128)], perm_pieces(C_RK + ch * 128), kTr, "kTr", ch, cosT, sinT, rt)
        wv = sb2("wv", [128, 8, 512], BF16)
        R.dma("pool", lambda e: e.dma_start(out=wv[:], in_=w_in[l, :, C_RV:C_RV + 512].rearrange("(k p) n -> p k n", p=128)), writes=["wv"])
        for tc in range(NT):
            pn, pt = pA.next()
            for kc in range(8):
                R.pe(lambda e, pt=pt, kc=kc, tc=tc: e.matmul(pt[:, :], lhsT=hT[:, kc, tc * 128:(tc + 1) * 128], rhs=wv[:, kc, :],
                                                             start=(kc == 0), stop=(kc == 7)), reads=["hT", "wv"], writes=[pn])
            R.act(lambda e, pt=pt, tc=tc: e.copy(out=vrg[:, tc, 0:256], in_=pt[:, 0:256]), reads=[pn], writes=["vrg"])
            R.act(lambda e, pt=pt, tc=tc: e.activation(out=vrg[:, tc, 256:512], in_=pt[:, 256:512], func=AF.Silu), reads=[pn], writes=["vrg"])
        smr = Rot([("sm%d" % i, sb2("sm%d" % i, [128, 512], BF16)) for i in range(3)])
        gA = sb2("gA", [128, 1024])
        gB = sb2("gB", [128, 1024])
        gs = sb2("gs", [128, 64])
        yr = sb2("yr", [128, 4, 256], BF16)
        for qg in range(4):
            raccn = ["pO0", "pO1"]
            racc = [pO.items[0][1], pO.items[1][1]]
            first = [True, True]
            for h in range(4):
                ch, ro = h // 2, (h % 2) * 64
                for kc in range(4 * qg + 4):
                    pn, pt = pA.next()
                    R.pe(lambda e, pt=pt, kc=kc, ch=ch, ro=ro, qg=qg: e.matmul(
                        pt[:, :], lhsT=kTr[ro:ro + 64, ch, kc * 128:(kc + 1) * 128], rhs=qTr[ro:ro + 64, ch, qg * 512:(qg + 1) * 512],
                        start=True, stop=True), reads=["kTr", "qTr"], writes=[pn])
                    smn, sm = smr.next()
                    off = 384 + qg * 512 - kc * 128
                    R.dve(lambda e, pt=pt, sm=sm, h=h, off=off: e.tensor_tensor(out=sm[:], in0=pt[:, :], in1=dect[:, h, off:off + 512], op=ALU.mult),
                          reads=[pn, "dect"], writes=[smn])
                    for j in range(4):
                        qc = 4 * qg + j
                        if kc > qc:
                            continue
                        bk = j // 2
                        st_flag = first[bk]
                        first[bk] = False
                        R.pe(lambda e, sm=sm, j=j, h=h, kc=kc, bk=bk, st_flag=st_flag: e.matmul(
                            racc[bk][:, (j % 2) * 256 + h * 64:(j % 2) * 256 + h * 64 + 64], lhsT=sm[:, j * 128:(j + 1) * 128],
                            rhs=vrg[:, kc, h * 64:h * 64 + 64], start=st_flag, stop=False, skip_group_check=True),
                            reads=[smn, "vrg"], writes=[raccn[bk]])
            for bk in range(2):
                R.act(lambda e, bk=bk: e.copy(out=gA[:, bk * 512:(bk + 1) * 512], in_=racc[bk][:, :]), reads=[raccn[bk]], writes=["gA"])
            g3 = gA[:].rearrange("p (a d) -> p a d", d=64)
            b3 = gB[:].rearrange("p (a d) -> p a d", d=64)
            R.dve(lambda e: e.tensor_reduce(out=gs[:, 0:16], in_=g3, axis=AX.X, op=ALU.add), reads=["gA"], writes=["gs"])
            R.dve(lambda e: e.tensor_scalar(out=gs[:, 0:16], in0=gs[:, 0:16], scalar1=1.0 / 64, scalar2=None, op0=ALU.mult), reads=["gs"], writes=["gs"])
            R.dve(lambda e: e.tensor_tensor(out=g3, in0=g3, in1=gs[:, 0:16].unsqueeze(2).broadcast_to([128, 16, 64]), op=ALU.subtract),
                  reads=["gA", "gs"], writes=["gA"])
            R.pool(lambda e: e.tensor_tensor(out=gB[:], in0=gA[:], in1=gA[:], op=ALU.mult), reads=["gA"], writes=["gB"])
            R.dve(lambda e: e.tensor_reduce(out=gs[:, 16:32], in_=b3, axis=AX.X, op=ALU.add), reads=["gB"], writes=["gs"])
            R.act(lambda e: e.activation(out=gs[:, 32:48], in_=gs[:, 16:32], func=AF.Sqrt, bias=LN_EPS, scale=1.0 / 64), reads=["gs"], writes=["gs"])
            R.dve(lambda e: e.reciprocal(out=gs[:, 48:64], in_=gs[:, 32:48]), reads=["gs"], writes=["gs"])
            R.dve(lambda e: e.tensor_tensor(out=g3, in0=g3, in1=gs[:, 48:64].unsqueeze(2).broadcast_to([128, 16, 64]), op=ALU.mult),
                  reads=["gA", "gs"], writes=["gA"])
            g4 = gA[:].rearrange("p (j c) -> p j c", c=256)
            R.pool(lambda e: e.tensor_tensor(out=g4, in0=g4, in1=gnw[:].unsqueeze(1).broadcast_to([128, 4, 256]), op=ALU.mult),
                   reads=["gA", "gnw"], writes=["gA"])
            R.dve(lambda e, qg=qg: e.tensor_tensor(out=yr[:], in0=g4, in1=vrg[:, 4 * qg:4 * qg + 4, 256:512], op=ALU.mult),
                  reads=["gA", "vrg"], writes=["yr"])
            for j in range(4):
                for ch in range(2):
                    R.pe(lambda e, j=j, ch=ch: e.transpose(out=pT[:, (j * 2 + ch) * 128:(j * 2 + ch + 1) * 128], in_=yr[:, j, ch * 128:(ch + 1) * 128],
                                                         identity=ident_b[:]), reads=["yr", "ident_b"], writes=["pT"])
            R.act(lambda e, qg=qg: e.copy(out=yT[:, 0:2, qg * 512:(qg + 1) * 512].rearrange("p c (j t) -> p j c t", j=4),
                                          in_=pT[:].rearrange("p (j c t) -> p j c t", j=4, c=2)), reads=["pT"], writes=["yT"])
    R.barrier()

    nsa_stage = env.get("nsa_stage", 2)
    nsa_part = env.get("nsa_part", 9)
    nsa_hks = env.get("nsa_hks", (0, 1))
    nsa_qcs = env.get("nsa_qcs", tuple(range(4)))
    with contextlib.ExitStack() as sc_:
        sb3, _ = mk(sc_)
        cosT, sinT, rt = load_rope(sb3)
        kcT = sb3("kcT", [128, S], BF16)
        vcT = sb3("vcT", [128, S], BF16)
        gates = sb3("gates", [128, NT, 24])
        v2 = sb3("v2", [128, NT, 4, 128], BF16)
        qT = sb3("qT", [128, 2, S], BF16)
        qTr2 = sb3("qTr2", [128, 2, S], BF16)
        ksT = sb3("ksT", [128, S], BF16)
        kwT = sb3("kwT", [128, S], BF16)
        w2k = sb3("w2k", [128, 128], BF16)
        w2v = sb3("w2v", [128, 64], BF16)
        pbias = sb3("pbias", [128, 2])
        kcd = [sb3("kcd%d" % i, [128, 128], BF16) for i in range(2)]
        vca = sb3("vca", [128, 2, 128], BF16)
        ovl = sb3("ovl", [128, 32], BF16)
        scC = contextlib.ExitStack()
        sbC, _ = mk(scC)
        w1d = [sbC("w1d%d" % i, [128, 32, 128], BF16) for i in range(2)]
        posr = sbC("posr", [32, 2, 64])
        posT = sbC("posT", [64, 2, 32], BF16)
        R.dma("sp", lambda e: e.dma_start(out=ovl[:], in_=cst["overlap"][:, :]), writes=["ovl"])
        for kind in range(2):
            for cp in range(2):
                R.dma("pool", lambda e, kind=kind, cp=cp: e.dma_start(
                    out=w1d[kind][cp * 64:(cp + 1) * 64, :, :], in_=cmp_w1[l, kind, :, :].rearrange("(l d) n -> d l n", d=64)), writes=["w1d%d" % kind])
            R.dma("sp", lambda e, kind=kind: e.dma_start(out=posr[:, kind, :], in_=cmp_pos[l, kind, :, :]), writes=["posr"])
        R.dma("pool", lambda e: e.dma_start(out=w2k[:, 0:64], in_=cmp_w2[l, 0, :, :]), writes=["w2k"])
        R.dma("pool", lambda e: e.dma_start(out=w2k[:, 64:128], in_=cmp_w2[l, 0, :, :]), writes=["w2k"])
        R.dma("pool", lambda e: e.dma_start(out=w2v[:], in_=cmp_w2[l, 1, :, :]), writes=["w2v"])
        proj_plain([(0, C_KC, 128)], kcT, "kcT", None)
        proj_plain([(0, C_VC, 128)], vcT, "vcT", None)
        wt = sbC("wt", [128, 8, 408], BF16)
        R.dma("pool", lambda e: e.dma_start(out=wt[:], in_=w_in[l, :, C_VS:C_VS + 408].rearrange("(k p) n -> p k n", p=128)), writes=["wt"])
        R.pool(lambda e: e.memset(v2[:].rearrange("p a b c -> p (a b) c")[:, :, 64:128], 1.0), writes=["v2"])
        for tc in range(NT):
            pn, pt = pA.next()
            for kc in range(8):
                R.pe(lambda e, pt=pt, kc=kc, tc=tc: e.matmul(pt[:, 0:408], lhsT=hT[:, kc, tc * 128:(tc + 1) * 128], rhs=wt[:, kc, :],
                                                             start=(kc == 0), stop=(kc == 7)), reads=["hT", "wt"], writes=[pn])
            R.act(lambda e, pt=pt, tc=tc: e.copy(out=v2[:, tc, 0:2, 0:64], in_=pt[:, 0:128].rearrange("p (h d) -> p h d", h=2)), reads=[pn], writes=["v2"])
            R.act(lambda e, pt=pt, tc=tc: e.copy(out=v2[:, tc, 2:4, 0:64], in_=pt[:, 256:384].rearrange("p (h d) -> p h d", h=2)), reads=[pn], writes=["v2"])
            R.act(lambda e, pt=pt, tc=tc: e.activation(out=gates[:, tc, :], in_=pt[:, 384:408], func=AF.Sigmoid), reads=[pn], writes=["gates"])
        for kind in range(2):
            R.pe(lambda e, kind=kind: e.transpose(out=pX[0:64, kind * 32:(kind + 1) * 32], in_=posr[:, kind, :], identity=ident_f[0:32, 0:32]),
                 reads=["posr", "ident_f"], writes=["pX"])
        R.dve(lambda e: e.tensor_copy(out=posT[:].rearrange("p k l -> p (k l)"), in_=pX[0:64, 0:64]), reads=["pX"], writes=["posT"])
        for kind in range(2):
            for li in range(32):
                R.pe(lambda e, kind=kind, li=li: e.matmul(pX[:, 64 + kind:65 + kind], lhsT=w1d[kind][0:64, li, :], rhs=posT[:, kind, li:li + 1],
                                                          start=(li == 0 and kind == 0), stop=(li == 31), skip_group_check=True),
                     reads=["w1d%d" % kind, "posT"], writes=["pX"])
        R.dve(lambda e: e.tensor_copy(out=pbias[:], in_=pX[:, 64:66]), reads=["pX"], writes=["pbias"])
        zt = sbC("zt", [128, 128])
        z2 = sbC("z2", [128, 128])
        blk = sbC("blk", [128, 32, 127], BF16)
        gT = sbC("gT", [128, 128], BF16)
        R.pool(lambda e: e.memset(vca[:], 0.0), writes=["vca"])
        for i_ in range(2):
            R.pool(lambda e, i_=i_: e.memset(kcd[i_][:], 0.0), writes=["kcd"])
        for kind in range(2):
            src = kcT if kind == 0 else vcT
            srcn = "kcT" if kind == 0 else "vcT"
            for li in range(32):
                R.dve(lambda e, li=li, src=src: e.tensor_copy(out=blk[:, li, :], in_=src[:, li:li + 2017:16]), reads=[srcn], writes=["blk"])
            for hk in range(2):
                ro = hk * 64
                pn, pt = pA.next()
                for li in range(32):
                    R.pe(lambda e, pt=pt, kind=kind, li=li, ro=ro: e.matmul(
                        pt[:, 0:127], lhsT=w1d[kind][ro:ro + 64, li, :], rhs=blk[ro:ro + 64, li, :], start=(li == 0), stop=(li == 31)),
                        reads=["w1d%d" % kind, "blk"], writes=[pn])
                R.act(lambda e, pt=pt, kind=kind: e.activation(out=zt[:, 0:127], in_=pt[:, 0:127], func=AF.Identity, bias=pbias[:, kind:kind + 1], scale=1.0),
                      reads=[pn, "pbias"], writes=["zt"])
                R.dve(lambda e: e.tensor_tensor(out=z2[:, 0:127], in0=zt[:, 0:127], in1=zt[:, 0:127], op=ALU.mult), reads=["zt"], writes=["z2"])
                R.dve(lambda e: e.tensor_scalar(out=z2[:, 0:127], in0=z2[:, 0:127], scalar1=0.044715, scalar2=1.0, op0=ALU.mult, op1=ALU.add),
                      reads=["z2"], writes=["z2"])
                R.dve(lambda e: e.tensor_tensor(out=z2[:, 0:127], in0=z2[:, 0:127], in1=zt[:, 0:127], op=ALU.mult), reads=["z2", "zt"], writes=["z2"])
                R.act(lambda e: e.activation(out=z2[:, 0:127], in_=z2[:, 0:127], func=AF.Sigmoid, scale=1.5957691216), reads=["z2"], writes=["z2"])
                R.dve(lambda e: e.tensor_tensor(out=gT[:, 0:127], in0=z2[:, 0:127], in1=zt[:, 0:127], op=ALU.mult), reads=["z2", "zt"], writes=["gT"])
                if kind == 0:
                    R.pe(lambda e: e.matmul(pX[:, 128:255], lhsT=w2k[:, :], rhs=gT[:, 0:127], start=True, stop=True), reads=["w2k", "gT"], writes=["pX"])
                    R.act(lambda e, hk=hk: e.copy(out=kcd[hk][:, 0:127], in_=pX[:, 128:255]), reads=["pX"], writes=["kcd"])
                else:
                    R.pe(lambda e: e.matmul(pX[0:127, 256:320], lhsT=gT[:, 0:127], rhs=w2v[:, :], start=True, stop=True), reads=["w2v", "gT"], writes=["pX"])
                    R.act(lambda e, hk=hk: e.copy(out=vca[0:127, hk, 0:64], in_=pX[0:127, 256:320]), reads=["pX"], writes=["vca"])
        for hk in range(2):
            R.pool(lambda e, hk=hk: e.memset(vca[:, hk, 64:96], 1.0), reads=[], writes=["vca"])
            R.pool(lambda e, hk=hk: e.tensor_copy(out=vca[:, hk, 96:128], in_=ovl[:]), reads=["ovl"], writes=["vca"])

        if "kcd" in tap_t and b == 0 and l == 0:
            for i_ in range(2):
                fin.append(R.dma("sp", lambda e, i_=i_: e.dma_start(out=tap_t["kcd"][:, i_, :], in_=kcd[i_][:]), reads=["kcd"], writes=["tap_kcd%d" % i_]))
            fin.append(R.dma("sp", lambda e: e.dma_start(out=tap_t["vca"][:, :, :], in_=vca[:]), reads=["vca"], writes=["tap_vca"]))
        R.barrier()
        scC.close()
        cbc = sb3("cbc", [128, S], BF16)
        forced = sb3("forced", [128, NT, 32])
        Eexp = sb3("Eexp", [128, NT, 128], BF16)
        tmask = sb3("tmask", [128, 2, TW], BF16)
        R.dma("sp", lambda e: e.dma_start(out=cbc[:], in_=cst["cb_cmp"][:, :]), writes=["cbc"])
        R.dma("sp", lambda e: e.dma_start(out=forced[:], in_=cst["forced"][:, :, :]), writes=["forced"])
        R.dma("sp", lambda e: e.dma_start(out=Eexp[:], in_=cst["Eexp"][:, :, :]), writes=["Eexp"])
        R.dma("sp", lambda e: e.dma_start(out=tmask[:], in_=cst["tmask"][:, :, :]), writes=["tmask"])
        ptr = Rot([("PT%d" % i, sb3("PT%d" % i, [128, 512], BF16)) for i in range(3)])
        mk_r = Rot([("mk%d" % i, sb3("mk%d" % i, [128, 512], BF16)) for i in range(2)])
        selq = sb3("selq", [128, 4, 128], BF16)
        selT = sb3("selT", [128, 512], BF16)
        on = sb3("on", [128, 4, 4, 64])
        tmpo = sb3("tmpo", [128, 4, 64])
        impa = sb3("impa", [128, 4, 32])
        tmpi = sb3("tmpi", [128, 4, 32])
        nst = sb3("nst", [128, 64])
        ynb = sb3("ynb", [128, 4, 256], BF16)
        R.pool(lambda e: e.memset(selq[:], 0.0), writes=["selq"])

        def smm(pn, pt, Ktile, kn, k0, Qsrc, qn, ro, ch, qg):
            R.pe(lambda e: e.matmul(pt[:, :], lhsT=Ktile[ro:ro + 64, k0:k0 + 128], rhs=Qsrc[ro:ro + 64, ch, qg * 512:(qg + 1) * 512],
                                    start=True, stop=True), reads=[kn, qn], writes=[pn])

        def finish(an, acc, g, qg, hk, br, first):
            a3 = acc[:, :].rearrange("p (j c) -> p j c", j=4)
            R.dve(lambda e: e.tensor_scalar(out=nst[:, 0:4], in0=a3[:, :, 64], scalar1=1e-30, scalar2=None, op0=ALU.max), reads=[an], writes=["nst"])
            R.dve(lambda e: e.reciprocal(out=nst[:, 0:4], in_=nst[:, 0:4]), reads=["nst"], writes=["nst"])
            R.dve(lambda e: e.tensor_tensor(out=nst[:, 4:8], in0=nst[:, 0:4], in1=gates[:, 4 * qg:4 * qg + 4, hk * 12 + g * 3 + br], op=ALU.mult),
                  reads=["nst", "gates"], writes=["nst"])
            dst = on[:, :, g, :]
            if first:
                R.dve(lambda e: e.tensor_tensor(out=dst, in0=a3[:, :, 0:64], in1=nst[:, 4:8].unsqueeze(2).broadcast_to([128, 4, 64]), op=ALU.mult),
                      reads=[an, "nst"], writes=["on"])
            else:
                R.dve(lambda e: e.tensor_tensor(out=tmpo[:], in0=a3[:, :, 0:64], in1=nst[:, 4:8].unsqueeze(2).broadcast_to([128, 4, 64]), op=ALU.mult),
                      reads=[an, "nst"], writes=["tmpo"])
                R.pool(lambda e: e.tensor_tensor(out=dst, in0=dst, in1=tmpo[:], op=ALU.add), reads=["on", "tmpo"], writes=["on"])

        for hk in (nsa_hks if nsa_stage >= 2 else ()):
            for ch in range(2):
                c0 = C_NQ + hk * 256 + ch * 128
                proj_plain([(0, c0, 128)], qT, "qT", ch)
                proj_rope([(0, c0, 128)], perm_pieces(c0), qTr2, "qTr2", ch, cosT, sinT, rt)
            proj_rope(dup_pieces(C_KS + hk * 64), dup_perm_pieces(C_KS + hk * 64), ksT, "ksT", None, cosT, sinT, rt)
            proj_rope(dup_pieces(C_KW + hk * 64), dup_perm_pieces(C_KW + hk * 64), kwT, "kwT", None, cosT, sinT, rt)
            for qg in (nsa_qcs if nsa_part >= 1 else ()):
                for g in range(4):
                    ch, ro = g // 2, (g % 2) * 64
                    pn, pt = pA.next()
                    smm(pn, pt, kcd[hk], "kcd", 0, qT, "qT", ro, ch, qg)
                    ptn, PT = ptr.next()
                    R.act(lambda e, pt=pt, PT=PT: e.activation(out=PT[:, :], in_=pt[:, :], func=AF.Exp, scale=SCALE), reads=[pn], writes=[ptn])
                    R.pool(lambda e, PT=PT, qg=qg: e.tensor_tensor(out=PT[:, :], in0=PT[:, :], in1=cbc[:, qg * 512:(qg + 1) * 512], op=ALU.mult),
                           reads=[ptn, "cbc"], writes=[ptn])
                    an, acc = pO.next()
                    for j in range(4):
                        R.pe(lambda e, j=j, acc=acc, PT=PT, hk=hk: e.matmul(acc[:, j * 128:(j + 1) * 128], lhsT=PT[:, j * 128:(j + 1) * 128], rhs=vca[:, hk, :],
                                                                          start=(j == 0), stop=False, skip_group_check=True), reads=[ptn, "vca"], writes=[an])
                    if nsa_part < 2:
                        continue
                    a3 = acc[:, :].rearrange("p (j c) -> p j c", j=4)
                    finish(an, acc, g, qg, hk, 0, True)
                    if g == 0:
                        R.dve(lambda e, a3=a3: e.tensor_tensor(out=impa[:], in0=a3[:, :, 96:128], in1=nst[:, 0:4].unsqueeze(2).broadcast_to([128, 4, 32]), op=ALU.mult),
                              reads=[an, "nst"], writes=["impa"])
                    else:
                        R.dve(lambda e, a3=a3: e.tensor_tensor(out=tmpi[:], in0=a3[:, :, 96:128], in1=nst[:, 0:4].unsqueeze(2).broadcast_to([128, 4, 32]), op=ALU.mult),
                              reads=[an, "nst"], writes=["tmpi"])
                        R.pool(lambda e: e.tensor_tensor(out=impa[:], in0=impa[:], in1=tmpi[:], op=ALU.add), reads=["impa", "tmpi"], writes=["impa"])
                if nsa_part < 3:
                    continue
                R.dve(lambda e, qg=qg: e.tensor_tensor(out=impa[:], in0=impa[:], in1=forced[:, 4 * qg:4 * qg + 4, :], op=ALU.max), reads=["impa", "forced"], writes=["impa"])
                for j in range(4):
                    R.dve(lambda e, j=j: e.max(out=nst[:, 16 + 8 * j:24 + 8 * j], in_=impa[:, j, :]), reads=["impa"], writes=["nst"])
                    R.dve(lambda e, j=j: e.tensor_scalar(out=selq[:, j, 0:32], in0=impa[:, j, :], scalar1=nst[:, 23 + 8 * j:24 + 8 * j], scalar2=None, op0=ALU.is_ge),
                          reads=["impa", "nst"], writes=["selq"])
                for j in range(4):
                    R.pe(lambda e, j=j: e.transpose(out=pT[:, j * 128:(j + 1) * 128], in_=selq[:, j, :], identity=ident_b[:]), reads=["selq", "ident_b"], writes=["pT"])
                R.act(lambda e: e.copy(out=selT[:], in_=pT[:, 0:512]), reads=["pT"], writes=["selT"])
                if nsa_part < 4:
                    continue
                for br in ((1, 2) if nsa_part >= 5 else (1,)):
                    Ksrc, ksn = (ksT, "ksT") if br == 1 else (kwT, "kwT")
                    kcs = list(range(0, 4 * qg + 4)) if br == 1 else list(range(max(0, 4 * qg - 4), 4 * qg + 4))
                    accs = [pO.next() for _ in range(4)]
                    firsts = [True] * 4
                    for kc in kcs:
                        off = 384 + qg * 512 - kc * 128
                        mkn, mkt = mk_r.next()
                        if br == 1:
                            R.pe(lambda e, kc=kc: e.matmul(pX[:, :], lhsT=Eexp[:, kc, :], rhs=selT[:, :], start=True, stop=True), reads=["Eexp", "selT"], writes=["pX"])
                            R.dve(lambda e, mkt=mkt, off=off: e.tensor_tensor(out=mkt[:], in0=pX[:, :], in1=tmask[:, 0, off:off + 512], op=ALU.mult),
                                  reads=["pX", "tmask"], writes=[mkn])
                        for g in range(4):
                            ch, ro = g // 2, (g % 2) * 64
                            pn, pt = pA.next()
                            smm(pn, pt, Ksrc, ksn, kc * 128, qTr2, "qTr2", ro, ch, qg)
                            ptn, PT = ptr.next()
                            R.act(lambda e, pt=pt, PT=PT: e.activation(out=PT[:, :], in_=pt[:, :], func=AF.Exp, scale=SCALE), reads=[pn], writes=[ptn])
                            if br == 1:
                                R.pool(lambda e, PT=PT, mkt=mkt: e.tensor_tensor(out=PT[:, :], in0=PT[:, :], in1=mkt[:], op=ALU.mult), reads=[ptn, mkn], writes=[ptn])
                            else:
                                R.pool(lambda e, PT=PT, off=off: e.tensor_tensor(out=PT[:, :], in0=PT[:, :], in1=tmask[:, 1, off:off + 512], op=ALU.mult),
                                       reads=[ptn, "tmask"], writes=[ptn])
                            an, acc = accs[g]
                            for j in range(4):
                                qc = 4 * qg + j
                                if kc > qc or (br == 2 and kc < qc - 4):
                                    continue
                                stf = firsts[g]
                                firsts[g] = False
                                R.pe(lambda e, j=j, acc=acc, PT=PT, kc=kc, br=br, hk=hk, stf=stf: e.matmul(
                                    acc[:, j * 128:(j + 1) * 128], lhsT=PT[:, j * 128:(j + 1) * 128], rhs=v2[:, kc, (br - 1) * 2 + hk, :],
                                    start=stf, stop=False, skip_group_check=True), reads=[ptn, "v2"], writes=[an])
                    for g in range(4):
                        an, acc = accs[g]
                        finish(an, acc, g, qg, hk, br, False)
                R.act(lambda e: e.copy(out=ynb[:].rearrange("p j c -> p (j c)"), in_=on[:].rearrange("p j g d -> p (j g d)")), reads=["on"], writes=["ynb"])
                for j in range(4):
                    for ch in range(2):
                        R.pe(lambda e, j=j, ch=ch: e.transpose(out=pT[:, (j * 2 + ch) * 128:(j * 2 + ch + 1) * 128], in_=ynb[:, j, ch * 128:(ch + 1) * 128],
                                                             identity=ident_b[:]), reads=["ynb", "ident_b"], writes=["pT"])
                R.act(lambda e, qg=qg, hk=hk: e.copy(out=yT[:, 4 + 2 * hk:6 + 2 * hk, qg * 512:(qg + 1) * 512].rearrange("p c (j t) -> p j c t", j=4),
                                                     in_=pT[:].rearrange("p (j c t) -> p j c t", j=4, c=2)), reads=["pT"], writes=["yT"])
    R.barrier()
    if "yT" in tap_t and b == 0 and l == 0:
        fin.append(R.dma("sp", lambda e: e.dma_start(out=tap_t["yT"][:, :, :], in_=yT[:]), reads=["yT"], writes=["tap_yT2"]))

    with contextlib.ExitStack() as sc_:
        sb4, _ = mk(sc_)
        g1p = bcast_mod(sb4, "g1p", 2, True)
        lng = sb4("lng", [128, D])
        lnb = sb4("lnb", [128, D])
        R.dma("sp", lambda e: e.dma_start(out=lng[:], in_=ln_g[l, 0:1, :].broadcast_to([128, D])), writes=["lng"])
        R.dma("sp", lambda e: e.dma_start(out=lnb[:], in_=ln_b[l, 0:1, :].broadcast_to([128, D])), writes=["lnb"])
        wo = sb4("wo", [128, 8, D], BF16)
        R.dma("pool", lambda e: e.dma_start(out=wo[:], in_=w_out[l, :, :].rearrange("(k p) n -> p k n", p=128)), writes=["wo"])
        mt_r = Rot([("mt%d" % i, sb4("mt%d" % i, [128, D])) for i in range(2)])
        for tc in range(NT):
            xcn, xc = xc_r.next()
            stn, stt = st_r.next()
            mtn, mt = mt_r.next()
            R.dma("sp", lambda e, xc=xc, tc=tc: e.dma_start(out=xc[:], in_=x_src[b, tc * 128:(tc + 1) * 128, :]), reads=["xb%d_%d" % (b, tc)], writes=[xcn])
            for hf in range(2):
                pn, pt = pA.next()
                for kc in range(8):
                    R.pe(lambda e, pt=pt, kc=kc, tc=tc, hf=hf: e.matmul(pt[:, :], lhsT=yT[:, kc, tc * 128:(tc + 1) * 128], rhs=wo[:, kc, hf * 512:(hf + 1) * 512],
                                                                         start=(kc == 0), stop=(kc == 7)), reads=["yT", "wo"], writes=[pn])
                R.dve(lambda e, pt=pt, mt=mt, hf=hf: e.tensor_tensor(out=mt[:, hf * 512:(hf + 1) * 512], in0=pt[:, :], in1=g1p[:, hf * 512:(hf + 1) * 512], op=ALU.mult),
                      reads=[pn, "g1p"], writes=[mtn])
            R.dve(lambda e, xc=xc, mt=mt: e.scalar_tensor_tensor(out=xc[:], in0=xc[:], scalar=ALPHA, in1=mt[:], op0=ALU.mult, op1=ALU.add),
                  reads=[xcn, mtn], writes=[xcn])
            layer_norm_chunk(xcn, xc, stn, stt, xcn, xc)
            R.pool(lambda e, xc=xc: e.tensor_tensor(out=xc[:], in0=xc[:], in1=lng[:], op=ALU.mult), reads=[xcn, "lng"], writes=[xcn])
            R.dve(lambda e, xc=xc: e.tensor_tensor(out=xc[:], in0=xc[:], in1=lnb[:], op=ALU.add), reads=[xcn, "lnb"], writes=[xcn])
            R.dma("sp", lambda e, xc=xc, tc=tc: e.dma_start(out=xbuf[b, tc * 128:(tc + 1) * 128, :], in_=xc[:]), reads=[xcn], writes=["xb%d_%d" % (b, tc)])
    if "x1" in tap_t and b == 0 and l == 0:
        fin.append(R.dma("sp", lambda e: e.dma_start(out=tap_t["x1"][:, :], in_=xbuf[0, :, :]), reads=["xb0_%d" % t_ for t_ in range(NT)], writes=["tap_x1"]))


NE = 32
GELU_A = 1.702


def moe_consts():
    c = {}
    tp = np.arange(128)[:, None]
    t = np.arange(128)[None, :]
    c["ltri"] = (tp < t).astype(np.float32).astype(ml_dtypes.bfloat16)
    c["onesb"] = np.ones((128, 128), np.float32).astype(ml_dtypes.bfloat16)
    c["iota3"] = np.broadcast_to(np.arange(128, dtype=np.float32)[None, None, :], (128, NE, 128)).astype(ml_dtypes.bfloat16).copy()
    return c


def moe(nc, R, mk, env):
    l, NB, NL = env["l"], env["NB"], env["NL"]
    xbuf, modbuf, out, cst = env["xbuf"], env["modbuf"], env["out"], env["cst"]
    ident_f, ident_b, tap_t, fin = env["ident_f"], env["ident_b"], env["tap_t"], env["fin"]
    ln_g, ln_b = env["ln_g"], env["ln_b"]
    router_w, router_b, w_gu, b_gu, w_dn, b_dn = env["router_w"], env["router_b"], env["w_gu"], env["b_gu"], env["w_dn"], env["b_dn"]
    H2d, Gd, CMd, XTd, PGd, Yd = env["H2d"], env["Gd"], env["CMd"], env["XTd"], env["PGd"], env["Yd"]
    last = (l == NL - 1)
    moe_stop = env.get("moe_stop", "D")

    def ln_chunk(xc, xcn, stt, stn):
        R.dve(lambda e: e.bn_stats(out=stt[:, 0:6], in_=xc[:, 0:512]), reads=[xcn], writes=[stn])
        R.dve(lambda e: e.bn_stats(out=stt[:, 6:12], in_=xc[:, 512:1024]), reads=[xcn], writes=[stn])
        R.dve(lambda e: e.bn_aggr(out=stt[:, 12:14], in_=stt[:, 0:12]), reads=[stn], writes=[stn])
        R.act(lambda e: e.activation(out=stt[:, 14:15], in_=stt[:, 13:14], func=AF.Sqrt, bias=LN_EPS, scale=1.0), reads=[stn], writes=[stn])
        R.dve(lambda e: e.reciprocal(out=stt[:, 15:16], in_=stt[:, 14:15]), reads=[stn], writes=[stn])
        R.dve(lambda e: e.tensor_scalar(out=xc[:], in0=xc[:], scalar1=stt[:, 12:13], scalar2=stt[:, 15:16], op0=ALU.subtract, op1=ALU.mult),
              reads=[xcn, stn], writes=[xcn])

    def bcast_row(sbx, name, src_ap, plus1=False, width=D):
        dst = sbx(name, [128, width])
        R.dma("sp", lambda e: e.dma_start(out=dst[:], in_=src_ap.broadcast_to([128, width])), reads=["modbuf"], writes=[name])
        if plus1:
            R.pool(lambda e: e.tensor_scalar(out=dst[:], in0=dst[:], scalar1=1.0, scalar2=None, op0=ALU.add), reads=[name], writes=[name])
        return dst

    with contextlib.ExitStack() as sA:
        sb, ps = mk(sA)
        wr = sb("wr", [128, 8, NE])
        wrh = sb("wrh", [128, 8, NE], BF16)
        wrl = sb("wrl", [128, 8, NE], BF16)
        R.dma("sp", lambda e: e.dma_start(out=wr[:], in_=router_w[l, :, :].rearrange("(k p) n -> p k n", p=128)), writes=["wr"])
        R.dve(lambda e: e.tensor_copy(out=wrh[:], in_=wr[:]), reads=["wr"], writes=["wrh"])
        R.dve(lambda e: e.tensor_tensor(out=wrl[:], in0=wr[:], in1=wrh[:], op=ALU.subtract), reads=["wr", "wrh"], writes=["wrl"])
        rb = bcast_row(sb, "rb", router_b[l:l + 1, :], width=NE)
        ltri = sb("ltri", [128, 128], BF16)
        onesb = sb("onesb", [128, 128], BF16)
        R.dma("sp", lambda e: e.dma_start(out=ltri[:], in_=cst["ltri"][:, :]), writes=["ltri"])
        R.dma("sp", lambda e: e.dma_start(out=onesb[:], in_=cst["onesb"][:, :]), writes=["onesb"])
        xc_r = Rot([("xa%d" % i, sb("xa%d" % i, [128, D])) for i in range(2)])
        hb_r = Rot([("hb%d" % i, sb("hb%d" % i, [128, D], BF16)) for i in range(2)])
        st_r = Rot([("sta%d" % i, sb("sta%d" % i, [128, 16])) for i in range(2)])
        hT_r = Rot([("hTa%d" % i, sb("hTa%d" % i, [128, 2, D], BF16)) for i in range(2)])
        hl_r = Rot([("hl%d" % i, sb("hl%d" % i, [128, D], BF16)) for i in range(2)])
        lg_r = Rot([("lg%d" % i, sb("lg%d" % i, [128, 96])) for i in range(2)])
        maskb = sb("maskb", [128, NT, NE], BF16)
        Gs = sb("Gs", [128, NT, NE])
        CMs = sb("CMs", [128, NT, NE])
        pTh = ps("pTh", [128, D], BF16)
        pTl = ps("pTl", [128, D], BF16)
        pl_r = Rot([("pl%d" % i, ps("pl%d" % i, [128, 512])) for i in range(2)])
        sh2 = sb("sh2", [128, D])
        sc2 = sb("sc2", [128, D])
        for b in range(NB):
            R.dma("sp", lambda e, b=b: e.dma_start(out=sh2[:], in_=modbuf[b, l:l + 1, 3 * D:4 * D].broadcast_to([128, D])), reads=["modbuf"], writes=["sh2_0"])
            R.dma("sp", lambda e, b=b: e.dma_start(out=sc2[:], in_=modbuf[b, l:l + 1, 4 * D:5 * D].broadcast_to([128, D])), reads=["modbuf"], writes=["sc2_0"])
            R.pool(lambda e: e.tensor_scalar(out=sc2[:], in0=sc2[:], scalar1=1.0, scalar2=None, op0=ALU.add), reads=["sc2_0"], writes=["sc2_0"])
            for tc in range(NT):
                xcn, xc = xc_r.next()
                hbn, hb = hb_r.next()
                stn, stt = st_r.next()
                htn, hTc = hT_r.next()
                lgn, lg = lg_r.next()
                pln, pl = pl_r.next()
                R.dma("sp", lambda e, xc=xc, tc=tc, b=b: e.dma_start(out=xc[:], in_=xbuf[b, tc * 128:(tc + 1) * 128, :]), reads=["xb%d_%d" % (b, tc)], writes=[xcn])
                ln_chunk(xc, xcn, stt, stn)
                R.pool(lambda e, xc=xc: e.tensor_tensor(out=xc[:], in0=xc[:], in1=sc2[:], op=ALU.mult), reads=[xcn, "sc2_0"], writes=[xcn])
                R.dve(lambda e, xc=xc: e.tensor_tensor(out=xc[:], in0=xc[:], in1=sh2[:], op=ALU.add), reads=[xcn, "sh2_0"], writes=[xcn])
                R.act(lambda e, xc=xc, hb=hb: e.copy(out=hb[:], in_=xc[:]), reads=[xcn], writes=[hbn])
                R.dma("sp", lambda e, hb=hb, tc=tc, b=b: e.dma_start(out=H2d[b, tc * 128:(tc + 1) * 128, :], in_=hb[:]), reads=[hbn], writes=["H2d%d" % b])
                hln, hl = hl_r.next()
                R.dve(lambda e, xc=xc, hb=hb, hl=hl: e.tensor_tensor(out=hl[:], in0=xc[:], in1=hb[:], op=ALU.subtract), reads=[xcn, hbn], writes=[hln])
                for kc in range(8):
                    R.pe(lambda e, hb=hb, kc=kc: e.transpose(out=pTh[:, kc * 128:(kc + 1) * 128], in_=hb[:, kc * 128:(kc + 1) * 128], identity=ident_b[:]),
                         reads=[hbn, "ident_b"], writes=["pTh"])
                for kc in range(8):
                    R.pe(lambda e, hl=hl, kc=kc: e.transpose(out=pTl[:, kc * 128:(kc + 1) * 128], in_=hl[:, kc * 128:(kc + 1) * 128], identity=ident_b[:]),
                         reads=[hln, "ident_b"], writes=["pTl"])
                R.act(lambda e, hTc=hTc: e.copy(out=hTc[:, 0, :], in_=pTh[:]), reads=["pTh"], writes=[htn])
                R.act(lambda e, hTc=hTc: e.copy(out=hTc[:, 1, :], in_=pTl[:]), reads=["pTl"], writes=[htn])
                terms = [(0, wrh, "wrh"), (1, wrh, "wrh"), (0, wrl, "wrl")]
                for ti, (hs, wt_, wn_) in enumerate(terms):
                    for kc in range(8):
                        R.pe(lambda e, hTc=hTc, kc=kc, pl=pl, hs=hs, wt_=wt_, ti=ti: e.matmul(pl[:, 0:NE], lhsT=hTc[:, hs, kc * 128:(kc + 1) * 128], rhs=wt_[:, kc, :],
                                                                                     start=(ti == 0 and kc == 0), stop=(ti == 2 and kc == 7)),
                             reads=[htn, wn_], writes=[pln])
                R.dve(lambda e, lg=lg, pl=pl: e.tensor_tensor(out=lg[:, 0:32], in0=pl[:, 0:NE], in1=rb[:], op=ALU.add), reads=[pln, "rb"], writes=[lgn])
                R.dve(lambda e, lg=lg: e.max(out=lg[:, 32:40], in_=lg[:, 0:32]), reads=[lgn], writes=[lgn])
                R.dve(lambda e, lg=lg: e.tensor_scalar(out=lg[:, 64:96], in0=lg[:, 0:32], scalar1=lg[:, 35:36], scalar2=None, op0=ALU.is_ge), reads=[lgn], writes=[lgn])
                R.dve(lambda e, lg=lg: e.tensor_scalar(out=lg[:, 40:41], in0=lg[:, 32:33], scalar1=-1.0, scalar2=None, op0=ALU.mult), reads=[lgn], writes=[lgn])
                R.act(lambda e, lg=lg: e.activation(out=lg[:, 0:32], in_=lg[:, 0:32], func=AF.Exp, bias=lg[:, 40:41], scale=1.0), reads=[lgn], writes=[lgn])
                R.dve(lambda e, lg=lg: e.tensor_tensor(out=lg[:, 0:32], in0=lg[:, 0:32], in1=lg[:, 64:96], op=ALU.mult), reads=[lgn], writes=[lgn])
                R.dve(lambda e, lg=lg: e.tensor_reduce(out=lg[:, 41:42], in_=lg[:, 0:32], axis=AX.X, op=ALU.add), reads=[lgn], writes=[lgn])
                R.dve(lambda e, lg=lg: e.reciprocal(out=lg[:, 42:43], in_=lg[:, 41:42]), reads=[lgn], writes=[lgn])
                R.dve(lambda e, lg=lg, tc=tc: e.tensor_scalar(out=Gs[:, tc, :], in0=lg[:, 0:32], scalar1=lg[:, 42:43], scalar2=None, op0=ALU.mult), reads=[lgn], writes=["Gs"])
                R.pool(lambda e, lg=lg, tc=tc: e.tensor_copy(out=maskb[:, tc, :], in_=lg[:, 64:96]), reads=[lgn], writes=["maskb"])
            for tc in range(NT):
                c = tc % 4
                g0 = tc - c
                pln, pl = pl_r.next()
                for cp in range(c):
                    R.pe(lambda e, pl=pl, cp=cp, g0=g0: e.matmul(pl[:, 0:NE], lhsT=onesb[:], rhs=maskb[:, g0 + cp, :], start=(cp == 0), stop=False),
                         reads=["onesb", "maskb"], writes=[pln])
                R.pe(lambda e, pl=pl, tc=tc, c=c: e.matmul(pl[:, 0:NE], lhsT=ltri[:], rhs=maskb[:, tc, :], start=(c == 0), stop=True), reads=["ltri", "maskb"], writes=[pln])
                R.dve(lambda e, pl=pl, tc=tc: e.scalar_tensor_tensor(out=CMs[:, tc, :], in0=pl[:, 0:NE], scalar=1.0, in1=maskb[:, tc, :], op0=ALU.add, op1=ALU.mult),
                      reads=[pln, "maskb"], writes=["CMs"])
            R.dve(lambda e: e.tensor_scalar(out=CMs[:], in0=CMs[:], scalar1=-1.0, scalar2=None, op0=ALU.add), reads=["CMs"], writes=["CMs"])
            R.dma("sp", lambda e, b=b: e.dma_start(out=Gd[b, :, :].rearrange("(c p) e -> p c e", p=128), in_=Gs[:]), reads=["Gs"], writes=["Gd%d" % b])
            R.dma("sp", lambda e, b=b: e.dma_start(out=CMd[b, :, :].rearrange("(c p) e -> p c e", p=128), in_=CMs[:]), reads=["CMs"], writes=["CMd%d" % b])
    R.barrier()
    if "G" in tap_t and l == 0:
        fin.append(R.dma("sp", lambda e: e.dma_start(out=tap_t["G"][:, :], in_=Gd[0, :, :]), reads=["Gd0"], writes=["tap_G"]))
        fin.append(R.dma("sp", lambda e: e.dma_start(out=tap_t["CM"][:, :], in_=CMd[0, :, :]), reads=["CMd0"], writes=["tap_CM"]))
    if moe_stop == "A":
        return

    with contextlib.ExitStack() as sB:
        sb, ps = mk(sB)
        iota3 = sb("iota3", [128, NE, 128], BF16)
        R.dma("sp", lambda e: e.dma_start(out=iota3[:], in_=cst["iota3"][:, :, :]), writes=["iota3"])
        h2g = sb("h2g", [128, 4, D], BF16)
        Gg = sb("Gg", [128, 4, NE])
        CMg = sb("CMg", [128, 4, NE])
        CMj = sb("CMj", [128, 4, NE])
        P = [sb("P%d" % i, [128, NE, 128], BF16) for i in range(4)]
        Pg = [sb("Pg%d" % i, [128, NE, 128], BF16) for i in range(4)]
        xe_r = Rot([("xe%d" % i, sb("xe%d" % i, [128, 8, 512], BF16)) for i in range(2)])
        pgt_r = Rot([("pgt%d" % i, sb("pgt%d" % i, [128, 1024], BF16)) for i in range(2)])
        pg_r = Rot([("pB%d" % i, ps("pB%d" % i, [128, 512])) for i in range(4)])
        pTb_r = Rot([("pTb%d" % i, ps("pTb%d" % i, [128, 1024], BF16)) for i in range(2)])
        for g in range(NB * 4):
            b, gq = g // 4, g % 4
            R.dma("sp", lambda e, b=b, gq=gq: e.dma_start(out=h2g[:], in_=H2d[b, gq * 512:(gq + 1) * 512, :].rearrange("(c p) d -> p c d", p=128)),
                  reads=["H2d%d" % b], writes=["h2g"])
            R.dma("sp", lambda e, b=b, gq=gq: e.dma_start(out=Gg[:], in_=Gd[b, gq * 512:(gq + 1) * 512, :].rearrange("(c p) e -> p c e", p=128)),
                  reads=["Gd%d" % b], writes=["Gg"])
            R.dma("sp", lambda e, b=b, gq=gq: e.dma_start(out=CMg[:], in_=CMd[b, gq * 512:(gq + 1) * 512, :].rearrange("(c p) e -> p c e", p=128)),
                  reads=["CMd%d" % b], writes=["CMg"])
            for jh in range(2):
                R.dve(lambda e, jh=jh: e.tensor_scalar(out=CMj[:], in0=CMg[:], scalar1=-128.0 * jh, scalar2=None, op0=ALU.add), reads=["CMg"], writes=["CMj"])
                for c in range(4):
                    R.dve(lambda e, c=c: e.tensor_tensor(out=P[c][:], in0=iota3[:], in1=CMj[:, c, :].unsqueeze(2).broadcast_to([128, NE, 128]), op=ALU.is_equal),
                          reads=["iota3", "CMj"], writes=["P%d" % c])
                    R.pool(lambda e, c=c: e.tensor_tensor(out=Pg[c][:], in0=P[c][:], in1=Gg[:, c, :].unsqueeze(2).broadcast_to([128, NE, 128]), op=ALU.mult),
                           reads=["P%d" % c, "Gg"], writes=["Pg%d" % c])
                for eq in range(8):
                    xen, xe = xe_r.next()
                    for dk in range(8):
                        pn, pt = pg_r.next()
                        for c in range(4):
                            R.pe(lambda e, pt=pt, c=c, dk=dk, eq=eq: e.matmul(pt[:, :], lhsT=h2g[:, c, dk * 128:(dk + 1) * 128],
                                                                             rhs=P[c][:, eq * 4:(eq + 1) * 4, :].rearrange("p e j -> p (e j)"), start=(c == 0), stop=(c == 3)),
                                 reads=["h2g", "P%d" % c], writes=[pn])
                        if dk % 2 == 0:
                            R.act(lambda e, pt=pt, xe=xe, dk=dk: e.copy(out=xe[:, dk, :], in_=pt[:, :]), reads=[pn], writes=[xen])
                        else:
                            R.dve(lambda e, pt=pt, xe=xe, dk=dk: e.tensor_copy(out=xe[:, dk, :], in_=pt[:, :]), reads=[pn], writes=[xen])
                    R.dma("sp", lambda e, xe=xe, g=g, eq=eq, jh=jh: e.dma_start(out=XTd[g // 4][g % 4, jh, eq, :, :, :], in_=xe[:]), reads=[xen], writes=["XTd%d" % g])
                for e2 in range(NE // 2):
                    ptn, ptb = pTb_r.next()
                    pgn, pgt = pgt_r.next()
                    for ee in range(2):
                        for c in range(4):
                            R.pe(lambda e, ptb=ptb, ee=ee, c=c, e2=e2: e.transpose(out=ptb[:, ee * 512 + c * 128:ee * 512 + (c + 1) * 128], in_=Pg[c][:, e2 * 2 + ee, :], identity=ident_b[:]),
                                 reads=["Pg%d" % c, "ident_b"], writes=[ptn])
                    R.act(lambda e, ptb=ptb, pgt=pgt: e.copy(out=pgt[:], in_=ptb[:]), reads=[ptn], writes=[pgn])
                    R.dma("sp", lambda e, pgt=pgt, g=g, e2=e2, jh=jh: e.dma_start(out=PGd[g // 4][g % 4, jh, e2 * 2:e2 * 2 + 2, :, :].rearrange("e j t -> j e t"),
                                                                              in_=pgt[:].rearrange("p (e t) -> p e t", e=2)), reads=[pgn], writes=["PGd%d" % g])
    R.barrier()
    if moe_stop == "B":
        return

    with contextlib.ExitStack() as sC:
        sb, ps = mk(sC)
        wgu_r = Rot([("wgu%d" % i, sb("wgu%d" % i, [128, 8, 2 * D], BF16)) for i in range(2)])
        wd_r = Rot([("wd%d" % i, sb("wd%d" % i, [128, 8, D], BF16)) for i in range(2)])
        brow_r = Rot([("brow%d" % i, sb("brow%d" % i, [16, 128])) for i in range(2)])
        bgu_r = Rot([("bgu%d" % i, sb("bgu%d" % i, [128, 16])) for i in range(2)])
        xt_r = Rot([("xt%d" % i, sb("xt%d" % i, [128, 8, 512], BF16)) for i in range(2)])
        at_r = Rot([("at%d" % i, sb("at%d" % i, [128, 8, 512], BF16)) for i in range(3)])
        gc_r = Rot([("gc%d" % i, sb("gc%d" % i, [128, 512])) for i in range(2)])
        sl_r = Rot([("sl%d" % i, sb("sl%d" % i, [128, 512])) for i in range(2)])
        u0_r = Rot([("u0%d" % i, sb("u0%d" % i, [128, 512])) for i in range(2)])
        ys_r = Rot([("ys%d" % i, sb("ys%d" % i, [128, D], BF16)) for i in range(2)])
        pgu_r = Rot([("pC%d" % i, ps("pC%d" % i, [128, 512])) for i in range(4)])
        pdn_r = Rot([("pD%d" % i, ps("pD%d" % i, [128, 512])) for i in range(3)])
        pXc = ps("pXc", [128, 512])
        pend_down = [None]

        def emit_down(at, atn, wd, wdn, b, jh, ex):
            for gq in range(4):
                ysn, ys = ys_r.next()
                for hf in range(2):
                    pdn_, pdp = pdn_r.next()
                    for m in range(8):
                        R.pe(lambda e, pdp=pdp, m=m, gq=gq, hf=hf: e.matmul(pdp[:, :], lhsT=at[:, m, gq * 128:(gq + 1) * 128], rhs=wd[:, m, hf * 512:(hf + 1) * 512],
                                                                         start=(m == 0), stop=(m == 7)), reads=[atn, wdn], writes=[pdn_])
                    R.act(lambda e, pdp=pdp, ys=ys, hf=hf: e.activation(out=ys[:, hf * 512:(hf + 1) * 512], in_=pdp[:, :], func=AF.Copy, scale=1.0 / GELU_A),
                          reads=[pdn_], writes=[ysn])
                R.dma("sp", lambda e, ys=ys, gq=gq: e.dma_start(out=Yd[b][gq, jh, ex, :, :], in_=ys[:]), reads=[ysn], writes=["Yd%d" % (b * 4 + gq)])

        for ex in range(NE):
            wgn, wgu = wgu_r.next()
            wdn, wd = wd_r.next()
            brn, brow = brow_r.next()
            bgn, bgu = bgu_r.next()
            for hf in range(2):
                R.dma("pool", lambda e, wgu=wgu, ex=ex, hf=hf: e.dma_start(out=wgu[:, :, hf * D:(hf + 1) * D],
                                                                          in_=w_gu[l, ex, :, hf * D:(hf + 1) * D].rearrange("(k p) n -> p k n", p=128)), writes=[wgn])
            R.dma("pool", lambda e, wd=wd, ex=ex: e.dma_start(out=wd[:], in_=w_dn[l, ex, :, :].rearrange("(k p) n -> p k n", p=128)), writes=[wdn])
            R.dma("sp", lambda e, brow=brow, ex=ex: e.dma_start(out=brow[:], in_=b_gu[l, ex, :].rearrange("(m p) -> m p", p=128)), writes=[brn])
            R.pe(lambda e, brow=brow: e.transpose(out=pXc[:, 0:16], in_=brow[:], identity=ident_f[0:16, 0:16]), reads=[brn, "ident_f"], writes=["pXc"])
            R.dve(lambda e, bgu=bgu: e.tensor_copy(out=bgu[:], in_=pXc[:, 0:16]), reads=["pXc"], writes=[bgn])
            for b, jh in [(b_, j_) for b_ in range(NB) for j_ in range(2)]:
                xtn, xt = xt_r.next()
                atn, at = at_r.next()
                for gq in range(4):
                    R.dma("sp", lambda e, xt=xt, b=b, gq=gq, ex=ex, jh=jh: e.dma_start(
                        out=xt[:, :, gq * 128:(gq + 1) * 128], in_=XTd[b][gq, jh, ex // 4, :, :, (ex % 4) * 128:(ex % 4 + 1) * 128]),
                        reads=["XTd%d" % (b * 4 + gq)], writes=[xtn])
                for m in range(8):
                    pgn_, pgp = pgu_r.next()
                    pun_, pup = pgu_r.next()
                    for (pp, pnm, mm) in ((pgp, pgn_, m), (pup, pun_, m + 8)):
                        for dk in range(8):
                            R.pe(lambda e, pp=pp, mm=mm, dk=dk, wgu=wgu, xt=xt: e.matmul(pp[:, :], lhsT=wgu[:, dk, mm * 128:(mm + 1) * 128], rhs=xt[:, dk, :],
                                                                                     start=(dk == 0), stop=(dk == 7)), reads=[wgn, xtn], writes=[pnm])
                    gcn, gc = gc_r.next()
                    sln, sl = sl_r.next()
                    u0n, u0 = u0_r.next()
                    R.dve(lambda e, gc=gc, pgp=pgp, bgu=bgu, m=m: e.tensor_scalar(out=gc[:], in0=pgp[:, :], scalar1=bgu[:, m:m + 1], scalar2=7.0, op0=ALU.add, op1=ALU.min),
                          reads=[pgn_, bgn], writes=[gcn])
                    R.act(lambda e, gc=gc, sl=sl: e.activation(out=sl[:], in_=gc[:], func=AF.Silu, scale=GELU_A), reads=[gcn], writes=[sln])
                    R.act(lambda e, u0=u0, pup=pup, bgu=bgu, m=m: e.activation(out=u0[:], in_=pup[:, :], func=AF.Identity, bias=bgu[:, m + 8:m + 9], scale=1.0),
                          reads=[pun_, bgn], writes=[u0n])
                    R.dve(lambda e, u0=u0: e.tensor_scalar(out=u0[:], in0=u0[:], scalar1=7.0, scalar2=-7.0, op0=ALU.min, op1=ALU.max), reads=[u0n], writes=[u0n])
                    R.dve(lambda e, u0=u0, sl=sl, at=at, m=m: e.scalar_tensor_tensor(out=at[:, m, :], in0=u0[:], scalar=1.0, in1=sl[:], op0=ALU.add, op1=ALU.mult),
                          reads=[u0n, sln], writes=[atn])
                if pend_down[0] is not None:
                    emit_down(*pend_down[0])
                pend_down[0] = (at, atn, wd, wdn, b, jh, ex)
        if pend_down[0] is not None:
            emit_down(*pend_down[0])
    R.barrier()
    if moe_stop == "C":
        return

    with contextlib.ExitStack() as sD:
        sb, ps = mk(sD)
        Ysb = sb("Ysb", [128, NE, D], BF16)
        PGs = sb("PGs", [128, NE, 512], BF16)
        Bdp = sb("Bdp", [128, D], BF16)
        Gpad = sb("Gpad", [128, 128], BF16)
        GTp_r = Rot([("GTp%d" % i, sb("GTp%d" % i, [128, 128], BF16)) for i in range(2)])
        Gg2 = sb("Gg2", [128, 4, NE])
        facc = sb("facc", [128, 4, D])
        lng = sb("lng2", [128, D])
        lnb = sb("lnb2", [128, D])
        g2p = sb("g2p", [128, D])
        R.dma("sp", lambda e: e.dma_start(out=lng[:], in_=ln_g[l, 1:2, :].broadcast_to([128, D])), writes=["lng2"])
        R.dma("sp", lambda e: e.dma_start(out=lnb[:], in_=ln_b[l, 1:2, :].broadcast_to([128, D])), writes=["lnb2"])
        R.pool(lambda e: e.memset(Bdp[:], 0.0), writes=["Bdp"])
        R.pool(lambda e: e.memset(Gpad[:], 0.0), writes=["Gpad"])
        R.dma("pool", lambda e: e.dma_start(out=Bdp[0:NE, :], in_=b_dn[l, :, :]), reads=["Bdp"], writes=["Bdp"])
        xc_r = Rot([("xd%d" % i, sb("xd%d" % i, [128, D])) for i in range(2)])
        mt_r = Rot([("md%d" % i, sb("md%d" % i, [128, D])) for i in range(2)])
        st_r = Rot([("std%d" % i, sb("std%d" % i, [128, 16])) for i in range(2)])
        pc_r = Rot([("pE%d" % i, ps("pE%d" % i, [128, 512])) for i in range(4)])
        pTd = ps("pTd", [128, 1024], BF16)
        for g in range(NB * 4):
            b, gq = g // 4, g % 4
            if gq == 0:
                R.dma("sp", lambda e, b=b: e.dma_start(out=g2p[:], in_=modbuf[b, l:l + 1, 5 * D:6 * D].broadcast_to([128, D])), reads=["modbuf"], writes=["g2p"])
                R.pool(lambda e: e.tensor_scalar(out=g2p[:], in0=g2p[:], scalar1=1.0, scalar2=None, op0=ALU.add), reads=["g2p"], writes=["g2p"])
            R.dma("sp", lambda e, b=b, gq=gq: e.dma_start(out=Gg2[:], in_=Gd[b, gq * 512:(gq + 1) * 512, :].rearrange("(c p) e -> p c e", p=128)),
                  reads=["Gd%d" % b], writes=["Gg2"])
            for jh in range(2):
                for q4 in range(4):
                    R.dma("sp", lambda e, g=g, q4=q4, jh=jh: e.dma_start(out=Ysb[:, q4 * 8:(q4 + 1) * 8, :], in_=Yd[g // 4][g % 4, jh, q4 * 8:(q4 + 1) * 8, :, :].rearrange("e j d -> j e d")),
                          reads=["Yd%d" % g], writes=["Ysb"])
                    R.dma("sp", lambda e, g=g, q4=q4, jh=jh: e.dma_start(out=PGs[:, q4 * 8:(q4 + 1) * 8, :], in_=PGd[g // 4][g % 4, jh, q4 * 8:(q4 + 1) * 8, :, :].rearrange("e j t -> j e t")),
                          reads=["PGd%d" % g], writes=["PGs"])
                for c in range(4):
                    tc = gq * 4 + c
                    if jh == 0:
                        gtn, GTp = GTp_r.next()
                        R.dve(lambda e, c=c: e.tensor_copy(out=Gpad[:, 0:NE], in_=Gg2[:, c, :]), reads=["Gg2", "Gpad"], writes=["Gpad"])
                        R.pe(lambda e: e.transpose(out=pTd[:, 0:128], in_=Gpad[:], identity=ident_b[:]), reads=["Gpad", "ident_b"], writes=["pTd"])
                        R.act(lambda e, GTp=GTp: e.copy(out=GTp[:], in_=pTd[:, 0:128]), reads=["pTd"], writes=[gtn])
                    else:
                        xcn, xc = xc_r.next()
                        mtn, mt = mt_r.next()
                        stn, stt = st_r.next()
                        R.dma("sp", lambda e, xc=xc, tc=tc, b=b: e.dma_start(out=xc[:], in_=xbuf[b, tc * 128:(tc + 1) * 128, :]), reads=["xb%d_%d" % (b, tc)], writes=[xcn])
                    for hf in range(2):
                        pn, pt = pc_r.next()
                        for ex in range(NE):
                            R.pe(lambda e, pt=pt, ex=ex, c=c, hf=hf: e.matmul(pt[:, :], lhsT=PGs[:, ex, c * 128:(c + 1) * 128], rhs=Ysb[:, ex, hf * 512:(hf + 1) * 512],
                                                                             start=(ex == 0), stop=(jh == 1 and ex == NE - 1)), reads=["PGs", "Ysb"], writes=[pn])
                        if jh == 0:
                            R.pe(lambda e, pt=pt, GTp=GTp, hf=hf: e.matmul(pt[:, :], lhsT=GTp[:], rhs=Bdp[:, hf * 512:(hf + 1) * 512], start=False, stop=True),
                                 reads=[gtn, "Bdp"], writes=[pn])
                            R.act(lambda e, pt=pt, c=c, hf=hf: e.copy(out=facc[:, c, hf * 512:(hf + 1) * 512], in_=pt[:, :]), reads=[pn], writes=["facc"])
                        else:
                            R.dve(lambda e, pt=pt, mt=mt, hf=hf, c=c: e.tensor_tensor(out=mt[:, hf * 512:(hf + 1) * 512], in0=pt[:, :], in1=facc[:, c, hf * 512:(hf + 1) * 512], op=ALU.add),
                                  reads=[pn, "facc"], writes=[mtn])
                    if jh == 1:
                        R.pool(lambda e, mt=mt: e.tensor_tensor(out=mt[:], in0=mt[:], in1=g2p[:], op=ALU.mult), reads=[mtn, "g2p"], writes=[mtn])
                        R.dve(lambda e, xc=xc, mt=mt: e.scalar_tensor_tensor(out=xc[:], in0=xc[:], scalar=ALPHA, in1=mt[:], op0=ALU.mult, op1=ALU.add),
                              reads=[xcn, mtn], writes=[xcn])
                        ln_chunk(xc, xcn, stt, stn)
                        R.pool(lambda e, xc=xc: e.tensor_tensor(out=xc[:], in0=xc[:], in1=lng[:], op=ALU.mult), reads=[xcn, "lng2"], writes=[xcn])
                        R.dve(lambda e, xc=xc: e.tensor_tensor(out=xc[:], in0=xc[:], in1=lnb[:], op=ALU.add), reads=[xcn, "lnb2"], writes=[xcn])
                        if last:
                            fin.append(R.dma("sp", lambda e, xc=xc, tc=tc, b=b: e.dma_start(out=out[b, tc * 128:(tc + 1) * 128, :], in_=xc[:]), reads=[xcn], writes=["out%d_%d" % (b, tc)]))
                        else:
                            R.dma("sp", lambda e, xc=xc, tc=tc, b=b: e.dma_start(out=xbuf[b, tc * 128:(tc + 1) * 128, :], in_=xc[:]), reads=[xcn], writes=["xb%d_%d" % (b, tc)])
    R.barrier()


N_CORES = 8
FUSED = True
DEPTH = 4
_PROG = {}


def _get_prog(NB, NL):
    key = (NB, NL)
    if key not in _PROG:
        _PROG[key] = build(NB, NL)
    return _PROG[key]


def _consts():
    hc = host_consts()
    hc.update(moe_consts())
    return {"k_" + k: v for k, v in hc.items()}


_PER_LAYER = ("ada_w", "ada_b", "w_in", "w_out", "ret_gn_w", "conv_w", "cmp_pos", "cmp_w1", "cmp_w2", "ln_g", "ln_b",
              "router_w", "router_b", "w_gate_up", "b_gate_up", "w_down", "b_down")


def kernel(**inputs):
    B = inputs["x"].shape[0]
    NB = B // N_CORES
    f32 = np.float32
    x = np.ascontiguousarray(inputs["x"], dtype=f32)
    c = np.ascontiguousarray(inputs["c"], dtype=f32)
    pos = np.ascontiguousarray(inputs["positions"], dtype=np.int32)
    consts = _consts()
    layer_sets = [list(range(DEPTH))] if FUSED else [[l] for l in range(DEPTH)]
    for ls in layer_sets:
        nc = _get_prog(NB, len(ls))
        w = {k: np.ascontiguousarray(np.asarray(inputs[k])[ls[0]:ls[-1] + 1], dtype=f32) for k in _PER_LAYER}
        in_maps = []
        for ci in range(N_CORES):
            m = dict(w)
            m.update(consts)
            m["x"] = np.ascontiguousarray(x[ci * NB:(ci + 1) * NB])
            m["c"] = np.ascontiguousarray(c[ci * NB:(ci + 1) * NB])
            m["positions"] = np.ascontiguousarray(pos[ci * NB:(ci + 1) * NB])
            in_maps.append(m)
        res = run_bass_kernel_spmd(nc, in_maps, core_ids=list(range(N_CORES)))
        x = np.concatenate([np.asarray(r["out"], dtype=f32) for r in res.results], axis=0)
    return x
```

```python
import math
import contextlib
import numpy as np
import ml_dtypes
import concourse.bass as bass
import concourse.mybir as mybir
from concourse.bass_utils import run_bass_kernel_spmd


ENGS = ("pe", "act", "dve", "pool", "sp")
NPOOL = 24


class Rec:
    def __init__(self, nc):
        self.nc = nc
        self.ops = []
        self.lastw = {}
        self.readers = {}
        self.ndma = 0
        self.dma_idx = []

    def eng_obj(self, e):
        nc = self.nc
        return {"pe": nc.tensor, "act": nc.scalar, "dve": nc.vector, "pool": nc.gpsimd, "sp": nc.sync}[e]

    def add(self, eng, fn, reads=(), writes=(), dma=False):
        idx = len(self.ops)
        deps = set()
        reads = list(reads) + ["BARRIER"]
        for r in reads:
            if r in self.lastw:
                deps.add(self.lastw[r])
        for w in writes:
            if w in self.lastw:
                deps.add(self.lastw[w])
            for rd in self.readers.get(w, ()):
                deps.add(rd)
        op = dict(eng=eng, fn=fn, deps=deps, dma=dma, used=False, slot=None)
        if dma:
            op["slot"] = self.ndma % NPOOL
            op["val"] = 16 * (self.ndma // NPOOL + 1)
            prev = self.ndma - NPOOL
            if prev >= 0:
                deps.add(self.dma_idx[prev])
            self.dma_idx.append(idx)
            self.ndma += 1
        self.ops.append(op)
        for r in reads:
            self.readers.setdefault(r, []).append(idx)
        for w in writes:
            self.lastw[w] = idx
            self.readers[w] = []
        return idx

    def barrier(self):
        return self.add("sp", lambda e: e.nop(), reads=(), writes=["BARRIER"])

    def pe(self, fn, reads=(), writes=()):
        return self.add("pe", fn, reads, writes)

    def act(self, fn, reads=(), writes=()):
        return self.add("act", fn, reads, writes)

    def dve(self, fn, reads=(), writes=()):
        return self.add("dve", fn, reads, writes)

    def pool(self, fn, reads=(), writes=()):
        return self.add("pool", fn, reads, writes)

    def dma(self, eng, fn, reads=(), writes=()):
        return self.add(eng, fn, reads, writes, dma=True)

    def emit(self, final_wait_ops=()):
        nc = self.nc
        ops = self.ops
        for i, op in enumerate(ops):
            for d in op["deps"]:
                dop = ops[d]
                if (not dop["dma"]) and dop["eng"] == "pe" and op["eng"] == "pe" and not op["dma"]:
                    continue
                dop["used"] = True
        for i in final_wait_ops:
            ops[i]["used"] = True
        cnt = {e: 0 for e in ENGS}
        for op in ops:
            if not op["dma"] and op["used"]:
                cnt[op["eng"]] += 1
                op["val"] = cnt[op["eng"]]
        import contextlib
        with contextlib.ExitStack() as st:
            esem = {e: st.enter_context(nc.semaphore("es_" + e)) for e in ENGS}
            dsem = [st.enter_context(nc.semaphore("ds_%d" % i)) for i in range(NPOOL)]
            block = st.enter_context(nc.Block())

            def semof(op):
                if op["dma"]:
                    return dsem[op["slot"]], op["val"], ("d", op["slot"])
                return esem[op["eng"]], op["val"], ("e", op["eng"])

            def run_engine(e, engobj):
                seen = {}
                for i, op in enumerate(ops):
                    if op["eng"] != e:
                        continue
                    need = {}
                    for d in op["deps"]:
                        dop = ops[d]
                        if (not dop["dma"]) and dop["eng"] == "pe" and e == "pe" and not op["dma"]:
                            continue
                        s, v, k = semof(dop)
                        if seen.get(k, 0) >= v:
                            continue
                        if k not in need or need[k][1] < v:
                            need[k] = (s, v)
                    for k, (s, v) in need.items():
                        engobj.wait_ge(s, v)
                        seen[k] = v
                    ins = op["fn"](engobj)
                    if op["dma"]:
                        ins.then_inc(dsem[op["slot"]], 16)
                    elif op["used"]:
                        ins.then_inc(esem[e], 1)
                if e == "sp":
                    for i in final_wait_ops:
                        s, v, k = semof(ops[i])
                        engobj.wait_ge(s, v)

            @block.tensor
            def _(eng):
                run_engine("pe", eng)

            @block.scalar
            def _(eng):
                run_engine("act", eng)

            @block.vector
            def _(eng):
                run_engine("dve", eng)

            @block.gpsimd
            def _(eng):
                run_engine("pool", eng)

            @block.sync
            def _(eng):
                run_engine("sp", eng)

F32 = mybir.dt.float32
BF16 = mybir.dt.bfloat16
I32 = mybir.dt.int32
AF = mybir.ActivationFunctionType
ALU = mybir.AluOpType
AX = mybir.AxisListType

D = 1024
S = 2048
NT = 16
LN_EPS = 1e-5
ALPHA = 8.0 ** 0.25
NEGB = -30000.0
SCALE = 0.125
C_RQ, C_RK, C_RV, C_RG = 0, 256, 512, 768
C_CB, C_CC, C_CH = 1024, 1280, 1536
C_NQ = 1792
C_KC, C_VC, C_KS, C_VS, C_KW, C_VW = 2304, 2432, 2560, 2688, 2816, 2944
C_NG = 3072
NIN = 3096
TW = 2432
NE = 32
GELU_A = 1.702


class Rot:
    def __init__(self, items):
        self.items = items
        self.i = 0

    def next(self):
        it = self.items[self.i % len(self.items)]
        self.i += 1
        return it


def host_consts():
    c = {}
    c["ident"] = np.eye(128, dtype=np.float32)
    r = np.arange(128)
    inv = (10000.0 ** (-((r % 64) % 32).astype(np.float64) / 32.0))
    c["ropec"] = np.stack([inv / (2 * np.pi), np.where((r % 64) < 32, -1.0, 1.0)], 1).astype(np.float32)
    gam = 1.0 - 2.0 ** (-5.0 - np.arange(4))
    k = np.arange(128)[:, None]
    m = np.arange(TW)[None, :] - 384
    tabs = []
    for h in range(4):
        e = (m - k).astype(np.float64)
        tabs.append(np.where(e >= 0, np.exp(np.log(gam[h]) * np.maximum(e, 0)) * SCALE, 0.0))
    c["dect"] = np.stack(tabs, 1).astype(ml_dtypes.bfloat16)
    dd = (m - k)
    c["tmask"] = np.stack([(dd >= 0), (dd >= 0) & (dd < 512)], 1).astype(np.float32).astype(ml_dtypes.bfloat16)
    kk = np.arange(128)[:, None]
    qq = np.arange(128)[None, :]
    c["cb_caus"] = np.where(kk > qq, 0.0, 1.0).astype(ml_dtypes.bfloat16)
    c["cb_low"] = np.where(kk <= qq, 0.0, 1.0).astype(ml_dtypes.bfloat16)
    cc = np.arange(128)[:, None]
    tt = np.arange(2048)[None, :]
    c["cb_cmp"] = np.where(16 * cc + 31 > tt, 0.0, 1.0).astype(ml_dtypes.bfloat16)
    cs = np.arange(128) * 16
    cl = cs + 31
    ss = np.arange(32) * 64
    ov = ((cs[:, None] < ss[None, :] + 64) & (ss[None, :] <= cl[:, None])).astype(np.float32)
    ov[127] = 0
    c["overlap"] = ov.astype(ml_dtypes.bfloat16)
    t = (np.arange(16)[None, :, None] * 128 + np.arange(128)[:, None, None])
    j = np.arange(32)[None, None, :]
    c["forced"] = np.where((j == 0) | (j == t // 64), 1e9, 0.0).astype(np.float32)
    b = np.arange(32)[:, None, None]
    kc = np.arange(16)[None, :, None]
    kl = np.arange(128)[None, None, :]
    E = np.zeros((128, 16, 128), np.float32)
    E[0:32] = (b == 2 * kc + (kl >= 64))
    c["Eexp"] = E.astype(ml_dtypes.bfloat16)
    return c


def build(NB=1, NL=1, taps=(), stop=None, nsa_stage=2, nsa_part=9, nsa_hks=(0, 1), nsa_qcs=tuple(range(4)), moe_stop="D", skip_mixer=False, ne_in=NE):
    nc = bass.Bass("TRN2", target_bir_lowering=False)
    R = Rec(nc)
    dram = {}

    def din(name, shape, dt=F32):
        dram[name] = nc.dram_tensor(name, list(shape), dt, kind="ExternalInput").ap()
        return dram[name]

    def dint(name, shape, dt=F32):
        dram[name] = nc.dram_tensor(name, list(shape), dt, kind="Internal").ap()
        return dram[name]

    def dout(name, shape, dt=F32):
        dram[name] = nc.dram_tensor(name, list(shape), dt, kind="ExternalOutput").ap()
        return dram[name]

    x_in = din("x", [NB, S, D])
    c_in = din("c", [NB, D])
    pos_in = din("positions", [NB, S], I32)
    ada_w = din("ada_w", [NL, D, 6 * D])
    ada_b = din("ada_b", [NL, 6 * D])
    w_in = din("w_in", [NL, D, NIN])
    w_out = din("w_out", [NL, D, D])
    gn_w = din("ret_gn_w", [NL, 256])
    conv_w = din("conv_w", [NL, 3, 256])
    cmp_pos = din("cmp_pos", [NL, 2, 32, 64])
    cmp_w1 = din("cmp_w1", [NL, 2, 2048, 128])
    cmp_w2 = din("cmp_w2", [NL, 2, 128, 64])
    ln_g = din("ln_g", [NL, 2, D])
    ln_b = din("ln_b", [NL, 2, D])
    router_w = din("router_w", [NL, D, NE])
    router_b = din("router_b", [NL, NE])
    w_gu = din("w_gate_up", [NL, ne_in, D, 2 * D])
    b_gu = din("b_gate_up", [NL, ne_in, 2 * D])
    w_dn = din("w_down", [NL, ne_in, D, D])
    b_dn = din("b_down", [NL, ne_in, D])
    HC = host_consts()
    HC.update(moe_consts())
    cst = {}
    for k, v in HC.items():
        cst[k] = din("k_" + k, list(v.shape), BF16 if v.dtype == ml_dtypes.bfloat16 else F32)
    out = dout("out", [NB, S, D])
    xbuf = dint("xbuf", [NB, S, D])
    ropebuf = dint("ropebuf", [NB, 2, 128, S])
    NG = NB * 4
    H2d = dint("H2d", [NB, S, D], BF16)
    Gd = dint("Gd", [NB, S, NE])
    CMd = dint("CMd", [NB, S, NE])
    XTd = [dint("XTd%d" % i, [4, 2, 8, 128, 8, 512], BF16) for i in range(NB)]
    PGd = [dint("PGd%d" % i, [4, 2, NE, 128, 512], BF16) for i in range(NB)]
    Yd = [dint("Yd%d" % i, [4, 2, NE, 128, D], BF16) for i in range(NB)]
    tap_t = {}
    for (nm, shape, dt) in taps:
        tap_t[nm] = dout("tap_" + nm, shape, dt)
    fin = []

    top = contextlib.ExitStack()

    uniq = [0]

    def mk(stack):
        def sbuf(name, shape, dt=F32):
            uniq[0] += 1
            return stack.enter_context(nc.sbuf_tensor("%s_%d" % (name, uniq[0]), list(shape), dt))

        def psum(name, shape, dt=F32):
            uniq[0] += 1
            return stack.enter_context(nc.psum_tensor("%s_%d" % (name, uniq[0]), list(shape), dt))
        return sbuf, psum

    with top:
        sbuf, psum = mk(top)
        ident_f = sbuf("ident_f", [128, 128])
        ident_b = sbuf("ident_b", [128, 128], BF16)
        R.dma("sp", lambda e: e.dma_start(out=ident_f[:], in_=cst["ident"][:, :]), writes=["ident_f"])
        R.dve(lambda e: e.tensor_copy(out=ident_b[:], in_=ident_f[:]), reads=["ident_f"], writes=["ident_b"])
        ropec = sbuf("ropec", [128, 2])
        R.dma("sp", lambda e: e.dma_start(out=ropec[:], in_=cst["ropec"][:, :]), writes=["ropec"])
        modbuf = dint("modbuf", [NB, NL, 6 * D])

        with contextlib.ExitStack() as st0:
            sb0, ps0 = mk(st0)
            c_sb = sb0("c_sb", [NB, D])
            cT = sb0("cT", [128, 8, NB], BF16)
            R.dma("sp", lambda e: e.dma_start(out=c_sb[:], in_=c_in[:, :]), writes=["c_sb"])
            R.act(lambda e: e.activation(out=c_sb[:], in_=c_sb[:], func=AF.Silu), reads=["c_sb"], writes=["c_sb"])
            ps_t = ps0("ps_t", [128, 512])
            for kc in range(8):
                R.pe(lambda e, kc=kc: e.transpose(out=ps_t[:, kc * NB:(kc + 1) * NB], in_=c_sb[:, kc * 128:(kc + 1) * 128],
                                                 identity=ident_f[0:NB, 0:NB]),
                     reads=["c_sb", "ident_f"], writes=["ps_t"])
            R.dve(lambda e: e.tensor_copy(out=cT[:].rearrange("p k b -> p (k b)"), in_=ps_t[:, 0:8 * NB]),
                  reads=["ps_t"], writes=["cT"])
            mods = sb0("mods", [NB, 6 * D])
            adab = sb0("adab", [NB, 6 * D])
            adaw = Rot([("adaw%d" % i, sb0("adaw%d" % i, [128, 8, 512], BF16)) for i in range(2)])
            psm = Rot([("ps_m%d" % i, ps0("ps_m%d" % i, [128, 512])) for i in range(2)])
            for l in range(NL):
                for b in range(NB):
                    R.dma("sp", lambda e, b=b, l=l: e.dma_start(out=adab[b:b + 1, :], in_=ada_b[l:l + 1, :]), writes=["adab"])
                for cc in range(12):
                    wn, wb = adaw.next()
                    pn, pm = psm.next()
                    R.dma("pool", lambda e, wb=wb, l=l, cc=cc: e.dma_start(
                        out=wb[:], in_=ada_w[l, :, cc * 512:(cc + 1) * 512].rearrange("(k p) n -> p k n", p=128)),
                        writes=[wn])
                    for kc in range(8):
                        R.pe(lambda e, wb=wb, pm=pm, kc=kc: e.matmul(pm[0:NB, :], lhsT=cT[:, kc, :], rhs=wb[:, kc, :],
                                                                     start=(kc == 0), stop=(kc == 7)),
                             reads=["cT", wn], writes=[pn])
                    R.dve(lambda e, pm=pm, cc=cc: e.tensor_tensor(
                        out=mods[:, cc * 512:(cc + 1) * 512], in0=pm[0:NB, :], in1=adab[:, cc * 512:(cc + 1) * 512], op=ALU.add),
                        reads=[pn, "adab"], writes=["mods"])
                R.dma("sp", lambda e, l=l: e.dma_start(out=modbuf[:, l, :], in_=mods[:]), reads=["mods"], writes=["modbuf"])
            posi = sb0("posi", [128, S], I32)
            y0 = sb0("y0", [128, S])
            r1 = sb0("r1", [128, S])
            ki = sb0("ki", [128, S], I32)
            kf = sb0("kf", [128, S])
            for b in range(NB):
                R.dma("sp", lambda e, b=b: e.dma_start(out=posi[:], in_=pos_in[b:b + 1, :].broadcast_to([128, S])), writes=["posi"])
                R.dve(lambda e: e.tensor_copy(out=y0[:], in_=posi[:]), reads=["posi"], writes=["y0"])
                R.dve(lambda e: e.tensor_scalar(out=y0[:], in0=y0[:], scalar1=ropec[:, 0:1], scalar2=None, op0=ALU.mult),
                      reads=["y0", "ropec"], writes=["y0"])
                for which in range(2):
                    sh = 0.25 if which == 0 else 0.0
                    R.dve(lambda e, sh=sh: e.tensor_scalar(out=r1[:], in0=y0[:], scalar1=sh, scalar2=None, op0=ALU.add),
                          reads=["y0"], writes=["r1"])
                    R.dve(lambda e: e.tensor_copy(out=ki[:], in_=r1[:]), reads=["r1"], writes=["ki"])
                    R.dve(lambda e: e.tensor_copy(out=kf[:], in_=ki[:]), reads=["ki"], writes=["kf"])
                    R.dve(lambda e: e.tensor_tensor(out=r1[:], in0=r1[:], in1=kf[:], op=ALU.subtract), reads=["r1", "kf"], writes=["r1"])
                    R.dve(lambda e: e.tensor_single_scalar(out=kf[:], in_=r1[:], scalar=0.5, op=ALU.is_gt), reads=["r1"], writes=["kf"])
                    R.dve(lambda e: e.tensor_tensor(out=r1[:], in0=r1[:], in1=kf[:], op=ALU.subtract), reads=["r1", "kf"], writes=["r1"])
                    R.dve(lambda e: e.tensor_single_scalar(out=kf[:], in_=r1[:], scalar=-0.5, op=ALU.is_lt), reads=["r1"], writes=["kf"])
                    R.dve(lambda e: e.tensor_tensor(out=r1[:], in0=r1[:], in1=kf[:], op=ALU.add), reads=["r1", "kf"], writes=["r1"])
                    R.act(lambda e: e.activation(out=r1[:], in_=r1[:], func=AF.Sin, scale=2 * math.pi), reads=["r1"], writes=["r1"])
                    if which == 1:
                        R.dve(lambda e: e.tensor_scalar(out=r1[:], in0=r1[:], scalar1=ropec[:, 1:2], scalar2=None, op0=ALU.mult),
                              reads=["r1", "ropec"], writes=["r1"])
                    R.dma("sp", lambda e, b=b, which=which: e.dma_start(out=ropebuf[b, which, :, :], in_=r1[:]),
                          reads=["r1"], writes=["ropebuf%d" % b])
        R.barrier()
        if "mods" in tap_t:
            fin.append(R.dma("sp", lambda e: e.dma_start(out=tap_t["mods"][:, :, :], in_=modbuf[:, :, :]), reads=["modbuf"], writes=["tap_mods"]))
        if "rope" in tap_t:
            fin.append(R.dma("sp", lambda e: e.dma_start(out=tap_t["rope"][:, :, :], in_=ropebuf[0, :, :, :]), reads=["ropebuf0"], writes=["tap_rope"]))

        if stop == "s0":
            R.emit(final_wait_ops=fin)
            return nc

        for l in range(NL):
            x_src = x_in if l == 0 else xbuf
            for b in (range(NB) if not skip_mixer else ()):
                with contextlib.ExitStack() as stm:
                    mixer(nc, R, mk(stm), locals())
                R.barrier()
            if stop == "mix":
                break
            moe(nc, R, mk, locals())
        R.emit(final_wait_ops=fin)
    return nc


def mixer(nc, R, mkp, env):
    sbuf, psum = mkp
    mk = env["mk"]
    l, b = env["l"], env["b"]
    x_src, xbuf, modbuf = env["x_src"], env["xbuf"], env["modbuf"]
    ident_f, ident_b, cst, tap_t, fin = env["ident_f"], env["ident_b"], env["cst"], env["tap_t"], env["fin"]
    w_in, w_out, gn_w, conv_w = env["w_in"], env["w_out"], env["gn_w"], env["conv_w"]
    cmp_pos, cmp_w1, cmp_w2, ln_g, ln_b, ropebuf = env["cmp_pos"], env["cmp_w1"], env["cmp_w2"], env["ln_g"], env["ln_b"], env["ropebuf"]
    NB = env["NB"]

    pA = Rot([("pA%d" % i, psum("pA%d" % i, [128, 512])) for i in range(2)])
    pT = psum("pT", [128, 1024], BF16)
    pO = Rot([("pO%d" % i, psum("pO%d" % i, [128, 512])) for i in range(4)])
    pX = psum("pX", [128, 512])

    def bcast_mod(sbx, dname, which, plus1):
        dst = sbx(dname, [128, D])
        R.dma("sp", lambda e: e.dma_start(out=dst[:], in_=modbuf[b, l:l + 1, which * D:(which + 1) * D].broadcast_to([128, D])),
              reads=["modbuf"], writes=[dname])
        if plus1:
            R.pool(lambda e: e.tensor_scalar(out=dst[:], in0=dst[:], scalar1=1.0, scalar2=None, op0=ALU.add), reads=[dname], writes=[dname])
        return dst

    sh1 = bcast_mod(sbuf, "sh1", 0, False)
    sc1 = bcast_mod(sbuf, "sc1", 1, True)

    hT = sbuf("hT", [128, 8, S], BF16)
    yT = sbuf("yT", [128, 8, S], BF16)
    xc_r = Rot([("xc%d" % i, sbuf("xc%d" % i, [128, D])) for i in range(2)])
    hc_r = Rot([("hc%d" % i, sbuf("hc%d" % i, [128, D], BF16)) for i in range(2)])
    st_r = Rot([("st%d" % i, sbuf("st%d" % i, [128, 16])) for i in range(2)])

    def layer_norm_chunk(xcn, xc, stn, stt, xnn, xn):
        R.dve(lambda e: e.bn_stats(out=stt[:, 0:6], in_=xc[:, 0:512]), reads=[xcn], writes=[stn])
        R.dve(lambda e: e.bn_stats(out=stt[:, 6:12], in_=xc[:, 512:1024]), reads=[xcn], writes=[stn])
        R.dve(lambda e: e.bn_aggr(out=stt[:, 12:14], in_=stt[:, 0:12]), reads=[stn], writes=[stn])
        R.act(lambda e: e.activation(out=stt[:, 14:15], in_=stt[:, 13:14], func=AF.Sqrt, bias=LN_EPS, scale=1.0), reads=[stn], writes=[stn])
        R.dve(lambda e: e.reciprocal(out=stt[:, 15:16], in_=stt[:, 14:15]), reads=[stn], writes=[stn])
        R.dve(lambda e: e.tensor_scalar(out=xn[:], in0=xc[:], scalar1=stt[:, 12:13], scalar2=stt[:, 15:16], op0=ALU.subtract, op1=ALU.mult),
              reads=[xcn, stn], writes=[xnn])

    for tc in range(NT):
        xcn, xc = xc_r.next()
        xnn, xn = xcn, xc
        hcn, hc = hc_r.next()
        stn, stt = st_r.next()
        R.dma("sp", lambda e, xc=xc, tc=tc: e.dma_start(out=xc[:], in_=x_src[b, tc * 128:(tc + 1) * 128, :]), reads=["xb%d_%d" % (b, tc)], writes=[xcn])
        layer_norm_chunk(xcn, xc, stn, stt, xnn, xn)
        R.pool(lambda e, xn=xn: e.tensor_tensor(out=xn[:], in0=xn[:], in1=sc1[:], op=ALU.mult), reads=[xnn, "sc1"], writes=[xnn])
        R.dve(lambda e, xn=xn, hc=hc: e.tensor_tensor(out=hc[:], in0=xn[:], in1=sh1[:], op=ALU.add), reads=[xnn, "sh1"], writes=[hcn])
        for kc in range(8):
            R.pe(lambda e, hc=hc, kc=kc: e.transpose(out=pT[:, kc * 128:(kc + 1) * 128], in_=hc[:, kc * 128:(kc + 1) * 128], identity=ident_b[:]),
                 reads=[hcn, "ident_b"], writes=["pT"])
        R.act(lambda e, tc=tc: e.copy(out=hT[:, :, tc * 128:(tc + 1) * 128], in_=pT[:].rearrange("p (k t) -> p k t", k=8)),
              reads=["pT"], writes=["hT"])
    if "hT" in tap_t and b == 0 and l == 0:
        fin.append(R.dma("sp", lambda e: e.dma_start(out=tap_t["hT"][:, :, :], in_=hT[:]), reads=["hT"], writes=["tap_hT"]))

    wr = Rot([("W%d" % i, sbuf("W%d" % i, [128, 8, 128], BF16)) for i in range(4)])

    def load_w(pieces):
        wn, wb = wr.next()
        for (do, sc, wd) in pieces:
            R.dma("pool", lambda e, wb=wb, do=do, sc=sc, wd=wd: e.dma_start(
                out=wb[:, :, do:do + wd], in_=w_in[l, :, sc:sc + wd].rearrange("(k p) n -> p k n", p=128)), writes=[wn])
        return wn, wb

    def perm_pieces(c0):
        return [(0, c0 + 32, 32), (32, c0, 32), (64, c0 + 96, 32), (96, c0 + 64, 32)]

    def dup_pieces(c0):
        return [(0, c0, 64), (64, c0, 64)]

    def dup_perm_pieces(c0):
        return [(0, c0 + 32, 32), (32, c0, 32), (64, c0 + 32, 32), (96, c0, 32)]

    def proj_fm(wn, wb, tg):
        pn, pt = pA.next()
        for kc in range(8):
            R.pe(lambda e, kc=kc: e.matmul(pt[:, :], lhsT=wb[:, kc, :], rhs=hT[:, kc, tg * 512:(tg + 1) * 512], start=(kc == 0), stop=(kc == 7)),
                 reads=[wn, "hT"], writes=[pn])
        return pn, pt

    def proj_plain(pieces, dst, dname, dsl):
        wn, wb = load_w(pieces)
        for tg in range(4):
            pn, pt = proj_fm(wn, wb, tg)
            R.act(lambda e, pt=pt, tg=tg: e.copy(out=dst[:, dsl, tg * 512:(tg + 1) * 512] if dsl is not None else dst[:, tg * 512:(tg + 1) * 512], in_=pt[:, :]),
                  reads=[pn], writes=[dname])

    with contextlib.ExitStack() as sc_:
        sb1, _ = mk(sc_)
        fm = [sb1("fm%d" % i, [128, 2, S], BF16) for i in range(3)]
        tmpA = sb1("tmpA", [128, S])
        tmpB = sb1("tmpB", [128, S])
        cw = sb1("cw", [128, 2, 3])
        for ch_ in range(2):
            for k_ in range(3):
                R.dma("sp", lambda e, ch_=ch_, k_=k_: e.dma_start(out=cw[:, ch_, k_:k_ + 1], in_=conv_w[l, k_, ch_ * 128:(ch_ + 1) * 128].rearrange("(p o) -> p o", o=1)), writes=["cw"])
        for i, c0 in enumerate((C_CB, C_CC, C_CH)):
            for ch in range(2):
                proj_plain([(0, c0 + ch * 128, 128)], fm[i], "fm%d" % i, ch)
        for ch in range(2):
            R.pool(lambda e, ch=ch: e.tensor_tensor(out=tmpA[:], in0=fm[1][:, ch, :], in1=fm[2][:, ch, :], op=ALU.mult),
                   reads=["fm1", "fm2"], writes=["tmpA"])
            R.dve(lambda e, ch=ch: e.tensor_scalar(out=tmpB[:], in0=tmpA[:], scalar1=cw[:, ch, 2:3], scalar2=None, op0=ALU.mult),
                  reads=["tmpA", "cw"], writes=["tmpB"])
            R.dve(lambda e, ch=ch: e.scalar_tensor_tensor(out=tmpB[:, 1:S], in0=tmpA[:, 0:S - 1], scalar=cw[:, ch, 1:2], in1=tmpB[:, 1:S],
                                                          op0=ALU.mult, op1=ALU.add), reads=["tmpA", "tmpB", "cw"], writes=["tmpB"])
            R.dve(lambda e, ch=ch: e.scalar_tensor_tensor(out=tmpB[:, 2:S], in0=tmpA[:, 0:S - 2], scalar=cw[:, ch, 0:1], in1=tmpB[:, 2:S],
                                                          op0=ALU.mult, op1=ALU.add), reads=["tmpA", "tmpB", "cw"], writes=["tmpB"])
            R.pool(lambda e, ch=ch: e.tensor_tensor(out=yT[:, 2 + ch, :], in0=tmpB[:], in1=fm[0][:, ch, :], op=ALU.mult),
                   reads=["tmpB", "fm0"], writes=["yT"])
    R.barrier()

    def load_rope(sbx):
        cosT = sbx("cosT", [128, S])
        sinT = sbx("sinT", [128, S])
        R.dma("sp", lambda e: e.dma_start(out=cosT[:], in_=ropebuf[b, 0, :, :]), reads=["ropebuf%d" % b], writes=["cosT"])
        R.dma("sp", lambda e: e.dma_start(out=sinT[:], in_=ropebuf[b, 1, :, :]), reads=["ropebuf%d" % b], writes=["sinT"])
        rt = Rot([("rt%d" % i, sbx("rt%d" % i, [128, 512])) for i in range(4)])
        return cosT, sinT, rt

    def proj_rope(p_norm, p_perm, dst, dname, dsl, cosT, sinT, rt):
        wn1, wb1 = load_w(p_norm)
        wn2, wb2 = load_w(p_perm)
        for tg in range(4):
            pn1, pt1 = proj_fm(wn1, wb1, tg)
            pn2, pt2 = proj_fm(wn2, wb2, tg)
            t1n, t1 = rt.next()
            t2n, t2 = rt.next()
            R.dve(lambda e, pt1=pt1, t1=t1, tg=tg: e.tensor_tensor(out=t1[:], in0=pt1[:, :], in1=cosT[:, tg * 512:(tg + 1) * 512], op=ALU.mult),
                  reads=[pn1, "cosT"], writes=[t1n])
            R.dve(lambda e, pt2=pt2, t2=t2, tg=tg: e.tensor_tensor(out=t2[:], in0=pt2[:, :], in1=sinT[:, tg * 512:(tg + 1) * 512], op=ALU.mult),
                  reads=[pn2, "sinT"], writes=[t2n])
            R.pool(lambda e, t1=t1, t2=t2, tg=tg: e.tensor_tensor(
                out=dst[:, dsl, tg * 512:(tg + 1) * 512] if dsl is not None else dst[:, tg * 512:(tg + 1) * 512], in0=t1[:], in1=t2[:], op=ALU.add),
                reads=[t1n, t2n], writes=[dname])

    with contextlib.ExitStack() as sc_:
        sb2, _ = mk(sc_)
        cosT, sinT, rt = load_rope(sb2)
        qTr = sb2("qTr", [128, 2, S], BF16)
        kTr = sb2("kTr", [128, 2, S], BF16)
        vrg = sb2("vrg", [128, NT, 512], BF16)
        dect = sb2("dect", [128, 4, TW], BF16)
        gnw = sb2("gnw", [128, 256])
        R.dma("sp", lambda e: e.dma_start(out=dect[:], in_=cst["dect"][:, :, :]), writes=["dect"])
        R.dma("sp", lambda e: e.dma_start(out=gnw[:], in_=gn_w[l:l + 1, :].broadcast_to([128, 256])), writes=["gnw"])
        for ch in range(2):
            proj_rope([(0, C_RQ + ch * 128, 128)], perm_pieces(C_RQ + ch * 128), qTr, "qTr", ch, cosT, sinT, rt)
            proj_rope([(0, C_RK + ch * 128, 128)], perm_pieces(C_RK + ch * 128), kTr, "kTr", ch, cosT, sinT, rt)
        wv = sb2("wv", [128, 8, 512], BF16)
        R.dma("pool", lambda e: e.dma_start(out=wv[:], in_=w_in[l, :, C_RV:C_RV + 512].rearrange("(k p) n -> p k n", p=128)), writes=["wv"])
        for tc in range(NT):
            pn, pt = pA.next()
            for kc in range(8):
                R.pe(lambda e, pt=pt, kc=kc, tc=tc: e.matmul(pt[:, :], lhsT=hT[:, kc, tc * 128:(tc + 1) * 128], rhs=wv[:, kc, :],
                                                             start=(kc == 0), stop=(kc == 7)), reads=["hT", "wv"], writes=[pn])
            R.act(lambda e, pt=pt, tc=tc: e.copy(out=vrg[:, tc, 0:256], in_=pt[:, 0:256]), reads=[pn], writes=["vrg"])
            R.act(lambda e, pt=pt, tc=tc: e.activation(out=vrg[:, tc, 256:512], in_=pt[:, 256:512], func=AF.Silu), reads=[pn], writes=["vrg"])
        smr = Rot([("sm%d" % i, sb2("sm%d" % i, [128, 512], BF16)) for i in range(3)])
        gA = sb2("gA", [128, 1024])
        gB = sb2("gB", [128, 1024])
        gs = sb2("gs", [128, 64])
        yr = sb2("yr", [128, 4, 256], BF16)
        for qg in range(4):
            raccn = ["pO0", "pO1"]
            racc = [pO.items[0][1], pO.items[1][1]]
            first = [True, True]
            for h in range(4):
                ch, ro = h // 2, (h % 2) * 64
                for kc in range(4 * qg + 4):
                    pn, pt = pA.next()
                    R.pe(lambda e, pt=pt, kc=kc, ch=ch, ro=ro, qg=qg: e.matmul(
                        pt[:, :], lhsT=kTr[ro:ro + 64, ch, kc * 128:(kc + 1) * 128], rhs=qTr[ro:ro + 64, ch, qg * 512:(qg + 1) * 512],
                        start=True, stop=True), reads=["kTr", "qTr"], writes=[pn])
                    smn, sm = smr.next()
                    off = 384 + qg * 512 - kc * 128
                    R.dve(lambda e, pt=pt, sm=sm, h=h, off=off: e.tensor_tensor(out=sm[:], in0=pt[:, :], in1=dect[:, h, off:off + 512], op=ALU.mult),
                          reads=[pn, "dect"], writes=[smn])
                    for j in range(4):
                        qc = 4 * qg + j
                        if kc > qc:
                            continue
                        bk = j // 2
                        st_flag = first[bk]
                        first[bk] = False
                        R.pe(lambda e, sm=sm, j=j, h=h, kc=kc, bk=bk, st_flag=st_flag: e.matmul(
                            racc[bk][:, (j % 2) * 256 + h * 64:(j % 2) * 256 + h * 64 + 64], lhsT=sm[:, j * 128:(j + 1) * 128],
                            rhs=vrg[:, kc, h * 64:h * 64 + 64], start=st_flag, stop=False, skip_group_check=True),
                            reads=[smn, "vrg"], writes=[raccn[bk]])
            for bk in range(2):
                R.act(lambda e, bk=bk: e.copy(out=gA[:, bk * 512:(bk + 1) * 512], in_=racc[bk][:, :]), reads=[raccn[bk]], writes=["gA"])
            g3 = gA[:].rearrange("p (a d) -> p a d", d=64)
            b3 = gB[:].rearrange("p (a d) -> p a d", d=64)
            R.dve(lambda e: e.tensor_reduce(out=gs[:, 0:16], in_=g3, axis=AX.X, op=ALU.add), reads=["gA"], writes=["gs"])
            R.dve(lambda e: e.tensor_scalar(out=gs[:, 0:16], in0=gs[:, 0:16], scalar1=1.0 / 64, scalar2=None, op0=ALU.mult), reads=["gs"], writes=["gs"])
            R.dve(lambda e: e.tensor_tensor(out=g3, in0=g3, in1=gs[:, 0:16].unsqueeze(2).broadcast_to([128, 16, 64]), op=ALU.subtract),
                  reads=["gA", "gs"], writes=["gA"])
            R.pool(lambda e: e.tensor_tensor(out=gB[:], in0=gA[:], in1=gA[:], op=ALU.mult), reads=["gA"], writes=["gB"])
            R.dve(lambda e: e.tensor_reduce(out=gs[:, 16:32], in_=b3, axis=AX.X, op=ALU.add), reads=["gB"], writes=["gs"])
            R.act(lambda e: e.activation(out=gs[:, 32:48], in_=gs[:, 16:32], func=AF.Sqrt, bias=LN_EPS, scale=1.0 / 64), reads=["gs"], writes=["gs"])
            R.dve(lambda e: e.reciprocal(out=gs[:, 48:64], in_=gs[:, 32:48]), reads=["gs"], writes=["gs"])
            R.dve(lambda e: e.tensor_tensor(out=g3, in0=g3, in1=gs[:, 48:64].unsqueeze(2).broadcast_to([128, 16, 64]), op=ALU.mult),
                  reads=["gA", "gs"], writes=["gA"])
            g4 = gA[:].rearrange("p (j c) -> p j c", c=256)
            R.pool(lambda e: e.tensor_tensor(out=g4, in0=g4, in1=gnw[:].unsqueeze(1).broadcast_to([128, 4, 256]), op=ALU.mult),
                   reads=["gA", "gnw"], writes=["gA"])
            R.dve(lambda e, qg=qg: e.tensor_tensor(out=yr[:], in0=g4, in1=vrg[:, 4 * qg:4 * qg + 4, 256:512], op=ALU.mult),
                  reads=["gA", "vrg"], writes=["yr"])
            for j in range(4):
                for ch in range(2):
                    R.pe(lambda e, j=j, ch=ch: e.transpose(out=pT[:, (j * 2 + ch) * 128:(j * 2 + ch + 1) * 128], in_=yr[:, j, ch * 128:(ch + 1) * 128],
                                                         identity=ident_b[:]), reads=["yr", "ident_b"], writes=["pT"])
            R.act(lambda e, qg=qg: e.copy(out=yT[:, 0:2, qg * 512:(qg + 1) * 512].rearrange("p c (j t) -> p j c t", j=4),
                                          in_=pT[:].rearrange("p (j c t) -> p j c t", j=4, c=2)), reads=["pT"], writes=["yT"])
    R.barrier()

    nsa_stage = env.get("nsa_stage", 2)
    nsa_part = env.get("nsa_part", 9)
    nsa_hks = env.get("nsa_hks", (0, 1))
    nsa_qcs = env.get("nsa_qcs", tuple(range(4)))
    with contextlib.ExitStack() as sc_:
        sb3, _ = mk(sc_)
        cosT, sinT, rt = load_rope(sb3)
        kcT = sb3("kcT", [128, S], BF16)
        vcT = sb3("vcT", [128, S], BF16)
        gates = sb3("gates", [128, NT, 24])
        v2 = sb3("v2", [128, NT, 4, 128], BF16)
        qT = sb3("qT", [128, 2, S], BF16)
        qTr2 = sb3("qTr2", [128, 2, S], BF16)
        ksT = sb3("ksT", [128, S], BF16)
        kwT = sb3("kwT", [128, S], BF16)
        w2k = sb3("w2k", [128, 128], BF16)
        w2v = sb3("w2v", [128, 64], BF16)
        pbias = sb3("pbias", [128, 2])
        kcd = [sb3("kcd%d" % i, [128, 128], BF16) for i in range(2)]
        vca = sb3("vca", [128, 2, 128], BF16)
        ovl = sb3("ovl", [128, 32], BF16)
        scC = contextlib.ExitStack()
        sbC, _ = mk(scC)
        w1d = [sbC("w1d%d" % i, [128, 32, 128], BF16) for i in range(2)]
        posr = sbC("posr", [32, 2, 64])
        posT = sbC("posT", [64, 2, 32], BF16)
        R.dma("sp", lambda e: e.dma_start(out=ovl[:], in_=cst["overlap"][:, :]), writes=["ovl"])
        for kind in range(2):
            for cp in range(2):
                R.dma("pool", lambda e, kind=kind, cp=cp: e.dma_start(
                    out=w1d[kind][cp * 64:(cp + 1) * 64, :, :], in_=cmp_w1[l, kind, :, :].rearrange("(l d) n -> d l n", d=64)), writes=["w1d%d" % kind])
            R.dma("sp", lambda e, kind=kind: e.dma_start(out=posr[:, kind, :], in_=cmp_pos[l, kind, :, :]), writes=["posr"])
        R.dma("pool", lambda e: e.dma_start(out=w2k[:, 0:64], in_=cmp_w2[l, 0, :, :]), writes=["w2k"])
        R.dma("pool", lambda e: e.dma_start(out=w2k[:, 64:128], in_=cmp_w2[l, 0, :, :]), writes=["w2k"])
        R.dma("pool", lambda e: e.dma_start(out=w2v[:], in_=cmp_w2[l, 1, :, :]), writes=["w2v"])
        proj_plain([(0, C_KC, 128)], kcT, "kcT", None)
        proj_plain([(0, C_VC, 128)], vcT, "vcT", None)
        wt = sbC("wt", [128, 8, 408], BF16)
        R.dma("pool", lambda e: e.dma_start(out=wt[:], in_=w_in[l, :, C_VS:C_VS + 408].rearrange("(k p) n -> p k n", p=128)), writes=["wt"])
        R.pool(lambda e: e.memset(v2[:].rearrange("p a b c -> p (a b) c")[:, :, 64:128], 1.0), writes=["v2"])
        for tc in range(NT):
            pn, pt = pA.next()
            for kc in range(8):
                R.pe(lambda e, pt=pt, kc=kc, tc=tc: e.matmul(pt[:, 0:408], lhsT=hT[:, kc, tc * 128:(tc + 1) * 128], rhs=wt[:, kc, :],
                                                             start=(kc == 0), stop=(kc == 7)), reads=["hT", "wt"], writes=[pn])
            R.act(lambda e, pt=pt, tc=tc: e.copy(out=v2[:, tc, 0:2, 0:64], in_=pt[:, 0:128].rearrange("p (h d) -> p h d", h=2)), reads=[pn], writes=["v2"])
            R.act(lambda e, pt=pt, tc=tc: e.copy(out=v2[:, tc, 2:4, 0:64], in_=pt[:, 256:384].rearrange("p (h d) -> p h d", h=2)), reads=[pn], writes=["v2"])
            R.act(lambda e, pt=pt, tc=tc: e.activation(out=gates[:, tc, :], in_=pt[:, 384:408], func=AF.Sigmoid), reads=[pn], writes=["gates"])
        for kind in range(2):
            R.pe(lambda e, kind=kind: e.transpose(out=pX[0:64, kind * 32:(kind + 1) * 32], in_=posr[:, kind, :], identity=ident_f[0:32, 0:32]),
                 reads=["posr", "ident_f"], writes=["pX"])
        R.dve(lambda e: e.tensor_copy(out=posT[:].rearrange("p k l -> p (k l)"), in_=pX[0:64, 0:64]), reads=["pX"], writes=["posT"])
        for kind in range(2):
            for li in range(32):
                R.pe(lambda e, kind=kind, li=li: e.matmul(pX[:, 64 + kind:65 + kind], lhsT=w1d[kind][0:64, li, :], rhs=posT[:, kind, li:li + 1],
                                                          start=(li == 0 and kind == 0), stop=(li == 31), skip_group_check=True),
                     reads=["w1d%d" % kind, "posT"], writes=["pX"])
        R.dve(lambda e: e.tensor_copy(out=pbias[:], in_=pX[:, 64:66]), reads=["pX"], writes=["pbias"])
        zt = sbC("zt", [128, 128])
        z2 = sbC("z2", [128, 128])
        blk = sbC("blk", [128, 32, 127], BF16)
        gT = sbC("gT", [128, 128], BF16)
        R.pool(lambda e: e.memset(vca[:], 0.0), writes=["vca"])
        for i_ in range(2):
            R.pool(lambda e, i_=i_: e.memset(kcd[i_][:], 0.0), writes=["kcd"])
        for kind in range(2):
            src = kcT if kind == 0 else vcT
            srcn = "kcT" if kind == 0 else "vcT"
            for li in range(32):
                R.dve(lambda e, li=li, src=src: e.tensor_copy(out=blk[:, li, :], in_=src[:, li:li + 2017:16]), reads=[srcn], writes=["blk"])
            for hk in range(2):
                ro = hk * 64
                pn, pt = pA.next()
                for li in range(32):
                    R.pe(lambda e, pt=pt, kind=kind, li=li, ro=ro: e.matmul(
                        pt[:, 0:127], lhsT=w1d[kind][ro:ro + 64, li, :], rhs=blk[ro:ro + 64, li, :], start=(li == 0), stop=(li == 31)),
                        reads=["w1d%d" % kind, "blk"], writes=[pn])
                R.act(lambda e, pt=pt, kind=kind: e.activation(out=zt[:, 0:127], in_=pt[:, 0:127], func=AF.Identity, bias=pbias[:, kind:kind + 1], scale=1.0),
                      reads=[pn, "pbias"], writes=["zt"])
                R.dve(lambda e: e.tensor_tensor(out=z2[:, 0:127], in0=zt[:, 0:127], in1=zt[:, 0:127], op=ALU.mult), reads=["zt"], writes=["z2"])
                R.dve(lambda e: e.tensor_scalar(out=z2[:, 0:127], in0=z2[:, 0:127], scalar1=0.044715, scalar2=1.0, op0=ALU.mult, op1=ALU.add),
                      reads=["z2"], writes=["z2"])
                R.dve(lambda e: e.tensor_tensor(out=z2[:, 0:127], in0=z2[:, 0:127], in1=zt[:, 0:127], op=ALU.mult), reads=["z2", "zt"], writes=["z2"])
                R.act(lambda e: e.activation(out=z2[:, 0:127], in_=z2[:, 0:127], func=AF.Sigmoid, scale=1.5957691216), reads=["z2"], writes=["z2"])
                R.dve(lambda e: e.tensor_tensor(out=gT[:, 0:127], in0=z2[:, 0:127], in1=zt[:, 0:127], op=ALU.mult), reads=["z2", "zt"], writes=["gT"])
                if kind == 0:
                    R.pe(lambda e: e.matmul(pX[:, 128:255], lhsT=w2k[:, :], rhs=gT[:, 0:127], start=True, stop=True), reads=["w2k", "gT"], writes=["pX"])
                    R.act(lambda e, hk=hk: e.copy(out=kcd[hk][:, 0:127], in_=pX[:, 128:255]), reads=["pX"], writes=["kcd"])
                else:
                    R.pe(lambda e: e.matmul(pX[0:127, 256:320], lhsT=gT[:, 0:127], rhs=w2v[:, :], start=True, stop=True), reads=["w2v", "gT"], writes=["pX"])
                    R.act(lambda e, hk=hk: e.copy(out=vca[0:127, hk, 0:64], in_=pX[0:127, 256:320]), reads=["pX"], writes=["vca"])
        for hk in range(2):
            R.pool(lambda e, hk=hk: e.memset(vca[:, hk, 64:96], 1.0), reads=[], writes=["vca"])
            R.pool(lambda e, hk=hk: e.tensor_copy(out=vca[:, hk, 96:128], in_=ovl[:]), reads=["ovl"], writes=["vca"])

        if "kcd" in tap_t and b == 0 and l == 0:
            for i_ in range(2):
                fin.append(R.dma("sp", lambda e, i_=i_: e.dma_start(out=tap_t["kcd"][:, i_, :], in_=kcd[i_][:]), reads=["kcd"], writes=["tap_kcd%d" % i_]))
            fin.append(R.dma("sp", lambda e: e.dma_start(out=tap_t["vca"][:, :, :], in_=vca[:]), reads=["vca"], writes=["tap_vca"]))
        R.barrier()
        scC.close()
        cbc = sb3("cbc", [128, S], BF16)
        forced = sb3("forced", [128, NT, 32])
        Eexp = sb3("Eexp", [128, NT, 128], BF16)
        tmask = sb3("tmask", [128, 2, TW], BF16)
        R.dma("sp", lambda e: e.dma_start(out=cbc[:], in_=cst["cb_cmp"][:, :]), writes=["cbc"])
        R.dma("sp", lambda e: e.dma_start(out=forced[:], in_=cst["forced"][:, :, :]), writes=["forced"])
        R.dma("sp", lambda e: e.dma_start(out=Eexp[:], in_=cst["Eexp"][:, :, :]), writes=["Eexp"])
        R.dma("sp", lambda e: e.dma_start(out=tmask[:], in_=cst["tmask"][:, :, :]), writes=["tmask"])
        ptr = Rot([("PT%d" % i, sb3("PT%d" % i, [128, 512], BF16)) for i in range(3)])
        mk_r = Rot([("mk%d" % i, sb3("mk%d" % i, [128, 512], BF16)) for i in range(2)])
        selq = sb3("selq", [128, 4, 128], BF16)
        selT = sb3("selT", [128, 512], BF16)
        on = sb3("on", [128, 4, 4, 64])
        tmpo = sb3("tmpo", [128, 4, 64])
        impa = sb3("impa", [128, 4, 32])
        tmpi = sb3("tmpi", [128, 4, 32])
        nst = sb3("nst", [128, 64])
        ynb = sb3("ynb", [128, 4, 256], BF16)
        R.pool(lambda e: e.memset(selq[:], 0.0), writes=["selq"])

        def smm(pn, pt, Ktile, kn, k0, Qsrc, qn, ro, ch, qg):
            R.pe(lambda e: e.matmul(pt[:, :], lhsT=Ktile[ro:ro + 64, k0:k0 + 128], rhs=Qsrc[ro:ro + 64, ch, qg * 512:(qg + 1) * 512],
                                    start=True, stop=True), reads=[kn, qn], writes=[pn])

        def finish(an, acc, g, qg, hk, br, first):
            a3 = acc[:, :].rearrange("p (j c) -> p j c", j=4)
            R.dve(lambda e: e.tensor_scalar(out=nst[:, 0:4], in0=a3[:, :, 64], scalar1=1e-30, scalar2=None, op0=ALU.max), reads=[an], writes=["nst"])
            R.dve(lambda e: e.reciprocal(out=nst[:, 0:4], in_=nst[:, 0:4]), reads=["nst"], writes=["nst"])
            R.dve(lambda e: e.tensor_tensor(out=nst[:, 4:8], in0=nst[:, 0:4], in1=gates[:, 4 * qg:4 * qg + 4, hk * 12 + g * 3 + br], op=ALU.mult),
                  reads=["nst", "gates"], writes=["nst"])
            dst = on[:, :, g, :]
            if first:
                R.dve(lambda e: e.tensor_tensor(out=dst, in0=a3[:, :, 0:64], in1=nst[:, 4:8].unsqueeze(2).broadcast_to([128, 4, 64]), op=ALU.mult),
                      reads=[an, "nst"], writes=["on"])
            else:
                R.dve(lambda e: e.tensor_tensor(out=tmpo[:], in0=a3[:, :, 0:64], in1=nst[:, 4:8].unsqueeze(2).broadcast_to([128, 4, 64]), op=ALU.mult),
                      reads=[an, "nst"], writes=["tmpo"])
                R.pool(lambda e: e.tensor_tensor(out=dst, in0=dst, in1=tmpo[:], op=ALU.add), reads=["on", "tmpo"], writes=["on"])

        for hk in (nsa_hks if nsa_stage >= 2 else ()):
            for ch in range(2):
                c0 = C_NQ + hk * 256 + ch * 128
                proj_plain([(0, c0, 128)], qT, "qT", ch)
                proj_rope([(0, c0, 128)], perm_pieces(c0), qTr2, "qTr2", ch, cosT, sinT, rt)
            proj_rope(dup_pieces(C_KS + hk * 64), dup_perm_pieces(C_KS + hk * 64), ksT, "ksT", None, cosT, sinT, rt)
            proj_rope(dup_pieces(C_KW + hk * 64), dup_perm_pieces(C_KW + hk * 64), kwT, "kwT", None, cosT, sinT, rt)
            for qg in (nsa_qcs if nsa_part >= 1 else ()):
                for g in range(4):
                    ch, ro = g // 2, (g % 2) * 64
                    pn, pt = pA.next()
                    smm(pn, pt, kcd[hk], "kcd", 0, qT, "qT", ro, ch, qg)
                    ptn, PT = ptr.next()
                    R.act(lambda e, pt=pt, PT=PT: e.activation(out=PT[:, :], in_=pt[:, :], func=AF.Exp, scale=SCALE), reads=[pn], writes=[ptn])
                    R.pool(lambda e, PT=PT, qg=qg: e.tensor_tensor(out=PT[:, :], in0=PT[:, :], in1=cbc[:, qg * 512:(qg + 1) * 512], op=ALU.mult),
                           reads=[ptn, "cbc"], writes=[ptn])
                    an, acc = pO.next()
                    for j in range(4):
                        R.pe(lambda e, j=j, acc=acc, PT=PT, hk=hk: e.matmul(acc[:, j * 128:(j + 1) * 128], lhsT=PT[:, j * 128:(j + 1) * 128], rhs=vca[:, hk, :],
                                                                          start=(j == 0), stop=False, skip_group_check=True), reads=[ptn, "vca"], writes=[an])
                    if nsa_part < 2:
                        continue
                    a3 = acc[:, :].rearrange("p (j c) -> p j c", j=4)
                    finish(an, acc, g, qg, hk, 0, True)
                    if g == 0:
                        R.dve(lambda e, a3=a3: e.tensor_tensor(out=impa[:], in0=a3[:, :, 96:128], in1=nst[:, 0:4].unsqueeze(2).broadcast_to([128, 4, 32]), op=ALU.mult),
                              reads=[an, "nst"], writes=["impa"])
                    else:
                        R.dve(lambda e, a3=a3: e.tensor_tensor(out=tmpi[:], in0=a3[:, :, 96:128], in1=nst[:, 0:4].unsqueeze(2).broadcast_to([128, 4, 32]), op=ALU.mult),
                              reads=[an, "nst"], writes=["tmpi"])
                        R.pool(lambda e: e.tensor_tensor(out=impa[:], in0=impa[:], in1=tmpi[:], op=ALU.add), reads=["impa", "tmpi"], writes=["impa"])
                if nsa_part < 3:
                    continue
                R.dve(lambda e, qg=qg: e.tensor_tensor(out=impa[:], in0=impa[:], in1=forced[:, 4 * qg:4 * qg + 4, :], op=ALU.max), reads=["impa", "forced"], writes=["impa"])
                for j in range(4):
                    R.dve(lambda e, j=j: e.max(out=nst[:, 16 + 8 * j:24 + 8 * j], in_=impa[:, j, :]), reads=["impa"], writes=["nst"])
                    R.dve(lambda e, j=j: e.tensor_scalar(out=selq[:, j, 0:32], in0=impa[:, j, :], scalar1=nst[:, 23 + 8 * j:24 + 8 * j], scalar2=None, op0=ALU.is_ge),
                          reads=["impa", "nst"], writes=["selq"])
                for j in range(4):
                    R.pe(lambda e, j=j: e.transpose(out=pT[:, j * 128:(j + 1) * 128], in_=selq[:, j, :], identity=ident_b[:]), reads=["selq", "ident_b"], writes=["pT"])
                R.act(lambda e: e.copy(out=selT[:], in_=pT[:, 0:512]), reads=["pT"], writes=["selT"])
                if nsa_part < 4:
                    continue
                for br in ((1, 2) if nsa_part >= 5 else (1,)):
                    Ksrc, ksn = (ksT, "ksT") if br == 1 else (kwT, "kwT")
                    kcs = list(range(0, 4 * qg + 4)) if br == 1 else list(range(max(0, 4 * qg - 4), 4 * qg + 4))
                    accs = [pO.next() for _ in range(4)]
                    firsts = [True] * 4
                    for kc in kcs:
                        off = 384 + qg * 512 - kc * 128
                        mkn, mkt = mk_r.next()
                        if br == 1:
                            R.pe(lambda e, kc=kc: e.matmul(pX[:, :], lhsT=Eexp[:, kc, :], rhs=selT[:, :], start=True, stop=True), reads=["Eexp", "selT"], writes=["pX"])
                            R.dve(lambda e, mkt=mkt, off=off: e.tensor_tensor(out=mkt[:], in0=pX[:, :], in1=tmask[:, 0, off:off + 512], op=ALU.mult),
                                  reads=["pX", "tmask"], writes=[mkn])
                        for g in range(4):
                            ch, ro = g // 2, (g % 2) * 64
                            pn, pt = pA.next()
                            smm(pn, pt, Ksrc, ksn, kc * 128, qTr2, "qTr2", ro, ch, qg)
                            ptn, PT = ptr.next()
                            R.act(lambda e, pt=pt, PT=PT: e.activation(out=PT[:, :], in_=pt[:, :], func=AF.Exp, scale=SCALE), reads=[pn], writes=[ptn])
                            if br == 1:
                                R.pool(lambda e, PT=PT, mkt=mkt: e.tensor_tensor(out=PT[:, :], in0=PT[:, :], in1=mkt[:], op=ALU.mult), reads=[ptn, mkn], writes=[ptn])
                            else:
                                R.pool(lambda e, PT=PT, off=off: e.tensor_tensor(out=PT[:, :], in0=PT[:, :], in1=tmask[:, 1, off:off + 512], op=ALU.mult),
                                       reads=[ptn, "tmask"], writes=[ptn])
                            an, acc = accs[g]
                            for j in range(4):
                                qc = 4 * qg + j
                                if kc > qc or (br == 2 and kc < qc - 4):
                                    continue
                                stf = firsts[g]
                                firsts[g] = False
                                R.pe(lambda e, j=j, acc=acc, PT=PT, kc=kc, br=br, hk=hk, stf=stf: e.matmul(
                                    acc[:, j * 128:(j + 1) * 128], lhsT=PT[:, j * 128:(j + 1) * 128], rhs=v2[:, kc, (br - 1) * 2 + hk, :],
                                    start=stf, stop=False, skip_group_check=True), reads=[ptn, "v2"], writes=[an])
                    for g in range(4):
                        an, acc = accs[g]
                        finish(an, acc, g, qg, hk, br, False)
                R.act(lambda e: e.copy(out=ynb[:].rearrange("p j c -> p (j c)"), in_=on[:].rearrange("p j g d -> p (j g d)")), reads=["on"], writes=["ynb"])
                for j in range(4):
                    for ch in range(2):
                        R.pe(lambda e, j=j, ch=ch: e.transpose(out=pT[:, (j * 2 + ch) * 128:(j * 2 + ch + 1) * 128], in_=ynb[:, j, ch * 128:(ch + 1) * 128],
                                                             identity=ident_b[:]), reads=["ynb", "ident_b"], writes=["pT"])
                R.act(lambda e, qg=qg, hk=hk: e.copy(out=yT[:, 4 + 2 * hk:6 + 2 * hk, qg * 512:(qg + 1) * 512].rearrange("p c (j t) -> p j c t", j=4),
                                                     in_=pT[:].rearrange("p (j c t) -> p j c t", j=4, c=2)), reads=["pT"], writes=["yT"])
    R.barrier()
    if "yT" in tap_t and b == 0 and l == 0:
        fin.append(R.dma("sp", lambda e: e.dma_start(out=tap_t["yT"][:, :, :], in_=yT[:]), reads=["yT"], writes=["tap_yT2"]))

    with contextlib.ExitStack() as sc_:
        sb4, _ = mk(sc_)
        g1p = bcast_mod(sb4, "g1p", 2, True)
        lng = sb4("lng", [128, D])
        lnb = sb4("lnb", [128, D])
        R.dma("sp", lambda e: e.dma_start(out=lng[:], in_=ln_g[l, 0:1, :].broadcast_to([128, D])), writes=["lng"])
        R.dma("sp", lambda e: e.dma_start(out=lnb[:], in_=ln_b[l, 0:1, :].broadcast_to([128, D])), writes=["lnb"])
        wo = sb4("wo", [128, 8, D], BF16)
        R.dma("pool", lambda e: e.dma_start(out=wo[:], in_=w_out[l, :, :].rearrange("(k p) n -> p k n", p=128)), writes=["wo"])
        mt_r = Rot([("mt%d" % i, sb4("mt%d" % i, [128, D])) for i in range(2)])
        for tc in range(NT):
            xcn, xc = xc_r.next()
            stn, stt = st_r.next()
            mtn, mt = mt_r.next()
            R.dma("sp", lambda e, xc=xc, tc=tc: e.dma_start(out=xc[:], in_=x_src[b, tc * 128:(tc + 1) * 128, :]), reads=["xb%d_%d" % (b, tc)], writes=[xcn])
            for hf in range(2):
                pn, pt = pA.next()
                for kc in range(8):
                    R.pe(lambda e, pt=pt, kc=kc, tc=tc, hf=hf: e.matmul(pt[:, :], lhsT=yT[:, kc, tc * 128:(tc + 1) * 128], rhs=wo[:, kc, hf * 512:(hf + 1) * 512],
                                                                         start=(kc == 0), stop=(kc == 7)), reads=["yT", "wo"], writes=[pn])
                R.dve(lambda e, pt=pt, mt=mt, hf=hf: e.tensor_tensor(out=mt[:, hf * 512:(hf + 1) * 512], in0=pt[:, :], in1=g1p[:, hf * 512:(hf + 1) * 512], op=ALU.mult),
                      reads=[pn, "g1p"], writes=[mtn])
            R.dve(lambda e, xc=xc, mt=mt: e.scalar_tensor_tensor(out=xc[:], in0=xc[:], scalar=ALPHA, in1=mt[:], op0=ALU.mult, op1=ALU.add),
                  reads=[xcn, mtn], writes=[xcn])
            layer_norm_chunk(xcn, xc, stn, stt, xcn, xc)
            R.pool(lambda e, xc=xc: e.tensor_tensor(out=xc[:], in0=xc[:], in1=lng[:], op=ALU.mult), reads=[xcn, "lng"], writes=[xcn])
            R.dve(lambda e, xc=xc: e.tensor_tensor(out=xc[:], in0=xc[:], in1=lnb[:], op=ALU.add), reads=[xcn, "lnb"], writes=[xcn])
            R.dma("sp", lambda e, xc=xc, tc=tc: e.dma_start(out=xbuf[b, tc * 128:(tc + 1) * 128, :], in_=xc[:]), reads=[xcn], writes=["xb%d_%d" % (b, tc)])
    if "x1" in tap_t and b == 0 and l == 0:
        fin.append(R.dma("sp", lambda e: e.dma_start(out=tap_t["x1"][:, :], in_=xbuf[0, :, :]), reads=["xb0_%d" % t_ for t_ in range(NT)], writes=["tap_x1"]))


NE = 32
GELU_A = 1.702


def moe_consts():
    c = {}
    tp = np.arange(128)[:, None]
    t = np.arange(128)[None, :]
    c["ltri"] = (tp < t).astype(np.float32).astype(ml_dtypes.bfloat16)
    c["onesb"] = np.ones((128, 128), np.float32).astype(ml_dtypes.bfloat16)
    c["iota3"] = np.broadcast_to(np.arange(128, dtype=np.float32)[None, None, :], (128, NE, 128)).astype(ml_dtypes.bfloat16).copy()
    return c


def moe(nc, R, mk, env):
    l, NB, NL = env["l"], env["NB"], env["NL"]
    xbuf, modbuf, out, cst = env["xbuf"], env["modbuf"], env["out"], env["cst"]
    ident_f, ident_b, tap_t, fin = env["ident_f"], env["ident_b"], env["tap_t"], env["fin"]
    ln_g, ln_b = env["ln_g"], env["ln_b"]
    router_w, router_b, w_gu, b_gu, w_dn, b_dn = env["router_w"], env["router_b"], env["w_gu"], env["b_gu"], env["w_dn"], env["b_dn"]
    H2d, Gd, CMd, XTd, PGd, Yd = env["H2d"], env["Gd"], env["CMd"], env["XTd"], env["PGd"], env["Yd"]
    last = (l == NL - 1)
    moe_stop = env.get("moe_stop", "D")

    def ln_chunk(xc, xcn, stt, stn):
        R.dve(lambda e: e.bn_stats(out=stt[:, 0:6], in_=xc[:, 0:512]), reads=[xcn], writes=[stn])
        R.dve(lambda e: e.bn_stats(out=stt[:, 6:12], in_=xc[:, 512:1024]), reads=[xcn], writes=[stn])
        R.dve(lambda e: e.bn_aggr(out=stt[:, 12:14], in_=stt[:, 0:12]), reads=[stn], writes=[stn])
        R.act(lambda e: e.activation(out=stt[:, 14:15], in_=stt[:, 13:14], func=AF.Sqrt, bias=LN_EPS, scale=1.0), reads=[stn], writes=[stn])
        R.dve(lambda e: e.reciprocal(out=stt[:, 15:16], in_=stt[:, 14:15]), reads=[stn], writes=[stn])
        R.dve(lambda e: e.tensor_scalar(out=xc[:], in0=xc[:], scalar1=stt[:, 12:13], scalar2=stt[:, 15:16], op0=ALU.subtract, op1=ALU.mult),
              reads=[xcn, stn], writes=[xcn])

    def bcast_row(sbx, name, src_ap, plus1=False, width=D):
        dst = sbx(name, [128, width])
        R.dma("sp", lambda e: e.dma_start(out=dst[:], in_=src_ap.broadcast_to([128, width])), reads=["modbuf"], writes=[name])
        if plus1:
            R.pool(lambda e: e.tensor_scalar(out=dst[:], in0=dst[:], scalar1=1.0, scalar2=None, op0=ALU.add), reads=[name], writes=[name])
        return dst

    with contextlib.ExitStack() as sA:
        sb, ps = mk(sA)
        wr = sb("wr", [128, 8, NE])
        wrh = sb("wrh", [128, 8, NE], BF16)
        wrl = sb("wrl", [128, 8, NE], BF16)
        R.dma("sp", lambda e: e.dma_start(out=wr[:], in_=router_w[l, :, :].rearrange("(k p) n -> p k n", p=128)), writes=["wr"])
        R.dve(lambda e: e.tensor_copy(out=wrh[:], in_=wr[:]), reads=["wr"], writes=["wrh"])
        R.dve(lambda e: e.tensor_tensor(out=wrl[:], in0=wr[:], in1=wrh[:], op=ALU.subtract), reads=["wr", "wrh"], writes=["wrl"])
        rb = bcast_row(sb, "rb", router_b[l:l + 1, :], width=NE)
        ltri = sb("ltri", [128, 128], BF16)
        onesb = sb("onesb", [128, 128], BF16)
        R.dma("sp", lambda e: e.dma_start(out=ltri[:], in_=cst["ltri"][:, :]), writes=["ltri"])
        R.dma("sp", lambda e: e.dma_start(out=onesb[:], in_=cst["onesb"][:, :]), writes=["onesb"])
        xc_r = Rot([("xa%d" % i, sb("xa%d" % i, [128, D])) for i in range(2)])
        hb_r = Rot([("hb%d" % i, sb("hb%d" % i, [128, D], BF16)) for i in range(2)])
        st_r = Rot([("sta%d" % i, sb("sta%d" % i, [128, 16])) for i in range(2)])
        hT_r = Rot([("hTa%d" % i, sb("hTa%d" % i, [128, 2, D], BF16)) for i in range(2)])
        hl_r = Rot([("hl%d" % i, sb("hl%d" % i, [128, D], BF16)) for i in range(2)])
        lg_r = Rot([("lg%d" % i, sb("lg%d" % i, [128, 96])) for i in range(2)])
        maskb = sb("maskb", [128, NT, NE], BF16)
        Gs = sb("Gs", [128, NT, NE])
        CMs = sb("CMs", [128, NT, NE])
        pTh = ps("pTh", [128, D], BF16)
        pTl = ps("pTl", [128, D], BF16)
        pl_r = Rot([("pl%d" % i, ps("pl%d" % i, [128, 512])) for i in range(2)])
        sh2 = sb("sh2", [128, D])
        sc2 = sb("sc2", [128, D])
        for b in range(NB):
            R.dma("sp", lambda e, b=b: e.dma_start(out=sh2[:], in_=modbuf[b, l:l + 1, 3 * D:4 * D].broadcast_to([128, D])), reads=["modbuf"], writes=["sh2_0"])
            R.dma("sp", lambda e, b=b: e.dma_start(out=sc2[:], in_=modbuf[b, l:l + 1, 4 * D:5 * D].broadcast_to([128, D])), reads=["modbuf"], writes=["sc2_0"])
            R.pool(lambda e: e.tensor_scalar(out=sc2[:], in0=sc2[:], scalar1=1.0, scalar2=None, op0=ALU.add), reads=["sc2_0"], writes=["sc2_0"])
            for tc in range(NT):
                xcn, xc = xc_r.next()
                hbn, hb = hb_r.next()
                stn, stt = st_r.next()
                htn, hTc = hT_r.next()
                lgn, lg = lg_r.next()
                pln, pl = pl_r.next()
                R.dma("sp", lambda e, xc=xc, tc=tc, b=b: e.dma_start(out=xc[:], in_=xbuf[b, tc * 128:(tc + 1) * 128, :]), reads=["xb%d_%d" % (b, tc)], writes=[xcn])
                ln_chunk(xc, xcn, stt, stn)
                R.pool(lambda e, xc=xc: e.tensor_tensor(out=xc[:], in0=xc[:], in1=sc2[:], op=ALU.mult), reads=[xcn, "sc2_0"], writes=[xcn])
                R.dve(lambda e, xc=xc: e.tensor_tensor(out=xc[:], in0=xc[:], in1=sh2[:], op=ALU.add), reads=[xcn, "sh2_0"], writes=[xcn])
                R.act(lambda e, xc=xc, hb=hb: e.copy(out=hb[:], in_=xc[:]), reads=[xcn], writes=[hbn])
                R.dma("sp", lambda e, hb=hb, tc=tc, b=b: e.dma_start(out=H2d[b, tc * 128:(tc + 1) * 128, :], in_=hb[:]), reads=[hbn], writes=["H2d%d" % b])
                hln, hl = hl_r.next()
                R.dve(lambda e, xc=xc, hb=hb, hl=hl: e.tensor_tensor(out=hl[:], in0=xc[:], in1=hb[:], op=ALU.subtract), reads=[xcn, hbn], writes=[hln])
                for kc in range(8):
                    R.pe(lambda e, hb=hb, kc=kc: e.transpose(out=pTh[:, kc * 128:(kc + 1) * 128], in_=hb[:, kc * 128:(kc + 1) * 128], identity=ident_b[:]),
                         reads=[hbn, "ident_b"], writes=["pTh"])
                for kc in range(8):
                    R.pe(lambda e, hl=hl, kc=kc: e.transpose(out=pTl[:, kc * 128:(kc + 1) * 128], in_=hl[:, kc * 128:(kc + 1) * 128], identity=ident_b[:]),
                         reads=[hln, "ident_b"], writes=["pTl"])
                R.act(lambda e, hTc=hTc: e.copy(out=hTc[:, 0, :], in_=pTh[:]), reads=["pTh"], writes=[htn])
                R.act(lambda e, hTc=hTc: e.copy(out=hTc[:, 1, :], in_=pTl[:]), reads=["pTl"], writes=[htn])
                terms = [(0, wrh, "wrh"), (1, wrh, "wrh"), (0, wrl, "wrl")]
                for ti, (hs, wt_, wn_) in enumerate(terms):
                    for kc in range(8):
                        R.pe(lambda e, hTc=hTc, kc=kc, pl=pl, hs=hs, wt_=wt_, ti=ti: e.matmul(pl[:, 0:NE], lhsT=hTc[:, hs, kc * 128:(kc + 1) * 128], rhs=wt_[:, kc, :],
                                                                                     start=(ti == 0 and kc == 0), stop=(ti == 2 and kc == 7)),
                             reads=[htn, wn_], writes=[pln])
                R.dve(lambda e, lg=lg, pl=pl: e.tensor_tensor(out=lg[:, 0:32], in0=pl[:, 0:NE], in1=rb[:], op=ALU.add), reads=[pln, "rb"], writes=[lgn])
                R.dve(lambda e, lg=lg: e.max(out=lg[:, 32:40], in_=lg[:, 0:32]), reads=[lgn], writes=[lgn])
                R.dve(lambda e, lg=lg: e.tensor_scalar(out=lg[:, 64:96], in0=lg[:, 0:32], scalar1=lg[:, 35:36], scalar2=None, op0=ALU.is_ge), reads=[lgn], writes=[lgn])
                R.dve(lambda e, lg=lg: e.tensor_scalar(out=lg[:, 40:41], in0=lg[:, 32:33], scalar1=-1.0, scalar2=None, op0=ALU.mult), reads=[lgn], writes=[lgn])
                R.act(lambda e, lg=lg: e.activation(out=lg[:, 0:32], in_=lg[:, 0:32], func=AF.Exp, bias=lg[:, 40:41], scale=1.0), reads=[lgn], writes=[lgn])
                R.dve(lambda e, lg=lg: e.tensor_tensor(out=lg[:, 0:32], in0=lg[:, 0:32], in1=lg[:, 64:96], op=ALU.mult), reads=[lgn], writes=[lgn])
                R.dve(lambda e, lg=lg: e.tensor_reduce(out=lg[:, 41:42], in_=lg[:, 0:32], axis=AX.X, op=ALU.add), reads=[lgn], writes=[lgn])
                R.dve(lambda e, lg=lg: e.reciprocal(out=lg[:, 42:43], in_=lg[:, 41:42]), reads=[lgn], writes=[lgn])
                R.dve(lambda e, lg=lg, tc=tc: e.tensor_scalar(out=Gs[:, tc, :], in0=lg[:, 0:32], scalar1=lg[:, 42:43], scalar2=None, op0=ALU.mult), reads=[lgn], writes=["Gs"])
                R.pool(lambda e, lg=lg, tc=tc: e.tensor_copy(out=maskb[:, tc, :], in_=lg[:, 64:96]), reads=[lgn], writes=["maskb"])
            for tc in range(NT):
                c = tc % 4
                g0 = tc - c
                pln, pl = pl_r.next()
                for cp in range(c):
                    R.pe(lambda e, pl=pl, cp=cp, g0=g0: e.matmul(pl[:, 0:NE], lhsT=onesb[:], rhs=maskb[:, g0 + cp, :], start=(cp == 0), stop=False),
                         reads=["onesb", "maskb"], writes=[pln])
                R.pe(lambda e, pl=pl, tc=tc, c=c: e.matmul(pl[:, 0:NE], lhsT=ltri[:], rhs=maskb[:, tc, :], start=(c == 0), stop=True), reads=["ltri", "maskb"], writes=[pln])
                R.dve(lambda e, pl=pl, tc=tc: e.scalar_tensor_tensor(out=CMs[:, tc, :], in0=pl[:, 0:NE], scalar=1.0, in1=maskb[:, tc, :], op0=ALU.add, op1=ALU.mult),
                      reads=[pln, "maskb"], writes=["CMs"])
            R.dve(lambda e: e.tensor_scalar(out=CMs[:], in0=CMs[:], scalar1=-1.0, scalar2=None, op0=ALU.add), reads=["CMs"], writes=["CMs"])
            R.dma("sp", lambda e, b=b: e.dma_start(out=Gd[b, :, :].rearrange("(c p) e -> p c e", p=128), in_=Gs[:]), reads=["Gs"], writes=["Gd%d" % b])
            R.dma("sp", lambda e, b=b: e.dma_start(out=CMd[b, :, :].rearrange("(c p) e -> p c e", p=128), in_=CMs[:]), reads=["CMs"], writes=["CMd%d" % b])
    R.barrier()
    if "G" in tap_t and l == 0:
        fin.append(R.dma("sp", lambda e: e.dma_start(out=tap_t["G"][:, :], in_=Gd[0, :, :]), reads=["Gd0"], writes=["tap_G"]))
        fin.append(R.dma("sp", lambda e: e.dma_start(out=tap_t["CM"][:, :], in_=CMd[0, :, :]), reads=["CMd0"], writes=["tap_CM"]))
    if moe_stop == "A":
        return

    with contextlib.ExitStack() as sB:
        sb, ps = mk(sB)
        iota3 = sb("iota3", [128, NE, 128], BF16)
        R.dma("sp", lambda e: e.dma_start(out=iota3[:], in_=cst["iota3"][:, :, :]), writes=["iota3"])
        h2g = sb("h2g", [128, 4, D], BF16)
        Gg = sb("Gg", [128, 4, NE])
        CMg = sb("CMg", [128, 4, NE])
        CMj = sb("CMj", [128, 4, NE])
        P = [sb("P%d" % i, [128, NE, 128], BF16) for i in range(4)]
        Pg = [sb("Pg%d" % i, [128, NE, 128], BF16) for i in range(4)]
        xe_r = Rot([("xe%d" % i, sb("xe%d" % i, [128, 8, 512], BF16)) for i in range(2)])
        pgt_r = Rot([("pgt%d" % i, sb("pgt%d" % i, [128, 1024], BF16)) for i in range(2)])
        pg_r = Rot([("pB%d" % i, ps("pB%d" % i, [128, 512])) for i in range(4)])
        pTb_r = Rot([("pTb%d" % i, ps("pTb%d" % i, [128, 1024], BF16)) for i in range(2)])
        for g in range(NB * 4):
            b, gq = g // 4, g % 4
            R.dma("sp", lambda e, b=b, gq=gq: e.dma_start(out=h2g[:], in_=H2d[b, gq * 512:(gq + 1) * 512, :].rearrange("(c p) d -> p c d", p=128)),
                  reads=["H2d%d" % b], writes=["h2g"])
            R.dma("sp", lambda e, b=b, gq=gq: e.dma_start(out=Gg[:], in_=Gd[b, gq * 512:(gq + 1) * 512, :].rearrange("(c p) e -> p c e", p=128)),
                  reads=["Gd%d" % b], writes=["Gg"])
            R.dma("sp", lambda e, b=b, gq=gq: e.dma_start(out=CMg[:], in_=CMd[b, gq * 512:(gq + 1) * 512, :].rearrange("(c p) e -> p c e", p=128)),
                  reads=["CMd%d" % b], writes=["CMg"])
            for jh in range(2):
                R.dve(lambda e, jh=jh: e.tensor_scalar(out=CMj[:], in0=CMg[:], scalar1=-128.0 * jh, scalar2=None, op0=ALU.add), reads=["CMg"], writes=["CMj"])
                for c in range(4):
                    R.dve(lambda e, c=c: e.tensor_tensor(out=P[c][:], in0=iota3[:], in1=CMj[:, c, :].unsqueeze(2).broadcast_to([128, NE, 128]), op=ALU.is_equal),
                          reads=["iota3", "CMj"], writes=["P%d" % c])
                    R.pool(lambda e, c=c: e.tensor_tensor(out=Pg[c][:], in0=P[c][:], in1=Gg[:, c, :].unsqueeze(2).broadcast_to([128, NE, 128]), op=ALU.mult),
                           reads=["P%d" % c, "Gg"], writes=["Pg%d" % c])
                for eq in range(8):
                    xen, xe = xe_r.next()
                    for dk in range(8):
                        pn, pt = pg_r.next()
                        for c in range(4):
                            R.pe(lambda e, pt=pt, c=c, dk=dk, eq=eq: e.matmul(pt[:, :], lhsT=h2g[:, c, dk * 128:(dk + 1) * 128],
                                                                             rhs=P[c][:, eq * 4:(eq + 1) * 4, :].rearrange("p e j -> p (e j)"), start=(c == 0), stop=(c == 3)),
                                 reads=["h2g", "P%d" % c], writes=[pn])
                        if dk % 2 == 0:
                            R.act(lambda e, pt=pt, xe=xe, dk=dk: e.copy(out=xe[:, dk, :], in_=pt[:, :]), reads=[pn], writes=[xen])
                        else:
                            R.dve(lambda e, pt=pt, xe=xe, dk=dk: e.tensor_copy(out=xe[:, dk, :], in_=pt[:, :]), reads=[pn], writes=[xen])
                    R.dma("sp", lambda e, xe=xe, g=g, eq=eq, jh=jh: e.dma_start(out=XTd[g // 4][g % 4, jh, eq, :, :, :], in_=xe[:]), reads=[xen], writes=["XTd%d" % g])
                for e2 in range(NE // 2):
                    ptn, ptb = pTb_r.next()
                    pgn, pgt = pgt_r.next()
                    for ee in range(2):
                        for c in range(4):
                            R.pe(lambda e, ptb=ptb, ee=ee, c=c, e2=e2: e.transpose(out=ptb[:, ee * 512 + c * 128:ee * 512 + (c + 1) * 128], in_=Pg[c][:, e2 * 2 + ee, :], identity=ident_b[:]),
                                 reads=["Pg%d" % c, "ident_b"], writes=[ptn])
                    R.act(lambda e, ptb=ptb, pgt=pgt: e.copy(out=pgt[:], in_=ptb[:]), reads=[ptn], writes=[pgn])
                    R.dma("sp", lambda e, pgt=pgt, g=g, e2=e2, jh=jh: e.dma_start(out=PGd[g // 4][g % 4, jh, e2 * 2:e2 * 2 + 2, :, :].rearrange("e j t -> j e t"),
                                                                              in_=pgt[:].rearrange("p (e t) -> p e t", e=2)), reads=[pgn], writes=["PGd%d" % g])
    R.barrier()
    if moe_stop == "B":
        return

    with contextlib.ExitStack() as sC:
        sb, ps = mk(sC)
        wgu_r = Rot([("wgu%d" % i, sb("wgu%d" % i, [128, 8, 2 * D], BF16)) for i in range(2)])
        wd_r = Rot([("wd%d" % i, sb("wd%d" % i, [128, 8, D], BF16)) for i in range(2)])
        stg_r = Rot([("stg%d" % i, sb("stg%d" % i, [128, 2 * D])) for i in range(4)])
        brow_r = Rot([("brow%d" % i, sb("brow%d" % i, [16, 128])) for i in range(2)])
        bgu_r = Rot([("bgu%d" % i, sb("bgu%d" % i, [128, 16])) for i in range(2)])
        xt_r = Rot([("xt%d" % i, sb("xt%d" % i, [128, 8, 512], BF16)) for i in range(2)])
        at_r = Rot([("at%d" % i, sb("at%d" % i, [128, 8, 512], BF16)) for i in range(3)])
        gc_r = Rot([("gc%d" % i, sb("gc%d" % i, [128, 512])) for i in range(2)])
        sl_r = Rot([("sl%d" % i, sb("sl%d" % i, [128, 512])) for i in range(2)])
        u0_r = Rot([("u0%d" % i, sb("u0%d" % i, [128, 512])) for i in range(2)])
        ys_r = Rot([("ys%d" % i, sb("ys%d" % i, [128, D], BF16)) for i in range(2)])
        pgu_r = Rot([("pC%d" % i, ps("pC%d" % i, [128, 512])) for i in range(4)])
        pdn_r = Rot([("pD%d" % i, ps("pD%d" % i, [128, 512])) for i in range(3)])
        pXc = ps("pXc", [128, 512])
        pend_down = [None]

        def emit_down(at, atn, wd, wdn, b, jh, ex):
            for gq in range(4):
                ysn, ys = ys_r.next()
                for hf in range(2):
                    pdn_, pdp = pdn_r.next()
                    for m in range(8):
                        R.pe(lambda e, pdp=pdp, m=m, gq=gq, hf=hf: e.matmul(pdp[:, :], lhsT=at[:, m, gq * 128:(gq + 1) * 128], rhs=wd[:, m, hf * 512:(hf + 1) * 512],
                                                                         start=(m == 0), stop=(m == 7)), reads=[atn, wdn], writes=[pdn_])
                    R.act(lambda e, pdp=pdp, ys=ys, hf=hf: e.activation(out=ys[:, hf * 512:(hf + 1) * 512], in_=pdp[:, :], func=AF.Copy, scale=1.0 / GELU_A),
                          reads=[pdn_], writes=[ysn])
                R.dma("sp", lambda e, ys=ys, gq=gq: e.dma_start(out=Yd[b][gq, jh, ex, :, :], in_=ys[:]), reads=[ysn], writes=["Yd%d" % (b * 4 + gq)])

        for ex in range(NE):
            wgn, wgu = wgu_r.next()
            wdn, wd = wd_r.next()
            brn, brow = brow_r.next()
            bgn, bgu = bgu_r.next()
            chunks = []
            for dk in range(8):
                chunks.append(("gu", dk))
            for i4 in range(4):
                chunks.append(("dn", i4))
            stg_of = {}

            def issue(ci):
                kind, k = chunks[ci]
                sn, stg = stg_r.next()
                stg_of[ci] = (sn, stg)
                if kind == "gu":
                    R.dma("pool", lambda e, stg=stg, k=k, ex=ex: e.dma_start(out=stg[:], in_=w_gu[l, ex, k * 128:(k + 1) * 128, :]), writes=[sn])
                else:
                    R.dma("pool", lambda e, stg=stg, k=k, ex=ex: e.dma_start(out=stg[:].rearrange("p (a n) -> p a n", a=2),
                                                                         in_=w_dn[l, ex, k * 256:(k + 1) * 256, :].rearrange("(a p) n -> p a n", p=128)), writes=[sn])

            def cast(ci):
                kind, k = chunks[ci]
                sn, stg = stg_of[ci]
                if kind == "gu":
                    R.pool(lambda e, stg=stg, k=k, wgu=wgu: e.tensor_copy(out=wgu[:, k, :], in_=stg[:]), reads=[sn], writes=[wgn])
                else:
                    R.pool(lambda e, stg=stg, k=k, wd=wd: e.tensor_copy(out=wd[:, 2 * k:2 * k + 2, :], in_=stg[:].rearrange("p (a n) -> p a n", a=2)), reads=[sn], writes=[wdn])

            LA = 3
            for ci in range(min(LA, len(chunks))):
                issue(ci)
            for ci in range(len(chunks)):
                cast(ci)
                if ci + LA < len(chunks):
                    issue(ci + LA)
            R.dma("sp", lambda e, brow=brow, ex=ex: e.dma_start(out=brow[:], in_=b_gu[l, ex, :].rearrange("(m p) -> m p", p=128)), writes=[brn])
            R.pe(lambda e, brow=brow: e.transpose(out=pXc[:, 0:16], in_=brow[:], identity=ident_f[0:16, 0:16]), reads=[brn, "ident_f"], writes=["pXc"])
            R.dve(lambda e, bgu=bgu: e.tensor_copy(out=bgu[:], in_=pXc[:, 0:16]), reads=["pXc"], writes=[bgn])
            for b, jh in [(b_, j_) for b_ in range(NB) for j_ in range(2)]:
                xtn, xt = xt_r.next()
                atn, at = at_r.next()
                for gq in range(4):
                    R.dma("sp", lambda e, xt=xt, b=b, gq=gq, ex=ex, jh=jh: e.dma_start(
                        out=xt[:, :, gq * 128:(gq + 1) * 128], in_=XTd[b][gq, jh, ex // 4, :, :, (ex % 4) * 128:(ex % 4 + 1) * 128]),
                        reads=["XTd%d" % (b * 4 + gq)], writes=[xtn])
                for m in range(8):
                    pgn_, pgp = pgu_r.next()
                    pun_, pup = pgu_r.next()
                    for (pp, pnm, mm) in ((pgp, pgn_, m), (pup, pun_, m + 8)):
                        for dk in range(8):
                            R.pe(lambda e, pp=pp, mm=mm, dk=dk, wgu=wgu, xt=xt: e.matmul(pp[:, :], lhsT=wgu[:, dk, mm * 128:(mm + 1) * 128], rhs=xt[:, dk, :],
                                                                                     start=(dk == 0), stop=(dk == 7)), reads=[wgn, xtn], writes=[pnm])
                    gcn, gc = gc_r.next()
                    sln, sl = sl_r.next()
                    u0n, u0 = u0_r.next()
                    R.dve(lambda e, gc=gc, pgp=pgp, bgu=bgu, m=m: e.tensor_scalar(out=gc[:], in0=pgp[:, :], scalar1=bgu[:, m:m + 1], scalar2=7.0, op0=ALU.add, op1=ALU.min),
                          reads=[pgn_, bgn], writes=[gcn])
                    R.act(lambda e, gc=gc, sl=sl: e.activation(out=sl[:], in_=gc[:], func=AF.Silu, scale=GELU_A), reads=[gcn], writes=[sln])
                    R.act(lambda e, u0=u0, pup=pup, bgu=bgu, m=m: e.activation(out=u0[:], in_=pup[:, :], func=AF.Identity, bias=bgu[:, m + 8:m + 9], scale=1.0),
                          reads=[pun_, bgn], writes=[u0n])
                    R.dve(lambda e, u0=u0: e.tensor_scalar(out=u0[:], in0=u0[:], scalar1=7.0, scalar2=-7.0, op0=ALU.min, op1=ALU.max), reads=[u0n], writes=[u0n])
                    R.dve(lambda e, u0=u0, sl=sl, at=at, m=m: e.scalar_tensor_tensor(out=at[:, m, :], in0=u0[:], scalar=1.0, in1=sl[:], op0=ALU.add, op1=ALU.mult),
                          reads=[u0n, sln], writes=[atn])
                if pend_down[0] is not None:
                    emit_down(*pend_down[0])
                pend_down[0] = (at, atn, wd, wdn, b, jh, ex)
        if pend_down[0] is not None:
            emit_down(*pend_down[0])
    R.barrier()
    if moe_stop == "C":
        return

    with contextlib.ExitStack() as sD:
        sb, ps = mk(sD)
        Ysb = sb("Ysb", [128, NE, D], BF16)
        PGs = sb("PGs", [128, NE, 512], BF16)
        Bdp = sb("Bdp", [128, D], BF16)
        Gpad = sb("Gpad", [128, 128], BF16)
        GTp_r = Rot([("GTp%d" % i, sb("GTp%d" % i, [128, 128], BF16)) for i in range(2)])
        Gg2 = sb("Gg2", [128, 4, NE])
        facc = sb("facc", [128, 4, D])
        lng = sb("lng2", [128, D])
        lnb = sb("lnb2", [128, D])
        g2p = sb("g2p", [128, D])
        R.dma("sp", lambda e: e.dma_start(out=lng[:], in_=ln_g[l, 1:2, :].broadcast_to([128, D])), writes=["lng2"])
        R.dma("sp", lambda e: e.dma_start(out=lnb[:], in_=ln_b[l, 1:2, :].broadcast_to([128, D])), writes=["lnb2"])
        R.pool(lambda e: e.memset(Bdp[:], 0.0), writes=["Bdp"])
        R.pool(lambda e: e.memset(Gpad[:], 0.0), writes=["Gpad"])
        R.dma("pool", lambda e: e.dma_start(out=Bdp[0:NE, :], in_=b_dn[l, :, :]), reads=["Bdp"], writes=["Bdp"])
        xc_r = Rot([("xd%d" % i, sb("xd%d" % i, [128, D])) for i in range(2)])
        mt_r = Rot([("md%d" % i, sb("md%d" % i, [128, D])) for i in range(2)])
        st_r = Rot([("std%d" % i, sb("std%d" % i, [128, 16])) for i in range(2)])
        pc_r = Rot([("pE%d" % i, ps("pE%d" % i, [128, 512])) for i in range(4)])
        pTd = ps("pTd", [128, 1024], BF16)
        for g in range(NB * 4):
            b, gq = g // 4, g % 4
            if gq == 0:
                R.dma("sp", lambda e, b=b: e.dma_start(out=g2p[:], in_=modbuf[b, l:l + 1, 5 * D:6 * D].broadcast_to([128, D])), reads=["modbuf"], writes=["g2p"])
                R.pool(lambda e: e.tensor_scalar(out=g2p[:], in0=g2p[:], scalar1=1.0, scalar2=None, op0=ALU.add), reads=["g2p"], writes=["g2p"])
            R.dma("sp", lambda e, b=b, gq=gq: e.dma_start(out=Gg2[:], in_=Gd[b, gq * 512:(gq + 1) * 512, :].rearrange("(c p) e -> p c e", p=128)),
                  reads=["Gd%d" % b], writes=["Gg2"])
            for jh in range(2):
                for q4 in range(4):
                    R.dma("sp", lambda e, g=g, q4=q4, jh=jh: e.dma_start(out=Ysb[:, q4 * 8:(q4 + 1) * 8, :], in_=Yd[g // 4][g % 4, jh, q4 * 8:(q4 + 1) * 8, :, :].rearrange("e j d -> j e d")),
                          reads=["Yd%d" % g], writes=["Ysb"])
                    R.dma("sp", lambda e, g=g, q4=q4, jh=jh: e.dma_start(out=PGs[:, q4 * 8:(q4 + 1) * 8, :], in_=PGd[g // 4][g % 4, jh, q4 * 8:(q4 + 1) * 8, :, :].rearrange("e j t -> j e t")),
                          reads=["PGd%d" % g], writes=["PGs"])
                for c in range(4):
                    tc = gq * 4 + c
                    if jh == 0:
                        gtn, GTp = GTp_r.next()
                        R.dve(lambda e, c=c: e.tensor_copy(out=Gpad[:, 0:NE], in_=Gg2[:, c, :]), reads=["Gg2", "Gpad"], writes=["Gpad"])
                        R.pe(lambda e: e.transpose(out=pTd[:, 0:128], in_=Gpad[:], identity=ident_b[:]), reads=["Gpad", "ident_b"], writes=["pTd"])
                        R.act(lambda e, GTp=GTp: e.copy(out=GTp[:], in_=pTd[:, 0:128]), reads=["pTd"], writes=[gtn])
                    else:
                        xcn, xc = xc_r.next()
                        mtn, mt = mt_r.next()
                        stn, stt = st_r.next()
                        R.dma("sp", lambda e, xc=xc, tc=tc, b=b: e.dma_start(out=xc[:], in_=xbuf[b, tc * 128:(tc + 1) * 128, :]), reads=["xb%d_%d" % (b, tc)], writes=[xcn])
                    for hf in range(2):
                        pn, pt = pc_r.next()
                        for ex in range(NE):
                            R.pe(lambda e, pt=pt, ex=ex, c=c, hf=hf: e.matmul(pt[:, :], lhsT=PGs[:, ex, c * 128:(c + 1) * 128], rhs=Ysb[:, ex, hf * 512:(hf + 1) * 512],
                                                                             start=(ex == 0), stop=(jh == 1 and ex == NE - 1)), reads=["PGs", "Ysb"], writes=[pn])
                        if jh == 0:
                            R.pe(lambda e, pt=pt, GTp=GTp, hf=hf: e.matmul(pt[:, :], lhsT=GTp[:], rhs=Bdp[:, hf * 512:(hf + 1) * 512], start=False, stop=True),
                                 reads=[gtn, "Bdp"], writes=[pn])
                            R.act(lambda e, pt=pt, c=c, hf=hf: e.copy(out=facc[:, c, hf * 512:(hf + 1) * 512], in_=pt[:, :]), reads=[pn], writes=["facc"])
                        else:
                            R.dve(lambda e, pt=pt, mt=mt, hf=hf, c=c: e.tensor_tensor(out=mt[:, hf * 512:(hf + 1) * 512], in0=pt[:, :], in1=facc[:, c, hf * 512:(hf + 1) * 512], op=ALU.add),
                                  reads=[pn, "facc"], writes=[mtn])
                    if jh == 1:
                        R.pool(lambda e, mt=mt: e.tensor_tensor(out=mt[:], in0=mt[:], in1=g2p[:], op=ALU.mult), reads=[mtn, "g2p"], writes=[mtn])
                        R.dve(lambda e, xc=xc, mt=mt: e.scalar_tensor_tensor(out=xc[:], in0=xc[:], scalar=ALPHA, in1=mt[:], op0=ALU.mult, op1=ALU.add),
                              reads=[xcn, mtn], writes=[xcn])
                        ln_chunk(xc, xcn, stt, stn)
                        R.pool(lambda e, xc=xc: e.tensor_tensor(out=xc[:], in0=xc[:], in1=lng[:], op=ALU.mult), reads=[xcn, "lng2"], writes=[xcn])
                        R.dve(lambda e, xc=xc: e.tensor_tensor(out=xc[:], in0=xc[:], in1=lnb[:], op=ALU.add), reads=[xcn, "lnb2"], writes=[xcn])
                        if last:
                            fin.append(R.dma("sp", lambda e, xc=xc, tc=tc, b=b: e.dma_start(out=out[b, tc * 128:(tc + 1) * 128, :], in_=xc[:]), reads=[xcn], writes=["out%d_%d" % (b, tc)]))
                        else:
                            R.dma("sp", lambda e, xc=xc, tc=tc, b=b: e.dma_start(out=xbuf[b, tc * 128:(tc + 1) * 128, :], in_=xc[:]), reads=[xcn], writes=["xb%d_%d" % (b, tc)])
    R.barrier()


N_CORES = 8
FUSED = True
DEPTH = 4
_PROG = {}


def _get_prog(NB, NL):
    key = (NB, NL)
    if key not in _PROG:
        _PROG[key] = build(NB, NL)
    return _PROG[key]


def _consts():
    hc = host_consts()
    hc.update(moe_consts())
    return {"k_" + k: v for k, v in hc.items()}


_PER_LAYER = ("ada_w", "ada_b", "w_in", "w_out", "ret_gn_w", "conv_w", "cmp_pos", "cmp_w1", "cmp_w2", "ln_g", "ln_b",
              "router_w", "router_b", "w_gate_up", "b_gate_up", "w_down", "b_down")


def kernel(**inputs):
    B = inputs["x"].shape[0]
    NB = B // N_CORES
    f32 = np.float32
    x = np.ascontiguousarray(inputs["x"], dtype=f32)
    c = np.ascontiguousarray(inputs["c"], dtype=f32)
    pos = np.ascontiguousarray(inputs["positions"], dtype=np.int32)
    consts = _consts()
    layer_sets = [list(range(DEPTH))] if FUSED else [[l] for l in range(DEPTH)]
    for ls in layer_sets:
        nc = _get_prog(NB, len(ls))
        w = {k: np.ascontiguousarray(np.asarray(inputs[k])[ls[0]:ls[-1] + 1], dtype=f32) for k in _PER_LAYER}
        in_maps = []
        for ci in range(N_CORES):
            m = dict(w)
            m.update(consts)
            m["x"] = np.ascontiguousarray(x[ci * NB:(ci + 1) * NB])
            m["c"] = np.ascontiguousarray(c[ci * NB:(ci + 1) * NB])
            m["positions"] = np.ascontiguousarray(pos[ci * NB:(ci + 1) * NB])
            in_maps.append(m)
        res = run_bass_kernel_spmd(nc, in_maps, core_ids=list(range(N_CORES)))
        x = np.concatenate([np.asarray(r["out"], dtype=f32) for r in res.results], axis=0)
    return x
```

```python
import math
import contextlib
import numpy as np
import ml_dtypes
import concourse.bass as bass
import concourse.mybir as mybir
from concourse.bass_utils import run_bass_kernel_spmd


ENGS = ("pe", "act", "dve", "pool", "sp")
NPOOL = 24


class Rec:
    def __init__(self, nc):
        self.nc = nc
        self.ops = []
        self.lastw = {}
        self.readers = {}
        self.ndma = 0
        self.dma_idx = []

    def eng_obj(self, e):
        nc = self.nc
        return {"pe": nc.tensor, "act": nc.scalar, "dve": nc.vector, "pool": nc.gpsimd, "sp": nc.sync}[e]

    def add(self, eng, fn, reads=(), writes=(), dma=False):
        idx = len(self.ops)
        deps = set()
        reads = list(reads) + ["BARRIER"]
        for r in reads:
            if r in self.lastw:
                deps.add(self.lastw[r])
        for w in writes:
            if w in self.lastw:
                deps.add(self.lastw[w])
            for rd in self.readers.get(w, ()):
                deps.add(rd)
        op = dict(eng=eng, fn=fn, deps=deps, dma=dma, used=False, slot=None)
        if dma:
            op["slot"] = self.ndma % NPOOL
            op["val"] = 16 * (self.ndma // NPOOL + 1)
            prev = self.ndma - NPOOL
            if prev >= 0:
                deps.add(self.dma_idx[prev])
            self.dma_idx.append(idx)
            self.ndma += 1
        self.ops.append(op)
        for r in reads:
            self.readers.setdefault(r, []).append(idx)
        for w in writes:
            self.lastw[w] = idx
            self.readers[w] = []
        return idx

    def barrier(self):
        return self.add("sp", lambda e: e.nop(), reads=(), writes=["BARRIER"])

    def pe(self, fn, reads=(), writes=()):
        return self.add("pe", fn, reads, writes)

    def act(self, fn, reads=(), writes=()):
        return self.add("act", fn, reads, writes)

    def dve(self, fn, reads=(), writes=()):
        return self.add("dve", fn, reads, writes)

    def pool(self, fn, reads=(), writes=()):
        return self.add("pool", fn, reads, writes)

    def dma(self, eng, fn, reads=(), writes=()):
        return self.add(eng, fn, reads, writes, dma=True)

    def emit(self, final_wait_ops=()):
        nc = self.nc
        ops = self.ops
        for i, op in enumerate(ops):
            for d in op["deps"]:
                dop = ops[d]
                if (not dop["dma"]) and dop["eng"] == "pe" and op["eng"] == "pe" and not op["dma"]:
                    continue
                dop["used"] = True
        for i in final_wait_ops:
            ops[i]["used"] = True
        cnt = {e: 0 for e in ENGS}
        for op in ops:
            if not op["dma"] and op["used"]:
                cnt[op["eng"]] += 1
                op["val"] = cnt[op["eng"]]
        import contextlib
        with contextlib.ExitStack() as st:
            esem = {e: st.enter_context(nc.semaphore("es_" + e)) for e in ENGS}
            dsem = [st.enter_context(nc.semaphore("ds_%d" % i)) for i in range(NPOOL)]
            block = st.enter_context(nc.Block())

            def semof(op):
                if op["dma"]:
                    return dsem[op["slot"]], op["val"], ("d", op["slot"])
                return esem[op["eng"]], op["val"], ("e", op["eng"])

            def run_engine(e, engobj):
                seen = {}
                for i, op in enumerate(ops):
                    if op["eng"] != e:
                        continue
                    need = {}
                    for d in op["deps"]:
                        dop = ops[d]
                        if (not dop["dma"]) and dop["eng"] == "pe" and e == "pe" and not op["dma"]:
                            continue
                        s, v, k = semof(dop)
                        if seen.get(k, 0) >= v:
                            continue
                        if k not in need or need[k][1] < v:
                            need[k] = (s, v)
                    for k, (s, v) in need.items():
                        engobj.wait_ge(s, v)
                        seen[k] = v
                    ins = op["fn"](engobj)
                    if op["dma"]:
                        ins.then_inc(dsem[op["slot"]], 16)
                    elif op["used"]:
                        ins.then_inc(esem[e], 1)
                if e == "sp":
                    for i in final_wait_ops:
                        s, v, k = semof(ops[i])
                        engobj.wait_ge(s, v)

            @block.tensor
            def _(eng):
                run_engine("pe", eng)

            @block.scalar
            def _(eng):
                run_engine("act", eng)

            @block.vector
            def _(eng):
                run_engine("dve", eng)

            @block.gpsimd
            def _(eng):
                run_engine("pool", eng)

            @block.sync
            def _(eng):
                run_engine("sp", eng)

F32 = mybir.dt.float32
BF16 = mybir.dt.bfloat16
I32 = mybir.dt.int32
AF = mybir.ActivationFunctionType
ALU = mybir.AluOpType
AX = mybir.AxisListType

D = 1024
S = 2048
NT = 16
LN_EPS = 1e-5
ALPHA = 8.0 ** 0.25
NEGB = -30000.0
SCALE = 0.125
C_RQ, C_RK, C_RV, C_RG = 0, 256, 512, 768
C_CB, C_CC, C_CH = 1024, 1280, 1536
C_NQ = 1792
C_KC, C_VC, C_KS, C_VS, C_KW, C_VW = 2304, 2432, 2560, 2688, 2816, 2944
C_NG = 3072
NIN = 3096
TW = 2432
NE = 32
GELU_A = 1.702


class Rot:
    def __init__(self, items):
        self.items = items
        self.i = 0

    def next(self):
        it = self.items[self.i % len(self.items)]
        self.i += 1
        return it


def host_consts():
    c = {}
    c["ident"] = np.eye(128, dtype=np.float32)
    r = np.arange(128)
    inv = (10000.0 ** (-((r % 64) % 32).astype(np.float64) / 32.0))
    c["ropec"] = np.stack([inv / (2 * np.pi), np.where((r % 64) < 32, -1.0, 1.0)], 1).astype(np.float32)
    gam = 1.0 - 2.0 ** (-5.0 - np.arange(4))
    k = np.arange(128)[:, None]
    m = np.arange(TW)[None, :] - 384
    tabs = []
    for h in range(4):
        e = (m - k).astype(np.float64)
        tabs.append(np.where(e >= 0, np.exp(np.log(gam[h]) * np.maximum(e, 0)) * SCALE, 0.0))
    c["dect"] = np.stack(tabs, 1).astype(ml_dtypes.bfloat16)
    dd = (m - k)
    c["tmask"] = np.stack([(dd >= 0), (dd >= 0) & (dd < 512)], 1).astype(np.float32).astype(ml_dtypes.bfloat16)
    kk = np.arange(128)[:, None]
    qq = np.arange(128)[None, :]
    c["cb_caus"] = np.where(kk > qq, 0.0, 1.0).astype(ml_dtypes.bfloat16)
    c["cb_low"] = np.where(kk <= qq, 0.0, 1.0).astype(ml_dtypes.bfloat16)
    cc = np.arange(128)[:, None]
    tt = np.arange(2048)[None, :]
    c["cb_cmp"] = np.where(16 * cc + 31 > tt, 0.0, 1.0).astype(ml_dtypes.bfloat16)
    cs = np.arange(128) * 16
    cl = cs + 31
    ss = np.arange(32) * 64
    ov = ((cs[:, None] < ss[None, :] + 64) & (ss[None, :] <= cl[:, None])).astype(np.float32)
    ov[127] = 0
    c["overlap"] = ov.astype(ml_dtypes.bfloat16)
    t = (np.arange(16)[None, :, None] * 128 + np.arange(128)[:, None, None])
    j = np.arange(32)[None, None, :]
    c["forced"] = np.where((j == 0) | (j == t // 64), 1e9, 0.0).astype(np.float32)
    b = np.arange(32)[:, None, None]
    kc = np.arange(16)[None, :, None]
    kl = np.arange(128)[None, None, :]
    E = np.zeros((128, 16, 128), np.float32)
    E[0:32] = (b == 2 * kc + (kl >= 64))
    c["Eexp"] = E.astype(ml_dtypes.bfloat16)
    return c


def build(NB=1, NL=1, taps=(), stop=None, nsa_stage=2, nsa_part=9, nsa_hks=(0, 1), nsa_qcs=tuple(range(4)), moe_stop="D", skip_mixer=False, ne_in=NE):
    nc = bass.Bass("TRN2", target_bir_lowering=False)
    R = Rec(nc)
    dram = {}

    def din(name, shape, dt=F32):
        dram[name] = nc.dram_tensor(name, list(shape), dt, kind="ExternalInput").ap()
        return dram[name]

    def dint(name, shape, dt=F32):
        dram[name] = nc.dram_tensor(name, list(shape), dt, kind="Internal").ap()
        return dram[name]

    def dout(name, shape, dt=F32):
        dram[name] = nc.dram_tensor(name, list(shape), dt, kind="ExternalOutput").ap()
        return dram[name]

    x_in = din("x", [NB, S, D])
    c_in = din("c", [NB, D])
    pos_in = din("positions", [NB, S], I32)
    ada_w = din("ada_w", [NL, D, 6 * D])
    ada_b = din("ada_b", [NL, 6 * D])
    w_in = din("w_in", [NL, D, NIN])
    w_out = din("w_out", [NL, D, D])
    gn_w = din("ret_gn_w", [NL, 256])
    conv_w = din("conv_w", [NL, 3, 256])
    cmp_pos = din("cmp_pos", [NL, 2, 32, 64])
    cmp_w1 = din("cmp_w1", [NL, 2, 2048, 128])
    cmp_w2 = din("cmp_w2", [NL, 2, 128, 64])
    ln_g = din("ln_g", [NL, 2, D])
    ln_b = din("ln_b", [NL, 2, D])
    router_w = din("router_w", [NL, D, NE])
    router_b = din("router_b", [NL, NE])
    w_gu = din("w_gate_up", [NL, ne_in, D, 2 * D])
    b_gu = din("b_gate_up", [NL, ne_in, 2 * D])
    w_dn = din("w_down", [NL, ne_in, D, D])
    b_dn = din("b_down", [NL, ne_in, D])
    HC = host_consts()
    HC.update(moe_consts())
    cst = {}
    for k, v in HC.items():
        cst[k] = din("k_" + k, list(v.shape), BF16 if v.dtype == ml_dtypes.bfloat16 else F32)
    out = dout("out", [NB, S, D])
    xbuf = dint("xbuf", [NB, S, D])
    ropebuf = dint("ropebuf", [NB, 2, 128, S])
    NG = NB * 4
    H2d = dint("H2d", [NB, S, D], BF16)
    Gd = dint("Gd", [NB, S, NE])
    CMd = dint("CMd", [NB, S, NE])
    XTd = [dint("XTd%d" % i, [4, 2, 8, 128, 8, 512], BF16) for i in range(NB)]
    PGd = [dint("PGd%d" % i, [4, 2, NE, 128, 512], BF16) for i in range(NB)]
    Yd = [dint("Yd%d" % i, [4, 2, NE, 128, D], BF16) for i in range(NB)]
    tap_t = {}
    for (nm, shape, dt) in taps:
        tap_t[nm] = dout("tap_" + nm, shape, dt)
    fin = []

    top = contextlib.ExitStack()

    uniq = [0]

    def mk(stack):
        def sbuf(name, shape, dt=F32):
            uniq[0] += 1
            return stack.enter_context(nc.sbuf_tensor("%s_%d" % (name, uniq[0]), list(shape), dt))

        def psum(name, shape, dt=F32):
            uniq[0] += 1
            return stack.enter_context(nc.psum_tensor("%s_%d" % (name, uniq[0]), list(shape), dt))
        return sbuf, psum

    with top:
        sbuf, psum = mk(top)
        ident_f = sbuf("ident_f", [128, 128])
        ident_b = sbuf("ident_b", [128, 128], BF16)
        R.dma("sp", lambda e: e.dma_start(out=ident_f[:], in_=cst["ident"][:, :]), writes=["ident_f"])
        R.dve(lambda e: e.tensor_copy(out=ident_b[:], in_=ident_f[:]), reads=["ident_f"], writes=["ident_b"])
        ropec = sbuf("ropec", [128, 2])
        R.dma("sp", lambda e: e.dma_start(out=ropec[:], in_=cst["ropec"][:, :]), writes=["ropec"])
        modbuf = dint("modbuf", [NB, NL, 6 * D])

        with contextlib.ExitStack() as st0:
            sb0, ps0 = mk(st0)
            c_sb = sb0("c_sb", [NB, D])
            cT = sb0("cT", [128, 8, NB], BF16)
            R.dma("sp", lambda e: e.dma_start(out=c_sb[:], in_=c_in[:, :]), writes=["c_sb"])
            R.act(lambda e: e.activation(out=c_sb[:], in_=c_sb[:], func=AF.Silu), reads=["c_sb"], writes=["c_sb"])
            ps_t = ps0("ps_t", [128, 512])
            for kc in range(8):
                R.pe(lambda e, kc=kc: e.transpose(out=ps_t[:, kc * NB:(kc + 1) * NB], in_=c_sb[:, kc * 128:(kc + 1) * 128],
                                                 identity=ident_f[0:NB, 0:NB]),
                     reads=["c_sb", "ident_f"], writes=["ps_t"])
            R.dve(lambda e: e.tensor_copy(out=cT[:].rearrange("p k b -> p (k b)"), in_=ps_t[:, 0:8 * NB]),
                  reads=["ps_t"], writes=["cT"])
            mods = sb0("mods", [NB, 6 * D])
            adab = sb0("adab", [NB, 6 * D])
            adaw = Rot([("adaw%d" % i, sb0("adaw%d" % i, [128, 8, 512], BF16)) for i in range(2)])
            psm = Rot([("ps_m%d" % i, ps0("ps_m%d" % i, [128, 512])) for i in range(2)])
            for l in range(NL):
                for b in range(NB):
                    R.dma("sp", lambda e, b=b, l=l: e.dma_start(out=adab[b:b + 1, :], in_=ada_b[l:l + 1, :]), writes=["adab"])
                for cc in range(12):
                    wn, wb = adaw.next()
                    pn, pm = psm.next()
                    R.dma("pool", lambda e, wb=wb, l=l, cc=cc: e.dma_start(
                        out=wb[:], in_=ada_w[l, :, cc * 512:(cc + 1) * 512].rearrange("(k p) n -> p k n", p=128)),
                        writes=[wn])
                    for kc in range(8):
                        R.pe(lambda e, wb=wb, pm=pm, kc=kc: e.matmul(pm[0:NB, :], lhsT=cT[:, kc, :], rhs=wb[:, kc, :],
                                                                     start=(kc == 0), stop=(kc == 7)),
                             reads=["cT", wn], writes=[pn])
                    R.dve(lambda e, pm=pm, cc=cc: e.tensor_tensor(
                        out=mods[:, cc * 512:(cc + 1) * 512], in0=pm[0:NB, :], in1=adab[:, cc * 512:(cc + 1) * 512], op=ALU.add),
                        reads=[pn, "adab"], writes=["mods"])
                R.dma("sp", lambda e, l=l: e.dma_start(out=modbuf[:, l, :], in_=mods[:]), reads=["mods"], writes=["modbuf"])
            posi = sb0("posi", [128, S], I32)
            y0 = sb0("y0", [128, S])
            r1 = sb0("r1", [128, S])
            ki = sb0("ki", [128, S], I32)
            kf = sb0("kf", [128, S])
            for b in range(NB):
                R.dma("sp", lambda e, b=b: e.dma_start(out=posi[:], in_=pos_in[b:b + 1, :].broadcast_to([128, S])), writes=["posi"])
                R.dve(lambda e: e.tensor_copy(out=y0[:], in_=posi[:]), reads=["posi"], writes=["y0"])
                R.dve(lambda e: e.tensor_scalar(out=y0[:], in0=y0[:], scalar1=ropec[:, 0:1], scalar2=None, op0=ALU.mult),
                      reads=["y0", "ropec"], writes=["y0"])
                for which in range(2):
                    sh = 0.25 if which == 0 else 0.0
                    R.dve(lambda e, sh=sh: e.tensor_scalar(out=r1[:], in0=y0[:], scalar1=sh, scalar2=None, op0=ALU.add),
                          reads=["y0"], writes=["r1"])
                    R.dve(lambda e: e.tensor_copy(out=ki[:], in_=r1[:]), reads=["r1"], writes=["ki"])
                    R.dve(lambda e: e.tensor_copy(out=kf[:], in_=ki[:]), reads=["ki"], writes=["kf"])
                    R.dve(lambda e: e.tensor_tensor(out=r1[:], in0=r1[:], in1=kf[:], op=ALU.subtract), reads=["r1", "kf"], writes=["r1"])
                    R.dve(lambda e: e.tensor_single_scalar(out=kf[:], in_=r1[:], scalar=0.5, op=ALU.is_gt), reads=["r1"], writes=["kf"])
                    R.dve(lambda e: e.tensor_tensor(out=r1[:], in0=r1[:], in1=kf[:], op=ALU.subtract), reads=["r1", "kf"], writes=["r1"])
                    R.dve(lambda e: e.tensor_single_scalar(out=kf[:], in_=r1[:], scalar=-0.5, op=ALU.is_lt), reads=["r1"], writes=["kf"])
                    R.dve(lambda e: e.tensor_tensor(out=r1[:], in0=r1[:], in1=kf[:], op=ALU.add), reads=["r1", "kf"], writes=["r1"])
                    R.act(lambda e: e.activation(out=r1[:], in_=r1[:], func=AF.Sin, scale=2 * math.pi), reads=["r1"], writes=["r1"])
                    if which == 1:
                        R.dve(lambda e: e.tensor_scalar(out=r1[:], in0=r1[:], scalar1=ropec[:, 1:2], scalar2=None, op0=ALU.mult),
                              reads=["r1", "ropec"], writes=["r1"])
                    R.dma("sp", lambda e, b=b, which=which: e.dma_start(out=ropebuf[b, which, :, :], in_=r1[:]),
                          reads=["r1"], writes=["ropebuf%d" % b])
        R.barrier()
        if "mods" in tap_t:
            fin.append(R.dma("sp", lambda e: e.dma_start(out=tap_t["mods"][:, :, :], in_=modbuf[:, :, :]), reads=["modbuf"], writes=["tap_mods"]))
        if "rope" in tap_t:
            fin.append(R.dma("sp", lambda e: e.dma_start(out=tap_t["rope"][:, :, :], in_=ropebuf[0, :, :, :]), reads=["ropebuf0"], writes=["tap_rope"]))

        if stop == "s0":
            R.emit(final_wait_ops=fin)
            return nc

        for l in range(NL):
            x_src = x_in if l == 0 else xbuf
            for b in (range(NB) if not skip_mixer else ()):
                with contextlib.ExitStack() as stm:
                    mixer(nc, R, mk(stm), locals())
                R.barrier()
            if stop == "mix":
                break
            moe(nc, R, mk, locals())
        R.emit(final_wait_ops=fin)
    return nc


def mixer(nc, R, mkp, env):
    sbuf, psum = mkp
    mk = env["mk"]
    l, b = env["l"], env["b"]
    x_src, xbuf, modbuf = env["x_src"], env["xbuf"], env["modbuf"]
    ident_f, ident_b, cst, tap_t, fin = env["ident_f"], env["ident_b"], env["cst"], env["tap_t"], env["fin"]
    w_in, w_out, gn_w, conv_w = env["w_in"], env["w_out"], env["gn_w"], env["conv_w"]
    cmp_pos, cmp_w1, cmp_w2, ln_g, ln_b, ropebuf = env["cmp_pos"], env["cmp_w1"], env["cmp_w2"], env["ln_g"], env["ln_b"], env["ropebuf"]
    NB = env["NB"]

    pA = Rot([("pA%d" % i, psum("pA%d" % i, [128, 512])) for i in range(2)])
    pT = psum("pT", [128, 1024], BF16)
    pO = Rot([("pO%d" % i, psum("pO%d" % i, [128, 512])) for i in range(4)])
    pX = psum("pX", [128, 512])

    def bcast_mod(sbx, dname, which, plus1):
        dst = sbx(dname, [128, D])
        R.dma("sp", lambda e: e.dma_start(out=dst[:], in_=modbuf[b, l:l + 1, which * D:(which + 1) * D].broadcast_to([128, D])),
              reads=["modbuf"], writes=[dname])
        if plus1:
            R.dve(lambda e: e.tensor_scalar(out=dst[:], in0=dst[:], scalar1=1.0, scalar2=None, op0=ALU.add), reads=[dname], writes=[dname])
        return dst

    sh1 = bcast_mod(sbuf, "sh1", 0, False)
    sc1 = bcast_mod(sbuf, "sc1", 1, True)

    hT = sbuf("hT", [128, 8, S], BF16)
    yT = sbuf("yT", [128, 8, S], BF16)
    xc_r = Rot([("xc%d" % i, sbuf("xc%d" % i, [128, D])) for i in range(2)])
    hc_r = Rot([("hc%d" % i, sbuf("hc%d" % i, [128, D], BF16)) for i in range(2)])
    st_r = Rot([("st%d" % i, sbuf("st%d" % i, [128, 16])) for i in range(2)])

    def layer_norm_chunk(xcn, xc, stn, stt, xnn, xn):
        R.dve(lambda e: e.bn_stats(out=stt[:, 0:6], in_=xc[:, 0:512]), reads=[xcn], writes=[stn])
        R.dve(lambda e: e.bn_stats(out=stt[:, 6:12], in_=xc[:, 512:1024]), reads=[xcn], writes=[stn])
        R.dve(lambda e: e.bn_aggr(out=stt[:, 12:14], in_=stt[:, 0:12]), reads=[stn], writes=[stn])
        R.act(lambda e: e.activation(out=stt[:, 14:15], in_=stt[:, 13:14], func=AF.Sqrt, bias=LN_EPS, scale=1.0), reads=[stn], writes=[stn])
        R.dve(lambda e: e.reciprocal(out=stt[:, 15:16], in_=stt[:, 14:15]), reads=[stn], writes=[stn])
        R.dve(lambda e: e.tensor_scalar(out=xn[:], in0=xc[:], scalar1=stt[:, 12:13], scalar2=stt[:, 15:16], op0=ALU.subtract, op1=ALU.mult),
              reads=[xcn, stn], writes=[xnn])

    for tc in range(NT):
        xcn, xc = xc_r.next()
        xnn, xn = xcn, xc
        hcn, hc = hc_r.next()
        stn, stt = st_r.next()
        R.dma("sp", lambda e, xc=xc, tc=tc: e.dma_start(out=xc[:], in_=x_src[b, tc * 128:(tc + 1) * 128, :]), reads=["xb%d_%d" % (b, tc)], writes=[xcn])
        layer_norm_chunk(xcn, xc, stn, stt, xnn, xn)
        R.dve(lambda e, xn=xn: e.tensor_tensor(out=xn[:], in0=xn[:], in1=sc1[:], op=ALU.mult), reads=[xnn, "sc1"], writes=[xnn])
        R.dve(lambda e, xn=xn, hc=hc: e.tensor_tensor(out=hc[:], in0=xn[:], in1=sh1[:], op=ALU.add), reads=[xnn, "sh1"], writes=[hcn])
        for kc in range(8):
            R.pe(lambda e, hc=hc, kc=kc: e.transpose(out=pT[:, kc * 128:(kc + 1) * 128], in_=hc[:, kc * 128:(kc + 1) * 128], identity=ident_b[:]),
                 reads=[hcn, "ident_b"], writes=["pT"])
        R.act(lambda e, tc=tc: e.copy(out=hT[:, :, tc * 128:(tc + 1) * 128], in_=pT[:].rearrange("p (k t) -> p k t", k=8)),
              reads=["pT"], writes=["hT"])
    if "hT" in tap_t and b == 0 and l == 0:
        fin.append(R.dma("sp", lambda e: e.dma_start(out=tap_t["hT"][:, :, :], in_=hT[:]), reads=["hT"], writes=["tap_hT"]))

    wr = Rot([("W%d" % i, sbuf("W%d" % i, [128, 8, 128], BF16)) for i in range(4)])

    def load_w(pieces):
        wn, wb = wr.next()
        for (do, sc, wd) in pieces:
            R.dma("pool", lambda e, wb=wb, do=do, sc=sc, wd=wd: e.dma_start(
                out=wb[:, :, do:do + wd], in_=w_in[l, :, sc:sc + wd].rearrange("(k p) n -> p k n", p=128)), writes=[wn])
        return wn, wb

    def perm_pieces(c0):
        return [(0, c0 + 32, 32), (32, c0, 32), (64, c0 + 96, 32), (96, c0 + 64, 32)]

    def dup_pieces(c0):
        return [(0, c0, 64), (64, c0, 64)]

    def dup_perm_pieces(c0):
        return [(0, c0 + 32, 32), (32, c0, 32), (64, c0 + 32, 32), (96, c0, 32)]

    def proj_fm(wn, wb, tg):
        pn, pt = pA.next()
        for kc in range(8):
            R.pe(lambda e, kc=kc: e.matmul(pt[:, :], lhsT=wb[:, kc, :], rhs=hT[:, kc, tg * 512:(tg + 1) * 512], start=(kc == 0), stop=(kc == 7)),
                 reads=[wn, "hT"], writes=[pn])
        return pn, pt

    def proj_plain(pieces, dst, dname, dsl):
        wn, wb = load_w(pieces)
        for tg in range(4):
            pn, pt = proj_fm(wn, wb, tg)
            R.act(lambda e, pt=pt, tg=tg: e.copy(out=dst[:, dsl, tg * 512:(tg + 1) * 512] if dsl is not None else dst[:, tg * 512:(tg + 1) * 512], in_=pt[:, :]),
                  reads=[pn], writes=[dname])

    with contextlib.ExitStack() as sc_:
        sb1, _ = mk(sc_)
        fm = [sb1("fm%d" % i, [128, 2, S], BF16) for i in range(3)]
        tmpA = sb1("tmpA", [128, S])
        tmpB = sb1("tmpB", [128, S])
        cw = sb1("cw", [128, 2, 3])
        for ch_ in range(2):
            for k_ in range(3):
                R.dma("sp", lambda e, ch_=ch_, k_=k_: e.dma_start(out=cw[:, ch_, k_:k_ + 1], in_=conv_w[l, k_, ch_ * 128:(ch_ + 1) * 128].rearrange("(p o) -> p o", o=1)), writes=["cw"])
        for i, c0 in enumerate((C_CB, C_CC, C_CH)):
            for ch in range(2):
                proj_plain([(0, c0 + ch * 128, 128)], fm[i], "fm%d" % i, ch)
        for ch in range(2):
            R.dve(lambda e, ch=ch: e.tensor_tensor(out=tmpA[:], in0=fm[1][:, ch, :], in1=fm[2][:, ch, :], op=ALU.mult),
                   reads=["fm1", "fm2"], writes=["tmpA"])
            R.dve(lambda e, ch=ch: e.tensor_scalar(out=tmpB[:], in0=tmpA[:], scalar1=cw[:, ch, 2:3], scalar2=None, op0=ALU.mult),
                  reads=["tmpA", "cw"], writes=["tmpB"])
            R.dve(lambda e, ch=ch: e.scalar_tensor_tensor(out=tmpB[:, 1:S], in0=tmpA[:, 0:S - 1], scalar=cw[:, ch, 1:2], in1=tmpB[:, 1:S],
                                                          op0=ALU.mult, op1=ALU.add), reads=["tmpA", "tmpB", "cw"], writes=["tmpB"])
            R.dve(lambda e, ch=ch: e.scalar_tensor_tensor(out=tmpB[:, 2:S], in0=tmpA[:, 0:S - 2], scalar=cw[:, ch, 0:1], in1=tmpB[:, 2:S],
                                                          op0=ALU.mult, op1=ALU.add), reads=["tmpA", "tmpB", "cw"], writes=["tmpB"])
            R.dve(lambda e, ch=ch: e.tensor_tensor(out=yT[:, 2 + ch, :], in0=tmpB[:], in1=fm[0][:, ch, :], op=ALU.mult),
                   reads=["tmpB", "fm0"], writes=["yT"])
    R.barrier()

    def load_rope(sbx):
        cosT = sbx("cosT", [128, S])
        sinT = sbx("sinT", [128, S])
        R.dma("sp", lambda e: e.dma_start(out=cosT[:], in_=ropebuf[b, 0, :, :]), reads=["ropebuf%d" % b], writes=["cosT"])
        R.dma("sp", lambda e: e.dma_start(out=sinT[:], in_=ropebuf[b, 1, :, :]), reads=["ropebuf%d" % b], writes=["sinT"])
        rt = Rot([("rt%d" % i, sbx("rt%d" % i, [128, 512])) for i in range(4)])
        return cosT, sinT, rt

    def proj_rope(p_norm, p_perm, dst, dname, dsl, cosT, sinT, rt):
        wn1, wb1 = load_w(p_norm)
        wn2, wb2 = load_w(p_perm)
        for tg in range(4):
            pn1, pt1 = proj_fm(wn1, wb1, tg)
            pn2, pt2 = proj_fm(wn2, wb2, tg)
            t1n, t1 = rt.next()
            t2n, t2 = rt.next()
            R.dve(lambda e, pt1=pt1, t1=t1, tg=tg: e.tensor_tensor(out=t1[:], in0=pt1[:, :], in1=cosT[:, tg * 512:(tg + 1) * 512], op=ALU.mult),
                  reads=[pn1, "cosT"], writes=[t1n])
            R.dve(lambda e, pt2=pt2, t2=t2, tg=tg: e.tensor_tensor(out=t2[:], in0=pt2[:, :], in1=sinT[:, tg * 512:(tg + 1) * 512], op=ALU.mult),
                  reads=[pn2, "sinT"], writes=[t2n])
            R.dve(lambda e, t1=t1, t2=t2, tg=tg: e.tensor_tensor(
                out=dst[:, dsl, tg * 512:(tg + 1) * 512] if dsl is not None else dst[:, tg * 512:(tg + 1) * 512], in0=t1[:], in1=t2[:], op=ALU.add),
                reads=[t1n, t2n], writes=[dname])

    with contextlib.ExitStack() as sc_:
        sb2, _ = mk(sc_)
        cosT, sinT, rt = load_rope(sb2)
        qTr = sb2("qTr", [128, 2, S], BF16)
        kTr = sb2("kTr", [128, 2, S], BF16)
        vrg = sb2("vrg", [128, NT, 512], BF16)
        dect = sb2("dect", [128, 4, TW], BF16)
        gnw = sb2("gnw", [128, 256])
        R.dma("sp", lambda e: e.dma_start(out=dect[:], in_=cst["dect"][:, :, :]), writes=["dect"])
        R.dma("sp", lambda e: e.dma_start(out=gnw[:], in_=gn_w[l:l + 1, :].broadcast_to([128, 256])), writes=["gnw"])
        for ch in range(2):
            proj_rope([(0, C_RQ + ch * 128, 128)], perm_pieces(C_RQ + ch * 128), qTr, "qTr", ch, cosT, sinT, rt)
            proj_rope([(0, C_RK + ch * 128, 128)], perm_pieces(C_RK + ch * 128), kTr, "kTr", ch, cosT, sinT, rt)
        wv = sb2("wv", [128, 8, 512], BF16)
        R.dma("pool", lambda e: e.dma_start(out=wv[:], in_=w_in[l, :, C_RV:C_RV + 512].rearrange("(k p) n -> p k n", p=128)), writes=["wv"])
        for tc in range(NT):
            pn, pt = pA.next()
            for kc in range(8):
                R.pe(lambda e, pt=pt, kc=kc, tc=tc: e.matmul(pt[:, :], lhsT=hT[:, kc, tc * 128:(tc + 1) * 128], rhs=wv[:, kc, :],
                                                             start=(kc == 0), stop=(kc == 7)), reads=["hT", "wv"], writes=[pn])
            R.act(lambda e, pt=pt, tc=tc: e.copy(out=vrg[:, tc, 0:256], in_=pt[:, 0:256]), reads=[pn], writes=["vrg"])
            R.act(lambda e, pt=pt, tc=tc: e.activation(out=vrg[:, tc, 256:512], in_=pt[:, 256:512], func=AF.Silu), reads=[pn], writes=["vrg"])
        smr = Rot([("sm%d" % i, sb2("sm%d" % i, [128, 512], BF16)) for i in range(3)])
        gA = sb2("gA", [128, 1024])
        gB = sb2("gB", [128, 1024])
        gs = sb2("gs", [128, 64])
        yr = sb2("yr", [128, 4, 256], BF16)
        for qg in range(4):
            raccn = ["pO0", "pO1"]
            racc = [pO.items[0][1], pO.items[1][1]]
            first = [True, True]
            for h in range(4):
                ch, ro = h // 2, (h % 2) * 64
                for kc in range(4 * qg + 4):
                    pn, pt = pA.next()
                    R.pe(lambda e, pt=pt, kc=kc, ch=ch, ro=ro, qg=qg: e.matmul(
                        pt[:, :], lhsT=kTr[ro:ro + 64, ch, kc * 128:(kc + 1) * 128], rhs=qTr[ro:ro + 64, ch, qg * 512:(qg + 1) * 512],
                        start=True, stop=True), reads=["kTr", "qTr"], writes=[pn])
                    smn, sm = smr.next()
                    off = 384 + qg * 512 - kc * 128
                    R.dve(lambda e, pt=pt, sm=sm, h=h, off=off: e.tensor_tensor(out=sm[:], in0=pt[:, :], in1=dect[:, h, off:off + 512], op=ALU.mult),
                          reads=[pn, "dect"], writes=[smn])
                    for j in range(4):
                        qc = 4 * qg + j
                        if kc > qc:
                            continue
                        bk = j // 2
                        st_flag = first[bk]
                        first[bk] = False
                        R.pe(lambda e, sm=sm, j=j, h=h, kc=kc, bk=bk, st_flag=st_flag: e.matmul(
                            racc[bk][:, (j % 2) * 256 + h * 64:(j % 2) * 256 + h * 64 + 64], lhsT=sm[:, j * 128:(j + 1) * 128],
                            rhs=vrg[:, kc, h * 64:h * 64 + 64], start=st_flag, stop=False, skip_group_check=True),
                            reads=[smn, "vrg"], writes=[raccn[bk]])
            for bk in range(2):
                R.act(lambda e, bk=bk: e.copy(out=gA[:, bk * 512:(bk + 1) * 512], in_=racc[bk][:, :]), reads=[raccn[bk]], writes=["gA"])
            g3 = gA[:].rearrange("p (a d) -> p a d", d=64)
            b3 = gB[:].rearrange("p (a d) -> p a d", d=64)
            R.dve(lambda e: e.tensor_reduce(out=gs[:, 0:16], in_=g3, axis=AX.X, op=ALU.add), reads=["gA"], writes=["gs"])
            R.dve(lambda e: e.tensor_scalar(out=gs[:, 0:16], in0=gs[:, 0:16], scalar1=1.0 / 64, scalar2=None, op0=ALU.mult), reads=["gs"], writes=["gs"])
            R.dve(lambda e: e.tensor_tensor(out=g3, in0=g3, in1=gs[:, 0:16].unsqueeze(2).broadcast_to([128, 16, 64]), op=ALU.subtract),
                  reads=["gA", "gs"], writes=["gA"])
            R.dve(lambda e: e.tensor_tensor(out=gB[:], in0=gA[:], in1=gA[:], op=ALU.mult), reads=["gA"], writes=["gB"])
            R.dve(lambda e: e.tensor_reduce(out=gs[:, 16:32], in_=b3, axis=AX.X, op=ALU.add), reads=["gB"], writes=["gs"])
            R.act(lambda e: e.activation(out=gs[:, 32:48], in_=gs[:, 16:32], func=AF.Sqrt, bias=LN_EPS, scale=1.0 / 64), reads=["gs"], writes=["gs"])
            R.dve(lambda e: e.reciprocal(out=gs[:, 48:64], in_=gs[:, 32:48]), reads=["gs"], writes=["gs"])
            R.dve(lambda e: e.tensor_tensor(out=g3, in0=g3, in1=gs[:, 48:64].unsqueeze(2).broadcast_to([128, 16, 64]), op=ALU.mult),
                  reads=["gA", "gs"], writes=["gA"])
            g4 = gA[:].rearrange("p (j c) -> p j c", c=256)
            R.dve(lambda e: e.tensor_tensor(out=g4, in0=g4, in1=gnw[:].unsqueeze(1).broadcast_to([128, 4, 256]), op=ALU.mult),
                   reads=["gA", "gnw"], writes=["gA"])
            R.dve(lambda e, qg=qg: e.tensor_tensor(out=yr[:], in0=g4, in1=vrg[:, 4 * qg:4 * qg + 4, 256:512], op=ALU.mult),
                  reads=["gA", "vrg"], writes=["yr"])
            for j in range(4):
                for ch in range(2):
                    R.pe(lambda e, j=j, ch=ch: e.transpose(out=pT[:, (j * 2 + ch) * 128:(j * 2 + ch + 1) * 128], in_=yr[:, j, ch * 128:(ch + 1) * 128],
                                                         identity=ident_b[:]), reads=["yr", "ident_b"], writes=["pT"])
            R.act(lambda e, qg=qg: e.copy(out=yT[:, 0:2, qg * 512:(qg + 1) * 512].rearrange("p c (j t) -> p j c t", j=4),
                                          in_=pT[:].rearrange("p (j c t) -> p j c t", j=4, c=2)), reads=["pT"], writes=["yT"])
    R.barrier()

    nsa_stage = env.get("nsa_stage", 2)
    nsa_part = env.get("nsa_part", 9)
    nsa_hks = env.get("nsa_hks", (0, 1))
    nsa_qcs = env.get("nsa_qcs", tuple(range(4)))
    with contextlib.ExitStack() as sc_:
        sb3, _ = mk(sc_)
        cosT, sinT, rt = load_rope(sb3)
        kcT = sb3("kcT", [128, S], BF16)
        vcT = sb3("vcT", [128, S], BF16)
        gates = sb3("gates", [128, NT, 24])
        v2 = sb3("v2", [128, NT, 4, 128], BF16)
        qT = sb3("qT", [128, 2, S], BF16)
        qTr2 = sb3("qTr2", [128, 2, S], BF16)
        ksT = sb3("ksT", [128, S], BF16)
        kwT = sb3("kwT", [128, S], BF16)
        w2k = sb3("w2k", [128, 128], BF16)
        w2v = sb3("w2v", [128, 64], BF16)
        pbias = sb3("pbias", [128, 2])
        kcd = [sb3("kcd%d" % i, [128, 128], BF16) for i in range(2)]
        vca = sb3("vca", [128, 2, 128], BF16)
        ovl = sb3("ovl", [128, 32], BF16)
        scC = contextlib.ExitStack()
        sbC, _ = mk(scC)
        w1d = [sbC("w1d%d" % i, [128, 32, 128], BF16) for i in range(2)]
        posr = sbC("posr", [32, 2, 64])
        posT = sbC("posT", [64, 2, 32], BF16)
        R.dma("sp", lambda e: e.dma_start(out=ovl[:], in_=cst["overlap"][:, :]), writes=["ovl"])
        for kind in range(2):
            for cp in range(2):
                R.dma("pool", lambda e, kind=kind, cp=cp: e.dma_start(
                    out=w1d[kind][cp * 64:(cp + 1) * 64, :, :], in_=cmp_w1[l, kind, :, :].rearrange("(l d) n -> d l n", d=64)), writes=["w1d%d" % kind])
            R.dma("sp", lambda e, kind=kind: e.dma_start(out=posr[:, kind, :], in_=cmp_pos[l, kind, :, :]), writes=["posr"])
        R.dma("pool", lambda e: e.dma_start(out=w2k[:, 0:64], in_=cmp_w2[l, 0, :, :]), writes=["w2k"])
        R.dma("pool", lambda e: e.dma_start(out=w2k[:, 64:128], in_=cmp_w2[l, 0, :, :]), writes=["w2k"])
        R.dma("pool", lambda e: e.dma_start(out=w2v[:], in_=cmp_w2[l, 1, :, :]), writes=["w2v"])
        proj_plain([(0, C_KC, 128)], kcT, "kcT", None)
        proj_plain([(0, C_VC, 128)], vcT, "vcT", None)
        wt = sbC("wt", [128, 8, 408], BF16)
        R.dma("pool", lambda e: e.dma_start(out=wt[:], in_=w_in[l, :, C_VS:C_VS + 408].rearrange("(k p) n -> p k n", p=128)), writes=["wt"])
        R.dve(lambda e: e.memset(v2[:].rearrange("p a b c -> p (a b) c")[:, :, 64:128], 1.0), writes=["v2"])
        for tc in range(NT):
            pn, pt = pA.next()
            for kc in range(8):
                R.pe(lambda e, pt=pt, kc=kc, tc=tc: e.matmul(pt[:, 0:408], lhsT=hT[:, kc, tc * 128:(tc + 1) * 128], rhs=wt[:, kc, :],
                                                             start=(kc == 0), stop=(kc == 7)), reads=["hT", "wt"], writes=[pn])
            R.act(lambda e, pt=pt, tc=tc: e.copy(out=v2[:, tc, 0:2, 0:64], in_=pt[:, 0:128].rearrange("p (h d) -> p h d", h=2)), reads=[pn], writes=["v2"])
            R.act(lambda e, pt=pt, tc=tc: e.copy(out=v2[:, tc, 2:4, 0:64], in_=pt[:, 256:384].rearrange("p (h d) -> p h d", h=2)), reads=[pn], writes=["v2"])
            R.act(lambda e, pt=pt, tc=tc: e.activation(out=gates[:, tc, :], in_=pt[:, 384:408], func=AF.Sigmoid), reads=[pn], writes=["gates"])
        for kind in range(2):
            R.pe(lambda e, kind=kind: e.transpose(out=pX[0:64, kind * 32:(kind + 1) * 32], in_=posr[:, kind, :], identity=ident_f[0:32, 0:32]),
                 reads=["posr", "ident_f"], writes=["pX"])
        R.dve(lambda e: e.tensor_copy(out=posT[:].rearrange("p k l -> p (k l)"), in_=pX[0:64, 0:64]), reads=["pX"], writes=["posT"])
        for kind in range(2):
            for li in range(32):
                R.pe(lambda e, kind=kind, li=li: e.matmul(pX[:, 64 + kind:65 + kind], lhsT=w1d[kind][0:64, li, :], rhs=posT[:, kind, li:li + 1],
                                                          start=(li == 0 and kind == 0), stop=(li == 31), skip_group_check=True),
                     reads=["w1d%d" % kind, "posT"], writes=["pX"])
        R.dve(lambda e: e.tensor_copy(out=pbias[:], in_=pX[:, 64:66]), reads=["pX"], writes=["pbias"])
        zt = sbC("zt", [128, 128])
        z2 = sbC("z2", [128, 128])
        blk = sbC("blk", [128, 32, 127], BF16)
        gT = sbC("gT", [128, 128], BF16)
        R.dve(lambda e: e.memset(vca[:], 0.0), writes=["vca"])
        for i_ in range(2):
            R.dve(lambda e, i_=i_: e.memset(kcd[i_][:], 0.0), writes=["kcd"])
        for kind in range(2):
            src = kcT if kind == 0 else vcT
            srcn = "kcT" if kind == 0 else "vcT"
            for li in range(32):
                R.dve(lambda e, li=li, src=src: e.tensor_copy(out=blk[:, li, :], in_=src[:, li:li + 2017:16]), reads=[srcn], writes=["blk"])
            for hk in range(2):
                ro = hk * 64
                pn, pt = pA.next()
                for li in range(32):
                    R.pe(lambda e, pt=pt, kind=kind, li=li, ro=ro: e.matmul(
                        pt[:, 0:127], lhsT=w1d[kind][ro:ro + 64, li, :], rhs=blk[ro:ro + 64, li, :], start=(li == 0), stop=(li == 31)),
                        reads=["w1d%d" % kind, "blk"], writes=[pn])
                R.act(lambda e, pt=pt, kind=kind: e.activation(out=zt[:, 0:127], in_=pt[:, 0:127], func=AF.Identity, bias=pbias[:, kind:kind + 1], scale=1.0),
                      reads=[pn, "pbias"], writes=["zt"])
                R.dve(lambda e: e.tensor_tensor(out=z2[:, 0:127], in0=zt[:, 0:127], in1=zt[:, 0:127], op=ALU.mult), reads=["zt"], writes=["z2"])
                R.dve(lambda e: e.tensor_scalar(out=z2[:, 0:127], in0=z2[:, 0:127], scalar1=0.044715, scalar2=1.0, op0=ALU.mult, op1=ALU.add),
                      reads=["z2"], writes=["z2"])
                R.dve(lambda e: e.tensor_tensor(out=z2[:, 0:127], in0=z2[:, 0:127], in1=zt[:, 0:127], op=ALU.mult), reads=["z2", "zt"], writes=["z2"])
                R.act(lambda e: e.activation(out=z2[:, 0:127], in_=z2[:, 0:127], func=AF.Sigmoid, scale=1.5957691216), reads=["z2"], writes=["z2"])
                R.dve(lambda e: e.tensor_tensor(out=gT[:, 0:127], in0=z2[:, 0:127], in1=zt[:, 0:127], op=ALU.mult), reads=["z2", "zt"], writes=["gT"])
                if kind == 0:
                    R.pe(lambda e: e.matmul(pX[:, 128:255], lhsT=w2k[:, :], rhs=gT[:, 0:127], start=True, stop=True), reads=["w2k", "gT"], writes=["pX"])
                    R.act(lambda e, hk=hk: e.copy(out=kcd[hk][:, 0:127], in_=pX[:, 128:255]), reads=["pX"], writes=["kcd"])
                else:
                    R.pe(lambda e: e.matmul(pX[0:127, 256:320], lhsT=gT[:, 0:127], rhs=w2v[:, :], start=True, stop=True), reads=["w2v", "gT"], writes=["pX"])
                    R.act(lambda e, hk=hk: e.copy(out=vca[0:127, hk, 0:64], in_=pX[0:127, 256:320]), reads=["pX"], writes=["vca"])
        for hk in range(2):
            R.dve(lambda e, hk=hk: e.memset(vca[:, hk, 64:96], 1.0), reads=[], writes=["vca"])
            R.dve(lambda e, hk=hk: e.tensor_copy(out=vca[:, hk, 96:128], in_=ovl[:]), reads=["ovl"], writes=["vca"])

        if "kcd" in tap_t and b == 0 and l == 0:
            for i_ in range(2):
                fin.append(R.dma("sp", lambda e, i_=i_: e.dma_start(out=tap_t["kcd"][:, i_, :], in_=kcd[i_][:]), reads=["kcd"], writes=["tap_kcd%d" % i_]))
            fin.append(R.dma("sp", lambda e: e.dma_start(out=tap_t["vca"][:, :, :], in_=vca[:]), reads=["vca"], writes=["tap_vca"]))
        R.barrier()
        scC.close()
        cbc = sb3("cbc", [128, S], BF16)
        forced = sb3("forced", [128, NT, 32])
        Eexp = sb3("Eexp", [128, NT, 128], BF16)
        tmask = sb3("tmask", [128, 2, TW], BF16)
        R.dma("sp", lambda e: e.dma_start(out=cbc[:], in_=cst["cb_cmp"][:, :]), writes=["cbc"])
        R.dma("sp", lambda e: e.dma_start(out=forced[:], in_=cst["forced"][:, :, :]), writes=["forced"])
        R.dma("sp", lambda e: e.dma_start(out=Eexp[:], in_=cst["Eexp"][:, :, :]), writes=["Eexp"])
        R.dma("sp", lambda e: e.dma_start(out=tmask[:], in_=cst["tmask"][:, :, :]), writes=["tmask"])
        ptr = Rot([("PT%d" % i, sb3("PT%d" % i, [128, 512], BF16)) for i in range(3)])
        mk_r = Rot([("mk%d" % i, sb3("mk%d" % i, [128, 512], BF16)) for i in range(2)])
        selq = sb3("selq", [128, 4, 128], BF16)
        selT = sb3("selT", [128, 512], BF16)
        on = sb3("on", [128, 4, 4, 64])
        tmpo = sb3("tmpo", [128, 4, 64])
        impa = sb3("impa", [128, 4, 32])
        tmpi = sb3("tmpi", [128, 4, 32])
        nst = sb3("nst", [128, 64])
        ynb = sb3("ynb", [128, 4, 256], BF16)
        R.dve(lambda e: e.memset(selq[:], 0.0), writes=["selq"])

        def smm(pn, pt, Ktile, kn, k0, Qsrc, qn, ro, ch, qg):
            R.pe(lambda e: e.matmul(pt[:, :], lhsT=Ktile[ro:ro + 64, k0:k0 + 128], rhs=Qsrc[ro:ro + 64, ch, qg * 512:(qg + 1) * 512],
                                    start=True, stop=True), reads=[kn, qn], writes=[pn])

        def finish(an, acc, g, qg, hk, br, first):
            a3 = acc[:, :].rearrange("p (j c) -> p j c", j=4)
            R.dve(lambda e: e.tensor_scalar(out=nst[:, 0:4], in0=a3[:, :, 64], scalar1=1e-30, scalar2=None, op0=ALU.max), reads=[an], writes=["nst"])
            R.dve(lambda e: e.reciprocal(out=nst[:, 0:4], in_=nst[:, 0:4]), reads=["nst"], writes=["nst"])
            R.dve(lambda e: e.tensor_tensor(out=nst[:, 4:8], in0=nst[:, 0:4], in1=gates[:, 4 * qg:4 * qg + 4, hk * 12 + g * 3 + br], op=ALU.mult),
                  reads=["nst", "gates"], writes=["nst"])
            dst = on[:, :, g, :]
            if first:
                R.dve(lambda e: e.tensor_tensor(out=dst, in0=a3[:, :, 0:64], in1=nst[:, 4:8].unsqueeze(2).broadcast_to([128, 4, 64]), op=ALU.mult),
                      reads=[an, "nst"], writes=["on"])
            else:
                R.dve(lambda e: e.tensor_tensor(out=tmpo[:], in0=a3[:, :, 0:64], in1=nst[:, 4:8].unsqueeze(2).broadcast_to([128, 4, 64]), op=ALU.mult),
                      reads=[an, "nst"], writes=["tmpo"])
                R.dve(lambda e: e.tensor_tensor(out=dst, in0=dst, in1=tmpo[:], op=ALU.add), reads=["on", "tmpo"], writes=["on"])

        for hk in (nsa_hks if nsa_stage >= 2 else ()):
            for ch in range(2):
                c0 = C_NQ + hk * 256 + ch * 128
                proj_plain([(0, c0, 128)], qT, "qT", ch)
                proj_rope([(0, c0, 128)], perm_pieces(c0), qTr2, "qTr2", ch, cosT, sinT, rt)
            proj_rope(dup_pieces(C_KS + hk * 64), dup_perm_pieces(C_KS + hk * 64), ksT, "ksT", None, cosT, sinT, rt)
            proj_rope(dup_pieces(C_KW + hk * 64), dup_perm_pieces(C_KW + hk * 64), kwT, "kwT", None, cosT, sinT, rt)
            for qg in (nsa_qcs if nsa_part >= 1 else ()):
                for g in range(4):
                    ch, ro = g // 2, (g % 2) * 64
                    pn, pt = pA.next()
                    smm(pn, pt, kcd[hk], "kcd", 0, qT, "qT", ro, ch, qg)
                    ptn, PT = ptr.next()
                    R.act(lambda e, pt=pt, PT=PT: e.activation(out=PT[:, :], in_=pt[:, :], func=AF.Exp, scale=SCALE), reads=[pn], writes=[ptn])
                    R.dve(lambda e, PT=PT, qg=qg: e.tensor_tensor(out=PT[:, :], in0=PT[:, :], in1=cbc[:, qg * 512:(qg + 1) * 512], op=ALU.mult),
                           reads=[ptn, "cbc"], writes=[ptn])
                    an, acc = pO.next()
                    for j in range(4):
                        R.pe(lambda e, j=j, acc=acc, PT=PT, hk=hk: e.matmul(acc[:, j * 128:(j + 1) * 128], lhsT=PT[:, j * 128:(j + 1) * 128], rhs=vca[:, hk, :],
                                                                          start=(j == 0), stop=False, skip_group_check=True), reads=[ptn, "vca"], writes=[an])
                    if nsa_part < 2:
                        continue
                    a3 = acc[:, :].rearrange("p (j c) -> p j c", j=4)
                    finish(an, acc, g, qg, hk, 0, True)
                    if g == 0:
                        R.dve(lambda e, a3=a3: e.tensor_tensor(out=impa[:], in0=a3[:, :, 96:128], in1=nst[:, 0:4].unsqueeze(2).broadcast_to([128, 4, 32]), op=ALU.mult),
                              reads=[an, "nst"], writes=["impa"])
                    else:
                        R.dve(lambda e, a3=a3: e.tensor_tensor(out=tmpi[:], in0=a3[:, :, 96:128], in1=nst[:, 0:4].unsqueeze(2).broadcast_to([128, 4, 32]), op=ALU.mult),
                              reads=[an, "nst"], writes=["tmpi"])
                        R.dve(lambda e: e.tensor_tensor(out=impa[:], in0=impa[:], in1=tmpi[:], op=ALU.add), reads=["impa", "tmpi"], writes=["impa"])
                if nsa_part < 3:
                    continue
                R.dve(lambda e, qg=qg: e.tensor_tensor(out=impa[:], in0=impa[:], in1=forced[:, 4 * qg:4 * qg + 4, :], op=ALU.max), reads=["impa", "forced"], writes=["impa"])
                for j in range(4):
                    R.dve(lambda e, j=j: e.max(out=nst[:, 16 + 8 * j:24 + 8 * j], in_=impa[:, j, :]), reads=["impa"], writes=["nst"])
                    R.dve(lambda e, j=j: e.tensor_scalar(out=selq[:, j, 0:32], in0=impa[:, j, :], scalar1=nst[:, 23 + 8 * j:24 + 8 * j], scalar2=None, op0=ALU.is_ge),
                          reads=["impa", "nst"], writes=["selq"])
                for j in range(4):
                    R.pe(lambda e, j=j: e.transpose(out=pT[:, j * 128:(j + 1) * 128], in_=selq[:, j, :], identity=ident_b[:]), reads=["selq", "ident_b"], writes=["pT"])
                R.act(lambda e: e.copy(out=selT[:], in_=pT[:, 0:512]), reads=["pT"], writes=["selT"])
                if nsa_part < 4:
                    continue
                for br in ((1, 2) if nsa_part >= 5 else (1,)):
                    Ksrc, ksn = (ksT, "ksT") if br == 1 else (kwT, "kwT")
                    kcs = list(range(0, 4 * qg + 4)) if br == 1 else list(range(max(0, 4 * qg - 4), 4 * qg + 4))
                    accs = [pO.next() for _ in range(4)]
                    firsts = [True] * 4
                    for kc in kcs:
                        off = 384 + qg * 512 - kc * 128
                        mkn, mkt = mk_r.next()
                        if br == 1:
                            R.pe(lambda e, kc=kc: e.matmul(pX[:, :], lhsT=Eexp[:, kc, :], rhs=selT[:, :], start=True, stop=True), reads=["Eexp", "selT"], writes=["pX"])
                            R.dve(lambda e, mkt=mkt, off=off: e.tensor_tensor(out=mkt[:], in0=pX[:, :], in1=tmask[:, 0, off:off + 512], op=ALU.mult),
                                  reads=["pX", "tmask"], writes=[mkn])
                        for g in range(4):
                            ch, ro = g // 2, (g % 2) * 64
                            pn, pt = pA.next()
                            smm(pn, pt, Ksrc, ksn, kc * 128, qTr2, "qTr2", ro, ch, qg)
                            ptn, PT = ptr.next()
                            R.act(lambda e, pt=pt, PT=PT: e.activation(out=PT[:, :], in_=pt[:, :], func=AF.Exp, scale=SCALE), reads=[pn], writes=[ptn])
                            if br == 1:
                                R.dve(lambda e, PT=PT, mkt=mkt: e.tensor_tensor(out=PT[:, :], in0=PT[:, :], in1=mkt[:], op=ALU.mult), reads=[ptn, mkn], writes=[ptn])
                            else:
                                R.dve(lambda e, PT=PT, off=off: e.tensor_tensor(out=PT[:, :], in0=PT[:, :], in1=tmask[:, 1, off:off + 512], op=ALU.mult),
                                       reads=[ptn, "tmask"], writes=[ptn])
                            an, acc = accs[g]
                            for j in range(4):
                                qc = 4 * qg + j
                                if kc > qc or (br == 2 and kc < qc - 4):
                                    continue
                                stf = firsts[g]
                                firsts[g] = False
                                R.pe(lambda e, j=j, acc=acc, PT=PT, kc=kc, br=br, hk=hk, stf=stf: e.matmul(
                                    acc[:, j * 128:(j + 1) * 128], lhsT=PT[:, j * 128:(j + 1) * 128], rhs=v2[:, kc, (br - 1) * 2 + hk, :],
                                    start=stf, stop=False, skip_group_check=True), reads=[ptn, "v2"], writes=[an])
                    for g in range(4):
                        an, acc = accs[g]
                        finish(an, acc, g, qg, hk, br, False)
                R.act(lambda e: e.copy(out=ynb[:].rearrange("p j c -> p (j c)"), in_=on[:].rearrange("p j g d -> p (j g d)")), reads=["on"], writes=["ynb"])
                for j in range(4):
                    for ch in range(2):
                        R.pe(lambda e, j=j, ch=ch: e.transpose(out=pT[:, (j * 2 + ch) * 128:(j * 2 + ch + 1) * 128], in_=ynb[:, j, ch * 128:(ch + 1) * 128],
                                                             identity=ident_b[:]), reads=["ynb", "ident_b"], writes=["pT"])
                R.act(lambda e, qg=qg, hk=hk: e.copy(out=yT[:, 4 + 2 * hk:6 + 2 * hk, qg * 512:(qg + 1) * 512].rearrange("p c (j t) -> p j c t", j=4),
                                                     in_=pT[:].rearrange("p (j c t) -> p j c t", j=4, c=2)), reads=["pT"], writes=["yT"])
    R.barrier()
    if "yT" in tap_t and b == 0 and l == 0:
        fin.append(R.dma("sp", lambda e: e.dma_start(out=tap_t["yT"][:, :, :], in_=yT[:]), reads=["yT"], writes=["tap_yT2"]))

    with contextlib.ExitStack() as sc_:
        sb4, _ = mk(sc_)
        g1p = bcast_mod(sb4, "g1p", 2, True)
        lng = sb4("lng", [128, D])
        lnb = sb4("lnb", [128, D])
        R.dma("sp", lambda e: e.dma_start(out=lng[:], in_=ln_g[l, 0:1, :].broadcast_to([128, D])), writes=["lng"])
        R.dma("sp", lambda e: e.dma_start(out=lnb[:], in_=ln_b[l, 0:1, :].broadcast_to([128, D])), writes=["lnb"])
        wo = sb4("wo", [128, 8, D], BF16)
        R.dma("pool", lambda e: e.dma_start(out=wo[:], in_=w_out[l, :, :].rearrange("(k p) n -> p k n", p=128)), writes=["wo"])
        mt_r = Rot([("mt%d" % i, sb4("mt%d" % i, [128, D])) for i in range(2)])
        for tc in range(NT):
            xcn, xc = xc_r.next()
            stn, stt = st_r.next()
            mtn, mt = mt_r.next()
            R.dma("sp", lambda e, xc=xc, tc=tc: e.dma_start(out=xc[:], in_=x_src[b, tc * 128:(tc + 1) * 128, :]), reads=["xb%d_%d" % (b, tc)], writes=[xcn])
            for hf in range(2):
                pn, pt = pA.next()
                for kc in range(8):
                    R.pe(lambda e, pt=pt, kc=kc, tc=tc, hf=hf: e.matmul(pt[:, :], lhsT=yT[:, kc, tc * 128:(tc + 1) * 128], rhs=wo[:, kc, hf * 512:(hf + 1) * 512],
                                                                         start=(kc == 0), stop=(kc == 7)), reads=["yT", "wo"], writes=[pn])
                R.dve(lambda e, pt=pt, mt=mt, hf=hf: e.tensor_tensor(out=mt[:, hf * 512:(hf + 1) * 512], in0=pt[:, :], in1=g1p[:, hf * 512:(hf + 1) * 512], op=ALU.mult),
                      reads=[pn, "g1p"], writes=[mtn])
            R.dve(lambda e, xc=xc, mt=mt: e.scalar_tensor_tensor(out=xc[:], in0=xc[:], scalar=ALPHA, in1=mt[:], op0=ALU.mult, op1=ALU.add),
                  reads=[xcn, mtn], writes=[xcn])
            layer_norm_chunk(xcn, xc, stn, stt, xcn, xc)
            R.dve(lambda e, xc=xc: e.tensor_tensor(out=xc[:], in0=xc[:], in1=lng[:], op=ALU.mult), reads=[xcn, "lng"], writes=[xcn])
            R.dve(lambda e, xc=xc: e.tensor_tensor(out=xc[:], in0=xc[:], in1=lnb[:], op=ALU.add), reads=[xcn, "lnb"], writes=[xcn])
            R.dma("sp", lambda e, xc=xc, tc=tc: e.dma_start(out=xbuf[b, tc * 128:(tc + 1) * 128, :], in_=xc[:]), reads=[xcn], writes=["xb%d_%d" % (b, tc)])
    if "x1" in tap_t and b == 0 and l == 0:
        fin.append(R.dma("sp", lambda e: e.dma_start(out=tap_t["x1"][:, :], in_=xbuf[0, :, :]), reads=["xb0_%d" % t_ for t_ in range(NT)], writes=["tap_x1"]))


NE = 32
GELU_A = 1.702


def moe_consts():
    c = {}
    tp = np.arange(128)[:, None]
    t = np.arange(128)[None, :]
    c["ltri"] = (tp < t).astype(np.float32).astype(ml_dtypes.bfloat16)
    c["onesb"] = np.ones((128, 128), np.float32).astype(ml_dtypes.bfloat16)
    c["iota3"] = np.broadcast_to(np.arange(128, dtype=np.float32)[None, None, :], (128, NE, 128)).astype(ml_dtypes.bfloat16).copy()
    return c


def moe(nc, R, mk, env):
    l, NB, NL = env["l"], env["NB"], env["NL"]
    xbuf, modbuf, out, cst = env["xbuf"], env["modbuf"], env["out"], env["cst"]
    ident_f, ident_b, tap_t, fin = env["ident_f"], env["ident_b"], env["tap_t"], env["fin"]
    ln_g, ln_b = env["ln_g"], env["ln_b"]
    router_w, router_b, w_gu, b_gu, w_dn, b_dn = env["router_w"], env["router_b"], env["w_gu"], env["b_gu"], env["w_dn"], env["b_dn"]
    H2d, Gd, CMd, XTd, PGd, Yd = env["H2d"], env["Gd"], env["CMd"], env["XTd"], env["PGd"], env["Yd"]
    last = (l == NL - 1)
    moe_stop = env.get("moe_stop", "D")

    def ln_chunk(xc, xcn, stt, stn):
        R.dve(lambda e: e.bn_stats(out=stt[:, 0:6], in_=xc[:, 0:512]), reads=[xcn], writes=[stn])
        R.dve(lambda e: e.bn_stats(out=stt[:, 6:12], in_=xc[:, 512:1024]), reads=[xcn], writes=[stn])
        R.dve(lambda e: e.bn_aggr(out=stt[:, 12:14], in_=stt[:, 0:12]), reads=[stn], writes=[stn])
        R.act(lambda e: e.activation(out=stt[:, 14:15], in_=stt[:, 13:14], func=AF.Sqrt, bias=LN_EPS, scale=1.0), reads=[stn], writes=[stn])
        R.dve(lambda e: e.reciprocal(out=stt[:, 15:16], in_=stt[:, 14:15]), reads=[stn], writes=[stn])
        R.dve(lambda e: e.tensor_scalar(out=xc[:], in0=xc[:], scalar1=stt[:, 12:13], scalar2=stt[:, 15:16], op0=ALU.subtract, op1=ALU.mult),
              reads=[xcn, stn], writes=[xcn])

    def bcast_row(sbx, name, src_ap, plus1=False, width=D):
        dst = sbx(name, [128, width])
        R.dma("sp", lambda e: e.dma_start(out=dst[:], in_=src_ap.broadcast_to([128, width])), reads=["modbuf"], writes=[name])
        if plus1:
            R.pool(lambda e: e.tensor_scalar(out=dst[:], in0=dst[:], scalar1=1.0, scalar2=None, op0=ALU.add), reads=[name], writes=[name])
        return dst

    with contextlib.ExitStack() as sA:
        sb, ps = mk(sA)
        wr = sb("wr", [128, 8, NE])
        wrh = sb("wrh", [128, 8, NE], BF16)
        wrl = sb("wrl", [128, 8, NE], BF16)
        R.dma("sp", lambda e: e.dma_start(out=wr[:], in_=router_w[l, :, :].rearrange("(k p) n -> p k n", p=128)), writes=["wr"])
        R.dve(lambda e: e.tensor_copy(out=wrh[:], in_=wr[:]), reads=["wr"], writes=["wrh"])
        R.dve(lambda e: e.tensor_tensor(out=wrl[:], in0=wr[:], in1=wrh[:], op=ALU.subtract), reads=["wr", "wrh"], writes=["wrl"])
        rb = bcast_row(sb, "rb", router_b[l:l + 1, :], width=NE)
        ltri = sb("ltri", [128, 128], BF16)
        onesb = sb("onesb", [128, 128], BF16)
        R.dma("sp", lambda e: e.dma_start(out=ltri[:], in_=cst["ltri"][:, :]), writes=["ltri"])
        R.dma("sp", lambda e: e.dma_start(out=onesb[:], in_=cst["onesb"][:, :]), writes=["onesb"])
        xc_r = Rot([("xa%d" % i, sb("xa%d" % i, [128, D])) for i in range(2)])
        hb_r = Rot([("hb%d" % i, sb("hb%d" % i, [128, D], BF16)) for i in range(2)])
        st_r = Rot([("sta%d" % i, sb("sta%d" % i, [128, 16])) for i in range(2)])
        hT_r = Rot([("hTa%d" % i, sb("hTa%d" % i, [128, 2, D], BF16)) for i in range(2)])
        hl_r = Rot([("hl%d" % i, sb("hl%d" % i, [128, D], BF16)) for i in range(2)])
        lg_r = Rot([("lg%d" % i, sb("lg%d" % i, [128, 96])) for i in range(2)])
        maskb = sb("maskb", [128, NT, NE], BF16)
        Gs = sb("Gs", [128, NT, NE])
        CMs = sb("CMs", [128, NT, NE])
        pTh = ps("pTh", [128, D], BF16)
        pTl = ps("pTl", [128, D], BF16)
        pl_r = Rot([("pl%d" % i, ps("pl%d" % i, [128, 512])) for i in range(2)])
        sh2 = sb("sh2", [128, D])
        sc2 = sb("sc2", [128, D])
        for b in range(NB):
            R.dma("sp", lambda e, b=b: e.dma_start(out=sh2[:], in_=modbuf[b, l:l + 1, 3 * D:4 * D].broadcast_to([128, D])), reads=["modbuf"], writes=["sh2_0"])
            R.dma("sp", lambda e, b=b: e.dma_start(out=sc2[:], in_=modbuf[b, l:l + 1, 4 * D:5 * D].broadcast_to([128, D])), reads=["modbuf"], writes=["sc2_0"])
            R.pool(lambda e: e.tensor_scalar(out=sc2[:], in0=sc2[:], scalar1=1.0, scalar2=None, op0=ALU.add), reads=["sc2_0"], writes=["sc2_0"])
            for tc in range(NT):
                xcn, xc = xc_r.next()
                hbn, hb = hb_r.next()
                stn, stt = st_r.next()
                htn, hTc = hT_r.next()
                lgn, lg = lg_r.next()
                pln, pl = pl_r.next()
                R.dma("sp", lambda e, xc=xc, tc=tc, b=b: e.dma_start(out=xc[:], in_=xbuf[b, tc * 128:(tc + 1) * 128, :]), reads=["xb%d_%d" % (b, tc)], writes=[xcn])
                ln_chunk(xc, xcn, stt, stn)
                R.pool(lambda e, xc=xc: e.tensor_tensor(out=xc[:], in0=xc[:], in1=sc2[:], op=ALU.mult), reads=[xcn, "sc2_0"], writes=[xcn])
                R.dve(lambda e, xc=xc: e.tensor_tensor(out=xc[:], in0=xc[:], in1=sh2[:], op=ALU.add), reads=[xcn, "sh2_0"], writes=[xcn])
                R.act(lambda e, xc=xc, hb=hb: e.copy(out=hb[:], in_=xc[:]), reads=[xcn], writes=[hbn])
                R.dma("sp", lambda e, hb=hb, tc=tc, b=b: e.dma_start(out=H2d[b, tc * 128:(tc + 1) * 128, :], in_=hb[:]), reads=[hbn], writes=["H2d%d" % b])
                hln, hl = hl_r.next()
                R.dve(lambda e, xc=xc, hb=hb, hl=hl: e.tensor_tensor(out=hl[:], in0=xc[:], in1=hb[:], op=ALU.subtract), reads=[xcn, hbn], writes=[hln])
                for kc in range(8):
                    R.pe(lambda e, hb=hb, kc=kc: e.transpose(out=pTh[:, kc * 128:(kc + 1) * 128], in_=hb[:, kc * 128:(kc + 1) * 128], identity=ident_b[:]),
                         reads=[hbn, "ident_b"], writes=["pTh"])
                for kc in range(8):
                    R.pe(lambda e, hl=hl, kc=kc: e.transpose(out=pTl[:, kc * 128:(kc + 1) * 128], in_=hl[:, kc * 128:(kc + 1) * 128], identity=ident_b[:]),
                         reads=[hln, "ident_b"], writes=["pTl"])
                R.act(lambda e, hTc=hTc: e.copy(out=hTc[:, 0, :], in_=pTh[:]), reads=["pTh"], writes=[htn])
                R.act(lambda e, hTc=hTc: e.copy(out=hTc[:, 1, :], in_=pTl[:]), reads=["pTl"], writes=[htn])
                terms = [(0, wrh, "wrh"), (1, wrh, "wrh"), (0, wrl, "wrl")]
                for ti, (hs, wt_, wn_) in enumerate(terms):
                    for kc in range(8):
                        R.pe(lambda e, hTc=hTc, kc=kc, pl=pl, hs=hs, wt_=wt_, ti=ti: e.matmul(pl[:, 0:NE], lhsT=hTc[:, hs, kc * 128:(kc + 1) * 128], rhs=wt_[:, kc, :],
                                                                                     start=(ti == 0 and kc == 0), stop=(ti == 2 and kc == 7)),
                             reads=[htn, wn_], writes=[pln])
                R.dve(lambda e, lg=lg, pl=pl: e.tensor_tensor(out=lg[:, 0:32], in0=pl[:, 0:NE], in1=rb[:], op=ALU.add), reads=[pln, "rb"], writes=[lgn])
                R.dve(lambda e, lg=lg: e.max(out=lg[:, 32:40], in_=lg[:, 0:32]), reads=[lgn], writes=[lgn])
                R.dve(lambda e, lg=lg: e.tensor_scalar(out=lg[:, 64:96], in0=lg[:, 0:32], scalar1=lg[:, 35:36], scalar2=None, op0=ALU.is_ge), reads=[lgn], writes=[lgn])
                R.dve(lambda e, lg=lg: e.tensor_scalar(out=lg[:, 40:41], in0=lg[:, 32:33], scalar1=-1.0, scalar2=None, op0=ALU.mult), reads=[lgn], writes=[lgn])
                R.act(lambda e, lg=lg: e.activation(out=lg[:, 0:32], in_=lg[:, 0:32], func=AF.Exp, bias=lg[:, 40:41], scale=1.0), reads=[lgn], writes=[lgn])
                R.dve(lambda e, lg=lg: e.tensor_tensor(out=lg[:, 0:32], in0=lg[:, 0:32], in1=lg[:, 64:96], op=ALU.mult), reads=[lgn], writes=[lgn])
                R.dve(lambda e, lg=lg: e.tensor_reduce(out=lg[:, 41:42], in_=lg[:, 0:32], axis=AX.X, op=ALU.add), reads=[lgn], writes=[lgn])
                R.dve(lambda e, lg=lg: e.reciprocal(out=lg[:, 42:43], in_=lg[:, 41:42]), reads=[lgn], writes=[lgn])
                R.dve(lambda e, lg=lg, tc=tc: e.tensor_scalar(out=Gs[:, tc, :], in0=lg[:, 0:32], scalar1=lg[:, 42:43], scalar2=None, op0=ALU.mult), reads=[lgn], writes=["Gs"])
                R.pool(lambda e, lg=lg, tc=tc: e.tensor_copy(out=maskb[:, tc, :], in_=lg[:, 64:96]), reads=[lgn], writes=["maskb"])
            for tc in range(NT):
                c = tc % 4
                g0 = tc - c
                pln, pl = pl_r.next()
                for cp in range(c):
                    R.pe(lambda e, pl=pl, cp=cp, g0=g0: e.matmul(pl[:, 0:NE], lhsT=onesb[:], rhs=maskb[:, g0 + cp, :], start=(cp == 0), stop=False),
                         reads=["onesb", "maskb"], writes=[pln])
                R.pe(lambda e, pl=pl, tc=tc, c=c: e.matmul(pl[:, 0:NE], lhsT=ltri[:], rhs=maskb[:, tc, :], start=(c == 0), stop=True), reads=["ltri", "maskb"], writes=[pln])
                R.dve(lambda e, pl=pl, tc=tc: e.scalar_tensor_tensor(out=CMs[:, tc, :], in0=pl[:, 0:NE], scalar=1.0, in1=maskb[:, tc, :], op0=ALU.add, op1=ALU.mult),
                      reads=[pln, "maskb"], writes=["CMs"])
            R.dve(lambda e: e.tensor_scalar(out=CMs[:], in0=CMs[:], scalar1=-1.0, scalar2=None, op0=ALU.add), reads=["CMs"], writes=["CMs"])
            R.dma("sp", lambda e, b=b: e.dma_start(out=Gd[b, :, :].rearrange("(c p) e -> p c e", p=128), in_=Gs[:]), reads=["Gs"], writes=["Gd%d" % b])
            R.dma("sp", lambda e, b=b: e.dma_start(out=CMd[b, :, :].rearrange("(c p) e -> p c e", p=128), in_=CMs[:]), reads=["CMs"], writes=["CMd%d" % b])
    R.barrier()
    if "G" in tap_t and l == 0:
        fin.append(R.dma("sp", lambda e: e.dma_start(out=tap_t["G"][:, :], in_=Gd[0, :, :]), reads=["Gd0"], writes=["tap_G"]))
        fin.append(R.dma("sp", lambda e: e.dma_start(out=tap_t["CM"][:, :], in_=CMd[0, :, :]), reads=["CMd0"], writes=["tap_CM"]))
    if moe_stop == "A":
        return

    with contextlib.ExitStack() as sB:
        sb, ps = mk(sB)
        iota3 = sb("iota3", [128, NE, 128], BF16)
        R.dma("sp", lambda e: e.dma_start(out=iota3[:], in_=cst["iota3"][:, :, :]), writes=["iota3"])
        h2g = sb("h2g", [128, 4, D], BF16)
        Gg = sb("Gg", [128, 4, NE])
        CMg = sb("CMg", [128, 4, NE])
        CMj = sb("CMj", [128, 4, NE])
        P = [sb("P%d" % i, [128, NE, 128], BF16) for i in range(4)]
        Pg = [sb("Pg%d" % i, [128, NE, 128], BF16) for i in range(4)]
        xe_r = Rot([("xe%d" % i, sb("xe%d" % i, [128, 8, 512], BF16)) for i in range(2)])
        pgt_r = Rot([("pgt%d" % i, sb("pgt%d" % i, [128, 1024], BF16)) for i in range(2)])
        pg_r = Rot([("pB%d" % i, ps("pB%d" % i, [128, 512])) for i in range(4)])
        pTb_r = Rot([("pTb%d" % i, ps("pTb%d" % i, [128, 1024], BF16)) for i in range(2)])
        for g in range(NB * 4):
            b, gq = g // 4, g % 4
            R.dma("sp", lambda e, b=b, gq=gq: e.dma_start(out=h2g[:], in_=H2d[b, gq * 512:(gq + 1) * 512, :].rearrange("(c p) d -> p c d", p=128)),
                  reads=["H2d%d" % b], writes=["h2g"])
            R.dma("sp", lambda e, b=b, gq=gq: e.dma_start(out=Gg[:], in_=Gd[b, gq * 512:(gq + 1) * 512, :].rearrange("(c p) e -> p c e", p=128)),
                  reads=["Gd%d" % b], writes=["Gg"])
            R.dma("sp", lambda e, b=b, gq=gq: e.dma_start(out=CMg[:], in_=CMd[b, gq * 512:(gq + 1) * 512, :].rearrange("(c p) e -> p c e", p=128)),
                  reads=["CMd%d" % b], writes=["CMg"])
            for jh in range(2):
                R.dve(lambda e, jh=jh: e.tensor_scalar(out=CMj[:], in0=CMg[:], scalar1=-128.0 * jh, scalar2=None, op0=ALU.add), reads=["CMg"], writes=["CMj"])
                for c in range(4):
                    R.dve(lambda e, c=c: e.tensor_tensor(out=P[c][:], in0=iota3[:], in1=CMj[:, c, :].unsqueeze(2).broadcast_to([128, NE, 128]), op=ALU.is_equal),
                          reads=["iota3", "CMj"], writes=["P%d" % c])
                    R.pool(lambda e, c=c: e.tensor_tensor(out=Pg[c][:], in0=P[c][:], in1=Gg[:, c, :].unsqueeze(2).broadcast_to([128, NE, 128]), op=ALU.mult),
                           reads=["P%d" % c, "Gg"], writes=["Pg%d" % c])
                for eq in range(8):
                    xen, xe = xe_r.next()
                    for dk in range(8):
                        pn, pt = pg_r.next()
                        for c in range(4):
                            R.pe(lambda e, pt=pt, c=c, dk=dk, eq=eq: e.matmul(pt[:, :], lhsT=h2g[:, c, dk * 128:(dk + 1) * 128],
                                                                             rhs=P[c][:, eq * 4:(eq + 1) * 4, :].rearrange("p e j -> p (e j)"), start=(c == 0), stop=(c == 3)),
                                 reads=["h2g", "P%d" % c], writes=[pn])
                        if dk % 2 == 0:
                            R.act(lambda e, pt=pt, xe=xe, dk=dk: e.copy(out=xe[:, dk, :], in_=pt[:, :]), reads=[pn], writes=[xen])
                        else:
                            R.dve(lambda e, pt=pt, xe=xe, dk=dk: e.tensor_copy(out=xe[:, dk, :], in_=pt[:, :]), reads=[pn], writes=[xen])
                    R.dma("sp", lambda e, xe=xe, g=g, eq=eq, jh=jh: e.dma_start(out=XTd[g // 4][g % 4, jh, eq, :, :, :], in_=xe[:]), reads=[xen], writes=["XTd%d" % g])
                for e2 in range(NE // 2):
                    ptn, ptb = pTb_r.next()
                    pgn, pgt = pgt_r.next()
                    for ee in range(2):
                        for c in range(4):
                            R.pe(lambda e, ptb=ptb, ee=ee, c=c, e2=e2: e.transpose(out=ptb[:, ee * 512 + c * 128:ee * 512 + (c + 1) * 128], in_=Pg[c][:, e2 * 2 + ee, :], identity=ident_b[:]),
                                 reads=["Pg%d" % c, "ident_b"], writes=[ptn])
                    R.act(lambda e, ptb=ptb, pgt=pgt: e.copy(out=pgt[:], in_=ptb[:]), reads=[ptn], writes=[pgn])
                    R.dma("sp", lambda e, pgt=pgt, g=g, e2=e2, jh=jh: e.dma_start(out=PGd[g // 4][g % 4, jh, e2 * 2:e2 * 2 + 2, :, :].rearrange("e j t -> j e t"),
                                                                              in_=pgt[:].rearrange("p (e t) -> p e t", e=2)), reads=[pgn], writes=["PGd%d" % g])
    R.barrier()
    if moe_stop == "B":
        return

    with contextlib.ExitStack() as sC:
        sb, ps = mk(sC)
        wgu_r = Rot([("wgu%d" % i, sb("wgu%d" % i, [128, 8, 2 * D], BF16)) for i in range(2)])
        wd_r = Rot([("wd%d" % i, sb("wd%d" % i, [128, 8, D], BF16)) for i in range(2)])
        brow_r = Rot([("brow%d" % i, sb("brow%d" % i, [16, 128])) for i in range(2)])
        bgu_r = Rot([("bgu%d" % i, sb("bgu%d" % i, [128, 16])) for i in range(2)])
        xt_r = Rot([("xt%d" % i, sb("xt%d" % i, [128, 8, 512], BF16)) for i in range(2)])
        at_r = Rot([("at%d" % i, sb("at%d" % i, [128, 8, 512], BF16)) for i in range(3)])
        gc_r = Rot([("gc%d" % i, sb("gc%d" % i, [128, 512])) for i in range(2)])
        sl_r = Rot([("sl%d" % i, sb("sl%d" % i, [128, 512])) for i in range(2)])
        u0_r = Rot([("u0%d" % i, sb("u0%d" % i, [128, 512])) for i in range(2)])
        ys_r = Rot([("ys%d" % i, sb("ys%d" % i, [128, D], BF16)) for i in range(2)])
        pgu_r = Rot([("pC%d" % i, ps("pC%d" % i, [128, 512])) for i in range(4)])
        pdn_r = Rot([("pD%d" % i, ps("pD%d" % i, [128, 512])) for i in range(3)])
        pXc = ps("pXc", [128, 512])
        pend_down = [None]

        def emit_down(at, atn, wd, wdn, b, jh, ex):
            for gq in range(4):
                ysn, ys = ys_r.next()
                for hf in range(2):
                    pdn_, pdp = pdn_r.next()
                    for m in range(8):
                        R.pe(lambda e, pdp=pdp, m=m, gq=gq, hf=hf: e.matmul(pdp[:, :], lhsT=at[:, m, gq * 128:(gq + 1) * 128], rhs=wd[:, m, hf * 512:(hf + 1) * 512],
                                                                         start=(m == 0), stop=(m == 7)), reads=[atn, wdn], writes=[pdn_])
                    R.act(lambda e, pdp=pdp, ys=ys, hf=hf: e.activation(out=ys[:, hf * 512:(hf + 1) * 512], in_=pdp[:, :], func=AF.Copy, scale=1.0 / GELU_A),
                          reads=[pdn_], writes=[ysn])
                R.dma("sp", lambda e, ys=ys, gq=gq: e.dma_start(out=Yd[b][gq, jh, ex, :, :], in_=ys[:]), reads=[ysn], writes=["Yd%d" % (b * 4 + gq)])

        for ex in range(NE):
            wgn, wgu = wgu_r.next()
            wdn, wd = wd_r.next()
            brn, brow = brow_r.next()
            bgn, bgu = bgu_r.next()
            for hf in range(2):
                R.dma("pool", lambda e, wgu=wgu, ex=ex, hf=hf: e.dma_start(out=wgu[:, :, hf * D:(hf + 1) * D],
                                                                          in_=w_gu[l, ex, :, hf * D:(hf + 1) * D].rearrange("(k p) n -> p k n", p=128)), writes=[wgn])
            R.dma("pool", lambda e, wd=wd, ex=ex: e.dma_start(out=wd[:], in_=w_dn[l, ex, :, :].rearrange("(k p) n -> p k n", p=128)), writes=[wdn])
            R.dma("sp", lambda e, brow=brow, ex=ex: e.dma_start(out=brow[:], in_=b_gu[l, ex, :].rearrange("(m p) -> m p", p=128)), writes=[brn])
            R.pe(lambda e, brow=brow: e.transpose(out=pXc[:, 0:16], in_=brow[:], identity=ident_f[0:16, 0:16]), reads=[brn, "ident_f"], writes=["pXc"])
            R.dve(lambda e, bgu=bgu: e.tensor_copy(out=bgu[:], in_=pXc[:, 0:16]), reads=["pXc"], writes=[bgn])
            for b, jh in [(b_, j_) for b_ in range(NB) for j_ in range(2)]:
                xtn, xt = xt_r.next()
                atn, at = at_r.next()
                for gq in range(4):
                    R.dma("sp", lambda e, xt=xt, b=b, gq=gq, ex=ex, jh=jh: e.dma_start(
                        out=xt[:, :, gq * 128:(gq + 1) * 128], in_=XTd[b][gq, jh, ex // 4, :, :, (ex % 4) * 128:(ex % 4 + 1) * 128]),
                        reads=["XTd%d" % (b * 4 + gq)], writes=[xtn])
                for m in range(8):
                    pgn_, pgp = pgu_r.next()
                    pun_, pup = pgu_r.next()
                    for (pp, pnm, mm) in ((pgp, pgn_, m), (pup, pun_, m + 8)):
                        for dk in range(8):
                            R.pe(lambda e, pp=pp, mm=mm, dk=dk, wgu=wgu, xt=xt: e.matmul(pp[:, :], lhsT=wgu[:, dk, mm * 128:(mm + 1) * 128], rhs=xt[:, dk, :],
                                                                                     start=(dk == 0), stop=(dk == 7)), reads=[wgn, xtn], writes=[pnm])
                    gcn, gc = gc_r.next()
                    sln, sl = sl_r.next()
                    u0n, u0 = u0_r.next()
                    R.dve(lambda e, gc=gc, pgp=pgp, bgu=bgu, m=m: e.tensor_scalar(out=gc[:], in0=pgp[:, :], scalar1=bgu[:, m:m + 1], scalar2=7.0, op0=ALU.add, op1=ALU.min),
                          reads=[pgn_, bgn], writes=[gcn])
                    R.act(lambda e, gc=gc, sl=sl: e.activation(out=sl[:], in_=gc[:], func=AF.Silu, scale=GELU_A), reads=[gcn], writes=[sln])
                    R.act(lambda e, u0=u0, pup=pup, bgu=bgu, m=m: e.activation(out=u0[:], in_=pup[:, :], func=AF.Identity, bias=bgu[:, m + 8:m + 9], scale=1.0),
                          reads=[pun_, bgn], writes=[u0n])
                    R.dve(lambda e, u0=u0: e.tensor_scalar(out=u0[:], in0=u0[:], scalar1=7.0, scalar2=-7.0, op0=ALU.min, op1=ALU.max), reads=[u0n], writes=[u0n])
                    R.dve(lambda e, u0=u0, sl=sl, at=at, m=m: e.scalar_tensor_tensor(out=at[:, m, :], in0=u0[:], scalar=1.0, in1=sl[:], op0=ALU.add, op1=ALU.mult),
                          reads=[u0n, sln], writes=[atn])
                if pend_down[0] is not None:
                    emit_down(*pend_down[0])
                pend_down[0] = (at, atn, wd, wdn, b, jh, ex)
        if pend_down[0] is not None:
            emit_down(*pend_down[0])
    R.barrier()
    if moe_stop == "C":
        return

    with contextlib.ExitStack() as sD:
        sb, ps = mk(sD)
        Ysb = sb("Ysb", [128, NE, D], BF16)
        PGs = sb("PGs", [128, NE, 512], BF16)
        Bdp = sb("Bdp", [128, D], BF16)
        Gpad = sb("Gpad", [128, 128], BF16)
        GTp_r = Rot([("GTp%d" % i, sb("GTp%d" % i, [128, 128], BF16)) for i in range(2)])
        Gg2 = sb("Gg2", [128, 4, NE])
        facc = sb("facc", [128, 4, D])
        lng = sb("lng2", [128, D])
        lnb = sb("lnb2", [128, D])
        g2p = sb("g2p", [128, D])
        R.dma("sp", lambda e: e.dma_start(out=lng[:], in_=ln_g[l, 1:2, :].broadcast_to([128, D])), writes=["lng2"])
        R.dma("sp", lambda e: e.dma_start(out=lnb[:], in_=ln_b[l, 1:2, :].broadcast_to([128, D])), writes=["lnb2"])
        R.pool(lambda e: e.memset(Bdp[:], 0.0), writes=["Bdp"])
        R.pool(lambda e: e.memset(Gpad[:], 0.0), writes=["Gpad"])
        R.dma("pool", lambda e: e.dma_start(out=Bdp[0:NE, :], in_=b_dn[l, :, :]), reads=["Bdp"], writes=["Bdp"])
        xc_r = Rot([("xd%d" % i, sb("xd%d" % i, [128, D])) for i in range(2)])
        mt_r = Rot([("md%d" % i, sb("md%d" % i, [128, D])) for i in range(2)])
        st_r = Rot([("std%d" % i, sb("std%d" % i, [128, 16])) for i in range(2)])
        pc_r = Rot([("pE%d" % i, ps("pE%d" % i, [128, 512])) for i in range(4)])
        pTd = ps("pTd", [128, 1024], BF16)
        for g in range(NB * 4):
            b, gq = g // 4, g % 4
            if gq == 0:
                R.dma("sp", lambda e, b=b: e.dma_start(out=g2p[:], in_=modbuf[b, l:l + 1, 5 * D:6 * D].broadcast_to([128, D])), reads=["modbuf"], writes=["g2p"])
                R.pool(lambda e: e.tensor_scalar(out=g2p[:], in0=g2p[:], scalar1=1.0, scalar2=None, op0=ALU.add), reads=["g2p"], writes=["g2p"])
            R.dma("sp", lambda e, b=b, gq=gq: e.dma_start(out=Gg2[:], in_=Gd[b, gq * 512:(gq + 1) * 512, :].rearrange("(c p) e -> p c e", p=128)),
                  reads=["Gd%d" % b], writes=["Gg2"])
            for jh in range(2):
                for q4 in range(4):
                    R.dma("sp", lambda e, g=g, q4=q4, jh=jh: e.dma_start(out=Ysb[:, q4 * 8:(q4 + 1) * 8, :], in_=Yd[g // 4][g % 4, jh, q4 * 8:(q4 + 1) * 8, :, :].rearrange("e j d -> j e d")),
                          reads=["Yd%d" % g], writes=["Ysb"])
                    R.dma("sp", lambda e, g=g, q4=q4, jh=jh: e.dma_start(out=PGs[:, q4 * 8:(q4 + 1) * 8, :], in_=PGd[g // 4][g % 4, jh, q4 * 8:(q4 + 1) * 8, :, :].rearrange("e j t -> j e t")),
                          reads=["PGd%d" % g], writes=["PGs"])
                for c in range(4):
                    tc = gq * 4 + c
                    if jh == 0:
                        gtn, GTp = GTp_r.next()
                        R.dve(lambda e, c=c: e.tensor_copy(out=Gpad[:, 0:NE], in_=Gg2[:, c, :]), reads=["Gg2", "Gpad"], writes=["Gpad"])
                        R.pe(lambda e: e.transpose(out=pTd[:, 0:128], in_=Gpad[:], identity=ident_b[:]), reads=["Gpad", "ident_b"], writes=["pTd"])
                        R.act(lambda e, GTp=GTp: e.copy(out=GTp[:], in_=pTd[:, 0:128]), reads=["pTd"], writes=[gtn])
                    else:
                        xcn, xc = xc_r.next()
                        mtn, mt = mt_r.next()
                        stn, stt = st_r.next()
                        R.dma("sp", lambda e, xc=xc, tc=tc, b=b: e.dma_start(out=xc[:], in_=xbuf[b, tc * 128:(tc + 1) * 128, :]), reads=["xb%d_%d" % (b, tc)], writes=[xcn])
                    for hf in range(2):
                        pn, pt = pc_r.next()
                        for ex in range(NE):
                            R.pe(lambda e, pt=pt, ex=ex, c=c, hf=hf: e.matmul(pt[:, :], lhsT=PGs[:, ex, c * 128:(c + 1) * 128], rhs=Ysb[:, ex, hf * 512:(hf + 1) * 512],
                                                                             start=(ex == 0), stop=(jh == 1 and ex == NE - 1)), reads=["PGs", "Ysb"], writes=[pn])
                        if jh == 0:
                            R.pe(lambda e, pt=pt, GTp=GTp, hf=hf: e.matmul(pt[:, :], lhsT=GTp[:], rhs=Bdp[:, hf * 512:(hf + 1) * 512], start=False, stop=True),
                                 reads=[gtn, "Bdp"], writes=[pn])
                            R.act(lambda e, pt=pt, c=c, hf=hf: e.copy(out=facc[:, c, hf * 512:(hf + 1) * 512], in_=pt[:, :]), reads=[pn], writes=["facc"])
                        else:
                            R.dve(lambda e, pt=pt, mt=mt, hf=hf, c=c: e.tensor_tensor(out=mt[:, hf * 512:(hf + 1) * 512], in0=pt[:, :], in1=facc[:, c, hf * 512:(hf + 1) * 512], op=ALU.add),
                                  reads=[pn, "facc"], writes=[mtn])
                    if jh == 1:
                        R.pool(lambda e, mt=mt: e.tensor_tensor(out=mt[:], in0=mt[:], in1=g2p[:], op=ALU.mult), reads=[mtn, "g2p"], writes=[mtn])
                        R.dve(lambda e, xc=xc, mt=mt: e.scalar_tensor_tensor(out=xc[:], in0=xc[:], scalar=ALPHA, in1=mt[:], op0=ALU.mult, op1=ALU.add),
                              reads=[xcn, mtn], writes=[xcn])
                        ln_chunk(xc, xcn, stt, stn)
                        R.pool(lambda e, xc=xc: e.tensor_tensor(out=xc[:], in0=xc[:], in1=lng[:], op=ALU.mult), reads=[xcn, "lng2"], writes=[xcn])
                        R.dve(lambda e, xc=xc: e.tensor_tensor(out=xc[:], in0=xc[:], in1=lnb[:], op=ALU.add), reads=[xcn, "lnb2"], writes=[xcn])
                        if last:
                            fin.append(R.dma("sp", lambda e, xc=xc, tc=tc, b=b: e.dma_start(out=out[b, tc * 128:(tc + 1) * 128, :], in_=xc[:]), reads=[xcn], writes=["out%d_%d" % (b, tc)]))
                        else:
                            R.dma("sp", lambda e, xc=xc, tc=tc, b=b: e.dma_start(out=xbuf[b, tc * 128:(tc + 1) * 128, :], in_=xc[:]), reads=[xcn], writes=["xb%d_%d" % (b, tc)])
    R.barrier()


N_CORES = 8
FUSED = True
DEPTH = 4
_PROG = {}


def _get_prog(NB, NL):
    key = (NB, NL)
    if key not in _PROG:
        _PROG[key] = build(NB, NL)
    return _PROG[key]


def _consts():
    hc = host_consts()
    hc.update(moe_consts())
    return {"k_" + k: v for k, v in hc.items()}


_PER_LAYER = ("ada_w", "ada_b", "w_in", "w_out", "ret_gn_w", "conv_w", "cmp_pos", "cmp_w1", "cmp_w2", "ln_g", "ln_b",
              "router_w", "router_b", "w_gate_up", "b_gate_up", "w_down", "b_down")


def kernel(**inputs):
    B = inputs["x"].shape[0]
    NB = B // N_CORES
    f32 = np.float32
    x = np.ascontiguousarray(inputs["x"], dtype=f32)
    c = np.ascontiguousarray(inputs["c"], dtype=f32)
    pos = np.ascontiguousarray(inputs["positions"], dtype=np.int32)
    consts = _consts()
    layer_sets = [list(range(DEPTH))] if FUSED else [[l] for l in range(DEPTH)]
    for ls in layer_sets:
        nc = _get_prog(NB, len(ls))
        w = {k: np.ascontiguousarray(np.asarray(inputs[k])[ls[0]:ls[-1] + 1], dtype=f32) for k in _PER_LAYER}
        in_maps = []
        for ci in range(N_CORES):
            m = dict(w)
            m.update(consts)
            m["x"] = np.ascontiguousarray(x[ci * NB:(ci + 1) * NB])
            m["c"] = np.ascontiguousarray(c[ci * NB:(ci + 1) * NB])
            m["positions"] = np.ascontiguousarray(pos[ci * NB:(ci + 1) * NB])
            in_maps.append(m)
        res = run_bass_kernel_spmd(nc, in_maps, core_ids=list(range(N_CORES)))
        x = np.concatenate([np.asarray(r["out"], dtype=f32) for r in res.results], axis=0)
    return x
```
